# Optimizing a Trainium2 kernel written in Bass

```python
import math
import jax
import jax.numpy as jnp
from jax import lax
import numpy as np

D_MODEL = 1024
BATCH = 16
SEQ = 2048
DEPTH = 4

N_MIXERS = 4
GROUP_WIDTH = D_MODEL // N_MIXERS
HEAD_DIM = 64
RET_HEADS = GROUP_WIDTH // HEAD_DIM
RET_CHUNK = 128
S5_CH = 16
S5_GROUPS = GROUP_WIDTH // S5_CH
S5_STATE = 64
S5_DT_MIN = 0.001
S5_DT_MAX = 0.1
SGU_GROUPS = 4
SGU_CHUNK = 128
NSA_HEADS = GROUP_WIDTH // HEAD_DIM
NSA_KV_HEADS = 2
NSA_GQA = NSA_HEADS // NSA_KV_HEADS
NSA_KV_WIDTH = NSA_KV_HEADS * HEAD_DIM
CMP_STRIDE = 16
CMP_BLOCK = 2 * CMP_STRIDE
CMP_HIDDEN = 128
SEL_BLOCK = 64
SEL_TOPK = 8
FORCE_BONUS = 1e3
WIN = 512
SEL_Q_BLOCK = 64
WIN_Q_BLOCK = 128
D_FF = 2816
CONV_W = 3
ROPE_THETA = 10000.0
EPS = 1e-6
NEG = -1e9
IN_COLS_LIST = [GROUP_WIDTH] * 4 + [GROUP_WIDTH] + [GROUP_WIDTH] * 2 + [GROUP_WIDTH] + [NSA_KV_WIDTH] * 6 + [3 * NSA_HEADS]
IN_COLS = 8 * GROUP_WIDTH + 6 * NSA_KV_WIDTH + 3 * NSA_HEADS

kernel_name = 'hybrid_parallel_group_decoder'


def rmsnorm(x, g):
    xf = x.astype(jnp.float32)
    y = xf * lax.rsqrt(jnp.mean(xf * xf, axis=-1, keepdims=True) + EPS)
    return (y * g.astype(jnp.float32)).astype(x.dtype)


def rope_tables(pos):
    inv = ROPE_THETA ** (-jnp.arange(0, HEAD_DIM, 2, dtype=jnp.float32) / HEAD_DIM)
    ang = pos.astype(jnp.float32)[:, None] * inv[None, :]
    return jnp.cos(ang), jnp.sin(ang)


def rope(x, cos, sin):
    half = x.shape[-1] // 2
    x1, x2 = x[..., :half], x[..., half:]
    cos = cos.astype(x.dtype)
    sin = sin.astype(x.dtype)
    return jnp.concatenate([x1 * cos - x2 * sin, x2 * cos + x1 * sin], axis=-1)


def retention(q, k, v, g, norm_g, cos, sin):
    B, S, _ = q.shape
    H, dh, C = RET_HEADS, HEAD_DIM, RET_CHUNK
    NC = S // C
    dt = q.dtype

    def heads(t):
        return t.reshape(B, S, H, dh).transpose(0, 2, 1, 3)

    qh = rope(heads(q), cos, sin) * (dh ** -0.5)
    kh = rope(heads(k), cos, sin)
    vh = heads(v)
    log_gamma = jnp.log1p(-(2.0 ** (-5.0 - jnp.arange(H, dtype=jnp.float32))))
    idx = jnp.arange(C, dtype=jnp.float32)
    diff = idx[:, None] - idx[None, :]
    decay_in = jnp.where(diff >= 0, jnp.exp(log_gamma[:, None, None] * jnp.maximum(diff, 0.0)), 0.0).astype(dt)
    zeta = jnp.exp(log_gamma[:, None] * (C - 1 - idx)[None, :]).astype(dt)
    xi = jnp.exp(log_gamma[:, None] * (idx + 1)[None, :]).astype(dt)
    gamma_chunk = jnp.exp(log_gamma * C)
    qc = qh.reshape(B, H, NC, C, dh)
    kc = kh.reshape(B, H, NC, C, dh)
    vc = vh.reshape(B, H, NC, C, dh)
    scores = jnp.einsum('bhnid,bhnjd->bhnij', qc, kc) * decay_in[:, None]
    o_inner = jnp.einsum('bhnij,bhnje->bhnie', scores, vc)
    U = jnp.einsum('bhnjd,bhnje->nbhde', kc * zeta[:, None, :, None], vc)

    def step(R, u):
        new = (gamma_chunk[None, :, None, None] * R + u).astype(R.dtype)
        return new, R

    _, R_prev = lax.scan(step, jnp.zeros_like(U[0]), U)
    o_cross = jnp.einsum('bhnid,nbhde->bhnie', qc, R_prev) * xi[:, None, :, None]
    o = (o_inner + o_cross).reshape(B, H, S, dh)
    o = rmsnorm(o, norm_g.reshape(H, 1, dh))
    o = o.transpose(0, 2, 1, 3).reshape(B, S, H * dh)
    return (jax.nn.silu(g) * o).astype(dt)


def s5_layer(u, lam_re, lam_im, log_dt, b_re, b_im, c_re, c_im, d, glu_w, glu_b):
    B, S, _ = u.shape
    G, Hc = S5_GROUPS, S5_CH
    f32 = jnp.float32
    ug = u.reshape(B, S, G, Hc)
    dts = jnp.exp(log_dt.astype(f32))[:, None]
    lr = lam_re.astype(f32)
    li = lam_im.astype(f32)
    mag = jnp.exp(lr * dts)
    a_re = mag * jnp.cos(li * dts)
    a_im = mag * jnp.sin(li * dts)
    den = lr * lr + li * li
    f_re = ((a_re - 1.0) * lr + a_im * li) / den
    f_im = (a_im * lr - (a_re - 1.0) * li) / den
    bu_re = jnp.einsum('bsgh,gph->bsgp', ug, b_re).astype(f32)
    bu_im = jnp.einsum('bsgh,gph->bsgp', ug, b_im).astype(f32)
    x_re = f_re * bu_re - f_im * bu_im
    x_im = f_re * bu_im + f_im * bu_re
    a_re_b = jnp.broadcast_to(a_re, x_re.shape)
    a_im_b = jnp.broadcast_to(a_im, x_re.shape)

    def combine(e1, e2):
        a1r, a1i, b1r, b1i = e1
        a2r, a2i, b2r, b2i = e2
        return (a2r * a1r - a2i * a1i, a2r * a1i + a2i * a1r,
                a2r * b1r - a2i * b1i + b2r, a2r * b1i + a2i * b1r + b2i)

    _, _, h_re, h_im = lax.associative_scan(combine, (a_re_b, a_im_b, x_re, x_im), axis=1)
    y = jnp.einsum('bsgp,ghp->bsgh', h_re, c_re.astype(f32)) - jnp.einsum('bsgp,ghp->bsgh', h_im, c_im.astype(f32))
    y = y.reshape(B, S, GROUP_WIDTH) + d.astype(f32) * u.astype(f32)
    y = jax.nn.gelu(y)
    out = y * jax.nn.sigmoid(y @ glu_w.astype(f32) + glu_b.astype(f32))
    return out.astype(u.dtype)


def spatial_gating(u, v, norm_g, w_s, b_s):
    B, S, _ = u.shape
    G, C = SGU_GROUPS, SGU_CHUNK
    dg = GROUP_WIDTH // G
    u = jax.nn.gelu(u)
    v = rmsnorm(jax.nn.gelu(v).reshape(B, S, G, dg), norm_g.reshape(G, dg))
    vc = v.reshape(B, S // C, C, G, dg)
    mask = jnp.tril(jnp.ones((C, C), dtype=bool))
    w = jnp.where(mask, w_s, 0.0).astype(v.dtype)
    s = jnp.einsum('gts,bnsgd->bntgd', w, vc) + b_s.T[None, None, :, :, None]
    return (u * s.reshape(B, S, GROUP_WIDTH)).astype(u.dtype)


def nsa(q, kc, vc, ks, vs, kw, vw, gate_logits, q_g, k_g, cmp_pe, cmp_w1, cmp_w2, cos, sin):
    B, S, _ = q.shape
    KVH, G, dh = NSA_KV_HEADS, NSA_GQA, HEAD_DIM
    f32 = jnp.float32
    pos = jnp.arange(S)
    qh = q.reshape(B, S, KVH, G, dh).transpose(0, 2, 3, 1, 4)
    qh = rope(rmsnorm(qh, q_g), cos, sin) * (dh ** -0.5)

    def kv_heads(t):
        return t.reshape(B, S, KVH, dh).transpose(0, 2, 1, 3)

    n_cmp = S // CMP_STRIDE - 1

    def compress(t, pe, w1, w2):
        halves = kv_heads(t).reshape(B, KVH, S // CMP_STRIDE, CMP_STRIDE, dh)
        blocks = jnp.concatenate([halves[:, :, :-1], halves[:, :, 1:]], axis=3) + pe
        hid = jax.nn.gelu(blocks.reshape(B, KVH, n_cmp, CMP_BLOCK * dh) @ w1)
        return hid @ w2

    cmp_end = jnp.arange(n_cmp) * CMP_STRIDE + CMP_BLOCK - 1
    cos_c, sin_c = rope_tables(cmp_end)
    k_cmp = rope(rmsnorm(compress(kc, cmp_pe[0], cmp_w1[0], cmp_w2[0]), k_g[0]), cos_c, sin_c)
    v_cmp = compress(vc, cmp_pe[1], cmp_w1[1], cmp_w2[1])
    s_c = jnp.einsum('bhgqd,bhnd->bhgqn', qh, k_cmp).astype(f32)
    valid_c = cmp_end[None, :] <= pos[:, None]
    p_c = jax.nn.softmax(jnp.where(valid_c, s_c, NEG), axis=-1) * valid_c
    o_cmp = jnp.einsum('bhgqn,bhnd->bhgqd', p_c.astype(v_cmp.dtype), v_cmp)

    n_sel = S // SEL_BLOCK
    ii = np.arange(n_cmp)[:, None]
    jj = np.arange(n_sel)[None, :]
    overlap = ((ii * CMP_STRIDE < (jj + 1) * SEL_BLOCK) & (ii * CMP_STRIDE + CMP_BLOCK > jj * SEL_BLOCK)).astype(np.float32)
    imp = jnp.einsum('bhgqn,nj->bhqj', p_c, jnp.asarray(overlap))
    blk = jnp.arange(n_sel)[None, :]
    cur = (pos // SEL_BLOCK)[:, None]
    forced = (blk == 0) | (blk == cur) | (blk == cur - 1)
    imp = jnp.where(forced, imp + FORCE_BONUS, imp)
    imp = jnp.where(blk <= cur, imp, NEG)
    k_top = min(SEL_TOPK, n_sel)
    _, sel_idx = lax.top_k(imp, k_top)

    k_sel = rope(rmsnorm(kv_heads(ks), k_g[1]), cos, sin).reshape(B, KVH, n_sel, SEL_BLOCK, dh)
    v_sel = kv_heads(vs).reshape(B, KVH, n_sel, SEL_BLOCK, dh)
    nqb = S // SEL_Q_BLOCK
    q_b = jnp.moveaxis(qh.reshape(B, KVH, G, nqb, SEL_Q_BLOCK, dh), 3, 0)
    idx_b = jnp.moveaxis(sel_idx.reshape(B, KVH, nqb, SEL_Q_BLOCK, k_top), 2, 0)
    pos_b = pos.reshape(nqb, SEL_Q_BLOCK)
    b_i = jnp.arange(B)[:, None, None, None]
    h_i = jnp.arange(KVH)[None, :, None, None]

    def sel_block(args):
        qb, ib, tb = args
        kg = k_sel[b_i, h_i, ib]
        vg = v_sel[b_i, h_i, ib]
        s = jnp.einsum('bhgqd,bhqnkd->bhgqnk', qb, kg).astype(f32)
        kpos = ib[..., None] * SEL_BLOCK + jnp.arange(SEL_BLOCK)
        ok = (kpos <= tb[None, None, :, None, None])[:, :, None]
        s = jnp.where(ok, s, NEG).reshape(B, KVH, G, SEL_Q_BLOCK, k_top * SEL_BLOCK)
        p = jax.nn.softmax(s, axis=-1).reshape(B, KVH, G, SEL_Q_BLOCK, k_top, SEL_BLOCK)
        return jnp.einsum('bhgqnk,bhqnkd->bhgqd', p.astype(vg.dtype), vg)

    o_sel = lax.map(sel_block, (q_b, idx_b, pos_b))
    o_sel = jnp.moveaxis(o_sel, 0, 3).reshape(B, KVH, G, S, dh)

    k_win = rope(rmsnorm(kv_heads(kw), k_g[2]), cos, sin)
    v_win = kv_heads(vw)
    pad = ((0, 0), (0, 0), (WIN, 0), (0, 0))
    kp = jnp.pad(k_win, pad)
    vp = jnp.pad(v_win, pad)
    nqw = S // WIN_Q_BLOCK
    band = jnp.arange(nqw)[:, None] * WIN_Q_BLOCK + jnp.arange(WIN + WIN_Q_BLOCK)[None, :]
    kb = kp[:, :, band]
    vb = vp[:, :, band]
    qw = qh.reshape(B, KVH, G, nqw, WIN_Q_BLOCK, dh)
    s_w = jnp.einsum('bhgnqd,bhnkd->bhgnqk', qw, kb).astype(f32)
    tq = pos.reshape(nqw, WIN_Q_BLOCK)
    kpos = band - WIN
    dist = tq[:, :, None] - kpos[:, None, :]
    ok_w = (dist >= 0) & (dist < WIN) & (kpos[:, None, :] >= 0)
    p_w = jax.nn.softmax(jnp.where(ok_w, s_w, NEG), axis=-1)
    o_win = jnp.einsum('bhgnqk,bhnkd->bhgnqd', p_w.astype(vb.dtype), vb).reshape(B, KVH, G, S, dh)

    gates = jax.nn.sigmoid(gate_logits.reshape(B, S, KVH, G, 3).transpose(0, 2, 3, 1, 4))
    o = gates[..., 0:1] * o_cmp + gates[..., 1:2] * o_sel + gates[..., 2:3] * o_win
    return o.transpose(0, 3, 1, 2, 4).reshape(B, S, GROUP_WIDTH).astype(q.dtype)


def conv_ffn(h, w_up, conv_w, conv_b, w_down):
    a = h @ w_up
    ch = a.shape[-1]
    a = lax.conv_general_dilated(a, conv_w.reshape(CONV_W, 1, ch).astype(a.dtype), window_strides=(1,),
                                 padding=((CONV_W - 1, 0),), dimension_numbers=('NWC', 'WIO', 'NWC'),
                                 feature_group_count=ch) + conv_b
    gate, up = jnp.split(a, 2, axis=-1)
    return (jax.nn.silu(gate) * up) @ w_down


def setup_inputs(seed: int = 0) -> dict:
    key = jax.random.key(seed)
    ks = iter(jax.random.split(key, 40))
    L = DEPTH

    def nrm(shape, scale):
        return jax.random.normal(next(ks), shape, jnp.float32) * scale

    def gain(shape):
        return 1.0 + nrm(shape, 0.05)

    n_idx = jnp.arange(S5_STATE, dtype=jnp.float32)
    x = nrm((BATCH, SEQ, D_MODEL), 1.0)
    c = nrm((BATCH, D_MODEL), 1.0)
    norm1_g = gain((L, D_MODEL))
    norm2_g = gain((L, D_MODEL))
    ada_w = nrm((L, D_MODEL, 6 * D_MODEL), 0.5 * D_MODEL ** -0.5)
    ada_b = nrm((L, 6 * D_MODEL), 0.01)
    w_in = nrm((L, D_MODEL, IN_COLS), D_MODEL ** -0.5)
    ret_norm_g = gain((L, GROUP_WIDTH))
    s5_lambda_re = -0.5 + nrm((L, S5_GROUPS, S5_STATE), 0.01)
    s5_lambda_im = math.pi * n_idx + nrm((L, S5_GROUPS, S5_STATE), 0.01)
    s5_log_dt = jax.random.uniform(next(ks), (L, S5_GROUPS), jnp.float32, math.log(S5_DT_MIN), math.log(S5_DT_MAX))
    s5_b_re = nrm((L, S5_GROUPS, S5_STATE, S5_CH), (2 * S5_CH) ** -0.5)
    s5_b_im = nrm((L, S5_GROUPS, S5_STATE, S5_CH), (2 * S5_CH) ** -0.5)
    s5_c_re = nrm((L, S5_GROUPS, S5_CH, S5_STATE), (2 * S5_STATE) ** -0.5)
    s5_c_im = nrm((L, S5_GROUPS, S5_CH, S5_STATE), (2 * S5_STATE) ** -0.5)
    s5_d = nrm((L, GROUP_WIDTH), 1.0)
    s5_glu_w = nrm((L, GROUP_WIDTH, GROUP_WIDTH), GROUP_WIDTH ** -0.5)
    s5_glu_b = nrm((L, GROUP_WIDTH), 0.01)
    sgu_norm_g = gain((L, GROUP_WIDTH))
    sgu_w = nrm((L, SGU_GROUPS, SGU_CHUNK, SGU_CHUNK), 0.05)
    sgu_b = 1.0 + nrm((L, SGU_GROUPS, SGU_CHUNK), 0.05)
    nsa_q_norm_g = gain((L, HEAD_DIM))
    nsa_k_norm_g = gain((L, 3, HEAD_DIM))
    nsa_cmp_pe = nrm((L, 2, CMP_BLOCK, HEAD_DIM), 0.1)
    nsa_cmp_w1 = nrm((L, 2, CMP_BLOCK * HEAD_DIM, CMP_HIDDEN), (CMP_BLOCK * HEAD_DIM) ** -0.5)
    nsa_cmp_w2 = nrm((L, 2, CMP_HIDDEN, HEAD_DIM), CMP_HIDDEN ** -0.5)
    mix_norm_g = gain((L, D_MODEL))
    w_out = nrm((L, D_MODEL, D_MODEL), D_MODEL ** -0.5)
    ffn_w_up = nrm((L, D_MODEL, 2 * D_FF), D_MODEL ** -0.5)
    ffn_conv_w = nrm((L, CONV_W, 2 * D_FF), CONV_W ** -0.5)
    ffn_conv_b = nrm((L, 2 * D_FF), 0.01)
    ffn_w_down = nrm((L, D_FF, D_MODEL), D_FF ** -0.5)
    return {'x': x, 'c': c, 'norm1_g': norm1_g, 'norm2_g': norm2_g, 'ada_w': ada_w, 'ada_b': ada_b,
            'w_in': w_in, 'ret_norm_g': ret_norm_g,
            's5_lambda_re': s5_lambda_re, 's5_lambda_im': s5_lambda_im, 's5_log_dt': s5_log_dt,
            's5_b_re': s5_b_re, 's5_b_im': s5_b_im, 's5_c_re': s5_c_re, 's5_c_im': s5_c_im,
            's5_d': s5_d, 's5_glu_w': s5_glu_w, 's5_glu_b': s5_glu_b,
            'sgu_norm_g': sgu_norm_g, 'sgu_w': sgu_w, 'sgu_b': sgu_b,
            'nsa_q_norm_g': nsa_q_norm_g, 'nsa_k_norm_g': nsa_k_norm_g, 'nsa_cmp_pe': nsa_cmp_pe,
            'nsa_cmp_w1': nsa_cmp_w1, 'nsa_cmp_w2': nsa_cmp_w2,
            'mix_norm_g': mix_norm_g, 'w_out': w_out,
            'ffn_w_up': ffn_w_up, 'ffn_conv_w': ffn_conv_w, 'ffn_conv_b': ffn_conv_b, 'ffn_w_down': ffn_w_down}


def reference(x, c, norm1_g, norm2_g, ada_w, ada_b, w_in, ret_norm_g,
              s5_lambda_re, s5_lambda_im, s5_log_dt, s5_b_re, s5_b_im, s5_c_re, s5_c_im,
              s5_d, s5_glu_w, s5_glu_b, sgu_norm_g, sgu_w, sgu_b,
              nsa_q_norm_g, nsa_k_norm_g, nsa_cmp_pe, nsa_cmp_w1, nsa_cmp_w2,
              mix_norm_g, w_out, ffn_w_up, ffn_conv_w, ffn_conv_b, ffn_w_down):
    B, S, D = x.shape
    cos, sin = rope_tables(jnp.arange(S))
    splits = [int(v) for v in np.cumsum(IN_COLS_LIST)[:-1]]
    c_act = jax.nn.silu(c)
    for l in range(DEPTH):
        mod = c_act @ ada_w[l] + ada_b[l]
        sh1, sc1, g1, sh2, sc2, g2 = [m[:, None, :] for m in jnp.split(mod, 6, axis=-1)]
        h = rmsnorm(x, norm1_g[l]) * (1.0 + sc1) + sh1
        proj = h @ w_in[l]
        (rq, rk, rv, rg, su, gu, gv, nq, nkc, nvc, nks, nvs, nkw, nvw, ngate) = jnp.split(proj, splits, axis=-1)
        y_ret = retention(rq, rk, rv, rg, ret_norm_g[l], cos, sin)
        y_s5 = s5_layer(su, s5_lambda_re[l], s5_lambda_im[l], s5_log_dt[l], s5_b_re[l], s5_b_im[l],
                        s5_c_re[l], s5_c_im[l], s5_d[l], s5_glu_w[l], s5_glu_b[l])
        y_sgu = spatial_gating(gu, gv, sgu_norm_g[l], sgu_w[l], sgu_b[l])
        y_nsa = nsa(nq, nkc, nvc, nks, nvs, nkw, nvw, ngate, nsa_q_norm_g[l], nsa_k_norm_g[l],
                    nsa_cmp_pe[l], nsa_cmp_w1[l], nsa_cmp_w2[l], cos, sin)
        y = jnp.concatenate([y_ret, y_s5, y_sgu, y_nsa], axis=-1).reshape(B, S, N_MIXERS, GROUP_WIDTH)
        y = rmsnorm(y, mix_norm_g[l].reshape(N_MIXERS, GROUP_WIDTH)).reshape(B, S, D)
        x = x + g1 * (y @ w_out[l])
        h = rmsnorm(x, norm2_g[l]) * (1.0 + sc2) + sh2
        x = x + g2 * conv_ffn(h, ffn_w_up[l], ffn_conv_w[l], ffn_conv_b[l], ffn_w_down[l])
    return x
```

```python
import math
import os
from contextlib import ExitStack

import numpy as np
import concourse.bass as bass
import concourse.mybir as mybir
from concourse.bass_utils import run_bass_kernel_spmd

F32 = mybir.dt.float32
BF16 = mybir.dt.bfloat16
AF = mybir.ActivationFunctionType
ALU = mybir.AluOpType
AX = mybir.AxisListType

S = 2048
D = 1024
KC = 8
NT = 16
DEPTH = 4
DFF = 2816
NFF = 22
INC = 2828
EPS = 1e-6
NEGB = -30000.0


class Tok:
    __slots__ = ("name", "w", "r", "dsem", "dtot", "dw", "uid", "psum")
    _n = [0]

    def __init__(self, name):
        Tok._n[0] += 1
        self.uid = Tok._n[0]
        self.psum = False
        self.name = name
        self.w = None
        self.r = {}
        self.dsem = None
        self.dtot = 0
        self.dw = 0


class V:
    __slots__ = ("ap", "toks")

    def __init__(self, ap, toks):
        self.ap = ap
        self.toks = toks

    def __getitem__(self, idx):
        return V(self.ap[idx], self.toks)

    def m(self, fn):
        return V(fn(self.ap), self.toks)

    def bc(self, shape):
        return V(self.ap.to_broadcast(shape), self.toks)

    def re(self, pat, **kw):
        return V(self.ap.rearrange(pat, **kw), self.toks)

    def bitcast(self, dt):
        return V(self.ap.bitcast(dt), self.toks)


class TK:
    def __init__(self, t, i):
        self.t = t
        self.i = i

    def __getitem__(self, idx):
        return V(self.t.h[idx], [self.t.toks[self.i]])


class T:
    def __init__(self, handle, name, ntok=1):
        self.h = handle
        self.toks = [Tok("%s.%d" % (name, i)) for i in range(ntok)]

    def __getitem__(self, idx):
        return V(self.h[idx], self.toks)

    def k(self, i):
        return TK(self, i)


class Eng:
    def __init__(self, name, obj):
        self.name = name
        self.obj = obj
        self.sems = []
        self.gen = -1
        self.cnt = 0
        self.known = {}
        self.dknown = {}


class Prog:
    ROT = 30000

    def __init__(self, nc, ctx):
        self.nc = nc
        self.ctx = ctx
        self.eng = {
            "pe": Eng("pe", nc.tensor),
            "act": Eng("act", nc.scalar),
            "dve": Eng("dve", nc.vector),
            "pool": Eng("pool", nc.gpsimd),
            "sp": Eng("sp", nc.sync),
        }
        self.nsem = 0
        for e in self.eng.values():
            self._newsem(e)
        self.nins = 0
        self.dfree = []
        self.dlive = {}

    def _sem(self, name):
        self.nsem += 1
        return self.ctx.enter_context(self.nc.semaphore("%s_%d" % (name, self.nsem)))

    def _newsem(self, e):
        e.sems.append(self._sem(e.name))
        e.gen += 1
        e.cnt = 0

    def sb(self, name, shape, dt, ntok=1):
        h = self.ctx.enter_context(self.nc.sbuf_tensor("sb_" + name, list(shape), dt))
        return T(h, name, ntok)

    def psum(self, name, shape, dt):
        h = self.ctx.enter_context(self.nc.psum_tensor(name, list(shape), dt))
        t = T(h, name, 1)
        t.toks[0].psum = True
        return t

    def _wait_ticket(self, E, tk):
        if tk is None:
            return
        e2, gen, val = tk
        if e2 is E and E.name == "pe":
            return
        k = E.known.get(e2.name)
        if k is not None and k >= (gen, val):
            return
        E.obj.wait_ge(e2.sems[gen], val)
        E.known[e2.name] = (gen, val)

    def _wait_dma(self, E, t, val):
        if val <= 0 or t.dsem is None:
            return
        if E.dknown.get(t.uid, 0) >= val:
            return
        E.obj.wait_ge(t.dsem, val)
        E.dknown[t.uid] = val

    def _deps(self, E, reads, writes):
        for t in reads:
            self._wait_ticket(E, t.w)
            self._wait_dma(E, t, t.dw)
            if t.psum:
                for en2, tk in t.r.items():
                    if en2 != E.name:
                        self._wait_ticket(E, tk)
        for t in writes:
            self._wait_ticket(E, t.w)
            for tk in t.r.values():
                self._wait_ticket(E, tk)
            self._wait_dma(E, t, t.dtot)

    def op(self, en, fn, reads=(), writes=()):
        E = self.eng[en]
        self._deps(E, reads, writes)
        ins = fn(E.obj)
        if E.cnt >= self.ROT:
            self._newsem(E)
        E.cnt += 1
        ins.then_inc(E.sems[E.gen], 1)
        tk = (E, E.gen, E.cnt)
        for t in reads:
            t.r[en] = tk
        for t in writes:
            t.w = tk
            t.r = {}
        self.nins += 1
        return ins

    def dma(self, qn, out, in_, reads=(), writes=(), **kw):
        E = self.eng[qn]
        self._deps(E, reads, writes)
        ins = E.obj.dma_start(out=out, in_=in_, **kw)
        t = (list(writes) + list(reads))[0]
        if t.dsem is None:
            if self.dfree:
                t.dsem, base = self.dfree.pop()
            else:
                t.dsem, base = self._sem("d"), 0
            t.dtot = base
            t.dw = 0
            self.dlive[t.uid] = t
        ins.then_inc(t.dsem, 16)
        t.dtot += 16
        if writes:
            t.dw = t.dtot
        self.nins += 1
        return ins

    def barrier(self, toks=()):
        SP = self.eng["sp"]
        for t in self.dlive.values():
            self._wait_dma(SP, t, t.dtot)
        self.op("sp", lambda e: e.nop(), reads=(), writes=())
        last = {n: (e, e.gen, e.cnt) for n, e in self.eng.items()}
        for n, E in self.eng.items():
            for n2, tk in last.items():
                if tk[2] > 0:
                    self._wait_ticket(E, tk)
        for t in self.dlive.values():
            self.dfree.append((t.dsem, t.dtot))
            t.dsem = None
            t.dtot = 0
            t.dw = 0
        self.dlive = {}
        for E in self.eng.values():
            E.dknown = {}

    def finish(self, toks):
        E = self.eng["sp"]
        for t in toks:
            self._wait_dma(E, t, t.dtot)

    @staticmethod
    def _tk(*vs):
        out = []
        for v in vs:
            if isinstance(v, V):
                out.extend(v.toks)
        return out

    @staticmethod
    def _a(v):
        return v.ap if isinstance(v, V) else v

    def mm(self, out, lhsT, rhs, start=True, stop=True, **kw):
        return self.op("pe", lambda e: e.matmul(out.ap, lhsT=lhsT.ap, rhs=rhs.ap, start=start, stop=stop, **kw),
                       reads=self._tk(lhsT, rhs), writes=out.toks)

    def tr(self, out, in_, ident):
        return self.op("pe", lambda e: e.transpose(out.ap, in_.ap, ident.ap),
                       reads=self._tk(in_, ident), writes=out.toks)

    def act(self, out, in_, func, bias=None, scale=None, accum_out=None):
        kw = {}
        if bias is not None:
            kw["bias"] = self._a(bias)
        if scale is not None:
            kw["scale"] = self._a(scale)
        if accum_out is not None:
            kw["accum_out"] = accum_out.ap
        return self.op("act", lambda e: e.activation(out=out.ap, in_=in_.ap, func=func, **kw),
                       reads=self._tk(in_, bias, scale), writes=self._tk(out, accum_out))

    def tt(self, en, out, a, b, op):
        return self.op(en, lambda e: e.tensor_tensor(out=out.ap, in0=a.ap, in1=b.ap, op=op),
                       reads=self._tk(a, b), writes=out.toks)

    def ts(self, en, out, a, s1, op0, s2=None, op1=None, accum_out=None):
        kw = {}
        if op1 is not None:
            kw["op1"] = op1
        if accum_out is not None:
            kw["accum_out"] = accum_out.ap
        return self.op(en, lambda e: e.tensor_scalar(out=out.ap, in0=a.ap, scalar1=self._a(s1), scalar2=self._a(s2),
                                                     op0=op0, **kw),
                       reads=self._tk(a, s1, s2), writes=self._tk(out, accum_out))

    def stt(self, out, a, scalar, b, op0, op1, accum_out=None):
        kw = {}
        if accum_out is not None:
            kw["accum_out"] = accum_out.ap
        return self.op("dve", lambda e: e.scalar_tensor_tensor(out=out.ap, in0=a.ap, scalar=self._a(scalar), in1=b.ap,
                                                               op0=op0, op1=op1, **kw),
                       reads=self._tk(a, scalar, b), writes=self._tk(out, accum_out))

    def copy(self, en, out, in_):
        if en == "act":
            return self.op("act", lambda e: e.copy(out=out.ap, in_=in_.ap), reads=in_.toks, writes=out.toks)
        return self.op(en, lambda e: e.tensor_copy(out=out.ap, in_=in_.ap), reads=in_.toks, writes=out.toks)

    def red(self, out, in_, op, axis=AX.X):
        return self.op("dve", lambda e: e.tensor_reduce(out=out.ap, in_=in_.ap, axis=axis, op=op),
                       reads=in_.toks, writes=out.toks)

    def memset(self, en, out, val):
        return self.op(en, lambda e: e.memset(out.ap, val), reads=(), writes=out.toks)

    def recip(self, out, in_):
        return self.op("dve", lambda e: e.reciprocal(out=out.ap, in_=in_.ap), reads=in_.toks, writes=out.toks)

    def load(self, dst, src_ap, q="sp", **kw):
        return self.dma(q, dst.ap, src_ap, writes=dst.toks, **kw)

    def store(self, dst_ap, src, q="sp", **kw):
        return self.dma(q, dst_ap, src.ap, reads=src.toks, **kw)


def host_consts():
    c = {}
    pos = np.arange(S, dtype=np.float64)
    inv = 10000.0 ** (-np.arange(0, 64, 2, dtype=np.float64) / 64)
    ang = (pos[:, None].astype(np.float32) * inv[None, :].astype(np.float32)).astype(np.float32)
    cos = np.cos(ang).astype(np.float32)
    sin = np.sin(ang).astype(np.float32)
    cs = np.concatenate([cos, sin], axis=1)
    c["cs_tab"] = np.ascontiguousarray(cs.reshape(NT, 128, 64).transpose(1, 0, 2))
    H = 4
    lg = np.log1p(-(2.0 ** (-5.0 - np.arange(H, dtype=np.float64))))
    idx = np.arange(128, dtype=np.float64)
    diff = idx[None, :] - idx[:, None]
    dec = np.where(diff >= 0, np.exp(lg[:, None, None] * np.maximum(diff, 0.0)), 0.0) * 0.125
    c["decT"] = np.ascontiguousarray(dec.transpose(1, 0, 2)).astype(np.float32)
    xi = np.exp(lg[:, None] * (idx + 1)[None, :]) * 0.125
    xit = np.zeros((128, 2, 128), np.float32)
    for h in range(H):
        xit[(h % 2) * 64:(h % 2) * 64 + 64, h // 2, :] = xi[h][None, :]
    c["xi_tab"] = xit
    zeta = np.exp(lg[:, None] * (127 - idx)[None, :])
    c["zeta_tab"] = np.ascontiguousarray(zeta.T).astype(np.float32)
    gch = np.exp(lg * 128)
    gct = np.zeros((128, 2), np.float32)
    for h in range(H):
        gct[(h % 2) * 64:(h % 2) * 64 + 64, h // 2] = gch[h]
    c["gc_tab"] = gct
    c["tril"] = np.tril(np.ones((128, 128), np.float32))
    c["triu"] = np.triu(np.ones((128, 128), np.float32))
    m8 = np.zeros((128, 8), np.float32)
    for g8 in range(8):
        m8[g8 * 16:(g8 + 1) * 16, g8] = 1.0
    c["mask8"] = m8
    c["iota129"] = np.broadcast_to(np.arange(129, dtype=np.float32)[None, :], (128, 129)).copy()
    kk_ = np.arange(128)[:, None]
    qq_ = np.arange(128)[None, :]
    nt = np.zeros((128, 2, 128), np.float32)
    nt[:, 0, :] = np.where(kk_ <= qq_, 0.0, NEGB)
    nt[:, 1, :] = np.where(kk_ > qq_, 0.0, NEGB)
    c["negT"] = nt
    nn = np.arange(128)[:, None]
    qpos = np.arange(S)[None, :]
    c["negvalid"] = np.where(16 * nn + 31 <= qpos, 0.0, NEGB).astype(np.float32)
    em = np.zeros((32, 16, 128), np.float32)
    for kb in range(16):
        em[2 * kb, kb, 0:64] = 1.0
        em[2 * kb + 1, kb, 64:128] = 1.0
    c["emat"] = em
    posf = np.arange(S)
    cur = posf // 64
    blk = np.arange(32)[None, :]
    bon = np.zeros((S, 32), np.float32)
    forced = (blk == 0) | (blk == cur[:, None]) | (blk == cur[:, None] - 1)
    bon[forced] = 1e3
    bon[np.broadcast_to(blk, (S, 32)) > cur[:, None]] = -1e9
    c["bonus"] = np.ascontiguousarray(bon.reshape(NT, 128, 32).transpose(1, 0, 2))
    ii = np.arange(127)[:, None]
    jj = np.arange(32)[None, :]
    ovl = ((ii * 16 < (jj + 1) * 64) & (ii * 16 + 32 > jj * 64)).astype(np.float32)
    o1 = np.zeros((128, 33), np.float32)
    o1[:127, :32] = ovl
    o1[:127, 32] = 1.0
    c["ovl1"] = o1
    cend = (np.arange(127) * 16 + 31).astype(np.float32)
    angc = (cend[:, None] * inv[None, :].astype(np.float32)).astype(np.float32)
    csc = np.zeros((128, 64), np.float32)
    csc[:127, :32] = np.cos(angc)
    csc[:127, 32:] = np.sin(angc)
    c["cs_c"] = csc
    sx = np.zeros((128, 2), np.float32)
    sx[:, 0] = np.arange(128)
    sx[:, 1] = -np.arange(128)
    c["sidx"] = sx
    c["ident"] = np.eye(128, dtype=np.float32)
    c["ones"] = np.ones((128, 128), np.float32)
    return c


CONST_SHAPES = {"cs_tab": (128, NT, 64), "decT": (128, 4, 128), "xi_tab": (128, 2, 128), "zeta_tab": (128, 4),
                "gc_tab": (128, 2), "ident": (128, 128), "ones": (128, 128), "tril": (128, 128),
                "triu": (128, 128), "mask8": (128, 8), "iota129": (128, 129), "sidx": (128, 2),
                "negT": (128, 2, 128), "negvalid": (128, S), "emat": (32, 16, 128), "bonus": (128, NT, 32),
                "ovl1": (128, 33), "cs_c": (128, 64)}

WEIGHT_SHAPES = {
    "norm1_g": (DEPTH, D), "norm2_g": (DEPTH, D), "ada_w": (DEPTH, D, 6 * D), "ada_b": (DEPTH, 6 * D),
    "w_in": (DEPTH, D, INC), "ret_norm_g": (DEPTH, 256),
    "s5_lambda_re": (DEPTH, 16, 64), "s5_lambda_im": (DEPTH, 16, 64), "s5_log_dt": (DEPTH, 16),
    "s5_b_re": (DEPTH, 16, 64, 16), "s5_b_im": (DEPTH, 16, 64, 16), "s5_c_re": (DEPTH, 16, 16, 64),
    "s5_c_im": (DEPTH, 16, 16, 64), "s5_d": (DEPTH, 256), "s5_glu_w": (DEPTH, 256, 256), "s5_glu_b": (DEPTH, 256),
    "sgu_norm_g": (DEPTH, 256), "sgu_w": (DEPTH, 4, 128, 128), "sgu_b": (DEPTH, 4, 128),
    "nsa_q_norm_g": (DEPTH, 64), "nsa_k_norm_g": (DEPTH, 3, 64), "nsa_cmp_pe": (DEPTH, 2, 32, 64),
    "nsa_cmp_w1": (DEPTH, 2, 2048, 128), "nsa_cmp_w2": (DEPTH, 2, 128, 64),
    "mix_norm_g": (DEPTH, D), "w_out": (DEPTH, D, D), "ffn_w_up": (DEPTH, D, 2 * DFF),
    "ffn_conv_w": (DEPTH, 3, 2 * DFF), "ffn_conv_b": (DEPTH, 2 * DFF), "ffn_w_down": (DEPTH, DFF, D),
}


def build(nseq=2, nlayers=DEPTH, stages=("ret", "s5", "sgu", "nsa", "ffn"), debug=(), wl=DEPTH):
    nc = bass.Bass("TRN2", target_bir_lowering=False)
    dr = {}
    dr["x"] = nc.dram_tensor("x", [nseq, S, D], F32, kind="ExternalInput").ap()
    dr["c"] = nc.dram_tensor("c", [nseq, D], F32, kind="ExternalInput").ap()
    for n, shp in WEIGHT_SHAPES.items():
        dr[n] = nc.dram_tensor(n, [wl] + list(shp[1:]), F32, kind="ExternalInput").ap()
    for n, shp in CONST_SHAPES.items():
        dr[n] = nc.dram_tensor(n, list(shp), F32, kind="ExternalInput").ap()
    dr["out"] = nc.dram_tensor("out", [nseq, S, D], F32, kind="ExternalOutput").ap()
    dbg = {}
    DBG_SHAPES = {"hT": [128, KC, S], "yret": [128, NT, 256], "xT": [128, KC, S], "mod": [128, wl * 48, 2], "ys5": [128, 2, S]}
    for n in debug:
        dbg[n] = nc.dram_tensor("dbg_" + n, DBG_SHAPES[n], BF16 if n in ("ys5", "yret") else F32, kind="ExternalOutput").ap()
    dbgtoks = []

    ucnt = [0]

    def SBT(name, shape, dt):
        ucnt[0] += 1
        return nc.sbuf_tensor("%s_u%d" % (name, ucnt[0]), list(shape), dt)

    with ExitStack() as ctx:
        P = Prog(nc, ctx)
        ctx.enter_context(nc.allow_non_contiguous_dma(reason="small param loads"))
        ctx.enter_context(nc.allow_low_precision(reason="bf16 matmul operands"))
        xT = P.sb("xT", [128, KC, S], F32, ntok=NT)
        hTbox = {}

        def tv(Tt, kcs, a, b):
            return V(Tt.h[:, kcs, a:b], [Tt.toks[i] for i in range(a // 128, (b + 127) // 128)])

        ps = [P.psum("ps%d" % i, [128, 512], F32) for i in range(8)]
        cs_tab = P.sb("cs_tab", [128, NT, 64], F32)
        zeta_tab = P.sb("zeta_tab", [128, 4], F32)
        gc_tab = P.sb("gc_tab", [128, 2], F32)
        ident = P.sb("ident", [128, 128], F32)
        identb = P.sb("identb", [128, 128], BF16)
        ones = P.sb("ones", [128, 128], F32)
        tril = P.sb("tril", [128, 128], F32)
        triu = P.sb("triu", [128, 128], F32)
        triub = P.sb("triub", [128, 128], BF16)
        mask8 = P.sb("mask8", [128, 8], F32)
        iota129 = P.sb("iota129", [128, 129], F32)
        sidx = P.sb("sidx", [128, 2], F32)
        for t, n in ((cs_tab, "cs_tab"), (zeta_tab, "zeta_tab"),
                     (gc_tab, "gc_tab"), (ident, "ident"), (ones, "ones"), (tril, "tril"),
                     (triu, "triu"), (mask8, "mask8"), (iota129, "iota129"), (sidx, "sidx")):
            P.load(t[:], dr[n])
        P.copy("dve", identb[:], ident[:])
        P.copy("dve", triub[:], triu[:])

        modT = P.sb("modT", [128, wl * 48, 2], F32)
        cT = P.sb("cT", [128, KC, 2], F32)
        g1n = P.sb("g1n", [128, wl, KC], F32)
        g2n = P.sb("g2n", [128, wl, KC], F32)
        scl = P.sb("scl", [128, 2, KC], F32)
        mg_tab = P.sb("mg_tab", [128, 256], F32)
        mgT = P.sb("mgT", [128, wl, KC], F32)

        for b_ in range(nseq):
            P.load(cT[:, :, b_], dr["c"][b_].rearrange("(k p) -> p k", p=128))
        if nseq < 2:
            P.memset("dve", cT[:, :, nseq:2], 0.0)
        P.act(cT[:], cT[:], AF.Silu)
        P.load(g1n[:], dr["norm1_g"].rearrange("l (k p) -> p l k", p=128))
        P.load(g2n[:], dr["norm2_g"].rearrange("l (k p) -> p l k", p=128))
        P.load(mgT[:], dr["mix_norm_g"].rearrange("l (k p) -> p l k", p=128))
        adab = P.sb("adab", [128, wl * 48], F32)
        P.load(adab[:], dr["ada_b"].rearrange("l (o p) -> p (l o)", p=128))
        with ExitStack() as c2:
            awb = [T(c2.enter_context(SBT("awb%d" % i, [128, KC, 512], F32)), "awb%d" % i) for i in range(2)]
            it = 0
            for l in range(nlayers):
                for cc in range(12):
                    buf = awb[it % 2]
                    it += 1
                    for kc in range(KC):
                        P.load(buf[:, kc, :], dr["ada_w"][l, kc * 128:(kc + 1) * 128, cc * 512:(cc + 1) * 512],
                               q="sp")
                    pst = ps[it % 2]
                    for j in range(4):
                        for kc in range(KC):
                            P.mm(pst[:, 2 * j:2 * j + 2], buf[:, kc, j * 128:(j + 1) * 128], cT[:, kc, :],
                                 start=(kc == 0), stop=(kc == KC - 1))
                    o0 = l * 48 + cc * 4
                    P.tt("dve", modT[:, o0:o0 + 4, :], pst[:, 0:8].re("p (j b) -> p j b", b=2),
                         adab[:, o0:o0 + 4].m(lambda a: a.unsqueeze(2)).bc([128, 4, 2]), ALU.add)
            P.barrier([b.toks[0] for b in awb])
        if "mod" in dbg:
            P.store(dbg["mod"], modT[:])
            dbgtoks.append(modT.toks[0])

        if "ys5" in dbg:
            ys5db = P.sb("ys5db", [128, 2, S], BF16)
        if "yret" in dbg:
            ydb = P.sb("ydb", [128, NT, 256], BF16)
        rr = {"ev": 0}

        def evac_eng():
            rr["ev"] += 1
            return "act" if rr["ev"] % 2 else "dve"

        def load_x(sq, c3):
            xin = [T(c3.enter_context(SBT("xin%d" % i, [128, D], F32)), "xin%d" % i) for i in range(2)]
            for t in range(NT):
                xb = xin[t % 2]
                P.load(xb[:], dr["x"][sq, t * 128:(t + 1) * 128, :])
                en = "act" if t % 2 else "dve"
                for half in range(2):
                    pst = ps[(2 * t + half) % 4]
                    for j in range(4):
                        kc = half * 4 + j
                        P.tr(pst[:, j * 128:(j + 1) * 128], xb[:, kc * 128:(kc + 1) * 128], ident[:])
                    P.copy(en, tv(xT, slice(half * 4, half * 4 + 4), t * 128, (t + 1) * 128),
                           pst[:].re("p (j c) -> p j c", c=128))
            P.barrier([b.toks[0] for b in xin])

        def store_x(sq, c3):
            xo = [T(c3.enter_context(SBT("xo%d" % i, [128, D], F32)), "xo%d" % i) for i in range(2)]
            for t in range(NT):
                xb = xo[t % 2]
                en = "act" if t % 2 else "dve"
                for half in range(2):
                    pst = ps[(2 * t + half) % 4]
                    for j in range(4):
                        kc = half * 4 + j
                        P.tr(pst[:, j * 128:(j + 1) * 128], tv(xT, kc, t * 128, (t + 1) * 128), ident[:])
                    P.copy(en, xb[:, half * 512:(half + 1) * 512], pst[:])
                P.store(dr["out"][sq, t * 128:(t + 1) * 128, :], xb[:])
            P.barrier([b.toks[0] for b in xo])
            return [b.toks[0] for b in xo]

        def norm_stage(l, sq, which, c3):
            gN = g1n if which == 0 else g2n
            mb = l * 48 + 24 * which
            P.stt(scl[:, which, :], modT[:, mb + 8:mb + 16, sq], 1.0, gN[:, l, :], ALU.add, ALU.mult)
            sqb = [T(c3.enter_context(SBT("nsq%d_%d" % (i, which), [128, 512], F32)), "nsq%d" % i) for i in range(3)]
            rs = [T(c3.enter_context(SBT("nrs%d_%d" % (i, which), [128, 512], F32)), "nrs%d" % i) for i in range(2)]
            tmp = [T(c3.enter_context(SBT("ntm%d_%d" % (i, which), [128, 512], F32)), "ntm%d" % i) for i in range(2)]
            n = 0
            for tc in range(4):
                a, b = tc * 512, (tc + 1) * 512
                pss = ps[4 + tc % 2]
                for kc in range(KC):
                    sb_ = sqb[n % 3]
                    n += 1
                    if kc % 2 == 0:
                        P.act(sb_[:], tv(xT, kc, a, b), AF.Square)
                    else:
                        P.tt("pool", sb_[:], tv(xT, kc, a, b), tv(xT, kc, a, b), ALU.mult)
                    P.mm(pss[:], ones[:], sb_[:], start=(kc == 0), stop=(kc == KC - 1))
                r = rs[tc % 2]
                P.act(r[:], pss[:], AF.Sqrt, bias=EPS, scale=1.0 / D)
                P.recip(r[:], r[:])
                for kc in range(KC):
                    tm = tmp[kc % 2]
                    P.tt("dve", tm[:], tv(xT, kc, a, b), r[:], ALU.mult)
                    P.act(tv(hTbox['h'], kc, a, b), tm[:], AF.Identity, scale=scl[:, which, kc:kc + 1],
                          bias=modT[:, mb + kc, sq:sq + 1])
            P.barrier()

        stg = [P.sb("stg%d" % i, [128, 1408], F32) for i in range(2)]
        stgn = {"n": 0}

        def load_w(dst, name, l, r0, nk, c0, ncol):
            for k in range(nk):
                for cc in range(0, ncol, 1408):
                    n = min(1408, ncol - cc)
                    st = stg[stgn["n"] % 2]
                    stgn["n"] += 1
                    P.load(st[:, 0:n], dr[name][l, r0 + k * 128:r0 + (k + 1) * 128, c0 + cc:c0 + cc + n])
                    P.copy("pool", dst[:, k, cc:cc + n], st[:, 0:n])

        def proj_tok(pst, W, wc0, ncol, t):
            for kc in range(KC):
                P.mm(pst[:, 0:ncol], tv(hTbox['h'], kc, t * 128, (t + 1) * 128), W[:, kc, wc0:wc0 + ncol],
                     start=(kc == 0), stop=(kc == KC - 1))

        def out_proj(l, sq, Wo, nyc, yT, tc, psx):
            a, b = tc * 512, (tc + 1) * 512
            for dc in range(KC):
                pst = psx[dc % len(psx)]
                for yc in range(nyc):
                    P.mm(pst[:], Wo[:, yc, dc * 128:(dc + 1) * 128], yT[:, yc, :], start=(yc == 0), stop=(yc == nyc - 1))
                P.stt(tv(xT, dc, a, b), pst[:], modT[:, l * 48 + 16 + dc, sq:sq + 1], tv(xT, dc, a, b), ALU.mult, ALU.add)

        class MixerTail:
            def __init__(self, l, sq, m, c3, psY, psX):
                self.l, self.sq, self.m, self.psY, self.psX = l, sq, m, psY, psX

                def sbt(name, shape, dt, n=1):
                    r = [T(c3.enter_context(SBT("mt_%s%d" % (name, i), shape, dt)), "mt_%s%d" % (name, i)) for i in range(n)]
                    return r if n > 1 else r[0]
                self.Wo = sbt("Wo", [128, 2, D], BF16)
                load_w(self.Wo, "w_out", l, m * 256, 2, 0, D)
                P.load(mg_tab[:], dr["mix_norm_g"][l:l + 1, m * 256:(m + 1) * 256].to_broadcast([128, 256]))
                self.sqo = sbt("sqo", [128, 256], F32)
                self.ss2 = sbt("ss2", [128, 1], F32, 2)
                self.yt = sbt("yt", [128, 256], BF16, 2)
                self.yT = sbt("yT", [128, 2, 512], BF16, 2)

            def run(self, t, A):
                s2, m, l, sq = t % 2, self.m, self.l, self.sq
                ss2, yt, yT = self.ss2, self.yt, self.yT
                P.act(self.sqo[:], A[:], AF.Square, accum_out=ss2[s2][:])
                P.act(ss2[s2][:], ss2[s2][:], AF.Sqrt, bias=EPS, scale=1.0 / 256)
                P.recip(ss2[s2][:], ss2[s2][:])
                P.stt(yt[s2][:], A[:], ss2[s2][:], mg_tab[:], ALU.mult, ALU.mult)
                if "yret" in dbg:
                    P.copy("dve", ydb[:, t, :], yt[s2][:])
                    if t == NT - 1:
                        P.store(dbg["yret"], ydb[:])
                        dbgtoks.append(ydb.toks[0])
                tc, j = t // 4, t % 4
                pYb = self.psY[:].bitcast(BF16)
                for yc in range(2):
                    P.tr(pYb[:, yc * 128:(yc + 1) * 128], yt[s2][:, yc * 128:(yc + 1) * 128], identb[:])
                P.copy("act", yT[tc % 2][:, :, j * 128:(j + 1) * 128], pYb[:, 0:256].re("p (c i) -> p c i", i=128))
                if j == 3:
                    out_proj(l, sq, self.Wo, 2, yT[tc % 2], tc, [self.psX])

        def retention_stage(l, sq, c3):
            def sbt(name, shape, dt, n=1):
                r = [T(c3.enter_context(SBT("rt_%s%d" % (name, i), shape, dt)), "rt_%s%d" % (name, i)) for i in range(n)]
                return r if n > 1 else r[0]
            W = sbt("W", [128, KC, 1024], BF16)
            load_w(W, "w_in", l, 0, KC, 0, 1024)
            ng = sbt("ng", [128, 256], F32)
            P.load(ng[:], dr["ret_norm_g"][l:l + 1, :].to_broadcast([128, 256]))
            decT = sbt("decT", [128, 4, 128], F32)
            xi_tab = sbt("xi_tab", [128, 2, 128], F32)
            P.load(decT[:], dr["decT"])
            P.load(xi_tab[:], dr["xi_tab"])
            psA, psB, psT, psS, psO, psU, psY, psX = ps
            qk = sbt("qk", [128, 512], BF16, 2)
            kz = sbt("kz", [128, 256], BF16, 2)
            vt = sbt("vt", [128, 256], BF16, 2)
            gs = sbt("gs", [128, 256], BF16, 2)
            tA = sbt("tA", [128, 256], F32, 2)
            tB = sbt("tB", [128, 256], F32, 2)
            qkT = sbt("qkT", [128, 512], BF16, 2)
            qxT = sbt("qxT", [128, 256], BF16, 2)
            sT = sbt("sT", [128, 512], BF16, 2)
            R = sbt("R", [128, 2, 64], F32)
            Rz = sbt("Rz", [128, 4, 64], BF16, 2)
            kTz = sbt("kTz", [128, 4, 128], BF16, 2)
            for i_ in range(2):
                P.memset("pool", Rz[i_][:], 0.0)
                P.memset("pool", kTz[i_][:], 0.0)
            sqo = sbt("sqo", [128, 256], F32)
            ss = sbt("ss", [128, 4], F32, 2)
            o1 = sbt("o1", [128, 256], F32, 2)
            A_ = sbt("A", [128, 256], F32, 2)
            tail = MixerTail(l, sq, 0, c3, psY, psX)
            P.memset("dve", R[:], 0.0)
            CUT = int(os.environ.get("RET_CUT", "99"))
            for t in range(NT):
                s2 = t % 2
                cos = cs_tab[:, t, 0:32].m(lambda a: a.unsqueeze(1)).bc([128, 8, 32])
                sin = cs_tab[:, t, 32:64].m(lambda a: a.unsqueeze(1)).bc([128, 8, 32])
                proj_tok(psA, W, 0, 512, t)
                proj_tok(psB, W, 512, 512, t)
                xv = psA[:].re("p (h two d) -> p h two d", two=2, d=32)
                x1, x2 = xv[:, :, 0, :], xv[:, :, 1, :]
                ov = qk[s2][:].re("p (h two d) -> p h two d", two=2, d=32)
                tAv = tA[s2][:].re("p (h d) -> p h d", d=32)
                tBv = tB[s2][:].re("p (h d) -> p h d", d=32)
                P.tt("dve", tAv, x1, cos, ALU.mult)
                P.tt("dve", tBv, x2, sin, ALU.mult)
                P.tt("pool", ov[:, :, 0, :], tAv, tBv, ALU.subtract)
                tAv2 = tA[1 - s2][:].re("p (h d) -> p h d", d=32)
                tBv2 = tB[1 - s2][:].re("p (h d) -> p h d", d=32)
                P.tt("dve", tAv2, x2, cos, ALU.mult)
                P.tt("dve", tBv2, x1, sin, ALU.mult)
                P.tt("pool", ov[:, :, 1, :], tAv2, tBv2, ALU.add)
                P.tt("pool", kz[s2][:].re("p (h d) -> p h d", d=64), qk[s2][:, 256:512].re("p (h d) -> p h d", d=64),
                     zeta_tab[:].m(lambda a: a.unsqueeze(2)).bc([128, 4, 64]), ALU.mult)
                P.copy("act", vt[s2][:], psB[:, 0:256])
                P.act(gs[s2][:], psB[:, 256:512], AF.Silu)
                if CUT <= 1:
                    continue
                pTb = psT[:].bitcast(BF16)
                for j in range(4):
                    P.tr(pTb[:, j * 128:(j + 1) * 128], qk[s2][:, j * 128:(j + 1) * 128], identb[:])
                P.copy("act", qkT[s2][:, 0:256], pTb[:, 0:256])
                kview = kTz[s2][:].re("p (pr two) i -> p pr two i", two=2)
                P.copy("act", kview[0:64, :, 0, :], pTb[0:64, 256:512].re("p (pr i) -> p pr i", i=128))
                P.copy("act", kview[64:128, :, 1, :], pTb[64:128, 256:512].re("p (pr i) -> p pr i", i=128))
                P.tt("dve", qxT[s2][:].re("p (a i) -> p a i", i=128), pTb[:, 0:256].re("p (a i) -> p a i", i=128),
                     xi_tab[:], ALU.mult)
                if CUT <= 2:
                    continue
                for h in range(4):
                    pr, b0 = h // 2, (h % 2) * 64
                    P.mm(psS[:, h * 128:(h + 1) * 128], kTz[s2][:, h, :], qkT[s2][:, pr * 128:(pr + 1) * 128])
                P.tt("dve", sT[s2][:].re("p (h i) -> p h i", i=128), psS[:].re("p (h i) -> p h i", i=128), decT[:], ALU.mult)
                if CUT <= 3:
                    continue
                for h in range(4):
                    pr, b0 = h // 2, (h % 2) * 64
                    P.mm(psO[:, h * 64:(h + 1) * 64], sT[s2][:, h * 128:(h + 1) * 128], vt[s2][:, h * 64:(h + 1) * 64],
                         start=True, stop=(t == 0))
                    if t > 0:
                        P.mm(psO[:, h * 64:(h + 1) * 64], qxT[s2][:, pr * 128:(pr + 1) * 128],
                             Rz[s2][:, h, :], start=False, stop=True)
                if CUT <= 4:
                    continue
                if t < NT - 1:
                    for h in range(4):
                        pr, b0 = h // 2, (h % 2) * 64
                        P.mm(psU[b0:b0 + 64, pr * 64:(pr + 1) * 64], kz[s2][:, h * 64:(h + 1) * 64],
                             vt[s2][:, h * 64:(h + 1) * 64])
                    for pr in range(2):
                        P.stt(R[:, pr, :], R[:, pr, :], gc_tab[:, pr:pr + 1], psU[:, pr * 64:(pr + 1) * 64], ALU.mult, ALU.add)
                    rzv = Rz[1 - s2][:].re("p (pr two) e -> p pr two e", two=2)
                    P.copy("pool", rzv[0:64, :, 0, :], R[0:64, :, :])
                    P.copy("pool", rzv[64:128, :, 1, :], R[64:128, :, :])
                if CUT <= 5:
                    continue
                P.act(sqo[:], psO[:, 0:256], AF.Square)
                P.red(ss[s2][:], sqo[:].re("p (h d) -> p h d", d=64), ALU.add)
                P.act(ss[s2][:], ss[s2][:], AF.Sqrt, bias=EPS, scale=1.0 / 64)
                P.recip(ss[s2][:], ss[s2][:])
                P.tt("dve", o1[s2][:].re("p (h d) -> p h d", d=64), psO[:, 0:256].re("p (h d) -> p h d", d=64),
                     ss[s2][:].m(lambda a: a.unsqueeze(2)).bc([128, 4, 64]), ALU.mult)
                P.tt("pool", o1[s2][:], o1[s2][:], gs[s2][:], ALU.mult)
                P.tt("pool", A_[s2][:], o1[s2][:], ng[:], ALU.mult)
                if CUT <= 6:
                    continue
                tail.run(t, A_[s2])
            P.barrier([W.toks[0], tail.Wo.toks[0], ng.toks[0], decT.toks[0], xi_tab.toks[0]])

        MAGIC = 12582912.0
        TWO_PI = 2.0 * math.pi
        C1 = 6.28125
        C2 = TWO_PI - C1

        def sincos(th, out_sin, out_cos, scr):
            k, r = scr
            for out, shift in ((out_sin, 0.0), (out_cos, 0.5 * math.pi)):
                if out is None:
                    continue
                if shift:
                    P.ts("dve", r, th, shift, ALU.add)
                    src = r
                else:
                    src = th
                P.ts("dve", k, src, 1.0 / TWO_PI, ALU.mult, MAGIC, ALU.add)
                P.ts("dve", k, k, -MAGIC, ALU.add)
                P.stt(r, k, -C1, src, ALU.mult, ALU.add)
                P.stt(r, k, -C2, r, ALU.mult, ALU.add)
                P.ts("dve", r, r, math.pi, ALU.min, -math.pi, ALU.max)
                P.act(out, r, AF.Sin)

        def s5_stage(l, sq, c3):
            def sbt(name, shape, dt, n=1):
                r = [T(c3.enter_context(SBT("s5_%s%d" % (name, i), shape, dt)), "s5_%s%d" % (name, i)) for i in range(n)]
                return r if n > 1 else r[0]
            W = sbt("W", [128, KC, 256], BF16)
            load_w(W, "w_in", l, 0, KC, 1024, 256)
            Wo = sbt("Wo", [128, 2, D], BF16)
            load_w(Wo, "w_out", l, 256, 2, 0, D)
            Wg = sbt("Wg", [128, 2, 256], BF16)
            load_w(Wg, "s5_glu_w", l, 0, 2, 0, 256)
            gb = sbt("gb", [128, 2], F32)
            P.load(gb[:], dr["s5_glu_b"][l].rearrange("(c p) -> p c", p=128))
            dT = sbt("dT", [128, 2], F32)
            P.load(dT[:], dr["s5_d"][l].rearrange("(c p) -> p c", p=128))
            BD = sbt("BD", [128, 2, 4, 512], BF16)
            ARi = sbt("ARi", [128, 2, 512], F32)
            AIi = sbt("AIi", [128, 2, 512], F32)
            Ar = sbt("Ar", [128, 2, 4, 129], F32)
            Ai = sbt("Ai", [128, 2, 4, 129], F32)
            Cre = sbt("Cre", [128, 2, 4, 64], BF16)
            nCre = sbt("nCre", [128, 2, 4, 64], BF16)
            nCim = sbt("nCim", [128, 2, 4, 64], BF16)
            with ExitStack() as c5:
                def tb(name, shape):
                    return T(c5.enter_context(SBT("s5t_" + name, shape, F32)), "s5t_" + name)
                lrE, liE, bRe, bIm = tb("lrE", [128, 2, 64]), tb("liE", [128, 2, 64]), tb("bRe", [128, 2, 64]), tb("bIm", [128, 2, 64])
                dtE = tb("dtE", [128, 2])
                for hf in range(2):
                    for g8 in range(8):
                        g = hf * 8 + g8
                        sl = slice(g8 * 16, (g8 + 1) * 16)
                        P.load(lrE[sl, hf, :], dr["s5_lambda_re"][l, g:g + 1, :].to_broadcast([16, 64]))
                        P.load(liE[sl, hf, :], dr["s5_lambda_im"][l, g:g + 1, :].to_broadcast([16, 64]))
                        P.load(dtE[sl, hf:hf + 1], dr["s5_log_dt"][l:l + 1, g:g + 1].to_broadcast([16, 1]))
                        P.load(bRe[sl, hf, :], dr["s5_b_re"][l, g].rearrange("p h -> h p"))
                        P.load(bIm[sl, hf, :], dr["s5_b_im"][l, g].rearrange("p h -> h p"))
                P.act(dtE[:], dtE[:], AF.Exp)
                dtb = dtE[:].m(lambda a: a.unsqueeze(2)).bc([128, 2, 64])
                lrdt, lidt = tb("lrdt", [128, 2, 64]), tb("lidt", [128, 2, 64])
                P.tt("dve", lrdt[:], lrE[:], dtb, ALU.mult)
                P.tt("dve", lidt[:], liE[:], dtb, ALU.mult)
                mag, sn, cs_, k1, k2 = tb("mag", [128, 2, 64]), tb("sn", [128, 2, 64]), tb("cs", [128, 2, 64]), tb("k1", [128, 2, 64]), tb("k2", [128, 2, 64])
                P.act(mag[:], lrdt[:], AF.Exp)
                sincos(lidt[:], sn[:], cs_[:], [k1[:], k2[:]])
                ar, ai = tb("ar", [128, 2, 64]), tb("ai", [128, 2, 64])
                P.tt("dve", ar[:], mag[:], cs_[:], ALU.mult)
                P.tt("dve", ai[:], mag[:], sn[:], ALU.mult)
                den = tb("den", [128, 2, 64])
                P.tt("dve", den[:], lrE[:], lrE[:], ALU.mult)
                P.tt("dve", k1[:], liE[:], liE[:], ALU.mult)
                P.tt("dve", den[:], den[:], k1[:], ALU.add)
                P.recip(den[:], den[:])
                P.ts("dve", ar[:], ar[:], -1.0, ALU.add)
                fre, fim = tb("fre", [128, 2, 64]), tb("fim", [128, 2, 64])
                P.tt("dve", k1[:], ar[:], lrE[:], ALU.mult)
                P.tt("dve", k2[:], ai[:], liE[:], ALU.mult)
                P.tt("dve", k1[:], k1[:], k2[:], ALU.add)
                P.tt("dve", fre[:], k1[:], den[:], ALU.mult)
                P.tt("dve", k1[:], ai[:], lrE[:], ALU.mult)
                P.tt("dve", k2[:], ar[:], liE[:], ALU.mult)
                P.tt("dve", k1[:], k1[:], k2[:], ALU.subtract)
                P.tt("dve", fim[:], k1[:], den[:], ALU.mult)
                Bfr, Bfi, nBfi = tb("Bfr", [128, 2, 64]), tb("Bfi", [128, 2, 64]), tb("nBfi", [128, 2, 64])
                P.tt("dve", k1[:], fre[:], bRe[:], ALU.mult)
                P.tt("dve", k2[:], fim[:], bIm[:], ALU.mult)
                P.tt("dve", Bfr[:], k1[:], k2[:], ALU.subtract)
                P.tt("dve", k1[:], fre[:], bIm[:], ALU.mult)
                P.tt("dve", k2[:], fim[:], bRe[:], ALU.mult)
                P.tt("dve", Bfi[:], k1[:], k2[:], ALU.add)
                P.ts("dve", nBfi[:], Bfi[:], -1.0, ALU.mult)
                m8b = mask8[:].m(lambda a: a.unsqueeze(2)).bc([128, 8, 64])
                for hf in range(2):
                    for q4, src in enumerate((Bfr, Bfi, nBfi, Bfr)):
                        P.tt("dve", BD[:, hf, q4, :].re("p (g q) -> p g q", q=64),
                             src[:, hf, :].m(lambda a: a.unsqueeze(1)).bc([128, 8, 64]), m8b, ALU.mult)
                P.barrier([t_.toks[0] for t_ in (lrE, liE, dtE, bRe, bIm)])
            with ExitStack() as c5:
                lrB, liB, snB, csB, kB1, kB2 = (tb(n, [128, 512]) for n in ("lrB", "liB", "snB", "csB", "kB1", "kB2"))
                dtB = tb("dtB", [128, 8])
                for hf in range(2):
                    P.load(lrB[:], dr["s5_lambda_re"][l:l + 1, hf * 8:(hf + 1) * 8, :].rearrange("o g p -> o (g p)").to_broadcast([128, 512]))
                    P.load(liB[:], dr["s5_lambda_im"][l:l + 1, hf * 8:(hf + 1) * 8, :].rearrange("o g p -> o (g p)").to_broadcast([128, 512]))
                    P.load(dtB[:], dr["s5_log_dt"][l:l + 1, hf * 8:(hf + 1) * 8].to_broadcast([128, 8]))
                    P.act(dtB[:], dtB[:], AF.Exp)
                    dtbb = dtB[:].m(lambda a: a.unsqueeze(2)).bc([128, 8, 64])
                    P.tt("dve", lrB[:].re("p (g q) -> p g q", q=64), lrB[:].re("p (g q) -> p g q", q=64), dtbb, ALU.mult)
                    P.tt("dve", liB[:].re("p (g q) -> p g q", q=64), liB[:].re("p (g q) -> p g q", q=64), dtbb, ALU.mult)
                    P.act(lrB[:], lrB[:], AF.Exp, scale=sidx[:, 1:2])
                    P.ts("dve", liB[:], liB[:], sidx[:, 0:1], ALU.mult)
                    sincos(liB[:], snB[:], csB[:], [kB1[:], kB2[:]])
                    P.tt("dve", ARi[:, hf, :], lrB[:], csB[:], ALU.mult)
                    P.stt(AIi[:, hf, :], snB[:], -1.0, lrB[:], ALU.mult, ALU.mult)
                P.barrier([t_.toks[0] for t_ in (lrB, liB, dtB)])
            with ExitStack() as c5:
                lrS, liS, dtS = tb("lrS", [128, 2, 4]), tb("liS", [128, 2, 4]), tb("dtS", [128, 2, 4])
                for hf in range(2):
                    P.load(lrS[:, hf, :], dr["s5_lambda_re"][l, hf * 8:(hf + 1) * 8, :].rearrange("(b gl) p -> (gl p) b", gl=2))
                    P.load(liS[:, hf, :], dr["s5_lambda_im"][l, hf * 8:(hf + 1) * 8, :].rearrange("(b gl) p -> (gl p) b", gl=2))
                    for gl in range(2):
                        P.load(dtS[gl * 64:(gl + 1) * 64, hf, :],
                               dr["s5_log_dt"][l:l + 1, hf * 8:(hf + 1) * 8].rearrange("o (b gl) -> o gl b", gl=2)[:, gl, :].to_broadcast([64, 4]))
                P.act(dtS[:], dtS[:], AF.Exp)
                P.tt("dve", lrS[:], lrS[:], dtS[:], ALU.mult)
                P.tt("dve", liS[:], liS[:], dtS[:], ALU.mult)
                io = iota129[:].m(lambda a: a.unsqueeze(1).unsqueeze(1)).bc([128, 2, 4, 129])
                exS, thS, snS, csS, kS1, kS2 = (tb(n, [128, 2, 4, 129]) for n in ("exS", "thS", "snS", "csS", "kS1", "kS2"))
                P.tt("dve", exS[:], io, lrS[:].m(lambda a: a.unsqueeze(3)).bc([128, 2, 4, 129]), ALU.mult)
                P.tt("dve", thS[:], io, liS[:].m(lambda a: a.unsqueeze(3)).bc([128, 2, 4, 129]), ALU.mult)
                P.act(exS[:], exS[:], AF.Exp)
                sincos(thS[:], snS[:], csS[:], [kS1[:], kS2[:]])
                P.tt("dve", Ar[:], exS[:], csS[:], ALU.mult)
                P.tt("dve", Ai[:], exS[:], snS[:], ALU.mult)
                P.barrier([t_.toks[0] for t_ in (lrS, liS, dtS)])
            with ExitStack() as c5:
                Cr, Ci = tb("Cr", [128, 2, 4, 64]), tb("Ci", [128, 2, 4, 64])
                P.memset("dve", Cr[:], 0.0)
                P.memset("dve", Ci[:], 0.0)
                for hf in range(2):
                    for b4 in range(4):
                        for gl in range(2):
                            g = hf * 8 + 2 * b4 + gl
                            co = (b4 % 2) * 32 + gl * 16
                            P.load(Cr[gl * 64:(gl + 1) * 64, hf, b4, co:co + 16], dr["s5_c_re"][l, g].rearrange("h p -> p h"))
                            P.load(Ci[gl * 64:(gl + 1) * 64, hf, b4, co:co + 16], dr["s5_c_im"][l, g].rearrange("h p -> p h"))
                P.copy("dve", Cre[:], Cr[:])
                P.ts("dve", nCre[:], Cr[:], -1.0, ALU.mult)
                P.ts("dve", nCim[:], Ci[:], -1.0, ALU.mult)
                P.barrier([t_.toks[0] for t_ in (Cr, Ci)])
            uT = sbt("uT", [128, 2, 512], BF16)
            X1 = sbt("X1", [128, 2, 512], BF16, 2)
            X2 = sbt("X2", [128, 2, 512], BF16, 2)
            Gc = [sbt("Gc", [128, 2, 512], F32)] * 2
            Pp = sbt("Pp", [128, 4, 512], BF16, 2)
            car = sbt("car", [128, 2, 2, 4], F32)
            ct = sbt("ct", [128, 4, 4], F32)
            yS = sbt("yS", [128, 2, 512], F32)
            gyb = sbt("gyb", [128, 2, 512], BF16)
            sg = [sbt("sg", [128, 512], F32)] * 2
            oT = sbt("oT", [128, 2, 512], F32)
            sqm = [sbt("sqm", [128, 512], F32)] * 2
            rs = sqm[0]
            yTb = sbt("yTb", [128, 2, 512], BF16, 2)
            P.memset("dve", car[:], 0.0)
            psZ, psG, psYh = ps[0:4], ps[4:6], ps[6:8]
            it = 0
            for w in range(4):
                a, b = w * 512, (w + 1) * 512
                for hf in range(2):
                    for kc in range(KC):
                        P.mm(psZ[hf][:], W[:, kc, hf * 128:(hf + 1) * 128], tv(hTbox['h'], kc, a, b), start=(kc == 0), stop=(kc == KC - 1))
                    P.copy("act", uT[:, hf, :], psZ[hf][:])
                for n in range(4):
                    i0 = n * 128
                    for hf in range(2):
                        s2 = it % 2
                        it += 1
                        for q4 in range(4):
                            P.mm(psZ[q4][:], uT[:, hf, i0:i0 + 128], BD[:, hf, q4, :])
                        P.tt("dve", X1[s2][:, 0, :], psZ[0][:], ARi[:, hf, :], ALU.mult)
                        P.tt("dve", X1[s2][:, 1, :], psZ[1][:], ARi[:, hf, :], ALU.mult)
                        P.tt("dve", X2[s2][:, 0, :], psZ[2][:], AIi[:, hf, :], ALU.mult)
                        P.tt("dve", X2[s2][:, 1, :], psZ[3][:], AIi[:, hf, :], ALU.mult)
                        for ri in range(2):
                            for b4 in range(4):
                                cb = slice(b4 * 128, (b4 + 1) * 128)
                                P.mm(psG[ri][:, cb], X1[s2][:, ri, cb], triub[:], start=True, stop=False)
                                P.mm(psG[ri][:, cb], X2[s2][:, ri, cb], triub[:], start=False, stop=True)
                        for ri in range(2):
                            P.tt("dve", Gc[s2][:, ri, :].re("p (b i) -> p b i", i=128), psG[ri][:].re("p (b i) -> p b i", i=128),
                                 car[:, hf, ri, :].m(lambda x: x.unsqueeze(2)).bc([128, 4, 128]), ALU.add)
                        Arv = Ar[:, hf, :, 0:128]
                        Aiv = Ai[:, hf, :, 0:128]
                        gre = Gc[s2][:, 0, :].re("p (b i) -> p b i", i=128)
                        gim = Gc[s2][:, 1, :].re("p (b i) -> p b i", i=128)
                        pv = [Pp[s2][:, i_, :].re("p (b i) -> p b i", i=128) for i_ in range(4)]
                        P.tt("dve", pv[0], gre, Arv, ALU.mult)
                        P.tt("pool", pv[1], gim, Aiv, ALU.mult)
                        P.tt("pool", pv[2], gre, Aiv, ALU.mult)
                        P.tt("dve", pv[3], gim, Arv, ALU.mult)
                        for b4 in range(4):
                            r0 = 64 * (b4 // 2)
                            o_ = psYh[hf][r0:r0 + 64, i0:i0 + 128]
                            cb = slice(b4 * 128, (b4 + 1) * 128)
                            P.mm(o_, Cre[:, hf, b4, :], Pp[s2][:, 0, cb], start=(b4 % 2 == 0), stop=False)
                            P.mm(o_, nCre[:, hf, b4, :], Pp[s2][:, 1, cb], start=False, stop=False)
                            P.mm(o_, nCim[:, hf, b4, :], Pp[s2][:, 2, cb], start=False, stop=False)
                            P.mm(o_, nCim[:, hf, b4, :], Pp[s2][:, 3, cb], start=False, stop=(b4 % 2 == 1))
                        g7r = Gc[s2][:, 0, :].re("p (b i) -> p b i", i=128)[:, :, 127]
                        g7i = Gc[s2][:, 1, :].re("p (b i) -> p b i", i=128)[:, :, 127]
                        a8r, a8i = Ar[:, hf, :, 128], Ai[:, hf, :, 128]
                        P.tt("dve", ct[:, 0, :], a8r, g7r, ALU.mult)
                        P.tt("dve", ct[:, 1, :], a8i, g7i, ALU.mult)
                        P.tt("dve", ct[:, 2, :], a8r, g7i, ALU.mult)
                        P.tt("dve", ct[:, 3, :], a8i, g7r, ALU.mult)
                        P.tt("dve", car[:, hf, 0, :], ct[:, 0, :], ct[:, 1, :], ALU.subtract)
                        P.tt("dve", car[:, hf, 1, :], ct[:, 2, :], ct[:, 3, :], ALU.add)
                for hf in range(2):
                    P.stt(yS[:, hf, :], uT[:, hf, :], dT[:, hf:hf + 1], psYh[hf][:], ALU.mult, ALU.add)
                P.act(gyb[:], yS[:], AF.Gelu_apprx_tanh)
                for mc in range(2):
                    for fc in range(2):
                        P.mm(psZ[mc][:], Wg[:, fc, mc * 128:(mc + 1) * 128], gyb[:, fc, :], start=(fc == 0), stop=(fc == 1))
                    P.act(sg[mc][:], psZ[mc][:], AF.Sigmoid, bias=gb[:, mc:mc + 1])
                    P.tt("dve", oT[:, mc, :], gyb[:, mc, :], sg[mc][:], ALU.mult)
                    P.act(sqm[mc][:], oT[:, mc, :], AF.Square)
                    P.mm(psZ[2][:], ones[:], sqm[mc][:], start=(mc == 0), stop=(mc == 1))
                P.act(rs[:], psZ[2][:], AF.Sqrt, bias=EPS, scale=1.0 / 256)
                P.recip(rs[:], rs[:])
                yb = yTb[w % 2]
                for mc in range(2):
                    P.stt(yb[:, mc, :], oT[:, mc, :], mgT[:, l, 2 + mc:3 + mc], rs[:], ALU.mult, ALU.mult)
                if "ys5" in dbg:
                    P.copy("dve", ys5db[:, :, a:b], yb[:])
                out_proj(l, sq, Wo, 2, yb, w, [psZ[3]])
            if "ys5" in dbg:
                P.store(dbg["ys5"], ys5db[:])
                dbgtoks.append(ys5db.toks[0])
            P.barrier([W.toks[0], Wo.toks[0], Wg.toks[0], gb.toks[0], dT.toks[0]])

        def sgu_stage(l, sq, c3):
            def sbt(name, shape, dt, n=1):
                r = [T(c3.enter_context(SBT("sg_%s%d" % (name, i), shape, dt)), "sg_%s%d" % (name, i)) for i in range(n)]
                return r if n > 1 else r[0]
            psA, psB, psT, psS, psO, psU, psY, psX = ps
            W = sbt("W", [128, KC, 512], BF16)
            load_w(W, "w_in", l, 0, KC, 1280, 512)
            sgt = sbt("sgt", [128, 256], F32)
            P.load(sgt[:], dr["sgu_norm_g"][l:l + 1, :].to_broadcast([128, 256]))
            wraw = sbt("wraw", [128, 4, 128], F32)
            P.load(wraw[:], dr["sgu_w"][l].rearrange("g t s -> t g s"))
            bT = sbt("bT", [128, 4], F32)
            P.load(bT[:], dr["sgu_b"][l].rearrange("g t -> t g"))
            wT = sbt("wT", [128, 4, 128], BF16)
            P.tt("dve", wraw[:], wraw[:], tril[:].m(lambda a: a.unsqueeze(1)).bc([128, 4, 128]), ALU.mult)
            for g in range(4):
                P.tr(psT[:, g * 128:(g + 1) * 128], wraw[:, g, :], ident[:])
            P.copy("dve", wT[:], psT[:].re("p (g t) -> p g t", t=128))
            tail = MixerTail(l, sq, 2, c3, psY, psX)
            u = sbt("u", [128, 256], F32, 2)
            gv = sbt("gv", [128, 256], F32, 2)
            sqv = sbt("sqv", [128, 256], F32)
            ss = sbt("ss", [128, 4], F32, 2)
            vt = sbt("vt", [128, 256], BF16, 2)
            A_ = sbt("A", [128, 256], F32, 2)
            for t in range(NT):
                s2 = t % 2
                proj_tok(psA, W, 0, 512, t)
                P.act(u[s2][:], psA[:, 0:256], AF.Gelu_apprx_tanh)
                P.act(gv[s2][:], psA[:, 256:512], AF.Gelu_apprx_tanh)
                P.act(sqv[:], gv[s2][:], AF.Square)
                P.red(ss[s2][:], sqv[:].re("p (h d) -> p h d", d=64), ALU.add)
                P.act(ss[s2][:], ss[s2][:], AF.Sqrt, bias=EPS, scale=1.0 / 64)
                P.recip(ss[s2][:], ss[s2][:])
                P.tt("dve", gv[s2][:].re("p (h d) -> p h d", d=64), gv[s2][:].re("p (h d) -> p h d", d=64),
                     ss[s2][:].m(lambda a: a.unsqueeze(2)).bc([128, 4, 64]), ALU.mult)
                P.tt("pool", vt[s2][:], gv[s2][:], sgt[:], ALU.mult)
                for g in range(4):
                    P.mm(psS[:, g * 64:(g + 1) * 64], wT[:, g, :], vt[s2][:, g * 64:(g + 1) * 64])
                P.tt("dve", A_[s2][:].re("p (h d) -> p h d", d=64), psS[:, 0:256].re("p (h d) -> p h d", d=64),
                     bT[:].m(lambda a: a.unsqueeze(2)).bc([128, 4, 64]), ALU.add)
                P.tt("pool", A_[s2][:], A_[s2][:], u[s2][:], ALU.mult)
                tail.run(t, A_[s2])
            P.barrier([W.toks[0], tail.Wo.toks[0], sgt.toks[0], wraw.toks[0], bT.toks[0]])

        def ffn_stage(l, sq, c3):
            def sbt(name, shape, dt, n=1):
                r = [T(c3.enter_context(SBT("ff_%s%d" % (name, i), shape, dt)), "ff_%s%d" % (name, i)) for i in range(n)]
                return r if n > 1 else r[0]
            h2 = sbt("h2", [128, KC, 512], BF16)
            zT = sbt("zT", [128, NFF, 512], BF16, 1)
            Wu = sbt("Wu", [128, KC, 1024], BF16, 2)
            Wd = sbt("Wd", [128, NFF, 128], BF16, 2)
            cw = sbt("cw", [128, 3, 2 * NFF], F32)
            cb = sbt("cb", [128, 2 * NFF], F32)
            P.load(cw[:], dr["ffn_conv_w"][l].rearrange("k (c p) -> p k c", p=128))
            P.load(cb[:], dr["ffn_conv_b"][l].rearrange("(c p) -> p c", p=128))
            atail = sbt("atail", [128, 2 * NFF, 2], F32)
            P.memset("dve", atail[:], 0.0)
            ab = sbt("ab", [128, 514], F32, 4)
            cg = sbt("cg", [128, 512], F32, 2)
            cu = sbt("cu", [128, 512], F32, 2)
            sqb = sbt("nsq", [128, 512], F32, 3)
            rs = sbt("nrs", [128, 512], F32)
            tmp = sbt("ntm", [128, 512], F32, 2)
            mb = l * 48 + 24
            P.stt(scl[:, 1, :], modT[:, mb + 8:mb + 16, sq], 1.0, g2n[:, l, :], ALU.add, ALU.mult)
            nq = 0
            na = 0
            ng_ = 0
            nd = 0
            for tc in range(4):
                a, b = tc * 512, (tc + 1) * 512
                pss = ps[7]
                for kc in range(KC):
                    sb_ = sqb[nq % 3]
                    nq += 1
                    if kc % 2 == 0:
                        P.act(sb_[:], tv(xT, kc, a, b), AF.Square)
                    else:
                        P.tt("pool", sb_[:], tv(xT, kc, a, b), tv(xT, kc, a, b), ALU.mult)
                    P.mm(pss[:], ones[:], sb_[:], start=(kc == 0), stop=(kc == KC - 1))
                P.act(rs[:], pss[:], AF.Sqrt, bias=EPS, scale=1.0 / D)
                P.recip(rs[:], rs[:])
                for kc in range(KC):
                    tm = tmp[kc % 2]
                    P.tt("dve", tm[:], tv(xT, kc, a, b), rs[:], ALU.mult)
                    P.act(h2[:, kc, :], tm[:], AF.Identity, scale=scl[:, 1, kc:kc + 1], bias=modT[:, mb + kc, sq:sq + 1])
                for g in range(6):
                    nch = 4 if g < 5 else 2
                    Wg = Wu[ng_ % 2]
                    ng_ += 1
                    load_w(Wg, "ffn_w_up", l, 0, KC, g * 512, nch * 128)
                    for kc in range(KC):
                        pass
                    for k in range(KC):
                        st = stg[stgn["n"] % 2]
                        stgn["n"] += 1
                        P.load(st[:, 0:nch * 128], dr["ffn_w_up"][l, k * 128:(k + 1) * 128, DFF + g * 512:DFF + g * 512 + nch * 128])
                        P.copy("pool", Wg[:, k, 512:512 + nch * 128], st[:, 0:nch * 128])
                    for jj in range(nch):
                        j = g * 4 + jj
                        res = []
                        for side in range(2):
                            pst = ps[(2 * j + side) % 4]
                            ch = j + side * NFF
                            for kc in range(KC):
                                P.mm(pst[:], Wg[:, kc, side * 512 + jj * 128:side * 512 + (jj + 1) * 128], h2[:, kc, :],
                                     start=(kc == 0), stop=(kc == KC - 1))
                            abf = ab[na % 4]
                            na += 1
                            P.copy("pool", abf[:, 0:2], atail[:, ch, :])
                            P.copy("act", abf[:, 2:514], pst[:])
                            P.copy("pool", atail[:, ch, :], abf[:, 512:514])
                            cc = (cg if side == 0 else cu)[j % 2]
                            P.act(cc[:], abf[:, 2:514], AF.Identity, scale=cw[:, 2, ch:ch + 1], bias=cb[:, ch:ch + 1])
                            P.stt(cc[:], abf[:, 1:513], cw[:, 1, ch:ch + 1], cc[:], ALU.mult, ALU.add)
                            P.stt(cc[:], abf[:, 0:512], cw[:, 0, ch:ch + 1], cc[:], ALU.mult, ALU.add)
                            res.append(cc)
                        P.act(res[0][:], res[0][:], AF.Silu)
                        P.tt("pool", zT[:, j, :], res[0][:], res[1][:], ALU.mult)
                for dc in range(KC):
                    Wdb = Wd[nd % 2]
                    nd += 1
                    for j0 in range(0, NFF, 11):
                        st = stg[stgn["n"] % 2]
                        stgn["n"] += 1
                        P.load(st[:, 0:11 * 128].re("p (j c) -> p j c", c=128),
                               dr["ffn_w_down"][l, j0 * 128:(j0 + 11) * 128, dc * 128:(dc + 1) * 128].rearrange("(j p) c -> p j c", p=128))
                        P.copy("pool", Wdb[:, j0:j0 + 11, :], st[:, 0:11 * 128].re("p (j c) -> p j c", c=128))
                    pst = ps[4 + dc % 2]
                    for j in range(NFF):
                        P.mm(pst[:], Wdb[:, j, :], zT[:, j, :], start=(j == 0), stop=(j == NFF - 1))
                    P.stt(tv(xT, dc, a, b), pst[:], modT[:, l * 48 + 40 + dc, sq:sq + 1], tv(xT, dc, a, b), ALU.mult, ALU.add)
            P.barrier([Wu[0].toks[0], Wu[1].toks[0], Wd[0].toks[0], Wd[1].toks[0], cw.toks[0], cb.toks[0]])

        def nsa_stage(l, sq, c3):
            def mk(cx, pre):
                def sbt(name, shape, dt, n=1):
                    r = [T(cx.enter_context(SBT("%s_%s%d" % (pre, name, i), shape, dt)), "%s_%s%d" % (pre, name, i)) for i in range(n)]
                    return r if n > 1 else r[0]
                return sbt
            sbt = mk(c3, "ns")
            tail = MixerTail(l, sq, 3, c3, ps[6], ps[7])
            qT = sbt("qT", [128, 2, S], BF16)
            ksTz = sbt("ksTz", [128, 2, S], BF16)
            kwTz = sbt("kwTz", [128, 2, S], BF16)
            vs1 = sbt("vs1", [128, NT, 2, 65], BF16)
            vw1 = sbt("vw1", [128, NT, 2, 65], BF16)
            gates = sbt("gates", [128, NT, 12], F32)
            kcmpTz = sbt("kcmpTz", [128, 2, 128], BF16)
            vc1 = sbt("vc1", [128, 2, 97], BF16)
            qg = sbt("qg", [128, 64], F32)
            kg = sbt("kg", [128, 3, 64], F32)
            csc = sbt("csc", [128, 64], F32)
            P.memset("pool", ksTz[:], 0.0)
            P.memset("pool", kwTz[:], 0.0)
            P.memset("pool", kcmpTz[:], 0.0)
            P.memset("pool", vs1[:], 1.0)
            P.memset("pool", vw1[:], 1.0)
            P.load(qg[:], dr["nsa_q_norm_g"][l:l + 1, :].to_broadcast([128, 64]))
            P.ts("dve", qg[:], qg[:], 0.125, ALU.mult)
            P.load(kg[:], dr["nsa_k_norm_g"][l:l + 1, :, :].rearrange("o a d -> o (a d)").to_broadcast([128, 192]))
            P.load(csc[:], dr["cs_c"])

            def norm_rope(sb2, src, npart, nh, gain, cos, sin, outs, iv=lambda v: v):
                sqb, ss, xn, tA, tB = sb2
                pp = slice(0, npart)
                P.act(sqb[pp, 0:nh, :], src, AF.Square)
                P.red(ss[pp, 0:nh], sqb[pp, 0:nh, :], ALU.add)
                P.act(ss[pp, 0:nh], ss[pp, 0:nh], AF.Sqrt, bias=EPS, scale=1.0 / 64)
                P.recip(ss[pp, 0:nh], ss[pp, 0:nh])
                P.tt("dve", xn[pp, 0:nh, :], src, ss[pp, 0:nh].m(lambda a: a.unsqueeze(2)).bc([npart, nh, 64]), ALU.mult)
                P.tt("pool", xn[pp, 0:nh, :], xn[pp, 0:nh, :], gain, ALU.mult)
                x1, x2 = xn[pp, 0:nh, 0:32], xn[pp, 0:nh, 32:64]
                P.tt("dve", tA[pp, 0:nh, :], x1, cos, ALU.mult)
                P.tt("pool", tB[pp, 0:nh, :], x2, sin, ALU.mult)
                P.tt("dve", outs(0), iv(tA[pp, 0:nh, :]), iv(tB[pp, 0:nh, :]), ALU.subtract)
                P.tt("dve", tA[pp, 0:nh, :], x2, cos, ALU.mult)
                P.tt("pool", tB[pp, 0:nh, :], x1, sin, ALU.mult)
                P.tt("dve", outs(1), iv(tA[pp, 0:nh, :]), iv(tB[pp, 0:nh, :]), ALU.add)

            with ExitStack() as c4:
                sb4 = mk(c4, "n4")
                W = sb4("W", [128, KC, 1036], BF16)
                load_w(W, "w_in", l, 0, KC, 1792, 1036)
                nr = (sb4("sqb", [128, 4, 64], F32), sb4("ss", [128, 4], F32), sb4("xn", [128, 4, 64], F32),
                      sb4("tA", [128, 4, 32], F32), sb4("tB", [128, 4, 32], F32))
                with ExitStack() as c5:
                    sb5 = mk(c5, "n5")
                    kvT = sb5("kvT", [128, 2, S], BF16)
                    for c in range(4):
                        for cb in range(2):
                            pst = ps[(2 * c + cb) % 4]
                            for k8 in range(KC):
                                P.mm(pst[:], W[:, k8, 256 + cb * 128:256 + (cb + 1) * 128], tv(hTbox['h'], k8, c * 512, (c + 1) * 512),
                                     start=(k8 == 0), stop=(k8 == KC - 1))
                            P.copy("act" if cb else "dve", kvT[:, cb, c * 512:(c + 1) * 512], pst[:])
                    W1 = sb5("W1", [128, 32, 128], BF16)
                    W2 = sb5("W2", [128, 64], BF16)
                    w2s = sb5("w2s", [128, 64], F32)
                    peT = sb5("peT", [128, 32], F32)
                    peb = sb5("peb", [128, 32], BF16)
                    bias = sb5("bias", [128, 1], F32)
                    hidb = sb5("hidb", [128, 128], BF16)
                    kct = sb5("kct", [128, 2, 64], BF16)
                    for side in range(2):
                        w1v = dr["nsa_cmp_w1"][l, side].rearrange("(p d) h -> d p h", d=64)
                        for half in range(2):
                            for pc in range(0, 32, 8):
                                st = stg[stgn["n"] % 2]
                                stgn["n"] += 1
                                hs = slice(half * 64, (half + 1) * 64)
                                P.load(st[hs, 0:1024].re("d (p h) -> d p h", h=128), w1v[:, pc:pc + 8, :])
                                P.copy("pool", W1[hs, pc:pc + 8, :], st[hs, 0:1024].re("d (p h) -> d p h", h=128))
                        P.load(w2s[:], dr["nsa_cmp_w2"][l, side])
                        P.copy("pool", W2[:], w2s[:])
                        for half in range(2):
                            P.load(peT[half * 64:(half + 1) * 64, :], dr["nsa_cmp_pe"][l, side].rearrange("p d -> d p"))
                        P.copy("dve", peb[:], peT[:])
                        for p_ in range(32):
                            P.mm(ps[3][:, 0:1], W1[0:64, p_, :], peb[0:64, p_:p_ + 1], start=(p_ == 0), stop=(p_ == 31))
                        P.copy("dve", bias[:], ps[3][:, 0:1])
                        for kvh in range(2):
                            pst = ps[kvh]
                            hs = slice(kvh * 64, (kvh + 1) * 64)
                            for p_ in range(32):
                                P.mm(pst[:, 0:127], W1[hs, p_, :], kvT[hs, side, p_:p_ + 16 * 126 + 1:16],
                                     start=(p_ == 0), stop=(p_ == 31))
                            P.act(hidb[:, 0:127], pst[:, 0:127], AF.Gelu_apprx_tanh, bias=bias[:, 0:1])
                            P.mm(ps[2][0:127, kvh * 64:(kvh + 1) * 64], hidb[:, 0:127], W2[:])
                        if side == 0:
                            norm_rope(nr, ps[2][0:127, 0:128].re("p (h d) -> p h d", d=64), 127, 2,
                                      kg[0:127, 0, :].m(lambda a: a.unsqueeze(1)).bc([127, 2, 64]),
                                      csc[0:127, 0:32].m(lambda a: a.unsqueeze(1)).bc([127, 2, 32]),
                                      csc[0:127, 32:64].m(lambda a: a.unsqueeze(1)).bc([127, 2, 32]),
                                      lambda hf_: kct[0:127, :, hf_ * 32:(hf_ + 1) * 32])
                            pTb = ps[3][:].bitcast(BF16)
                            P.tr(pTb[:, 0:127], kct[0:127, :, :].re("p h d -> p (h d)"), identb[0:127, 0:127])
                            P.copy("act", kcmpTz[0:64, 0, 0:127], pTb[0:64, 0:127])
                            P.copy("act", kcmpTz[64:128, 1, 0:127], pTb[64:128, 0:127])
                        else:
                            P.copy("act", vc1[0:127, :, 0:64], ps[2][0:127, 0:128].re("p (h d) -> p h d", d=64))
                            o1f = sb5("o1f", [128, 33], F32)
                            P.load(o1f[:], dr["ovl1"])
                            P.copy("dve", vc1[0:127, :, 64:97], o1f[0:127, :].m(lambda a: a.unsqueeze(1)).bc([127, 2, 33]))
                    P.barrier([w2s.toks[0], peT.toks[0]])
                with ExitStack() as c5:
                    sb5 = mk(c5, "n6")
                    kk = sb5("kk", [128, 4, 64], F32, 2)
                    qtok = sb5("qtok", [128, 256], BF16, 2)
                    ktok = sb5("ktok", [128, 256], BF16, 2)
                    kg4 = sb5("kg4", [128, 4, 64], F32)
                    P.copy("dve", kg4[:, 0:2, :], kg[:, 1, :].m(lambda a: a.unsqueeze(1)).bc([128, 2, 64]))
                    P.copy("dve", kg4[:, 2:4, :], kg[:, 2, :].m(lambda a: a.unsqueeze(1)).bc([128, 2, 64]))
                    for t in range(NT):
                        s2 = t % 2
                        psA, psB, psC, psT = ps[0], ps[1], ps[2], ps[3]
                        proj_tok(psA, W, 0, 256, t)
                        proj_tok(psB, W, 512, 512, t)
                        proj_tok(psC, W, 1024, 12, t)
                        cos = cs_tab[:, t, 0:32].m(lambda a: a.unsqueeze(1)).bc([128, 4, 32])
                        sin = cs_tab[:, t, 32:64].m(lambda a: a.unsqueeze(1)).bc([128, 4, 32])
                        qo = qtok[s2][:].re("p (g k two d) -> p k g two d", g=2, k=2, two=2, d=32)
                        norm_rope(nr, psA[:, 0:256].re("p (h d) -> p h d", d=64), 128, 4,
                                  qg[:].m(lambda a: a.unsqueeze(1)).bc([128, 4, 64]), cos, sin,
                                  lambda hf_: qo[:, :, :, hf_, :], iv=lambda v: v.re("p (k g) d -> p k g d", g=2))
                        P.copy("act", kk[s2][:, 0:2, :], psB[:, 0:128].re("p (h d) -> p h d", d=64))
                        P.copy("act", kk[s2][:, 2:4, :], psB[:, 256:384].re("p (h d) -> p h d", d=64))
                        ko = ktok[s2][:].re("p (h two d) -> p h two d", two=2, d=32)
                        norm_rope(nr, kk[s2][:], 128, 4, kg4[:], cos, sin, lambda hf_: ko[:, :, hf_, :])
                        P.copy("act", vs1[:, t, :, 0:64], psB[:, 128:256].re("p (h d) -> p h d", d=64))
                        P.copy("act", vw1[:, t, :, 0:64], psB[:, 384:512].re("p (h d) -> p h d", d=64))
                        P.act(gates[:, t, :], psC[:, 0:12], AF.Sigmoid)
                        pTb = psT[:].bitcast(BF16)
                        for j in range(2):
                            P.tr(pTb[:, j * 128:(j + 1) * 128], qtok[s2][:, j * 128:(j + 1) * 128], identb[:])
                            P.tr(pTb[:, (2 + j) * 128:(3 + j) * 128], ktok[s2][:, j * 128:(j + 1) * 128], identb[:])
                        tsl = slice(t * 128, (t + 1) * 128)
                        P.copy("act", qT[:, :, tsl], pTb[:, 0:256].re("p (g i) -> p g i", i=128))
                        P.copy("dve", ksTz[0:64, 0, tsl], pTb[0:64, 256:384])
                        P.copy("dve", ksTz[64:128, 1, tsl], pTb[64:128, 256:384])
                        P.copy("dve", kwTz[0:64, 0, tsl], pTb[0:64, 384:512])
                        P.copy("dve", kwTz[64:128, 1, tsl], pTb[64:128, 384:512])
                    P.barrier()
                P.barrier([W.toks[0]])
            with ExitStack() as c4:
                sb4 = mk(c4, "n7")
                negv = sb4("negv", [128, S], BF16)
                for cc in range(0, S, 1024):
                    st = stg[stgn["n"] % 2]
                    stgn["n"] += 1
                    P.load(st[:, 0:1024], dr["negvalid"][:, cc:cc + 1024])
                    P.copy("pool", negv[:, cc:cc + 1024], st[:, 0:1024])
                negTf = sb4("negTf", [128, 2, 128], F32)
                negTb = sb4("negTb", [128, 2, 128], BF16)
                P.load(negTf[:], dr["negT"])
                P.copy("dve", negTb[:], negTf[:])
                emf = sb4("emf", [32, 16, 128], F32)
                emb = sb4("emb", [32, 16, 128], BF16)
                P.load(emf[:], dr["emat"])
                P.copy("dve", emb[:], emf[:])
                bonus = sb4("bonus", [128, NT, 32], F32)
                P.load(bonus[:], dr["bonus"])
                Ych = sb4("Ych", [128, 4, 256], F32, 2)
                eb = sb4("eb", [128, 512], BF16, 2)
                rsc = sb4("rsc", [128, 2, 4], F32)
                imp = sb4("imp", [128, 4, 32], F32)
                imp2 = sb4("imp2", [128, 4, 32], F32)
                m8 = sb4("m8", [128, 8], F32)
                selm = sb4("selm", [128, 32], F32)
                negsel = sb4("negsel", [128, 4, 32], BF16)
                nsT = sb4("nsT", [32, 512], BF16)
                coef = sb4("coef", [128, 4], F32, 2)
                tmpo = sb4("tmpo", [128, 4, 64], F32, 2)
                ne = 0
                for c in range(4):
                    q0, q1 = c * 512, (c + 1) * 512
                    Y = Ych[c % 2]
                    for kvh in range(2):
                        psOc = [ps[2], ps[3]]
                        for g in range(2):
                            psc = ps[ne % 2]
                            e_ = eb[ne % 2]
                            ne += 1
                            P.mm(psc[0:127, :], kcmpTz[:, kvh, 0:127], qT[:, g, q0:q1], start=True, stop=False)
                            P.mm(psc[0:127, :], identb[0:127, 0:127], negv[0:127, q0:q1], start=False, stop=True)
                            P.act(e_[0:127, :], psc[0:127, :], AF.Exp)
                            for j in range(4):
                                P.mm(psOc[g][:, j * 97:(j + 1) * 97], e_[0:127, j * 128:(j + 1) * 128], vc1[0:127, kvh, :])
                            ocv = psOc[g][:, 0:388].re("p (j e) -> p j e", e=97)
                            P.ts("dve", rsc[:, g, :], ocv[:, :, 96], 1e-30, ALU.add)
                            P.recip(rsc[:, g, :], rsc[:, g, :])
                        oc0 = psOc[0][:, 0:388].re("p (j e) -> p j e", e=97)
                        oc1 = psOc[1][:, 0:388].re("p (j e) -> p j e", e=97)
                        P.tt("dve", imp[:], oc0[:, :, 64:96], rsc[:, 0, :].m(lambda a: a.unsqueeze(2)).bc([128, 4, 32]), ALU.mult)
                        P.tt("dve", imp2[:], oc1[:, :, 64:96], rsc[:, 1, :].m(lambda a: a.unsqueeze(2)).bc([128, 4, 32]), ALU.mult)
                        P.tt("dve", imp[:], imp[:], imp2[:], ALU.add)
                        P.tt("dve", imp[:], imp[:], bonus[:, 4 * c:4 * c + 4, :], ALU.add)
                        pTb = ps[6][:].bitcast(BF16)
                        for j in range(4):
                            P.op("dve", lambda e, j=j: e.max(out=m8.h[:], in_=imp.h[:, j, :]), reads=imp.toks, writes=m8.toks)
                            P.ts("dve", selm[:], imp[:, j, :], m8[:, 7:8], ALU.is_ge)
                            P.ts("dve", negsel[:, j, :], selm[:], -NEGB, ALU.mult, NEGB, ALU.add)
                            P.tr(pTb[0:32, j * 128:(j + 1) * 128], negsel[:, j, :], identb[:])
                        P.copy("act", nsT[:], pTb[0:32, 0:512])
                        for g in range(2):
                            qh = 2 * kvh + g
                            ocv = psOc[g][:, 0:388].re("p (j e) -> p j e", e=97)
                            cf = coef[g]
                            P.tt("dve", cf[:], rsc[:, g, :], gates[:, 4 * c:4 * c + 4, qh * 3 + 0], ALU.mult)
                            P.tt("dve", Y[:, :, qh * 64:(qh + 1) * 64], ocv[:, :, 0:64],
                                 cf[:].m(lambda a: a.unsqueeze(2)).bc([128, 4, 64]), ALU.mult)
                        for g in range(2):
                            qh = 2 * kvh + g
                            for br, (KTz, V1, pso) in enumerate(((ksTz, vs1, ps[4]), (kwTz, vw1, ps[5]))):
                                kb0 = 0 if br == 0 else max(0, 4 * c - 4)
                                first = True
                                for kb in range(kb0, 4 * c + 4):
                                    r = kb - 4 * c
                                    ja = max(0, r)
                                    jb = 3 if br == 0 else min(3, r + 4)
                                    ca, cbn = ja * 128, (jb + 1) * 128
                                    pss = ps[ne % 2]
                                    e_ = eb[ne % 2]
                                    ne += 1
                                    more = (br == 0) or (r >= 0) or (r <= -1)
                                    P.mm(pss[:, ca:cbn], KTz[:, kvh, kb * 128:(kb + 1) * 128], qT[:, g, q0 + ca:q0 + cbn],
                                         start=True, stop=False)
                                    if br == 0:
                                        P.mm(pss[:, ca:cbn], emb[:, kb, :], nsT[:, ca:cbn], start=False, stop=(r < 0))
                                    if r >= 0:
                                        P.mm(pss[:, r * 128:(r + 1) * 128], identb[:], negTb[:, 0, :], start=False, stop=True)
                                    if br == 1 and r <= -1:
                                        P.mm(pss[:, (r + 4) * 128:(r + 5) * 128], identb[:], negTb[:, 1, :], start=False, stop=True)
                                    P.act(e_[:, ca:cbn], pss[:, ca:cbn], AF.Exp)
                                    for j in range(ja, jb + 1):
                                        P.mm(pso[:, j * 65:(j + 1) * 65], e_[:, j * 128:(j + 1) * 128], V1[:, kb, kvh, :],
                                             start=first, stop=(kb == 4 * c + j), skip_group_check=True)
                                        first = False
                                ov = pso[:, 0:260].re("p (j e) -> p j e", e=65)
                                cf = coef[br]
                                P.recip(cf[:], ov[:, :, 64])
                                P.tt("dve", cf[:], cf[:], gates[:, 4 * c:4 * c + 4, qh * 3 + 1 + br], ALU.mult)
                                P.tt("dve", tmpo[br][:], ov[:, :, 0:64], cf[:].m(lambda a: a.unsqueeze(2)).bc([128, 4, 64]), ALU.mult)
                                P.tt("pool", Y[:, :, qh * 64:(qh + 1) * 64], Y[:, :, qh * 64:(qh + 1) * 64], tmpo[br][:], ALU.add)
                    for j in range(4):
                        tail.run(4 * c + j, Y[:, j, :])
                P.barrier([negTf.toks[0], emf.toks[0], bonus.toks[0]])
            P.barrier([tail.Wo.toks[0], qg.toks[0], kg.toks[0], csc.toks[0]])

        outtoks = []
        for sq in range(nseq):
            with ExitStack() as c3:
                load_x(sq, c3)
            for l in range(nlayers):
              with ExitStack() as c4:
                hTbox['h'] = T(c4.enter_context(SBT("hT_%d_%d" % (sq, l), [128, KC, S], BF16)), "hT", NT)
                with ExitStack() as c3:
                    norm_stage(l, sq, 0, c3)
                if "hT" in dbg and l == 0 and sq == 0:
                    with ExitStack() as c3:
                        hd = T(c3.enter_context(SBT("hd", [128, KC, S], F32)), "hd")
                        P.copy("dve", hd[:], hTbox['h'][:])
                        P.store(dbg["hT"], hd[:])
                        P.barrier([hd.toks[0]])
                if "ret" in stages:
                    with ExitStack() as c3:
                        retention_stage(l, sq, c3)
                if "s5" in stages:
                    with ExitStack() as c3:
                        s5_stage(l, sq, c3)
                if "sgu" in stages:
                    with ExitStack() as c3:
                        sgu_stage(l, sq, c3)
                if "nsa" in stages:
                    with ExitStack() as c3:
                        nsa_stage(l, sq, c3)
                P.barrier()
              if True:
                if "ffn" in stages:
                    with ExitStack() as c3:
                        ffn_stage(l, sq, c3)
            if "xT" in dbg and sq == 0:
                P.store(dbg["xT"], xT[:])
                dbgtoks.append(xT.toks[0])
            with ExitStack() as c3:
                outtoks += store_x(sq, c3)
        P.finish(outtoks + dbgtoks)
        print("instructions:", P.nins, "sems:", P.nsem)
    return nc


_CACHE = {}


def kernel(**inputs):
    n = 8
    if "nc" not in _CACHE:
        _CACHE["nc"] = build(nseq=2, nlayers=DEPTH, wl=DEPTH)
        _CACHE["consts"] = host_consts()
    nc = _CACHE["nc"]
    consts = _CACHE["consts"]
    x = np.asarray(inputs["x"], dtype=np.float32)
    c = np.asarray(inputs["c"], dtype=np.float32)
    w = {k: np.ascontiguousarray(np.asarray(inputs[k], dtype=np.float32)) for k in WEIGHT_SHAPES}
    in_maps = []
    for i in range(n):
        m = {"x": np.ascontiguousarray(x[2 * i:2 * i + 2]), "c": np.ascontiguousarray(c[2 * i:2 * i + 2])}
        m.update(w)
        m.update(consts)
        in_maps.append(m)
    res = run_bass_kernel_spmd(nc, in_maps, core_ids=list(range(n)))
    return np.concatenate([np.asarray(r["out"], dtype=np.float32) for r in res.results], axis=0)
```

```python
import math
import os
from contextlib import ExitStack

import numpy as np
import concourse.bass as bass
import concourse.mybir as mybir
from concourse.bass_utils import run_bass_kernel_spmd

F32 = mybir.dt.float32
BF16 = mybir.dt.bfloat16
AF = mybir.ActivationFunctionType
ALU = mybir.AluOpType
AX = mybir.AxisListType

S = 2048
D = 1024
KC = 8
NT = 16
DEPTH = 4
DFF = 2816
NFF = 22
INC = 2828
EPS = 1e-6
NEGB = -30000.0


class Tok:
    __slots__ = ("name", "w", "r", "dsem", "dtot", "dw", "uid", "psum")
    _n = [0]

    def __init__(self, name):
        Tok._n[0] += 1
        self.uid = Tok._n[0]
        self.psum = False
        self.name = name
        self.w = None
        self.r = {}
        self.dsem = None
        self.dtot = 0
        self.dw = 0


class V:
    __slots__ = ("ap", "toks")

    def __init__(self, ap, toks):
        self.ap = ap
        self.toks = toks

    def __getitem__(self, idx):
        return V(self.ap[idx], self.toks)

    def m(self, fn):
        return V(fn(self.ap), self.toks)

    def bc(self, shape):
        return V(self.ap.to_broadcast(shape), self.toks)

    def re(self, pat, **kw):
        return V(self.ap.rearrange(pat, **kw), self.toks)

    def bitcast(self, dt):
        return V(self.ap.bitcast(dt), self.toks)


class TK:
    def __init__(self, t, i):
        self.t = t
        self.i = i

    def __getitem__(self, idx):
        return V(self.t.h[idx], [self.t.toks[self.i]])


class T:
    def __init__(self, handle, name, ntok=1):
        self.h = handle
        self.toks = [Tok("%s.%d" % (name, i)) for i in range(ntok)]

    def __getitem__(self, idx):
        return V(self.h[idx], self.toks)

    def k(self, i):
        return TK(self, i)


class Eng:
    def __init__(self, name, obj):
        self.name = name
        self.obj = obj
        self.sems = []
        self.gen = -1
        self.cnt = 0
        self.known = {}
        self.dknown = {}


class Prog:
    ROT = 30000

    def __init__(self, nc, ctx):
        self.nc = nc
        self.ctx = ctx
        self.eng = {
            "pe": Eng("pe", nc.tensor),
            "act": Eng("act", nc.scalar),
            "dve": Eng("dve", nc.vector),
            "pool": Eng("pool", nc.gpsimd),
            "sp": Eng("sp", nc.sync),
        }
        self.nsem = 0
        for e in self.eng.values():
            self._newsem(e)
        self.nins = 0
        self.dfree = []
        self.dlive = {}

    def _sem(self, name):
        self.nsem += 1
        return self.ctx.enter_context(self.nc.semaphore("%s_%d" % (name, self.nsem)))

    def _newsem(self, e):
        e.sems.append(self._sem(e.name))
        e.gen += 1
        e.cnt = 0

    def sb(self, name, shape, dt, ntok=1):
        h = self.ctx.enter_context(self.nc.sbuf_tensor("sb_" + name, list(shape), dt))
        return T(h, name, ntok)

    def psum(self, name, shape, dt):
        h = self.ctx.enter_context(self.nc.psum_tensor(name, list(shape), dt))
        t = T(h, name, 1)
        t.toks[0].psum = True
        return t

    def _wait_ticket(self, E, tk):
        if tk is None:
            return
        e2, gen, val = tk
        if e2 is E and E.name == "pe":
            return
        k = E.known.get(e2.name)
        if k is not None and k >= (gen, val):
            return
        E.obj.wait_ge(e2.sems[gen], val)
        E.known[e2.name] = (gen, val)

    def _wait_dma(self, E, t, val):
        if val <= 0 or t.dsem is None:
            return
        if E.dknown.get(t.uid, 0) >= val:
            return
        E.obj.wait_ge(t.dsem, val)
        E.dknown[t.uid] = val

    def _deps(self, E, reads, writes):
        for t in reads:
            self._wait_ticket(E, t.w)
            self._wait_dma(E, t, t.dw)
            if t.psum:
                for en2, tk in t.r.items():
                    if en2 != E.name:
                        self._wait_ticket(E, tk)
        for t in writes:
            self._wait_ticket(E, t.w)
            for tk in t.r.values():
                self._wait_ticket(E, tk)
            self._wait_dma(E, t, t.dtot)

    def op(self, en, fn, reads=(), writes=()):
        E = self.eng[en]
        self._deps(E, reads, writes)
        ins = fn(E.obj)
        if E.cnt >= self.ROT:
            self._newsem(E)
        E.cnt += 1
        ins.then_inc(E.sems[E.gen], 1)
        tk = (E, E.gen, E.cnt)
        for t in reads:
            t.r[en] = tk
        for t in writes:
            t.w = tk
            t.r = {}
        self.nins += 1
        return ins

    def dma(self, qn, out, in_, reads=(), writes=(), **kw):
        E = self.eng[qn]
        self._deps(E, reads, writes)
        ins = E.obj.dma_start(out=out, in_=in_, **kw)
        t = (list(writes) + list(reads))[0]
        if t.dsem is None:
            if self.dfree:
                t.dsem, base = self.dfree.pop()
            else:
                t.dsem, base = self._sem("d"), 0
            t.dtot = base
            t.dw = 0
            self.dlive[t.uid] = t
        ins.then_inc(t.dsem, 16)
        t.dtot += 16
        if writes:
            t.dw = t.dtot
        self.nins += 1
        return ins

    def barrier(self, toks=()):
        SP = self.eng["sp"]
        for t in self.dlive.values():
            self._wait_dma(SP, t, t.dtot)
        self.op("sp", lambda e: e.nop(), reads=(), writes=())
        last = {n: (e, e.gen, e.cnt) for n, e in self.eng.items()}
        for n, E in self.eng.items():
            for n2, tk in last.items():
                if tk[2] > 0:
                    self._wait_ticket(E, tk)
        for t in self.dlive.values():
            self.dfree.append((t.dsem, t.dtot))
            t.dsem = None
            t.dtot = 0
            t.dw = 0
        self.dlive = {}
        for E in self.eng.values():
            E.dknown = {}

    def finish(self, toks):
        E = self.eng["sp"]
        for t in toks:
            self._wait_dma(E, t, t.dtot)

    @staticmethod
    def _tk(*vs):
        out = []
        for v in vs:
            if isinstance(v, V):
                out.extend(v.toks)
        return out

    @staticmethod
    def _a(v):
        return v.ap if isinstance(v, V) else v

    def mm(self, out, lhsT, rhs, start=True, stop=True, **kw):
        return self.op("pe", lambda e: e.matmul(out.ap, lhsT=lhsT.ap, rhs=rhs.ap, start=start, stop=stop, **kw),
                       reads=self._tk(lhsT, rhs), writes=out.toks)

    def tr(self, out, in_, ident):
        return self.op("pe", lambda e: e.transpose(out.ap, in_.ap, ident.ap),
                       reads=self._tk(in_, ident), writes=out.toks)

    def act(self, out, in_, func, bias=None, scale=None, accum_out=None):
        kw = {}
        if bias is not None:
            kw["bias"] = self._a(bias)
        if scale is not None:
            kw["scale"] = self._a(scale)
        if accum_out is not None:
            kw["accum_out"] = accum_out.ap
        return self.op("act", lambda e: e.activation(out=out.ap, in_=in_.ap, func=func, **kw),
                       reads=self._tk(in_, bias, scale), writes=self._tk(out, accum_out))

    def tt(self, en, out, a, b, op):
        return self.op(en, lambda e: e.tensor_tensor(out=out.ap, in0=a.ap, in1=b.ap, op=op),
                       reads=self._tk(a, b), writes=out.toks)

    def ts(self, en, out, a, s1, op0, s2=None, op1=None, accum_out=None):
        kw = {}
        if op1 is not None:
            kw["op1"] = op1
        if accum_out is not None:
            kw["accum_out"] = accum_out.ap
        return self.op(en, lambda e: e.tensor_scalar(out=out.ap, in0=a.ap, scalar1=self._a(s1), scalar2=self._a(s2),
                                                     op0=op0, **kw),
                       reads=self._tk(a, s1, s2), writes=self._tk(out, accum_out))

    def stt(self, out, a, scalar, b, op0, op1, accum_out=None):
        kw = {}
        if accum_out is not None:
            kw["accum_out"] = accum_out.ap
        return self.op("dve", lambda e: e.scalar_tensor_tensor(out=out.ap, in0=a.ap, scalar=self._a(scalar), in1=b.ap,
                                                               op0=op0, op1=op1, **kw),
                       reads=self._tk(a, scalar, b), writes=self._tk(out, accum_out))

    def copy(self, en, out, in_):
        if en == "act":
            return self.op("act", lambda e: e.copy(out=out.ap, in_=in_.ap), reads=in_.toks, writes=out.toks)
        return self.op(en, lambda e: e.tensor_copy(out=out.ap, in_=in_.ap), reads=in_.toks, writes=out.toks)

    def red(self, out, in_, op, axis=AX.X):
        return self.op("dve", lambda e: e.tensor_reduce(out=out.ap, in_=in_.ap, axis=axis, op=op),
                       reads=in_.toks, writes=out.toks)

    def memset(self, en, out, val):
        return self.op(en, lambda e: e.memset(out.ap, val), reads=(), writes=out.toks)

    def recip(self, out, in_):
        return self.op("dve", lambda e: e.reciprocal(out=out.ap, in_=in_.ap), reads=in_.toks, writes=out.toks)

    def load(self, dst, src_ap, q="sp", **kw):
        return self.dma(q, dst.ap, src_ap, writes=dst.toks, **kw)

    def store(self, dst_ap, src, q="sp", **kw):
        return self.dma(q, dst_ap, src.ap, reads=src.toks, **kw)


def host_consts():
    c = {}
    pos = np.arange(S, dtype=np.float64)
    inv = 10000.0 ** (-np.arange(0, 64, 2, dtype=np.float64) / 64)
    ang = (pos[:, None].astype(np.float32) * inv[None, :].astype(np.float32)).astype(np.float32)
    cos = np.cos(ang).astype(np.float32)
    sin = np.sin(ang).astype(np.float32)
    cs = np.concatenate([cos, sin], axis=1)
    c["cs_tab"] = np.ascontiguousarray(cs.reshape(NT, 128, 64).transpose(1, 0, 2))
    H = 4
    lg = np.log1p(-(2.0 ** (-5.0 - np.arange(H, dtype=np.float64))))
    idx = np.arange(128, dtype=np.float64)
    diff = idx[None, :] - idx[:, None]
    dec = np.where(diff >= 0, np.exp(lg[:, None, None] * np.maximum(diff, 0.0)), 0.0) * 0.125
    c["decT"] = np.ascontiguousarray(dec.transpose(1, 0, 2)).astype(np.float32)
    xi = np.exp(lg[:, None] * (idx + 1)[None, :]) * 0.125
    xit = np.zeros((128, 2, 128), np.float32)
    for h in range(H):
        xit[(h % 2) * 64:(h % 2) * 64 + 64, h // 2, :] = xi[h][None, :]
    c["xi_tab"] = xit
    zeta = np.exp(lg[:, None] * (127 - idx)[None, :])
    c["zeta_tab"] = np.ascontiguousarray(zeta.T).astype(np.float32)
    gch = np.exp(lg * 128)
    gct = np.zeros((128, 2), np.float32)
    for h in range(H):
        gct[(h % 2) * 64:(h % 2) * 64 + 64, h // 2] = gch[h]
    c["gc_tab"] = gct
    c["tril"] = np.tril(np.ones((128, 128), np.float32))
    c["triu"] = np.triu(np.ones((128, 128), np.float32))
    m8 = np.zeros((128, 8), np.float32)
    for g8 in range(8):
        m8[g8 * 16:(g8 + 1) * 16, g8] = 1.0
    c["mask8"] = m8
    c["iota129"] = np.broadcast_to(np.arange(129, dtype=np.float32)[None, :], (128, 129)).copy()
    kk_ = np.arange(128)[:, None]
    qq_ = np.arange(128)[None, :]
    nt = np.zeros((128, 2, 128), np.float32)
    nt[:, 0, :] = np.where(kk_ <= qq_, 0.0, NEGB)
    nt[:, 1, :] = np.where(kk_ > qq_, 0.0, NEGB)
    c["negT"] = nt
    nn = np.arange(128)[:, None]
    qpos = np.arange(S)[None, :]
    c["negvalid"] = np.where(16 * nn + 31 <= qpos, 0.0, NEGB).astype(np.float32)
    em = np.zeros((32, 16, 128), np.float32)
    for kb in range(16):
        em[2 * kb, kb, 0:64] = 1.0
        em[2 * kb + 1, kb, 64:128] = 1.0
    c["emat"] = em
    posf = np.arange(S)
    cur = posf // 64
    blk = np.arange(32)[None, :]
    bon = np.zeros((S, 32), np.float32)
    forced = (blk == 0) | (blk == cur[:, None]) | (blk == cur[:, None] - 1)
    bon[forced] = 1e3
    bon[np.broadcast_to(blk, (S, 32)) > cur[:, None]] = -1e9
    c["bonus"] = np.ascontiguousarray(bon.reshape(NT, 128, 32).transpose(1, 0, 2))
    ii = np.arange(127)[:, None]
    jj = np.arange(32)[None, :]
    ovl = ((ii * 16 < (jj + 1) * 64) & (ii * 16 + 32 > jj * 64)).astype(np.float32)
    o1 = np.zeros((128, 33), np.float32)
    o1[:127, :32] = ovl
    o1[:127, 32] = 1.0
    c["ovl1"] = o1
    cend = (np.arange(127) * 16 + 31).astype(np.float32)
    angc = (cend[:, None] * inv[None, :].astype(np.float32)).astype(np.float32)
    csc = np.zeros((128, 64), np.float32)
    csc[:127, :32] = np.cos(angc)
    csc[:127, 32:] = np.sin(angc)
    c["cs_c"] = csc
    sx = np.zeros((128, 2), np.float32)
    sx[:, 0] = np.arange(128)
    sx[:, 1] = -np.arange(128)
    c["sidx"] = sx
    c["ident"] = np.eye(128, dtype=np.float32)
    c["ones"] = np.ones((128, 128), np.float32)
    return c


CONST_SHAPES = {"cs_tab": (128, NT, 64), "decT": (128, 4, 128), "xi_tab": (128, 2, 128), "zeta_tab": (128, 4),
                "gc_tab": (128, 2), "ident": (128, 128), "ones": (128, 128), "tril": (128, 128),
                "triu": (128, 128), "mask8": (128, 8), "iota129": (128, 129), "sidx": (128, 2),
                "negT": (128, 2, 128), "negvalid": (128, S), "emat": (32, 16, 128), "bonus": (128, NT, 32),
                "ovl1": (128, 33), "cs_c": (128, 64)}

WEIGHT_SHAPES = {
    "norm1_g": (DEPTH, D), "norm2_g": (DEPTH, D), "ada_w": (DEPTH, D, 6 * D), "ada_b": (DEPTH, 6 * D),
    "w_in": (DEPTH, D, INC), "ret_norm_g": (DEPTH, 256),
    "s5_lambda_re": (DEPTH, 16, 64), "s5_lambda_im": (DEPTH, 16, 64), "s5_log_dt": (DEPTH, 16),
    "s5_b_re": (DEPTH, 16, 64, 16), "s5_b_im": (DEPTH, 16, 64, 16), "s5_c_re": (DEPTH, 16, 16, 64),
    "s5_c_im": (DEPTH, 16, 16, 64), "s5_d": (DEPTH, 256), "s5_glu_w": (DEPTH, 256, 256), "s5_glu_b": (DEPTH, 256),
    "sgu_norm_g": (DEPTH, 256), "sgu_w": (DEPTH, 4, 128, 128), "sgu_b": (DEPTH, 4, 128),
    "nsa_q_norm_g": (DEPTH, 64), "nsa_k_norm_g": (DEPTH, 3, 64), "nsa_cmp_pe": (DEPTH, 2, 32, 64),
    "nsa_cmp_w1": (DEPTH, 2, 2048, 128), "nsa_cmp_w2": (DEPTH, 2, 128, 64),
    "mix_norm_g": (DEPTH, D), "w_out": (DEPTH, D, D), "ffn_w_up": (DEPTH, D, 2 * DFF),
    "ffn_conv_w": (DEPTH, 3, 2 * DFF), "ffn_conv_b": (DEPTH, 2 * DFF), "ffn_w_down": (DEPTH, DFF, D),
}


def build(nseq=2, nlayers=DEPTH, stages=("ret", "s5", "sgu", "nsa", "ffn"), debug=(), wl=DEPTH):
    nc = bass.Bass("TRN2", target_bir_lowering=False)
    dr = {}
    dr["x"] = nc.dram_tensor("x", [nseq, S, D], F32, kind="ExternalInput").ap()
    dr["c"] = nc.dram_tensor("c", [nseq, D], F32, kind="ExternalInput").ap()
    for n, shp in WEIGHT_SHAPES.items():
        dr[n] = nc.dram_tensor(n, [wl] + list(shp[1:]), F32, kind="ExternalInput").ap()
    for n, shp in CONST_SHAPES.items():
        dr[n] = nc.dram_tensor(n, list(shp), F32, kind="ExternalInput").ap()
    dr["out"] = nc.dram_tensor("out", [nseq, S, D], F32, kind="ExternalOutput").ap()
    dbg = {}
    DBG_SHAPES = {"hT": [128, KC, S], "yret": [128, NT, 256], "xT": [128, KC, S], "mod": [128, wl * 48, 2], "ys5": [128, 2, S]}
    for n in debug:
        dbg[n] = nc.dram_tensor("dbg_" + n, DBG_SHAPES[n], BF16 if n in ("ys5", "yret") else F32, kind="ExternalOutput").ap()
    dbgtoks = []

    ucnt = [0]

    def SBT(name, shape, dt):
        ucnt[0] += 1
        return nc.sbuf_tensor("%s_u%d" % (name, ucnt[0]), list(shape), dt)

    with ExitStack() as ctx:
        P = Prog(nc, ctx)
        ctx.enter_context(nc.allow_non_contiguous_dma(reason="small param loads"))
        ctx.enter_context(nc.allow_low_precision(reason="bf16 matmul operands"))
        xT = P.sb("xT", [128, KC, S], F32, ntok=NT)
        hTbox = {}

        def tv(Tt, kcs, a, b):
            return V(Tt.h[:, kcs, a:b], [Tt.toks[i] for i in range(a // 128, (b + 127) // 128)])

        ps = [P.psum("ps%d" % i, [128, 512], F32) for i in range(8)]
        cs_tab = P.sb("cs_tab", [128, NT, 64], F32)
        zeta_tab = P.sb("zeta_tab", [128, 4], F32)
        gc_tab = P.sb("gc_tab", [128, 2], F32)
        ident = P.sb("ident", [128, 128], F32)
        identb = P.sb("identb", [128, 128], BF16)
        ones = P.sb("ones", [128, 128], F32)
        tril = P.sb("tril", [128, 128], F32)
        triu = P.sb("triu", [128, 128], F32)
        triub = P.sb("triub", [128, 128], BF16)
        mask8 = P.sb("mask8", [128, 8], F32)
        iota129 = P.sb("iota129", [128, 129], F32)
        sidx = P.sb("sidx", [128, 2], F32)
        for t, n in ((cs_tab, "cs_tab"), (zeta_tab, "zeta_tab"),
                     (gc_tab, "gc_tab"), (ident, "ident"), (ones, "ones"), (tril, "tril"),
                     (triu, "triu"), (mask8, "mask8"), (iota129, "iota129"), (sidx, "sidx")):
            P.load(t[:], dr[n])
        P.copy("dve", identb[:], ident[:])
        P.copy("dve", triub[:], triu[:])

        modT = P.sb("modT", [128, wl * 48, 2], F32)
        cT = P.sb("cT", [128, KC, 2], F32)
        g1n = P.sb("g1n", [128, wl, KC], F32)
        g2n = P.sb("g2n", [128, wl, KC], F32)
        scl = P.sb("scl", [128, 2, KC], F32)
        mg_tab = P.sb("mg_tab", [128, 256], F32)
        mgT = P.sb("mgT", [128, wl, KC], F32)

        for b_ in range(nseq):
            P.load(cT[:, :, b_], dr["c"][b_].rearrange("(k p) -> p k", p=128))
        if nseq < 2:
            P.memset("dve", cT[:, :, nseq:2], 0.0)
        P.act(cT[:], cT[:], AF.Silu)
        P.load(g1n[:], dr["norm1_g"].rearrange("l (k p) -> p l k", p=128))
        P.load(g2n[:], dr["norm2_g"].rearrange("l (k p) -> p l k", p=128))
        P.load(mgT[:], dr["mix_norm_g"].rearrange("l (k p) -> p l k", p=128))
        adab = P.sb("adab", [128, wl * 48], F32)
        P.load(adab[:], dr["ada_b"].rearrange("l (o p) -> p (l o)", p=128))
        with ExitStack() as c2:
            awb = [T(c2.enter_context(SBT("awb%d" % i, [128, KC, 512], F32)), "awb%d" % i) for i in range(2)]
            it = 0
            for l in range(nlayers):
                for cc in range(12):
                    buf = awb[it % 2]
                    it += 1
                    for kc in range(KC):
                        P.load(buf[:, kc, :], dr["ada_w"][l, kc * 128:(kc + 1) * 128, cc * 512:(cc + 1) * 512],
                               q="sp")
                    pst = ps[it % 2]
                    for j in range(4):
                        for kc in range(KC):
                            P.mm(pst[:, 2 * j:2 * j + 2], buf[:, kc, j * 128:(j + 1) * 128], cT[:, kc, :],
                                 start=(kc == 0), stop=(kc == KC - 1))
                    o0 = l * 48 + cc * 4
                    P.tt("dve", modT[:, o0:o0 + 4, :], pst[:, 0:8].re("p (j b) -> p j b", b=2),
                         adab[:, o0:o0 + 4].m(lambda a: a.unsqueeze(2)).bc([128, 4, 2]), ALU.add)
            P.barrier([b.toks[0] for b in awb])
        if "mod" in dbg:
            P.store(dbg["mod"], modT[:])
            dbgtoks.append(modT.toks[0])

        if "ys5" in dbg:
            ys5db = P.sb("ys5db", [128, 2, S], BF16)
        if "yret" in dbg:
            ydb = P.sb("ydb", [128, NT, 256], BF16)
        rr = {"ev": 0}

        def run_pipelined(make, n):
            DONE = object()
            act_ = []
            nxt = [0]

            def start():
                act_.append([make(nxt[0]), False])
                nxt[0] += 1
            start()
            while act_:
                if len(act_) == 1 and act_[0][1] and nxt[0] < n:
                    start()
                for ent in list(act_):
                    r = next(ent[0], DONE)
                    if r is DONE:
                        act_.remove(ent)
                    elif r == "mark":
                        ent[1] = True
                if not act_ and nxt[0] < n:
                    start()

        def evac_eng():
            rr["ev"] += 1
            return "act" if rr["ev"] % 2 else "dve"

        def load_x(sq, c3):
            xin = [T(c3.enter_context(SBT("xin%d" % i, [128, D], F32)), "xin%d" % i) for i in range(2)]
            for t in range(NT):
                xb = xin[t % 2]
                P.load(xb[:], dr["x"][sq, t * 128:(t + 1) * 128, :])
                en = "act" if t % 2 else "dve"
                for half in range(2):
                    pst = ps[(2 * t + half) % 4]
                    for j in range(4):
                        kc = half * 4 + j
                        P.tr(pst[:, j * 128:(j + 1) * 128], xb[:, kc * 128:(kc + 1) * 128], ident[:])
                    P.copy(en, tv(xT, slice(half * 4, half * 4 + 4), t * 128, (t + 1) * 128),
                           pst[:].re("p (j c) -> p j c", c=128))
            P.barrier([b.toks[0] for b in xin])

        def store_x(sq, c3):
            xo = [T(c3.enter_context(SBT("xo%d" % i, [128, D], F32)), "xo%d" % i) for i in range(2)]
            for t in range(NT):
                xb = xo[t % 2]
                en = "act" if t % 2 else "dve"
                for half in range(2):
                    pst = ps[(2 * t + half) % 4]
                    for j in range(4):
                        kc = half * 4 + j
                        P.tr(pst[:, j * 128:(j + 1) * 128], tv(xT, kc, t * 128, (t + 1) * 128), ident[:])
                    P.copy(en, xb[:, half * 512:(half + 1) * 512], pst[:])
                P.store(dr["out"][sq, t * 128:(t + 1) * 128, :], xb[:])
            P.barrier([b.toks[0] for b in xo])
            return [b.toks[0] for b in xo]

        def norm_stage(l, sq, which, c3):
            gN = g1n if which == 0 else g2n
            mb = l * 48 + 24 * which
            P.stt(scl[:, which, :], modT[:, mb + 8:mb + 16, sq], 1.0, gN[:, l, :], ALU.add, ALU.mult)
            sqb = [T(c3.enter_context(SBT("nsq%d_%d" % (i, which), [128, 512], F32)), "nsq%d" % i) for i in range(3)]
            rs = [T(c3.enter_context(SBT("nrs%d_%d" % (i, which), [128, 512], F32)), "nrs%d" % i) for i in range(2)]
            tmp = [T(c3.enter_context(SBT("ntm%d_%d" % (i, which), [128, 512], F32)), "ntm%d" % i) for i in range(2)]
            n = 0
            for tc in range(4):
                a, b = tc * 512, (tc + 1) * 512
                pss = ps[4 + tc % 2]
                for kc in range(KC):
                    sb_ = sqb[n % 3]
                    n += 1
                    if kc % 2 == 0:
                        P.act(sb_[:], tv(xT, kc, a, b), AF.Square)
                    else:
                        P.tt("pool", sb_[:], tv(xT, kc, a, b), tv(xT, kc, a, b), ALU.mult)
                    P.mm(pss[:], ones[:], sb_[:], start=(kc == 0), stop=(kc == KC - 1))
                r = rs[tc % 2]
                P.act(r[:], pss[:], AF.Sqrt, bias=EPS, scale=1.0 / D)
                P.recip(r[:], r[:])
                for kc in range(KC):
                    tm = tmp[kc % 2]
                    P.tt("dve", tm[:], tv(xT, kc, a, b), r[:], ALU.mult)
                    P.act(tv(hTbox['h'], kc, a, b), tm[:], AF.Identity, scale=scl[:, which, kc:kc + 1],
                          bias=modT[:, mb + kc, sq:sq + 1])
            P.barrier()

        stg = [P.sb("stg%d" % i, [128, 1408], F32) for i in range(2)]
        stgn = {"n": 0, "c": 0}

        def cast_eng():
            stgn["c"] += 1
            return "act" if stgn["c"] % 2 else "dve"

        def load_w(dst, name, l, r0, nk, c0, ncol):
            for k in range(nk):
                for cc in range(0, ncol, 1408):
                    n = min(1408, ncol - cc)
                    st = stg[stgn["n"] % 2]
                    stgn["n"] += 1
                    P.load(st[:, 0:n], dr[name][l, r0 + k * 128:r0 + (k + 1) * 128, c0 + cc:c0 + cc + n])
                    P.copy(cast_eng(), dst[:, k, cc:cc + n], st[:, 0:n])

        def proj_tok(pst, W, wc0, ncol, t):
            for kc in range(KC):
                P.mm(pst[:, 0:ncol], tv(hTbox['h'], kc, t * 128, (t + 1) * 128), W[:, kc, wc0:wc0 + ncol],
                     start=(kc == 0), stop=(kc == KC - 1))

        def out_proj(l, sq, Wo, nyc, yT, tc, psx):
            a, b = tc * 512, (tc + 1) * 512
            for dc in range(KC):
                pst = psx[dc % len(psx)]
                for yc in range(nyc):
                    P.mm(pst[:], Wo[:, yc, dc * 128:(dc + 1) * 128], yT[:, yc, :], start=(yc == 0), stop=(yc == nyc - 1))
                P.stt(tv(xT, dc, a, b), pst[:], modT[:, l * 48 + 16 + dc, sq:sq + 1], tv(xT, dc, a, b), ALU.mult, ALU.add)

        class MixerTail:
            def __init__(self, l, sq, m, c3, psY, psX):
                self.l, self.sq, self.m, self.psY, self.psX = l, sq, m, psY, psX

                def sbt(name, shape, dt, n=1):
                    r = [T(c3.enter_context(SBT("mt_%s%d" % (name, i), shape, dt)), "mt_%s%d" % (name, i)) for i in range(n)]
                    return r if n > 1 else r[0]
                self.Wo = sbt("Wo", [128, 2, D], BF16)
                load_w(self.Wo, "w_out", l, m * 256, 2, 0, D)
                P.load(mg_tab[:], dr["mix_norm_g"][l:l + 1, m * 256:(m + 1) * 256].to_broadcast([128, 256]))
                self.sqo = sbt("sqo", [128, 256], F32, 2)
                self.ss2 = sbt("ss2", [128, 1], F32, 2)
                self.yt = sbt("yt", [128, 256], BF16, 2)
                self.yT = sbt("yT", [128, 2, 512], BF16, 2)

            def run(self, t, A):
                for _ in self.run_g(t, A):
                    pass

            def run_g(self, t, A):
                s2, m, l, sq = t % 2, self.m, self.l, self.sq
                ss2, yt, yT = self.ss2, self.yt, self.yT
                P.act(self.sqo[s2][:], A[:], AF.Square, accum_out=ss2[s2][:])
                P.act(ss2[s2][:], ss2[s2][:], AF.Sqrt, bias=EPS, scale=1.0 / 256)
                yield
                P.recip(ss2[s2][:], ss2[s2][:])
                P.stt(yt[s2][:], A[:], ss2[s2][:], mg_tab[:], ALU.mult, ALU.mult)
                yield
                if "yret" in dbg:
                    P.copy("dve", ydb[:, t, :], yt[s2][:])
                    if t == NT - 1:
                        P.store(dbg["yret"], ydb[:])
                        dbgtoks.append(ydb.toks[0])
                tc, j = t // 4, t % 4
                pYb = self.psY[:].bitcast(BF16)
                for yc in range(2):
                    P.tr(pYb[:, yc * 128:(yc + 1) * 128], yt[s2][:, yc * 128:(yc + 1) * 128], identb[:])
                P.copy("act", yT[tc % 2][:, :, j * 128:(j + 1) * 128], pYb[:, 0:256].re("p (c i) -> p c i", i=128))
                yield
                if j == 3:
                    out_proj(l, sq, self.Wo, 2, yT[tc % 2], tc, [self.psX])
                    yield

        def retention_stage(l, sq, c3):
            def sbt(name, shape, dt, n=1):
                r = [T(c3.enter_context(SBT("rt_%s%d" % (name, i), shape, dt)), "rt_%s%d" % (name, i)) for i in range(n)]
                return r if n > 1 else r[0]
            W = sbt("W", [128, KC, 1024], BF16)
            load_w(W, "w_in", l, 0, KC, 0, 1024)
            ng = sbt("ng", [128, 256], F32)
            P.load(ng[:], dr["ret_norm_g"][l:l + 1, :].to_broadcast([128, 256]))
            decT = sbt("decT", [128, 4, 128], F32)
            xi_tab = sbt("xi_tab", [128, 2, 128], F32)
            P.load(decT[:], dr["decT"])
            P.load(xi_tab[:], dr["xi_tab"])
            psA, psB, psT, psS, psO, psU, psY, psX = ps
            qk = sbt("qk", [128, 512], BF16, 2)
            kz = sbt("kz", [128, 256], BF16, 2)
            vt = sbt("vt", [128, 256], BF16, 2)
            gs = sbt("gs", [128, 256], BF16, 2)
            tA = sbt("tA", [128, 256], F32, 2)
            tB = sbt("tB", [128, 256], F32, 2)
            qkT = sbt("qkT", [128, 512], BF16, 2)
            qxT = sbt("qxT", [128, 256], BF16, 2)
            sT = sbt("sT", [128, 512], BF16, 2)
            R = sbt("R", [128, 2, 64], F32)
            Rz = sbt("Rz", [128, 4, 64], BF16, 2)
            kTz = sbt("kTz", [128, 4, 128], BF16, 2)
            for i_ in range(2):
                P.memset("pool", Rz[i_][:], 0.0)
                P.memset("pool", kTz[i_][:], 0.0)
            sqo = sbt("sqo", [128, 256], F32)
            ss = sbt("ss", [128, 4], F32, 2)
            o1 = sbt("o1", [128, 256], F32, 2)
            A_ = sbt("A", [128, 256], F32, 2)
            tail = MixerTail(l, sq, 0, c3, psY, psX)
            P.memset("dve", R[:], 0.0)
            def body(t):
                s2 = t % 2
                cos = cs_tab[:, t, 0:32].m(lambda a: a.unsqueeze(1)).bc([128, 8, 32])
                sin = cs_tab[:, t, 32:64].m(lambda a: a.unsqueeze(1)).bc([128, 8, 32])
                proj_tok(psA, W, 0, 512, t)
                yield
                proj_tok(psB, W, 512, 512, t)
                yield
                xv = psA[:].re("p (h two d) -> p h two d", two=2, d=32)
                x1, x2 = xv[:, :, 0, :], xv[:, :, 1, :]
                ov = qk[s2][:].re("p (h two d) -> p h two d", two=2, d=32)
                tAv = tA[s2][:].re("p (h d) -> p h d", d=32)
                tBv = tB[s2][:].re("p (h d) -> p h d", d=32)
                P.tt("dve", tAv, x1, cos, ALU.mult)
                P.tt("dve", tBv, x2, sin, ALU.mult)
                yield
                P.tt("pool", ov[:, :, 0, :], tAv, tBv, ALU.subtract)
                tAv2 = tA[1 - s2][:].re("p (h d) -> p h d", d=32)
                tBv2 = tB[1 - s2][:].re("p (h d) -> p h d", d=32)
                P.tt("dve", tAv2, x2, cos, ALU.mult)
                P.tt("dve", tBv2, x1, sin, ALU.mult)
                yield
                P.tt("pool", ov[:, :, 1, :], tAv2, tBv2, ALU.add)
                yield
                P.tt("pool", kz[s2][:].re("p (h d) -> p h d", d=64), qk[s2][:, 256:512].re("p (h d) -> p h d", d=64),
                     zeta_tab[:].m(lambda a: a.unsqueeze(2)).bc([128, 4, 64]), ALU.mult)
                P.copy("act", vt[s2][:], psB[:, 0:256])
                P.act(gs[s2][:], psB[:, 256:512], AF.Silu)
                yield
                yield
                pTb = psT[:].bitcast(BF16)
                for j in range(4):
                    P.tr(pTb[:, j * 128:(j + 1) * 128], qk[s2][:, j * 128:(j + 1) * 128], identb[:])
                P.copy("act", qkT[s2][:, 0:256], pTb[:, 0:256])
                kview = kTz[s2][:].re("p (pr two) i -> p pr two i", two=2)
                P.copy("act", kview[0:64, :, 0, :], pTb[0:64, 256:512].re("p (pr i) -> p pr i", i=128))
                P.copy("act", kview[64:128, :, 1, :], pTb[64:128, 256:512].re("p (pr i) -> p pr i", i=128))
                P.tt("dve", qxT[s2][:].re("p (a i) -> p a i", i=128), pTb[:, 0:256].re("p (a i) -> p a i", i=128),
                     xi_tab[:], ALU.mult)
                yield
                for h in range(4):
                    pr, b0 = h // 2, (h % 2) * 64
                    P.mm(psS[:, h * 128:(h + 1) * 128], kTz[s2][:, h, :], qkT[s2][:, pr * 128:(pr + 1) * 128])
                P.tt("dve", sT[s2][:].re("p (h i) -> p h i", i=128), psS[:].re("p (h i) -> p h i", i=128), decT[:], ALU.mult)
                yield
                for h in range(4):
                    pr, b0 = h // 2, (h % 2) * 64
                    P.mm(psO[:, h * 64:(h + 1) * 64], sT[s2][:, h * 128:(h + 1) * 128], vt[s2][:, h * 64:(h + 1) * 64],
                         start=True, stop=(t == 0))
                    if t > 0:
                        P.mm(psO[:, h * 64:(h + 1) * 64], qxT[s2][:, pr * 128:(pr + 1) * 128],
                             Rz[s2][:, h, :], start=False, stop=True)
                yield
                if t < NT - 1:
                    for h in range(4):
                        pr, b0 = h // 2, (h % 2) * 64
                        P.mm(psU[b0:b0 + 64, pr * 64:(pr + 1) * 64], kz[s2][:, h * 64:(h + 1) * 64],
                             vt[s2][:, h * 64:(h + 1) * 64])
                    for pr in range(2):
                        P.stt(R[:, pr, :], R[:, pr, :], gc_tab[:, pr:pr + 1], psU[:, pr * 64:(pr + 1) * 64], ALU.mult, ALU.add)
                    rzv = Rz[1 - s2][:].re("p (pr two) e -> p pr two e", two=2)
                    P.copy("pool", rzv[0:64, :, 0, :], R[0:64, :, :])
                    P.copy("pool", rzv[64:128, :, 1, :], R[64:128, :, :])
                yield "mark"
                yield
                P.act(sqo[:], psO[:, 0:256], AF.Square)
                P.red(ss[s2][:], sqo[:].re("p (h d) -> p h d", d=64), ALU.add)
                P.act(ss[s2][:], ss[s2][:], AF.Sqrt, bias=EPS, scale=1.0 / 64)
                yield
                P.recip(ss[s2][:], ss[s2][:])
                P.tt("dve", o1[s2][:].re("p (h d) -> p h d", d=64), psO[:, 0:256].re("p (h d) -> p h d", d=64),
                     ss[s2][:].m(lambda a: a.unsqueeze(2)).bc([128, 4, 64]), ALU.mult)
                yield
                P.tt("pool", o1[s2][:], o1[s2][:], gs[s2][:], ALU.mult)
                yield
                P.tt("pool", A_[s2][:], o1[s2][:], ng[:], ALU.mult)
                yield
                yield from tail.run_g(t, A_[s2])
            run_pipelined(body, NT)
            P.barrier([W.toks[0], tail.Wo.toks[0], ng.toks[0], decT.toks[0], xi_tab.toks[0]])

        MAGIC = 12582912.0
        TWO_PI = 2.0 * math.pi
        C1 = 6.28125
        C2 = TWO_PI - C1

        def sincos(th, out_sin, out_cos, scr):
            k, r = scr
            for out, shift in ((out_sin, 0.0), (out_cos, 0.5 * math.pi)):
                if out is None:
                    continue
                if shift:
                    P.ts("dve", r, th, shift, ALU.add)
                    src = r
                else:
                    src = th
                P.ts("dve", k, src, 1.0 / TWO_PI, ALU.mult, MAGIC, ALU.add)
                P.ts("dve", k, k, -MAGIC, ALU.add)
                P.stt(r, k, -C1, src, ALU.mult, ALU.add)
                P.stt(r, k, -C2, r, ALU.mult, ALU.add)
                P.ts("dve", r, r, math.pi, ALU.min, -math.pi, ALU.max)
                P.act(out, r, AF.Sin)

        def s5_stage(l, sq, c3):
            def sbt(name, shape, dt, n=1):
                r = [T(c3.enter_context(SBT("s5_%s%d" % (name, i), shape, dt)), "s5_%s%d" % (name, i)) for i in range(n)]
                return r if n > 1 else r[0]
            W = sbt("W", [128, KC, 256], BF16)
            load_w(W, "w_in", l, 0, KC, 1024, 256)
            Wo = sbt("Wo", [128, 2, D], BF16)
            load_w(Wo, "w_out", l, 256, 2, 0, D)
            Wg = sbt("Wg", [128, 2, 256], BF16)
            load_w(Wg, "s5_glu_w", l, 0, 2, 0, 256)
            gb = sbt("gb", [128, 2], F32)
            P.load(gb[:], dr["s5_glu_b"][l].rearrange("(c p) -> p c", p=128))
            dT = sbt("dT", [128, 2], F32)
            P.load(dT[:], dr["s5_d"][l].rearrange("(c p) -> p c", p=128))
            BD = sbt("BD", [128, 2, 4, 512], BF16)
            ARi = sbt("ARi", [128, 2, 512], F32)
            AIi = sbt("AIi", [128, 2, 512], F32)
            Ar = sbt("Ar", [128, 2, 4, 129], F32)
            Ai = sbt("Ai", [128, 2, 4, 129], F32)
            Cre = sbt("Cre", [128, 2, 4, 64], BF16)
            nCre = sbt("nCre", [128, 2, 4, 64], BF16)
            nCim = sbt("nCim", [128, 2, 4, 64], BF16)
            with ExitStack() as c5:
                def tb(name, shape):
                    return T(c5.enter_context(SBT("s5t_" + name, shape, F32)), "s5t_" + name)
                lrE, liE, bRe, bIm = tb("lrE", [128, 2, 64]), tb("liE", [128, 2, 64]), tb("bRe", [128, 2, 64]), tb("bIm", [128, 2, 64])
                dtE = tb("dtE", [128, 2])
                for hf in range(2):
                    for g8 in range(8):
                        g = hf * 8 + g8
                        sl = slice(g8 * 16, (g8 + 1) * 16)
                        P.load(lrE[sl, hf, :], dr["s5_lambda_re"][l, g:g + 1, :].to_broadcast([16, 64]))
                        P.load(liE[sl, hf, :], dr["s5_lambda_im"][l, g:g + 1, :].to_broadcast([16, 64]))
                        P.load(dtE[sl, hf:hf + 1], dr["s5_log_dt"][l:l + 1, g:g + 1].to_broadcast([16, 1]))
                        P.load(bRe[sl, hf, :], dr["s5_b_re"][l, g].rearrange("p h -> h p"))
                        P.load(bIm[sl, hf, :], dr["s5_b_im"][l, g].rearrange("p h -> h p"))
                P.act(dtE[:], dtE[:], AF.Exp)
                dtb = dtE[:].m(lambda a: a.unsqueeze(2)).bc([128, 2, 64])
                lrdt, lidt = tb("lrdt", [128, 2, 64]), tb("lidt", [128, 2, 64])
                P.tt("dve", lrdt[:], lrE[:], dtb, ALU.mult)
                P.tt("dve", lidt[:], liE[:], dtb, ALU.mult)
                mag, sn, cs_, k1, k2 = tb("mag", [128, 2, 64]), tb("sn", [128, 2, 64]), tb("cs", [128, 2, 64]), tb("k1", [128, 2, 64]), tb("k2", [128, 2, 64])
                P.act(mag[:], lrdt[:], AF.Exp)
                sincos(lidt[:], sn[:], cs_[:], [k1[:], k2[:]])
                ar, ai = tb("ar", [128, 2, 64]), tb("ai", [128, 2, 64])
                P.tt("dve", ar[:], mag[:], cs_[:], ALU.mult)
                P.tt("dve", ai[:], mag[:], sn[:], ALU.mult)
                den = tb("den", [128, 2, 64])
                P.tt("dve", den[:], lrE[:], lrE[:], ALU.mult)
                P.tt("dve", k1[:], liE[:], liE[:], ALU.mult)
                P.tt("dve", den[:], den[:], k1[:], ALU.add)
                P.recip(den[:], den[:])
                P.ts("dve", ar[:], ar[:], -1.0, ALU.add)
                fre, fim = tb("fre", [128, 2, 64]), tb("fim", [128, 2, 64])
                P.tt("dve", k1[:], ar[:], lrE[:], ALU.mult)
                P.tt("dve", k2[:], ai[:], liE[:], ALU.mult)
                P.tt("dve", k1[:], k1[:], k2[:], ALU.add)
                P.tt("dve", fre[:], k1[:], den[:], ALU.mult)
                P.tt("dve", k1[:], ai[:], lrE[:], ALU.mult)
                P.tt("dve", k2[:], ar[:], liE[:], ALU.mult)
                P.tt("dve", k1[:], k1[:], k2[:], ALU.subtract)
                P.tt("dve", fim[:], k1[:], den[:], ALU.mult)
                Bfr, Bfi, nBfi = tb("Bfr", [128, 2, 64]), tb("Bfi", [128, 2, 64]), tb("nBfi", [128, 2, 64])
                P.tt("dve", k1[:], fre[:], bRe[:], ALU.mult)
                P.tt("dve", k2[:], fim[:], bIm[:], ALU.mult)
                P.tt("dve", Bfr[:], k1[:], k2[:], ALU.subtract)
                P.tt("dve", k1[:], fre[:], bIm[:], ALU.mult)
                P.tt("dve", k2[:], fim[:], bRe[:], ALU.mult)
                P.tt("dve", Bfi[:], k1[:], k2[:], ALU.add)
                P.ts("dve", nBfi[:], Bfi[:], -1.0, ALU.mult)
                m8b = mask8[:].m(lambda a: a.unsqueeze(2)).bc([128, 8, 64])
                for hf in range(2):
                    for q4, src in enumerate((Bfr, Bfi, nBfi, Bfr)):
                        P.tt("dve", BD[:, hf, q4, :].re("p (g q) -> p g q", q=64),
                             src[:, hf, :].m(lambda a: a.unsqueeze(1)).bc([128, 8, 64]), m8b, ALU.mult)
                P.barrier([t_.toks[0] for t_ in (lrE, liE, dtE, bRe, bIm)])
            with ExitStack() as c5:
                lrB, liB, snB, csB, kB1, kB2 = (tb(n, [128, 512]) for n in ("lrB", "liB", "snB", "csB", "kB1", "kB2"))
                dtB = tb("dtB", [128, 8])
                for hf in range(2):
                    P.load(lrB[:], dr["s5_lambda_re"][l:l + 1, hf * 8:(hf + 1) * 8, :].rearrange("o g p -> o (g p)").to_broadcast([128, 512]))
                    P.load(liB[:], dr["s5_lambda_im"][l:l + 1, hf * 8:(hf + 1) * 8, :].rearrange("o g p -> o (g p)").to_broadcast([128, 512]))
                    P.load(dtB[:], dr["s5_log_dt"][l:l + 1, hf * 8:(hf + 1) * 8].to_broadcast([128, 8]))
                    P.act(dtB[:], dtB[:], AF.Exp)
                    dtbb = dtB[:].m(lambda a: a.unsqueeze(2)).bc([128, 8, 64])
                    P.tt("dve", lrB[:].re("p (g q) -> p g q", q=64), lrB[:].re("p (g q) -> p g q", q=64), dtbb, ALU.mult)
                    P.tt("dve", liB[:].re("p (g q) -> p g q", q=64), liB[:].re("p (g q) -> p g q", q=64), dtbb, ALU.mult)
                    P.act(lrB[:], lrB[:], AF.Exp, scale=sidx[:, 1:2])
                    P.ts("dve", liB[:], liB[:], sidx[:, 0:1], ALU.mult)
                    sincos(liB[:], snB[:], csB[:], [kB1[:], kB2[:]])
                    P.tt("dve", ARi[:, hf, :], lrB[:], csB[:], ALU.mult)
                    P.stt(AIi[:, hf, :], snB[:], -1.0, lrB[:], ALU.mult, ALU.mult)
                P.barrier([t_.toks[0] for t_ in (lrB, liB, dtB)])
            with ExitStack() as c5:
                lrS, liS, dtS = tb("lrS", [128, 2, 4]), tb("liS", [128, 2, 4]), tb("dtS", [128, 2, 4])
                for hf in range(2):
                    P.load(lrS[:, hf, :], dr["s5_lambda_re"][l, hf * 8:(hf + 1) * 8, :].rearrange("(b gl) p -> (gl p) b", gl=2))
                    P.load(liS[:, hf, :], dr["s5_lambda_im"][l, hf * 8:(hf + 1) * 8, :].rearrange("(b gl) p -> (gl p) b", gl=2))
                    for gl in range(2):
                        P.load(dtS[gl * 64:(gl + 1) * 64, hf, :],
                               dr["s5_log_dt"][l:l + 1, hf * 8:(hf + 1) * 8].rearrange("o (b gl) -> o gl b", gl=2)[:, gl, :].to_broadcast([64, 4]))
                P.act(dtS[:], dtS[:], AF.Exp)
                P.tt("dve", lrS[:], lrS[:], dtS[:], ALU.mult)
                P.tt("dve", liS[:], liS[:], dtS[:], ALU.mult)
                io = iota129[:].m(lambda a: a.unsqueeze(1).unsqueeze(1)).bc([128, 2, 4, 129])
                exS, thS, snS, csS, kS1, kS2 = (tb(n, [128, 2, 4, 129]) for n in ("exS", "thS", "snS", "csS", "kS1", "kS2"))
                P.tt("dve", exS[:], io, lrS[:].m(lambda a: a.unsqueeze(3)).bc([128, 2, 4, 129]), ALU.mult)
                P.tt("dve", thS[:], io, liS[:].m(lambda a: a.unsqueeze(3)).bc([128, 2, 4, 129]), ALU.mult)
                P.act(exS[:], exS[:], AF.Exp)
                sincos(thS[:], snS[:], csS[:], [kS1[:], kS2[:]])
                P.tt("dve", Ar[:], exS[:], csS[:], ALU.mult)
                P.tt("dve", Ai[:], exS[:], snS[:], ALU.mult)
                P.barrier([t_.toks[0] for t_ in (lrS, liS, dtS)])
            with ExitStack() as c5:
                Cr, Ci = tb("Cr", [128, 2, 4, 64]), tb("Ci", [128, 2, 4, 64])
                P.memset("dve", Cr[:], 0.0)
                P.memset("dve", Ci[:], 0.0)
                for hf in range(2):
                    for b4 in range(4):
                        for gl in range(2):
                            g = hf * 8 + 2 * b4 + gl
                            co = (b4 % 2) * 32 + gl * 16
                            P.load(Cr[gl * 64:(gl + 1) * 64, hf, b4, co:co + 16], dr["s5_c_re"][l, g].rearrange("h p -> p h"))
                            P.load(Ci[gl * 64:(gl + 1) * 64, hf, b4, co:co + 16], dr["s5_c_im"][l, g].rearrange("h p -> p h"))
                P.copy("dve", Cre[:], Cr[:])
                P.ts("dve", nCre[:], Cr[:], -1.0, ALU.mult)
                P.ts("dve", nCim[:], Ci[:], -1.0, ALU.mult)
                P.barrier([t_.toks[0] for t_ in (Cr, Ci)])
            uT = sbt("uT", [128, 2, 512], BF16)
            X1 = sbt("X1", [128, 2, 512], BF16, 2)
            X2 = sbt("X2", [128, 2, 512], BF16, 2)
            Gc = sbt("Gc", [128, 2, 512], F32, 2)
            Pp = sbt("Pp", [128, 4, 512], BF16, 2)
            car = sbt("car", [128, 2, 2, 4], F32)
            ct = sbt("ct", [128, 4, 4], F32)
            yS = sbt("yS", [128, 2, 512], F32)
            gyb = sbt("gyb", [128, 2, 512], BF16)
            sg = [sbt("sg", [128, 512], F32)] * 2
            oT = sbt("oT", [128, 2, 512], F32)
            sqm = [sbt("sqm", [128, 512], F32)] * 2
            rs = sqm[0]
            yTb = sbt("yTb", [128, 2, 512], BF16, 2)
            P.memset("dve", car[:], 0.0)
            psZ, psG, psYh = ps[0:4], ps[4:6], ps[6:8]
            it = 0
            for w in range(4):
                a, b = w * 512, (w + 1) * 512
                for hf in range(2):
                    for kc in range(KC):
                        P.mm(psZ[hf][:], W[:, kc, hf * 128:(hf + 1) * 128], tv(hTbox['h'], kc, a, b), start=(kc == 0), stop=(kc == KC - 1))
                    P.copy("act", uT[:, hf, :], psZ[hf][:])
                def s5_body(idx, w=w):
                    n, hf = idx // 2, idx % 2
                    i0 = n * 128
                    s2 = idx % 2
                    for q4 in range(4):
                        P.mm(psZ[q4][:], uT[:, hf, i0:i0 + 128], BD[:, hf, q4, :])
                    yield
                    P.tt("dve", X1[s2][:, 0, :], psZ[0][:], ARi[:, hf, :], ALU.mult)
                    P.tt("dve", X1[s2][:, 1, :], psZ[1][:], ARi[:, hf, :], ALU.mult)
                    P.tt("dve", X2[s2][:, 0, :], psZ[2][:], AIi[:, hf, :], ALU.mult)
                    P.tt("dve", X2[s2][:, 1, :], psZ[3][:], AIi[:, hf, :], ALU.mult)
                    yield "mark"
                    for ri in range(2):
                        for b4 in range(4):
                            cb = slice(b4 * 128, (b4 + 1) * 128)
                            P.mm(psG[ri][:, cb], X1[s2][:, ri, cb], triub[:], start=True, stop=False)
                            P.mm(psG[ri][:, cb], X2[s2][:, ri, cb], triub[:], start=False, stop=True)
                    yield
                    for ri in range(2):
                        P.tt("dve", Gc[s2][:, ri, :].re("p (b i) -> p b i", i=128), psG[ri][:].re("p (b i) -> p b i", i=128),
                             car[:, hf, ri, :].m(lambda x: x.unsqueeze(2)).bc([128, 4, 128]), ALU.add)
                    Arv = Ar[:, hf, :, 0:128]
                    Aiv = Ai[:, hf, :, 0:128]
                    gre = Gc[s2][:, 0, :].re("p (b i) -> p b i", i=128)
                    gim = Gc[s2][:, 1, :].re("p (b i) -> p b i", i=128)
                    pv = [Pp[s2][:, i_, :].re("p (b i) -> p b i", i=128) for i_ in range(4)]
                    P.tt("dve", pv[0], gre, Arv, ALU.mult)
                    P.tt("pool", pv[1], gim, Aiv, ALU.mult)
                    P.tt("pool", pv[2], gre, Aiv, ALU.mult)
                    P.tt("dve", pv[3], gim, Arv, ALU.mult)
                    yield
                    for b4 in range(4):
                        r0 = 64 * (b4 // 2)
                        o_ = psYh[hf][r0:r0 + 64, i0:i0 + 128]
                        cb = slice(b4 * 128, (b4 + 1) * 128)
                        P.mm(o_, Cre[:, hf, b4, :], Pp[s2][:, 0, cb], start=(b4 % 2 == 0), stop=False)
                        P.mm(o_, nCre[:, hf, b4, :], Pp[s2][:, 1, cb], start=False, stop=False)
                        P.mm(o_, nCim[:, hf, b4, :], Pp[s2][:, 2, cb], start=False, stop=False)
                        P.mm(o_, nCim[:, hf, b4, :], Pp[s2][:, 3, cb], start=False, stop=(b4 % 2 == 1))
                    yield
                    g7r = Gc[s2][:, 0, :].re("p (b i) -> p b i", i=128)[:, :, 127]
                    g7i = Gc[s2][:, 1, :].re("p (b i) -> p b i", i=128)[:, :, 127]
                    a8r, a8i = Ar[:, hf, :, 128], Ai[:, hf, :, 128]
                    P.tt("dve", ct[:, 0, :], a8r, g7r, ALU.mult)
                    P.tt("dve", ct[:, 1, :], a8i, g7i, ALU.mult)
                    P.tt("dve", ct[:, 2, :], a8r, g7i, ALU.mult)
                    P.tt("dve", ct[:, 3, :], a8i, g7r, ALU.mult)
                    P.tt("dve", car[:, hf, 0, :], ct[:, 0, :], ct[:, 1, :], ALU.subtract)
                    P.tt("dve", car[:, hf, 1, :], ct[:, 2, :], ct[:, 3, :], ALU.add)

                run_pipelined(s5_body, 8)
                for hf in range(2):
                    P.stt(yS[:, hf, :], uT[:, hf, :], dT[:, hf:hf + 1], psYh[hf][:], ALU.mult, ALU.add)
                P.act(gyb[:], yS[:], AF.Gelu_apprx_tanh)
                for mc in range(2):
                    for fc in range(2):
                        P.mm(psZ[mc][:], Wg[:, fc, mc * 128:(mc + 1) * 128], gyb[:, fc, :], start=(fc == 0), stop=(fc == 1))
                    P.act(sg[mc][:], psZ[mc][:], AF.Sigmoid, bias=gb[:, mc:mc + 1])
                    P.tt("dve", oT[:, mc, :], gyb[:, mc, :], sg[mc][:], ALU.mult)
                    P.act(sqm[mc][:], oT[:, mc, :], AF.Square)
                    P.mm(psZ[2][:], ones[:], sqm[mc][:], start=(mc == 0), stop=(mc == 1))
                P.act(rs[:], psZ[2][:], AF.Sqrt, bias=EPS, scale=1.0 / 256)
                P.recip(rs[:], rs[:])
                yb = yTb[w % 2]
                for mc in range(2):
                    P.stt(yb[:, mc, :], oT[:, mc, :], mgT[:, l, 2 + mc:3 + mc], rs[:], ALU.mult, ALU.mult)
                if "ys5" in dbg:
                    P.copy("dve", ys5db[:, :, a:b], yb[:])
                out_proj(l, sq, Wo, 2, yb, w, [psZ[3]])
            if "ys5" in dbg:
                P.store(dbg["ys5"], ys5db[:])
                dbgtoks.append(ys5db.toks[0])
            P.barrier([W.toks[0], Wo.toks[0], Wg.toks[0], gb.toks[0], dT.toks[0]])

        def sgu_stage(l, sq, c3):
            def sbt(name, shape, dt, n=1):
                r = [T(c3.enter_context(SBT("sg_%s%d" % (name, i), shape, dt)), "sg_%s%d" % (name, i)) for i in range(n)]
                return r if n > 1 else r[0]
            psA, psB, psT, psS, psO, psU, psY, psX = ps
            W = sbt("W", [128, KC, 512], BF16)
            load_w(W, "w_in", l, 0, KC, 1280, 512)
            sgt = sbt("sgt", [128, 256], F32)
            P.load(sgt[:], dr["sgu_norm_g"][l:l + 1, :].to_broadcast([128, 256]))
            wraw = sbt("wraw", [128, 4, 128], F32)
            P.load(wraw[:], dr["sgu_w"][l].rearrange("g t s -> t g s"))
            bT = sbt("bT", [128, 4], F32)
            P.load(bT[:], dr["sgu_b"][l].rearrange("g t -> t g"))
            wT = sbt("wT", [128, 4, 128], BF16)
            P.tt("dve", wraw[:], wraw[:], tril[:].m(lambda a: a.unsqueeze(1)).bc([128, 4, 128]), ALU.mult)
            for g in range(4):
                P.tr(psT[:, g * 128:(g + 1) * 128], wraw[:, g, :], ident[:])
            P.copy("dve", wT[:], psT[:].re("p (g t) -> p g t", t=128))
            tail = MixerTail(l, sq, 2, c3, psY, psX)
            u = sbt("u", [128, 256], F32, 2)
            gv = sbt("gv", [128, 256], F32, 2)
            sqv = sbt("sqv", [128, 256], F32, 2)
            ss = sbt("ss", [128, 4], F32, 2)
            vt = sbt("vt", [128, 256], BF16, 2)
            A_ = sbt("A", [128, 256], F32, 2)

            def body(t):
                s2 = t % 2
                proj_tok(psA, W, 0, 512, t)
                yield
                P.act(u[s2][:], psA[:, 0:256], AF.Gelu_apprx_tanh)
                P.act(gv[s2][:], psA[:, 256:512], AF.Gelu_apprx_tanh)
                P.act(sqv[s2][:], gv[s2][:], AF.Square)
                yield
                P.red(ss[s2][:], sqv[s2][:].re("p (h d) -> p h d", d=64), ALU.add)
                P.act(ss[s2][:], ss[s2][:], AF.Sqrt, bias=EPS, scale=1.0 / 64)
                yield
                P.recip(ss[s2][:], ss[s2][:])
                P.tt("dve", gv[s2][:].re("p (h d) -> p h d", d=64), gv[s2][:].re("p (h d) -> p h d", d=64),
                     ss[s2][:].m(lambda a: a.unsqueeze(2)).bc([128, 4, 64]), ALU.mult)
                yield
                P.tt("pool", vt[s2][:], gv[s2][:], sgt[:], ALU.mult)
                yield "mark"
                for g in range(4):
                    P.mm(psS[:, g * 64:(g + 1) * 64], wT[:, g, :], vt[s2][:, g * 64:(g + 1) * 64])
                yield
                P.tt("dve", A_[s2][:].re("p (h d) -> p h d", d=64), psS[:, 0:256].re("p (h d) -> p h d", d=64),
                     bT[:].m(lambda a: a.unsqueeze(2)).bc([128, 4, 64]), ALU.add)
                yield
                P.tt("pool", A_[s2][:], A_[s2][:], u[s2][:], ALU.mult)
                yield
                yield from tail.run_g(t, A_[s2])
            run_pipelined(body, NT)
            P.barrier([W.toks[0], tail.Wo.toks[0], sgt.toks[0], wraw.toks[0], bT.toks[0]])

        def ffn_stage(l, sq, c3):
            def sbt(name, shape, dt, n=1):
                r = [T(c3.enter_context(SBT("ff_%s%d" % (name, i), shape, dt)), "ff_%s%d" % (name, i)) for i in range(n)]
                return r if n > 1 else r[0]
            FT = 1024
            NSC = FT // 512
            h2 = sbt("h2", [128, KC, FT], BF16, 1)
            zT = sbt("zT", [128, NFF, FT], BF16, 1)
            Wu = sbt("Wu", [128, KC, 512], BF16, 2)
            Wd = sbt("Wd", [128, NFF, 128], BF16, 2)
            cw = sbt("cw", [128, 3, 2 * NFF], F32)
            cb = sbt("cb", [128, 2 * NFF], F32)
            P.load(cw[:], dr["ffn_conv_w"][l].rearrange("k (c p) -> p k c", p=128))
            P.load(cb[:], dr["ffn_conv_b"][l].rearrange("(c p) -> p c", p=128))
            atail = sbt("atail", [128, 2 * NFF, 2], F32)
            P.memset("dve", atail[:], 0.0)
            cg = sbt("cg", [128, 512], F32, 2)
            cu = sbt("cu", [128, 512], F32, 2)
            sqb = sbt("nsq", [128, 512], F32, 2)
            rs = sbt("nrs", [128, 512], F32)
            tmp = sbt("ntm", [128, 512], F32, 2)
            mb = l * 48 + 24
            P.stt(scl[:, 1, :], modT[:, mb + 8:mb + 16, sq], 1.0, g2n[:, l, :], ALU.add, ALU.mult)
            nq = 0
            na = 0
            ng_ = 0
            nd = 0
            npb = 0
            for tp in range(S // FT):
                for sc in range(NSC):
                    a, b = tp * FT + sc * 512, tp * FT + (sc + 1) * 512
                    pss = ps[6 + sc % 2]
                    for kc in range(KC):
                        sb_ = sqb[nq % 2]
                        nq += 1
                        P.act(sb_[:], tv(xT, kc, a, b), AF.Square)
                        P.mm(pss[:], ones[:], sb_[:], start=(kc == 0), stop=(kc == KC - 1))
                    P.act(rs[:], pss[:], AF.Sqrt, bias=EPS, scale=1.0 / D)
                    P.recip(rs[:], rs[:])
                    for kc in range(KC):
                        tm = tmp[kc % 2]
                        P.tt("dve", tm[:], tv(xT, kc, a, b), rs[:], ALU.mult)
                        P.act(h2[:, kc, sc * 512:(sc + 1) * 512], tm[:], AF.Identity, scale=scl[:, 1, kc:kc + 1],
                              bias=modT[:, mb + kc, sq:sq + 1])
                for g in range(NFF // 2):
                    Wg = Wu[ng_ % 2]
                    ng_ += 1
                    for side in range(2):
                        for k0 in range(0, KC, 4):
                            st = stg[stgn["n"] % 2]
                            stgn["n"] += 1
                            c0 = side * DFF + g * 256
                            P.load(st[:, 0:1024].re("p (k c) -> p k c", c=256),
                                   dr["ffn_w_up"][l, k0 * 128:(k0 + 4) * 128, c0:c0 + 256].rearrange("(k p) c -> p k c", p=128))
                            P.copy("act", Wg[:, k0:k0 + 4, side * 256:(side + 1) * 256], st[:, 0:1024].re("p (k c) -> p k c", c=256))
                    for jj in range(2):
                        j = g * 2 + jj
                        for sc in range(NSC):
                            res = []
                            for side in range(2):
                                pst = ps[npb % 4]
                                npb += 1
                                ch = j + side * NFF
                                for kc in range(KC):
                                    P.mm(pst[:], Wg[:, kc, side * 256 + jj * 128:side * 256 + (jj + 1) * 128],
                                         h2[:, kc, sc * 512:(sc + 1) * 512], start=(kc == 0), stop=(kc == KC - 1))
                                cc = (cg if side == 0 else cu)[(j * NSC + sc) % 2]
                                P.act(cc[:], pst[:], AF.Identity, scale=cw[:, 2, ch:ch + 1], bias=cb[:, ch:ch + 1])
                                P.stt(cc[:, 1:512], pst[:, 0:511], cw[:, 1, ch:ch + 1], cc[:, 1:512], ALU.mult, ALU.add)
                                P.stt(cc[:, 2:512], pst[:, 0:510], cw[:, 0, ch:ch + 1], cc[:, 2:512], ALU.mult, ALU.add)
                                P.stt(cc[:, 0:1], atail[:, ch, 1:2], cw[:, 1, ch:ch + 1], cc[:, 0:1], ALU.mult, ALU.add)
                                P.stt(cc[:, 0:2], atail[:, ch, 0:2], cw[:, 0, ch:ch + 1], cc[:, 0:2], ALU.mult, ALU.add)
                                P.copy("dve", atail[:, ch, :], pst[:, 510:512])
                                res.append(cc)
                            P.act(res[0][:], res[0][:], AF.Silu)
                            P.tt("pool", zT[:, j, sc * 512:(sc + 1) * 512], res[0][:], res[1][:], ALU.mult)
                for dc in range(KC):
                    Wdb = Wd[nd % 2]
                    nd += 1
                    for j0 in range(0, NFF, 11):
                        st = stg[stgn["n"] % 2]
                        stgn["n"] += 1
                        P.load(st[:, 0:11 * 128].re("p (j c) -> p j c", c=128),
                               dr["ffn_w_down"][l, j0 * 128:(j0 + 11) * 128, dc * 128:(dc + 1) * 128].rearrange("(j p) c -> p j c", p=128))
                        P.copy("act", Wdb[:, j0:j0 + 11, :], st[:, 0:11 * 128].re("p (j c) -> p j c", c=128))
                    for sc in range(NSC):
                        a, b = tp * FT + sc * 512, tp * FT + (sc + 1) * 512
                        pst = ps[4 + (dc * NSC + sc) % 2]
                        for j in range(NFF):
                            P.mm(pst[:], Wdb[:, j, :], zT[:, j, sc * 512:(sc + 1) * 512], start=(j == 0), stop=(j == NFF - 1))
                        P.stt(tv(xT, dc, a, b), pst[:], modT[:, l * 48 + 40 + dc, sq:sq + 1], tv(xT, dc, a, b), ALU.mult, ALU.add)
            P.barrier()

        def nsa_stage(l, sq, c3):
            def mk(cx, pre):
                def sbt(name, shape, dt, n=1):
                    r = [T(cx.enter_context(SBT("%s_%s%d" % (pre, name, i), shape, dt)), "%s_%s%d" % (pre, name, i)) for i in range(n)]
                    return r if n > 1 else r[0]
                return sbt
            sbt = mk(c3, "ns")
            tail = MixerTail(l, sq, 3, c3, ps[6], ps[7])
            qT = sbt("qT", [128, 2, S], BF16)
            ksTz = sbt("ksTz", [128, 2, S], BF16)
            kwTz = sbt("kwTz", [128, 2, S], BF16)
            vs1 = sbt("vs1", [128, NT, 2, 65], BF16)
            vw1 = sbt("vw1", [128, NT, 2, 65], BF16)
            gates = sbt("gates", [128, NT, 12], F32)
            kcmpTz = sbt("kcmpTz", [128, 2, 128], BF16)
            vc1 = sbt("vc1", [128, 2, 97], BF16)
            qg = sbt("qg", [128, 64], F32)
            kg = sbt("kg", [128, 3, 64], F32)
            csc = sbt("csc", [128, 64], F32)
            P.memset("pool", ksTz[:], 0.0)
            P.memset("pool", kwTz[:], 0.0)
            P.memset("pool", kcmpTz[:], 0.0)
            P.memset("pool", vs1[:], 1.0)
            P.memset("pool", vw1[:], 1.0)
            P.load(qg[:], dr["nsa_q_norm_g"][l:l + 1, :].to_broadcast([128, 64]))
            P.ts("dve", qg[:], qg[:], 0.125, ALU.mult)
            P.load(kg[:], dr["nsa_k_norm_g"][l:l + 1, :, :].rearrange("o a d -> o (a d)").to_broadcast([128, 192]))
            P.load(csc[:], dr["cs_c"])

            def norm_rope(sb2, src, npart, nh, gain, cos, sin, outs, iv=lambda v: v):
                for _ in norm_rope_g(sb2, src, npart, nh, gain, cos, sin, outs, iv):
                    pass

            def norm_rope_g(sb2, src, npart, nh, gain, cos, sin, outs, iv=lambda v: v):
                sqb, ss, xn, tA, tB, tC, tD = sb2
                pp = slice(0, npart)
                P.act(sqb[pp, 0:nh, :], src, AF.Square)
                yield
                P.red(ss[pp, 0:nh], sqb[pp, 0:nh, :], ALU.add)
                P.act(ss[pp, 0:nh], ss[pp, 0:nh], AF.Sqrt, bias=EPS, scale=1.0 / 64)
                yield
                P.recip(ss[pp, 0:nh], ss[pp, 0:nh])
                P.tt("dve", xn[pp, 0:nh, :], src, ss[pp, 0:nh].m(lambda a: a.unsqueeze(2)).bc([npart, nh, 64]), ALU.mult)
                yield
                P.tt("pool", xn[pp, 0:nh, :], xn[pp, 0:nh, :], gain, ALU.mult)
                yield
                x1, x2 = xn[pp, 0:nh, 0:32], xn[pp, 0:nh, 32:64]
                P.tt("dve", tA[pp, 0:nh, :], x1, cos, ALU.mult)
                P.tt("pool", tB[pp, 0:nh, :], x2, sin, ALU.mult)
                P.tt("dve", tC[pp, 0:nh, :], x2, cos, ALU.mult)
                P.tt("pool", tD[pp, 0:nh, :], x1, sin, ALU.mult)
                yield
                P.tt("dve", outs(0), iv(tA[pp, 0:nh, :]), iv(tB[pp, 0:nh, :]), ALU.subtract)
                P.tt("dve", outs(1), iv(tC[pp, 0:nh, :]), iv(tD[pp, 0:nh, :]), ALU.add)
                yield

            with ExitStack() as c4:
                sb4 = mk(c4, "n4")
                W = sb4("W", [128, KC, 1036], BF16)
                load_w(W, "w_in", l, 0, KC, 1792, 1036)
                nrs = [(sb4("sqb%d" % i_, [128, 4, 64], F32), sb4("ss%d" % i_, [128, 4], F32), sb4("xn%d" % i_, [128, 4, 64], F32),
                        sb4("tA%d" % i_, [128, 4, 32], F32), sb4("tB%d" % i_, [128, 4, 32], F32),
                        sb4("tC%d" % i_, [128, 4, 32], F32), sb4("tD%d" % i_, [128, 4, 32], F32)) for i_ in range(2)]
                nr = nrs[0]
                with ExitStack() as c5:
                    sb5 = mk(c5, "n5")
                    kvT = sb5("kvT", [128, 2, S], BF16)
                    for c in range(4):
                        for cb in range(2):
                            pst = ps[(2 * c + cb) % 4]
                            for k8 in range(KC):
                                P.mm(pst[:], W[:, k8, 256 + cb * 128:256 + (cb + 1) * 128], tv(hTbox['h'], k8, c * 512, (c + 1) * 512),
                                     start=(k8 == 0), stop=(k8 == KC - 1))
                            P.copy("act" if cb else "dve", kvT[:, cb, c * 512:(c + 1) * 512], pst[:])
                    W1 = sb5("W1", [128, 32, 128], BF16)
                    W2 = sb5("W2", [128, 64], BF16)
                    w2s = sb5("w2s", [128, 64], F32)
                    peT = sb5("peT", [128, 32], F32)
                    peb = sb5("peb", [128, 32], BF16)
                    bias = sb5("bias", [128, 1], F32)
                    hidb = sb5("hidb", [128, 128], BF16)
                    kct = sb5("kct", [128, 2, 64], BF16)
                    for side in range(2):
                        w1v = dr["nsa_cmp_w1"][l, side].rearrange("(p d) h -> d p h", d=64)
                        for half in range(2):
                            for pc in range(0, 32, 8):
                                st = stg[stgn["n"] % 2]
                                stgn["n"] += 1
                                hs = slice(half * 64, (half + 1) * 64)
                                P.load(st[hs, 0:1024].re("d (p h) -> d p h", h=128), w1v[:, pc:pc + 8, :])
                                P.copy("pool", W1[hs, pc:pc + 8, :], st[hs, 0:1024].re("d (p h) -> d p h", h=128))
                        P.load(w2s[:], dr["nsa_cmp_w2"][l, side])
                        P.copy("pool", W2[:], w2s[:])
                        for half in range(2):
                            P.load(peT[half * 64:(half + 1) * 64, :], dr["nsa_cmp_pe"][l, side].rearrange("p d -> d p"))
                        P.copy("dve", peb[:], peT[:])
                        for p_ in range(32):
                            P.mm(ps[3][:, 0:1], W1[0:64, p_, :], peb[0:64, p_:p_ + 1], start=(p_ == 0), stop=(p_ == 31))
                        P.copy("dve", bias[:], ps[3][:, 0:1])
                        for kvh in range(2):
                            pst = ps[kvh]
                            hs = slice(kvh * 64, (kvh + 1) * 64)
                            for p_ in range(32):
                                P.mm(pst[:, 0:127], W1[hs, p_, :], kvT[hs, side, p_:p_ + 16 * 126 + 1:16],
                                     start=(p_ == 0), stop=(p_ == 31))
                            P.act(hidb[:, 0:127], pst[:, 0:127], AF.Gelu_apprx_tanh, bias=bias[:, 0:1])
                            P.mm(ps[2][0:127, kvh * 64:(kvh + 1) * 64], hidb[:, 0:127], W2[:])
                        if side == 0:
                            norm_rope(nr, ps[2][0:127, 0:128].re("p (h d) -> p h d", d=64), 127, 2,
                                      kg[0:127, 0, :].m(lambda a: a.unsqueeze(1)).bc([127, 2, 64]),
                                      csc[0:127, 0:32].m(lambda a: a.unsqueeze(1)).bc([127, 2, 32]),
                                      csc[0:127, 32:64].m(lambda a: a.unsqueeze(1)).bc([127, 2, 32]),
                                      lambda hf_: kct[0:127, :, hf_ * 32:(hf_ + 1) * 32])
                            pTb = ps[3][:].bitcast(BF16)
                            P.tr(pTb[:, 0:127], kct[0:127, :, :].re("p h d -> p (h d)"), identb[0:127, 0:127])
                            P.copy("act", kcmpTz[0:64, 0, 0:127], pTb[0:64, 0:127])
                            P.copy("act", kcmpTz[64:128, 1, 0:127], pTb[64:128, 0:127])
                        else:
                            P.copy("act", vc1[0:127, :, 0:64], ps[2][0:127, 0:128].re("p (h d) -> p h d", d=64))
                            o1f = sb5("o1f", [128, 33], F32)
                            P.load(o1f[:], dr["ovl1"])
                            P.copy("dve", vc1[0:127, :, 64:97], o1f[0:127, :].m(lambda a: a.unsqueeze(1)).bc([127, 2, 33]))
                    P.barrier([w2s.toks[0], peT.toks[0]])
                with ExitStack() as c5:
                    sb5 = mk(c5, "n6")
                    kk = sb5("kk", [128, 4, 64], F32, 2)
                    qtok = sb5("qtok", [128, 256], BF16, 2)
                    ktok = sb5("ktok", [128, 256], BF16, 2)
                    kg4 = sb5("kg4", [128, 4, 64], F32)
                    P.copy("dve", kg4[:, 0:2, :], kg[:, 1, :].m(lambda a: a.unsqueeze(1)).bc([128, 2, 64]))
                    P.copy("dve", kg4[:, 2:4, :], kg[:, 2, :].m(lambda a: a.unsqueeze(1)).bc([128, 2, 64]))
                    def pa_body(t):
                        s2 = t % 2
                        psA, psB, psC, psT = ps[0], ps[1], ps[2], ps[3]
                        proj_tok(psA, W, 0, 256, t)
                        yield
                        proj_tok(psB, W, 512, 512, t)
                        yield
                        proj_tok(psC, W, 1024, 12, t)
                        yield
                        cos = cs_tab[:, t, 0:32].m(lambda a: a.unsqueeze(1)).bc([128, 4, 32])
                        sin = cs_tab[:, t, 32:64].m(lambda a: a.unsqueeze(1)).bc([128, 4, 32])
                        qo = qtok[s2][:].re("p (g k two d) -> p k g two d", g=2, k=2, two=2, d=32)
                        yield from norm_rope_g(nrs[s2], psA[:, 0:256].re("p (h d) -> p h d", d=64), 128, 4,
                                  qg[:].m(lambda a: a.unsqueeze(1)).bc([128, 4, 64]), cos, sin,
                                  lambda hf_: qo[:, :, :, hf_, :], iv=lambda v: v.re("p (k g) d -> p k g d", g=2))
                        P.copy("act", kk[s2][:, 0:2, :], psB[:, 0:128].re("p (h d) -> p h d", d=64))
                        P.copy("act", kk[s2][:, 2:4, :], psB[:, 256:384].re("p (h d) -> p h d", d=64))
                        ko = ktok[s2][:].re("p (h two d) -> p h two d", two=2, d=32)
                        P.copy("act", vs1[:, t, :, 0:64], psB[:, 128:256].re("p (h d) -> p h d", d=64))
                        P.copy("act", vw1[:, t, :, 0:64], psB[:, 384:512].re("p (h d) -> p h d", d=64))
                        P.act(gates[:, t, :], psC[:, 0:12], AF.Sigmoid)
                        yield
                        yield "mark"
                        yield from norm_rope_g(nrs[s2], kk[s2][:], 128, 4, kg4[:], cos, sin, lambda hf_: ko[:, :, hf_, :])
                        pTb = psT[:].bitcast(BF16)
                        for j in range(2):
                            P.tr(pTb[:, j * 128:(j + 1) * 128], qtok[s2][:, j * 128:(j + 1) * 128], identb[:])
                            P.tr(pTb[:, (2 + j) * 128:(3 + j) * 128], ktok[s2][:, j * 128:(j + 1) * 128], identb[:])
                        yield
                        tsl = slice(t * 128, (t + 1) * 128)
                        P.copy("act", qT[:, :, tsl], pTb[:, 0:256].re("p (g i) -> p g i", i=128))
                        P.copy("dve", ksTz[0:64, 0, tsl], pTb[0:64, 256:384])
                        P.copy("dve", ksTz[64:128, 1, tsl], pTb[64:128, 256:384])
                        P.copy("dve", kwTz[0:64, 0, tsl], pTb[0:64, 384:512])
                        P.copy("dve", kwTz[64:128, 1, tsl], pTb[64:128, 384:512])
                    run_pipelined(pa_body, NT)
                    P.barrier()
                P.barrier([W.toks[0]])
            with ExitStack() as c4:
                sb4 = mk(c4, "n7")
                negv = sb4("negv", [128, S], BF16)
                for cc in range(0, S, 1024):
                    st = stg[stgn["n"] % 2]
                    stgn["n"] += 1
                    P.load(st[:, 0:1024], dr["negvalid"][:, cc:cc + 1024])
                    P.copy("pool", negv[:, cc:cc + 1024], st[:, 0:1024])
                negTf = sb4("negTf", [128, 2, 128], F32)
                negTb = sb4("negTb", [128, 2, 128], BF16)
                P.load(negTf[:], dr["negT"])
                P.copy("dve", negTb[:], negTf[:])
                emf = sb4("emf", [32, 16, 128], F32)
                emb = sb4("emb", [32, 16, 128], BF16)
                P.load(emf[:], dr["emat"])
                P.copy("dve", emb[:], emf[:])
                bonus = sb4("bonus", [128, NT, 32], F32)
                P.load(bonus[:], dr["bonus"])
                Ych = sb4("Ych", [128, 4, 256], F32, 2)
                eb = sb4("eb", [128, 512], BF16, 2)
                rsc = sb4("rsc", [128, 2, 4], F32)
                imp = sb4("imp", [128, 4, 32], F32)
                imp2 = sb4("imp2", [128, 4, 32], F32)
                m8 = sb4("m8", [128, 8], F32)
                selm = sb4("selm", [128, 32], F32)
                negsel = sb4("negsel", [128, 4, 32], BF16)
                nsT = sb4("nsT", [32, 512], BF16)
                coef = sb4("coef", [128, 4], F32, 2)
                tmpo = sb4("tmpo", [128, 4, 64], F32, 2)
                ne = 0
                for c in range(4):
                    q0, q1 = c * 512, (c + 1) * 512
                    Y = Ych[c % 2]
                    for kvh in range(2):
                        psOc = [ps[2], ps[3]]
                        for g in range(2):
                            psc = ps[ne % 2]
                            e_ = eb[ne % 2]
                            ne += 1
                            P.mm(psc[0:127, :], kcmpTz[:, kvh, 0:127], qT[:, g, q0:q1], start=True, stop=False)
                            P.mm(psc[0:127, :], identb[0:127, 0:127], negv[0:127, q0:q1], start=False, stop=True)
                            P.act(e_[0:127, :], psc[0:127, :], AF.Exp)
                            for j in range(4):
                                P.mm(psOc[g][:, j * 97:(j + 1) * 97], e_[0:127, j * 128:(j + 1) * 128], vc1[0:127, kvh, :])
                            ocv = psOc[g][:, 0:388].re("p (j e) -> p j e", e=97)
                            P.ts("dve", rsc[:, g, :], ocv[:, :, 96], 1e-30, ALU.add)
                            P.recip(rsc[:, g, :], rsc[:, g, :])
                        oc0 = psOc[0][:, 0:388].re("p (j e) -> p j e", e=97)
                        oc1 = psOc[1][:, 0:388].re("p (j e) -> p j e", e=97)
                        P.tt("dve", imp[:], oc0[:, :, 64:96], rsc[:, 0, :].m(lambda a: a.unsqueeze(2)).bc([128, 4, 32]), ALU.mult)
                        P.tt("dve", imp2[:], oc1[:, :, 64:96], rsc[:, 1, :].m(lambda a: a.unsqueeze(2)).bc([128, 4, 32]), ALU.mult)
                        P.tt("dve", imp[:], imp[:], imp2[:], ALU.add)
                        P.tt("dve", imp[:], imp[:], bonus[:, 4 * c:4 * c + 4, :], ALU.add)
                        pTb = ps[6][:].bitcast(BF16)
                        for j in range(4):
                            P.op("dve", lambda e, j=j: e.max(out=m8.h[:], in_=imp.h[:, j, :]), reads=imp.toks, writes=m8.toks)
                            P.ts("dve", selm[:], imp[:, j, :], m8[:, 7:8], ALU.is_ge)
                            P.ts("dve", negsel[:, j, :], selm[:], -NEGB, ALU.mult, NEGB, ALU.add)
                            P.tr(pTb[0:32, j * 128:(j + 1) * 128], negsel[:, j, :], identb[:])
                        P.copy("act", nsT[:], pTb[0:32, 0:512])
                        for g in range(2):
                            qh = 2 * kvh + g
                            ocv = psOc[g][:, 0:388].re("p (j e) -> p j e", e=97)
                            cf = coef[g]
                            P.tt("dve", cf[:], rsc[:, g, :], gates[:, 4 * c:4 * c + 4, qh * 3 + 0], ALU.mult)
                            P.tt("dve", Y[:, :, qh * 64:(qh + 1) * 64], ocv[:, :, 0:64],
                                 cf[:].m(lambda a: a.unsqueeze(2)).bc([128, 4, 64]), ALU.mult)
                        items = []
                        for g in range(2):
                            qh = 2 * kvh + g
                            for br, (KTz, V1, pso) in enumerate(((ksTz, vs1, ps[4]), (kwTz, vw1, ps[5]))):
                                kb0 = 0 if br == 0 else max(0, 4 * c - 4)
                                kbs = list(range(kb0, 4 * c + 4))
                                for ii, kb in enumerate(kbs):
                                    items.append((g, qh, br, KTz, V1, pso, kb, ii == 0, ii == len(kbs) - 1))

                        def mkA(it_, slot):
                            g, qh, br, KTz, V1, pso, kb, isfirst, islast = it_
                            r = kb - 4 * c
                            ja = max(0, r)
                            jb = 3 if br == 0 else min(3, r + 4)
                            ca, cbn = ja * 128, (jb + 1) * 128
                            pss, e_ = ps[slot], eb[slot]

                            def A():
                                P.mm(pss[:, ca:cbn], KTz[:, kvh, kb * 128:(kb + 1) * 128], qT[:, g, q0 + ca:q0 + cbn],
                                     start=True, stop=False)
                                if br == 0:
                                    P.mm(pss[:, ca:cbn], emb[:, kb, :], nsT[:, ca:cbn], start=False, stop=(r < 0))
                                if r >= 0:
                                    P.mm(pss[:, r * 128:(r + 1) * 128], identb[:], negTb[:, 0, :], start=False, stop=True)
                                if br == 1 and r <= -1:
                                    P.mm(pss[:, (r + 4) * 128:(r + 5) * 128], identb[:], negTb[:, 1, :], start=False, stop=True)
                                P.act(e_[:, ca:cbn], pss[:, ca:cbn], AF.Exp)

                            def B():
                                first = isfirst
                                for j in range(ja, jb + 1):
                                    P.mm(pso[:, j * 65:(j + 1) * 65], e_[:, j * 128:(j + 1) * 128], V1[:, kb, kvh, :],
                                         start=first, stop=(kb == 4 * c + j), skip_group_check=True)
                                    first = False
                                if islast:
                                    ov = pso[:, 0:260].re("p (j e) -> p j e", e=65)
                                    cf = coef[br]
                                    P.recip(cf[:], ov[:, :, 64])
                                    P.tt("dve", cf[:], cf[:], gates[:, 4 * c:4 * c + 4, qh * 3 + 1 + br], ALU.mult)
                                    P.tt("dve", tmpo[br][:], ov[:, :, 0:64], cf[:].m(lambda a: a.unsqueeze(2)).bc([128, 4, 64]), ALU.mult)
                                    P.tt("pool", Y[:, :, qh * 64:(qh + 1) * 64], Y[:, :, qh * 64:(qh + 1) * 64], tmpo[br][:], ALU.add)
                            return A, B
                        prevB = None
                        for it_ in items:
                            A, B = mkA(it_, ne % 2)
                            ne += 1
                            A()
                            if prevB is not None:
                                prevB()
                            prevB = B
                        if prevB is not None:
                            prevB()
                    for j in range(4):
                        tail.run(4 * c + j, Y[:, j, :])
                P.barrier([negTf.toks[0], emf.toks[0], bonus.toks[0]])
            P.barrier([tail.Wo.toks[0], qg.toks[0], kg.toks[0], csc.toks[0]])

        outtoks = []
        for sq in range(nseq):
            with ExitStack() as c3:
                load_x(sq, c3)
            for l in range(nlayers):
              with ExitStack() as c4:
                hTbox['h'] = T(c4.enter_context(SBT("hT_%d_%d" % (sq, l), [128, KC, S], BF16)), "hT", NT)
                with ExitStack() as c3:
                    norm_stage(l, sq, 0, c3)
                if "hT" in dbg and l == 0 and sq == 0:
                    with ExitStack() as c3:
                        hd = T(c3.enter_context(SBT("hd", [128, KC, S], F32)), "hd")
                        P.copy("dve", hd[:], hTbox['h'][:])
                        P.store(dbg["hT"], hd[:])
                        P.barrier([hd.toks[0]])
                if "ret" in stages:
                    with ExitStack() as c3:
                        retention_stage(l, sq, c3)
                if "s5" in stages:
                    with ExitStack() as c3:
                        s5_stage(l, sq, c3)
                if "sgu" in stages:
                    with ExitStack() as c3:
                        sgu_stage(l, sq, c3)
                if "nsa" in stages:
                    with ExitStack() as c3:
                        nsa_stage(l, sq, c3)
                P.barrier()
              if True:
                if "ffn" in stages:
                    with ExitStack() as c3:
                        ffn_stage(l, sq, c3)
            if "xT" in dbg and sq == 0:
                P.store(dbg["xT"], xT[:])
                dbgtoks.append(xT.toks[0])
            with ExitStack() as c3:
                outtoks += store_x(sq, c3)
        P.finish(outtoks + dbgtoks)
        print("instructions:", P.nins, "sems:", P.nsem)
    return nc


_CACHE = {}


def kernel(**inputs):
    n = 8
    if "nc" not in _CACHE:
        _CACHE["nc"] = build(nseq=2, nlayers=DEPTH, wl=DEPTH)
        _CACHE["consts"] = host_consts()
    nc = _CACHE["nc"]
    consts = _CACHE["consts"]
    x = np.asarray(inputs["x"], dtype=np.float32)
    c = np.asarray(inputs["c"], dtype=np.float32)
    w = {k: np.ascontiguousarray(np.asarray(inputs[k], dtype=np.float32)) for k in WEIGHT_SHAPES}
    in_maps = []
    for i in range(n):
        m = {"x": np.ascontiguousarray(x[2 * i:2 * i + 2]), "c": np.ascontiguousarray(c[2 * i:2 * i + 2])}
        m.update(w)
        m.update(consts)
        in_maps.append(m)
    res = run_bass_kernel_spmd(nc, in_maps, core_ids=list(range(n)))
    return np.concatenate([np.asarray(r["out"], dtype=np.float32) for r in res.results], axis=0)
```

```python
import math
import os
from contextlib import ExitStack

import numpy as np
import concourse.bass as bass
import concourse.mybir as mybir
from concourse.bass_utils import run_bass_kernel_spmd

F32 = mybir.dt.float32
BF16 = mybir.dt.bfloat16
AF = mybir.ActivationFunctionType
ALU = mybir.AluOpType
AX = mybir.AxisListType

S = 2048
D = 1024
KC = 8
NT = 16
DEPTH = 4
DFF = 2816
NFF = 22
INC = 2828
EPS = 1e-6
NEGB = -30000.0


class Tok:
    __slots__ = ("name", "w", "r", "dsem", "dtot", "dw", "uid", "psum")
    _n = [0]

    def __init__(self, name):
        Tok._n[0] += 1
        self.uid = Tok._n[0]
        self.psum = False
        self.name = name
        self.w = None
        self.r = {}
        self.dsem = None
        self.dtot = 0
        self.dw = 0


class V:
    __slots__ = ("ap", "toks")

    def __init__(self, ap, toks):
        self.ap = ap
        self.toks = toks

    def __getitem__(self, idx):
        return V(self.ap[idx], self.toks)

    def m(self, fn):
        return V(fn(self.ap), self.toks)

    def bc(self, shape):
        return V(self.ap.to_broadcast(shape), self.toks)

    def re(self, pat, **kw):
        return V(self.ap.rearrange(pat, **kw), self.toks)

    def bitcast(self, dt):
        return V(self.ap.bitcast(dt), self.toks)


class TK:
    def __init__(self, t, i):
        self.t = t
        self.i = i

    def __getitem__(self, idx):
        return V(self.t.h[idx], [self.t.toks[self.i]])


class T:
    def __init__(self, handle, name, ntok=1):
        self.h = handle
        self.toks = [Tok("%s.%d" % (name, i)) for i in range(ntok)]

    def __getitem__(self, idx):
        return V(self.h[idx], self.toks)

    def k(self, i):
        return TK(self, i)


class Eng:
    def __init__(self, name, obj):
        self.name = name
        self.obj = obj
        self.sems = []
        self.gen = -1
        self.cnt = 0
        self.known = {}
        self.dknown = {}


class Prog:
    ROT = 30000

    def __init__(self, nc, ctx):
        self.nc = nc
        self.ctx = ctx
        self.eng = {
            "pe": Eng("pe", nc.tensor),
            "act": Eng("act", nc.scalar),
            "dve": Eng("dve", nc.vector),
            "pool": Eng("pool", nc.gpsimd),
            "sp": Eng("sp", nc.sync),
        }
        self.nsem = 0
        for e in self.eng.values():
            self._newsem(e)
        self.nins = 0
        self.dfree = []
        self.dlive = {}

    def _sem(self, name):
        self.nsem += 1
        return self.ctx.enter_context(self.nc.semaphore("%s_%d" % (name, self.nsem)))

    def _newsem(self, e):
        e.sems.append(self._sem(e.name))
        e.gen += 1
        e.cnt = 0

    def sb(self, name, shape, dt, ntok=1):
        h = self.ctx.enter_context(self.nc.sbuf_tensor("sb_" + name, list(shape), dt))
        return T(h, name, ntok)

    def psum(self, name, shape, dt):
        h = self.ctx.enter_context(self.nc.psum_tensor(name, list(shape), dt))
        t = T(h, name, 1)
        t.toks[0].psum = True
        return t

    def _wait_ticket(self, E, tk):
        if tk is None:
            return
        e2, gen, val = tk
        if e2 is E and E.name == "pe":
            return
        k = E.known.get(e2.name)
        if k is not None and k >= (gen, val):
            return
        E.obj.wait_ge(e2.sems[gen], val)
        E.known[e2.name] = (gen, val)

    def _wait_dma(self, E, t, val):
        if val <= 0 or t.dsem is None:
            return
        if E.dknown.get(t.uid, 0) >= val:
            return
        E.obj.wait_ge(t.dsem, val)
        E.dknown[t.uid] = val

    def _deps(self, E, reads, writes):
        for t in reads:
            self._wait_ticket(E, t.w)
            self._wait_dma(E, t, t.dw)
            if t.psum:
                for en2, tk in t.r.items():
                    if en2 != E.name:
                        self._wait_ticket(E, tk)
        for t in writes:
            self._wait_ticket(E, t.w)
            for tk in t.r.values():
                self._wait_ticket(E, tk)
            self._wait_dma(E, t, t.dtot)

    def op(self, en, fn, reads=(), writes=()):
        E = self.eng[en]
        self._deps(E, reads, writes)
        ins = fn(E.obj)
        if E.cnt >= self.ROT:
            self._newsem(E)
        E.cnt += 1
        ins.then_inc(E.sems[E.gen], 1)
        tk = (E, E.gen, E.cnt)
        for t in reads:
            t.r[en] = tk
        for t in writes:
            t.w = tk
            t.r = {}
        self.nins += 1
        return ins

    def dma(self, qn, out, in_, reads=(), writes=(), **kw):
        E = self.eng[qn]
        self._deps(E, reads, writes)
        ins = E.obj.dma_start(out=out, in_=in_, **kw)
        t = (list(writes) + list(reads))[0]
        if t.dsem is None:
            if self.dfree:
                t.dsem, base = self.dfree.pop()
            else:
                t.dsem, base = self._sem("d"), 0
            t.dtot = base
            t.dw = 0
            self.dlive[t.uid] = t
        ins.then_inc(t.dsem, 16)
        t.dtot += 16
        if writes:
            t.dw = t.dtot
        self.nins += 1
        return ins

    def barrier(self, toks=()):
        SP = self.eng["sp"]
        for t in self.dlive.values():
            self._wait_dma(SP, t, t.dtot)
        self.op("sp", lambda e: e.nop(), reads=(), writes=())
        last = {n: (e, e.gen, e.cnt) for n, e in self.eng.items()}
        for n, E in self.eng.items():
            for n2, tk in last.items():
                if tk[2] > 0:
                    self._wait_ticket(E, tk)
        for t in self.dlive.values():
            self.dfree.append((t.dsem, t.dtot))
            t.dsem = None
            t.dtot = 0
            t.dw = 0
        self.dlive = {}
        for E in self.eng.values():
            E.dknown = {}

    def finish(self, toks):
        E = self.eng["sp"]
        for t in toks:
            self._wait_dma(E, t, t.dtot)

    @staticmethod
    def _tk(*vs):
        out = []
        for v in vs:
            if isinstance(v, V):
                out.extend(v.toks)
        return out

    @staticmethod
    def _a(v):
        return v.ap if isinstance(v, V) else v

    def mm(self, out, lhsT, rhs, start=True, stop=True, **kw):
        return self.op("pe", lambda e: e.matmul(out.ap, lhsT=lhsT.ap, rhs=rhs.ap, start=start, stop=stop, **kw),
                       reads=self._tk(lhsT, rhs), writes=out.toks)

    def tr(self, out, in_, ident):
        return self.op("pe", lambda e: e.transpose(out.ap, in_.ap, ident.ap),
                       reads=self._tk(in_, ident), writes=out.toks)

    def act(self, out, in_, func, bias=None, scale=None, accum_out=None):
        kw = {}
        if bias is not None:
            kw["bias"] = self._a(bias)
        if scale is not None:
            kw["scale"] = self._a(scale)
        if accum_out is not None:
            kw["accum_out"] = accum_out.ap
        return self.op("act", lambda e: e.activation(out=out.ap, in_=in_.ap, func=func, **kw),
                       reads=self._tk(in_, bias, scale), writes=self._tk(out, accum_out))

    def tt(self, en, out, a, b, op):
        return self.op(en, lambda e: e.tensor_tensor(out=out.ap, in0=a.ap, in1=b.ap, op=op),
                       reads=self._tk(a, b), writes=out.toks)

    def ts(self, en, out, a, s1, op0, s2=None, op1=None, accum_out=None):
        kw = {}
        if op1 is not None:
            kw["op1"] = op1
        if accum_out is not None:
            kw["accum_out"] = accum_out.ap
        return self.op(en, lambda e: e.tensor_scalar(out=out.ap, in0=a.ap, scalar1=self._a(s1), scalar2=self._a(s2),
                                                     op0=op0, **kw),
                       reads=self._tk(a, s1, s2), writes=self._tk(out, accum_out))

    def stt(self, out, a, scalar, b, op0, op1, accum_out=None):
        kw = {}
        if accum_out is not None:
            kw["accum_out"] = accum_out.ap
        return self.op("dve", lambda e: e.scalar_tensor_tensor(out=out.ap, in0=a.ap, scalar=self._a(scalar), in1=b.ap,
                                                               op0=op0, op1=op1, **kw),
                       reads=self._tk(a, scalar, b), writes=self._tk(out, accum_out))

    def copy(self, en, out, in_):
        if en == "act":
            return self.op("act", lambda e: e.copy(out=out.ap, in_=in_.ap), reads=in_.toks, writes=out.toks)
        return self.op(en, lambda e: e.tensor_copy(out=out.ap, in_=in_.ap), reads=in_.toks, writes=out.toks)

    def red(self, out, in_, op, axis=AX.X):
        return self.op("dve", lambda e: e.tensor_reduce(out=out.ap, in_=in_.ap, axis=axis, op=op),
                       reads=in_.toks, writes=out.toks)

    def memset(self, en, out, val):
        return self.op(en, lambda e: e.memset(out.ap, val), reads=(), writes=out.toks)

    def recip(self, out, in_):
        return self.op("dve", lambda e: e.reciprocal(out=out.ap, in_=in_.ap), reads=in_.toks, writes=out.toks)

    def load(self, dst, src_ap, q="sp", **kw):
        return self.dma(q, dst.ap, src_ap, writes=dst.toks, **kw)

    def store(self, dst_ap, src, q="sp", **kw):
        return self.dma(q, dst_ap, src.ap, reads=src.toks, **kw)


def host_consts():
    c = {}
    pos = np.arange(S, dtype=np.float64)
    inv = 10000.0 ** (-np.arange(0, 64, 2, dtype=np.float64) / 64)
    ang = (pos[:, None].astype(np.float32) * inv[None, :].astype(np.float32)).astype(np.float32)
    cos = np.cos(ang).astype(np.float32)
    sin = np.sin(ang).astype(np.float32)
    cs = np.concatenate([cos, sin], axis=1)
    c["cs_tab"] = np.ascontiguousarray(cs.reshape(NT, 128, 64).transpose(1, 0, 2))
    H = 4
    lg = np.log1p(-(2.0 ** (-5.0 - np.arange(H, dtype=np.float64))))
    idx = np.arange(128, dtype=np.float64)
    diff = idx[None, :] - idx[:, None]
    dec = np.where(diff >= 0, np.exp(lg[:, None, None] * np.maximum(diff, 0.0)), 0.0) * 0.125
    c["decT"] = np.ascontiguousarray(dec.transpose(1, 0, 2)).astype(np.float32)
    xi = np.exp(lg[:, None] * (idx + 1)[None, :]) * 0.125
    xit = np.zeros((128, 2, 128), np.float32)
    for h in range(H):
        xit[(h % 2) * 64:(h % 2) * 64 + 64, h // 2, :] = xi[h][None, :]
    c["xi_tab"] = xit
    zeta = np.exp(lg[:, None] * (127 - idx)[None, :])
    c["zeta_tab"] = np.ascontiguousarray(zeta.T).astype(np.float32)
    gch = np.exp(lg * 128)
    gct = np.zeros((128, 2), np.float32)
    for h in range(H):
        gct[(h % 2) * 64:(h % 2) * 64 + 64, h // 2] = gch[h]
    c["gc_tab"] = gct
    c["tril"] = np.tril(np.ones((128, 128), np.float32))
    c["triu"] = np.triu(np.ones((128, 128), np.float32))
    m8 = np.zeros((128, 8), np.float32)
    for g8 in range(8):
        m8[g8 * 16:(g8 + 1) * 16, g8] = 1.0
    c["mask8"] = m8
    c["iota129"] = np.broadcast_to(np.arange(129, dtype=np.float32)[None, :], (128, 129)).copy()
    kk_ = np.arange(128)[:, None]
    qq_ = np.arange(128)[None, :]
    nt = np.zeros((128, 2, 128), np.float32)
    nt[:, 0, :] = np.where(kk_ <= qq_, 0.0, NEGB)
    nt[:, 1, :] = np.where(kk_ > qq_, 0.0, NEGB)
    c["negT"] = nt
    nn = np.arange(128)[:, None]
    qpos = np.arange(S)[None, :]
    c["negvalid"] = np.where(16 * nn + 31 <= qpos, 0.0, NEGB).astype(np.float32)
    em = np.zeros((32, 16, 128), np.float32)
    for kb in range(16):
        em[2 * kb, kb, 0:64] = 1.0
        em[2 * kb + 1, kb, 64:128] = 1.0
    c["emat"] = em
    posf = np.arange(S)
    cur = posf // 64
    blk = np.arange(32)[None, :]
    bon = np.zeros((S, 32), np.float32)
    forced = (blk == 0) | (blk == cur[:, None]) | (blk == cur[:, None] - 1)
    bon[forced] = 1e3
    bon[np.broadcast_to(blk, (S, 32)) > cur[:, None]] = -1e9
    c["bonus"] = np.ascontiguousarray(bon.reshape(NT, 128, 32).transpose(1, 0, 2))
    ii = np.arange(127)[:, None]
    jj = np.arange(32)[None, :]
    ovl = ((ii * 16 < (jj + 1) * 64) & (ii * 16 + 32 > jj * 64)).astype(np.float32)
    o1 = np.zeros((128, 33), np.float32)
    o1[:127, :32] = ovl
    o1[:127, 32] = 1.0
    c["ovl1"] = o1
    cend = (np.arange(127) * 16 + 31).astype(np.float32)
    angc = (cend[:, None] * inv[None, :].astype(np.float32)).astype(np.float32)
    csc = np.zeros((128, 64), np.float32)
    csc[:127, :32] = np.cos(angc)
    csc[:127, 32:] = np.sin(angc)
    c["cs_c"] = csc
    sx = np.zeros((128, 2), np.float32)
    sx[:, 0] = np.arange(128)
    sx[:, 1] = -np.arange(128)
    c["sidx"] = sx
    c["ident"] = np.eye(128, dtype=np.float32)
    c["ones"] = np.ones((128, 128), np.float32)
    return c


CONST_SHAPES = {"cs_tab": (128, NT, 64), "decT": (128, 4, 128), "xi_tab": (128, 2, 128), "zeta_tab": (128, 4),
                "gc_tab": (128, 2), "ident": (128, 128), "ones": (128, 128), "tril": (128, 128),
                "triu": (128, 128), "mask8": (128, 8), "iota129": (128, 129), "sidx": (128, 2),
                "negT": (128, 2, 128), "negvalid": (128, S), "emat": (32, 16, 128), "bonus": (128, NT, 32),
                "ovl1": (128, 33), "cs_c": (128, 64)}

WEIGHT_SHAPES = {
    "norm1_g": (DEPTH, D), "norm2_g": (DEPTH, D), "ada_w": (DEPTH, D, 6 * D), "ada_b": (DEPTH, 6 * D),
    "w_in": (DEPTH, D, INC), "ret_norm_g": (DEPTH, 256),
    "s5_lambda_re": (DEPTH, 16, 64), "s5_lambda_im": (DEPTH, 16, 64), "s5_log_dt": (DEPTH, 16),
    "s5_b_re": (DEPTH, 16, 64, 16), "s5_b_im": (DEPTH, 16, 64, 16), "s5_c_re": (DEPTH, 16, 16, 64),
    "s5_c_im": (DEPTH, 16, 16, 64), "s5_d": (DEPTH, 256), "s5_glu_w": (DEPTH, 256, 256), "s5_glu_b": (DEPTH, 256),
    "sgu_norm_g": (DEPTH, 256), "sgu_w": (DEPTH, 4, 128, 128), "sgu_b": (DEPTH, 4, 128),
    "nsa_q_norm_g": (DEPTH, 64), "nsa_k_norm_g": (DEPTH, 3, 64), "nsa_cmp_pe": (DEPTH, 2, 32, 64),
    "nsa_cmp_w1": (DEPTH, 2, 2048, 128), "nsa_cmp_w2": (DEPTH, 2, 128, 64),
    "mix_norm_g": (DEPTH, D), "w_out": (DEPTH, D, D), "ffn_w_up": (DEPTH, D, 2 * DFF),
    "ffn_conv_w": (DEPTH, 3, 2 * DFF), "ffn_conv_b": (DEPTH, 2 * DFF), "ffn_w_down": (DEPTH, DFF, D),
}


def build(nseq=2, nlayers=DEPTH, stages=("ret", "s5", "sgu", "nsa", "ffn"), debug=(), wl=DEPTH):
    nc = bass.Bass("TRN2", target_bir_lowering=False)
    dr = {}
    dr["x"] = nc.dram_tensor("x", [nseq, S, D], F32, kind="ExternalInput").ap()
    dr["c"] = nc.dram_tensor("c", [nseq, D], F32, kind="ExternalInput").ap()
    for n, shp in WEIGHT_SHAPES.items():
        dr[n] = nc.dram_tensor(n, [wl] + list(shp[1:]), F32, kind="ExternalInput").ap()
    for n, shp in CONST_SHAPES.items():
        dr[n] = nc.dram_tensor(n, list(shp), F32, kind="ExternalInput").ap()
    dr["out"] = nc.dram_tensor("out", [nseq, S, D], F32, kind="ExternalOutput").ap()
    S5TAB = {"BD": ([128, 2, 4, 512], BF16), "ARi": ([128, 2, 512], F32), "AIi": ([128, 2, 512], F32),
             "Ar": ([128, 2, 4, 129], F32), "Ai": ([128, 2, 4, 129], F32),
             "Cre": ([128, 2, 4, 64], BF16), "nCre": ([128, 2, 4, 64], BF16), "nCim": ([128, 2, 4, 64], BF16)}
    s5c = {}
    for l_ in range(nlayers):
        for n, (shp, dt_) in S5TAB.items():
            s5c[(l_, n)] = (nc.dram_tensor("s5c_%s_%d" % (n, l_), shp, dt_, kind="Internal").ap(), Tok("s5c_%s_%d" % (n, l_)))
    dbg = {}
    DBG_SHAPES = {"hT": [128, KC, S], "yret": [128, NT, 256], "xT": [128, KC, S], "mod": [128, wl * 48, 2], "ys5": [128, 2, S]}
    for n in debug:
        dbg[n] = nc.dram_tensor("dbg_" + n, DBG_SHAPES[n], BF16 if n in ("ys5", "yret") else F32, kind="ExternalOutput").ap()
    dbgtoks = []

    ucnt = [0]

    def SBT(name, shape, dt):
        ucnt[0] += 1
        return nc.sbuf_tensor("%s_u%d" % (name, ucnt[0]), list(shape), dt)

    with ExitStack() as ctx:
        P = Prog(nc, ctx)
        ctx.enter_context(nc.allow_non_contiguous_dma(reason="small param loads"))
        ctx.enter_context(nc.allow_low_precision(reason="bf16 matmul operands"))
        xT = P.sb("xT", [128, KC, S], F32, ntok=NT)
        hTbox = {}

        def tv(Tt, kcs, a, b):
            return V(Tt.h[:, kcs, a:b], [Tt.toks[i] for i in range(a // 128, (b + 127) // 128)])

        ps = [P.psum("ps%d" % i, [128, 512], F32) for i in range(8)]
        cs_tab = P.sb("cs_tab", [128, NT, 64], F32)
        zeta_tab = P.sb("zeta_tab", [128, 4], F32)
        gc_tab = P.sb("gc_tab", [128, 2], F32)
        ident = P.sb("ident", [128, 128], F32)
        identb = P.sb("identb", [128, 128], BF16)
        ones = P.sb("ones", [128, 128], F32)
        tril = P.sb("tril", [128, 128], F32)
        triu = P.sb("triu", [128, 128], F32)
        triub = P.sb("triub", [128, 128], BF16)
        mask8 = P.sb("mask8", [128, 8], F32)
        iota129 = P.sb("iota129", [128, 129], F32)
        sidx = P.sb("sidx", [128, 2], F32)
        for t, n in ((cs_tab, "cs_tab"), (zeta_tab, "zeta_tab"),
                     (gc_tab, "gc_tab"), (ident, "ident"), (ones, "ones"), (tril, "tril"),
                     (triu, "triu"), (mask8, "mask8"), (iota129, "iota129"), (sidx, "sidx")):
            P.load(t[:], dr[n])
        P.copy("dve", identb[:], ident[:])
        P.copy("dve", triub[:], triu[:])

        modT = P.sb("modT", [128, wl * 48, 2], F32)
        cT = P.sb("cT", [128, KC, 2], F32)
        g1n = P.sb("g1n", [128, wl, KC], F32)
        g2n = P.sb("g2n", [128, wl, KC], F32)
        scl = P.sb("scl", [128, 2, KC], F32)
        mg_tab = P.sb("mg_tab", [128, 256], F32)
        mgT = P.sb("mgT", [128, wl, KC], F32)

        for b_ in range(nseq):
            P.load(cT[:, :, b_], dr["c"][b_].rearrange("(k p) -> p k", p=128))
        if nseq < 2:
            P.memset("dve", cT[:, :, nseq:2], 0.0)
        P.act(cT[:], cT[:], AF.Silu)
        P.load(g1n[:], dr["norm1_g"].rearrange("l (k p) -> p l k", p=128))
        P.load(g2n[:], dr["norm2_g"].rearrange("l (k p) -> p l k", p=128))
        P.load(mgT[:], dr["mix_norm_g"].rearrange("l (k p) -> p l k", p=128))
        adab = P.sb("adab", [128, wl * 48], F32)
        P.load(adab[:], dr["ada_b"].rearrange("l (o p) -> p (l o)", p=128))
        with ExitStack() as c2:
            awb = [T(c2.enter_context(SBT("awb%d" % i, [128, KC, 512], F32)), "awb%d" % i) for i in range(2)]
            it = 0
            for l in range(nlayers):
                for cc in range(12):
                    buf = awb[it % 2]
                    it += 1
                    for kc in range(KC):
                        P.load(buf[:, kc, :], dr["ada_w"][l, kc * 128:(kc + 1) * 128, cc * 512:(cc + 1) * 512],
                               q="sp")
                    pst = ps[it % 2]
                    for j in range(4):
                        for kc in range(KC):
                            P.mm(pst[:, 2 * j:2 * j + 2], buf[:, kc, j * 128:(j + 1) * 128], cT[:, kc, :],
                                 start=(kc == 0), stop=(kc == KC - 1))
                    o0 = l * 48 + cc * 4
                    P.tt("dve", modT[:, o0:o0 + 4, :], pst[:, 0:8].re("p (j b) -> p j b", b=2),
                         adab[:, o0:o0 + 4].m(lambda a: a.unsqueeze(2)).bc([128, 4, 2]), ALU.add)
            P.barrier([b.toks[0] for b in awb])
        if "mod" in dbg:
            P.store(dbg["mod"], modT[:])
            dbgtoks.append(modT.toks[0])

        if "ys5" in dbg:
            ys5db = P.sb("ys5db", [128, 2, S], BF16)
        if "yret" in dbg:
            ydb = P.sb("ydb", [128, NT, 256], BF16)
        rr = {"ev": 0}

        def run_pipelined(make, n):
            DONE = object()
            act_ = []
            nxt = [0]

            def start():
                act_.append([make(nxt[0]), False])
                nxt[0] += 1
            start()
            while act_:
                if len(act_) == 1 and act_[0][1] and nxt[0] < n:
                    start()
                for ent in list(act_):
                    r = next(ent[0], DONE)
                    if r is DONE:
                        act_.remove(ent)
                    elif r == "mark":
                        ent[1] = True
                if not act_ and nxt[0] < n:
                    start()

        def evac_eng():
            rr["ev"] += 1
            return "act" if rr["ev"] % 2 else "dve"

        def load_x(sq, c3):
            xin = [T(c3.enter_context(SBT("xin%d" % i, [128, D], F32)), "xin%d" % i) for i in range(2)]
            for t in range(NT):
                xb = xin[t % 2]
                P.load(xb[:], dr["x"][sq, t * 128:(t + 1) * 128, :])
                en = "act" if t % 2 else "dve"
                for half in range(2):
                    pst = ps[(2 * t + half) % 4]
                    for j in range(4):
                        kc = half * 4 + j
                        P.tr(pst[:, j * 128:(j + 1) * 128], xb[:, kc * 128:(kc + 1) * 128], ident[:])
                    P.copy(en, tv(xT, slice(half * 4, half * 4 + 4), t * 128, (t + 1) * 128),
                           pst[:].re("p (j c) -> p j c", c=128))
            P.barrier([b.toks[0] for b in xin])

        def store_x(sq, c3):
            xo = [T(c3.enter_context(SBT("xo%d" % i, [128, D], F32)), "xo%d" % i) for i in range(2)]
            for t in range(NT):
                xb = xo[t % 2]
                en = "act" if t % 2 else "dve"
                for half in range(2):
                    pst = ps[(2 * t + half) % 4]
                    for j in range(4):
                        kc = half * 4 + j
                        P.tr(pst[:, j * 128:(j + 1) * 128], tv(xT, kc, t * 128, (t + 1) * 128), ident[:])
                    P.copy(en, xb[:, half * 512:(half + 1) * 512], pst[:])
                P.store(dr["out"][sq, t * 128:(t + 1) * 128, :], xb[:])
            P.barrier([b.toks[0] for b in xo])
            return [b.toks[0] for b in xo]

        def norm_stage(l, sq, which, c3):
            gN = g1n if which == 0 else g2n
            mb = l * 48 + 24 * which
            P.stt(scl[:, which, :], modT[:, mb + 8:mb + 16, sq], 1.0, gN[:, l, :], ALU.add, ALU.mult)
            sqb = [T(c3.enter_context(SBT("nsq%d_%d" % (i, which), [128, 512], F32)), "nsq%d" % i) for i in range(3)]
            rs = [T(c3.enter_context(SBT("nrs%d_%d" % (i, which), [128, 512], F32)), "nrs%d" % i) for i in range(2)]
            tmp = [T(c3.enter_context(SBT("ntm%d_%d" % (i, which), [128, 512], F32)), "ntm%d" % i) for i in range(2)]
            n = 0
            for tc in range(4):
                a, b = tc * 512, (tc + 1) * 512
                pss = ps[4 + tc % 2]
                for kc in range(KC):
                    sb_ = sqb[n % 3]
                    n += 1
                    if kc % 2 == 0:
                        P.act(sb_[:], tv(xT, kc, a, b), AF.Square)
                    else:
                        P.tt("pool", sb_[:], tv(xT, kc, a, b), tv(xT, kc, a, b), ALU.mult)
                    P.mm(pss[:], ones[:], sb_[:], start=(kc == 0), stop=(kc == KC - 1))
                r = rs[tc % 2]
                P.act(r[:], pss[:], AF.Sqrt, bias=EPS, scale=1.0 / D)
                P.recip(r[:], r[:])
                for kc in range(KC):
                    tm = tmp[kc % 2]
                    P.tt("dve", tm[:], tv(xT, kc, a, b), r[:], ALU.mult)
                    P.act(tv(hTbox['h'], kc, a, b), tm[:], AF.Identity, scale=scl[:, which, kc:kc + 1],
                          bias=modT[:, mb + kc, sq:sq + 1])
            P.barrier()

        stg = [P.sb("stg%d" % i, [128, 1408], F32) for i in range(2)]
        stgn = {"n": 0, "c": 0}

        def cast_eng():
            stgn["c"] += 1
            return "act" if stgn["c"] % 2 else "dve"

        def load_w(dst, name, l, r0, nk, c0, ncol):
            for k in range(nk):
                for cc in range(0, ncol, 1408):
                    n = min(1408, ncol - cc)
                    st = stg[stgn["n"] % 2]
                    stgn["n"] += 1
                    P.load(st[:, 0:n], dr[name][l, r0 + k * 128:r0 + (k + 1) * 128, c0 + cc:c0 + cc + n])
                    P.copy(cast_eng(), dst[:, k, cc:cc + n], st[:, 0:n])

        def proj_tok(pst, W, wc0, ncol, t):
            for kc in range(KC):
                P.mm(pst[:, 0:ncol], tv(hTbox['h'], kc, t * 128, (t + 1) * 128), W[:, kc, wc0:wc0 + ncol],
                     start=(kc == 0), stop=(kc == KC - 1))

        def out_proj(l, sq, Wo, nyc, yT, tc, psx):
            a, b = tc * 512, (tc + 1) * 512
            for dc in range(KC):
                pst = psx[dc % len(psx)]
                for yc in range(nyc):
                    P.mm(pst[:], Wo[:, yc, dc * 128:(dc + 1) * 128], yT[:, yc, :], start=(yc == 0), stop=(yc == nyc - 1))
                P.stt(tv(xT, dc, a, b), pst[:], modT[:, l * 48 + 16 + dc, sq:sq + 1], tv(xT, dc, a, b), ALU.mult, ALU.add)

        class MixerTail:
            def __init__(self, l, sq, m, c3, psY, psX):
                self.l, self.sq, self.m, self.psY, self.psX = l, sq, m, psY, psX

                def sbt(name, shape, dt, n=1):
                    r = [T(c3.enter_context(SBT("mt_%s%d" % (name, i), shape, dt)), "mt_%s%d" % (name, i)) for i in range(n)]
                    return r if n > 1 else r[0]
                self.Wo = sbt("Wo", [128, 2, D], BF16)
                load_w(self.Wo, "w_out", l, m * 256, 2, 0, D)
                P.load(mg_tab[:], dr["mix_norm_g"][l:l + 1, m * 256:(m + 1) * 256].to_broadcast([128, 256]))
                self.sqo = sbt("sqo", [128, 256], F32, 2)
                self.ss2 = sbt("ss2", [128, 1], F32, 2)
                self.yt = sbt("yt", [128, 256], BF16, 2)
                self.yT = sbt("yT", [128, 2, 512], BF16, 2)

            def run(self, t, A):
                for _ in self.run_g(t, A):
                    pass

            def run_g(self, t, A):
                s2, m, l, sq = t % 2, self.m, self.l, self.sq
                ss2, yt, yT = self.ss2, self.yt, self.yT
                P.act(self.sqo[s2][:], A[:], AF.Square, accum_out=ss2[s2][:])
                P.act(ss2[s2][:], ss2[s2][:], AF.Sqrt, bias=EPS, scale=1.0 / 256)
                yield
                P.recip(ss2[s2][:], ss2[s2][:])
                P.stt(yt[s2][:], A[:], ss2[s2][:], mg_tab[:], ALU.mult, ALU.mult)
                yield
                if "yret" in dbg:
                    P.copy("dve", ydb[:, t, :], yt[s2][:])
                    if t == NT - 1:
                        P.store(dbg["yret"], ydb[:])
                        dbgtoks.append(ydb.toks[0])
                tc, j = t // 4, t % 4
                pYb = self.psY[:].bitcast(BF16)
                for yc in range(2):
                    P.tr(pYb[:, yc * 128:(yc + 1) * 128], yt[s2][:, yc * 128:(yc + 1) * 128], identb[:])
                P.copy("act", yT[tc % 2][:, :, j * 128:(j + 1) * 128], pYb[:, 0:256].re("p (c i) -> p c i", i=128))
                yield
                if j == 3:
                    out_proj(l, sq, self.Wo, 2, yT[tc % 2], tc, [self.psX])
                    yield

        def retention_stage(l, sq, c3):
            def sbt(name, shape, dt, n=1):
                r = [T(c3.enter_context(SBT("rt_%s%d" % (name, i), shape, dt)), "rt_%s%d" % (name, i)) for i in range(n)]
                return r if n > 1 else r[0]
            W = sbt("W", [128, KC, 1024], BF16)
            load_w(W, "w_in", l, 0, KC, 0, 1024)
            ng = sbt("ng", [128, 256], F32)
            P.load(ng[:], dr["ret_norm_g"][l:l + 1, :].to_broadcast([128, 256]))
            decT = sbt("decT", [128, 4, 128], F32)
            xi_tab = sbt("xi_tab", [128, 2, 128], F32)
            P.load(decT[:], dr["decT"])
            P.load(xi_tab[:], dr["xi_tab"])
            psA, psB, psT, psS, psO, psU, psY, psX = ps
            qk = sbt("qk", [128, 512], BF16, 2)
            kz = sbt("kz", [128, 256], BF16, 2)
            vt = sbt("vt", [128, 256], BF16, 2)
            gs = sbt("gs", [128, 256], BF16, 2)
            tA = sbt("tA", [128, 256], F32, 2)
            tB = sbt("tB", [128, 256], F32, 2)
            qkT = sbt("qkT", [128, 512], BF16, 2)
            qxT = sbt("qxT", [128, 256], BF16, 2)
            sT = sbt("sT", [128, 512], BF16, 2)
            R = sbt("R", [128, 2, 64], F32)
            Rz = sbt("Rz", [128, 4, 64], BF16, 2)
            kTz = sbt("kTz", [128, 4, 128], BF16, 2)
            for i_ in range(2):
                P.memset("pool", Rz[i_][:], 0.0)
                P.memset("pool", kTz[i_][:], 0.0)
            sqo = sbt("sqo", [128, 256], F32)
            ss = sbt("ss", [128, 4], F32, 2)
            o1 = sbt("o1", [128, 256], F32, 2)
            A_ = sbt("A", [128, 256], F32, 2)
            tail = MixerTail(l, sq, 0, c3, psY, psX)
            P.memset("dve", R[:], 0.0)
            def body(t):
                s2 = t % 2
                cos = cs_tab[:, t, 0:32].m(lambda a: a.unsqueeze(1)).bc([128, 8, 32])
                sin = cs_tab[:, t, 32:64].m(lambda a: a.unsqueeze(1)).bc([128, 8, 32])
                proj_tok(psA, W, 0, 512, t)
                yield
                proj_tok(psB, W, 512, 512, t)
                yield
                xv = psA[:].re("p (h two d) -> p h two d", two=2, d=32)
                x1, x2 = xv[:, :, 0, :], xv[:, :, 1, :]
                ov = qk[s2][:].re("p (h two d) -> p h two d", two=2, d=32)
                tAv = tA[s2][:].re("p (h d) -> p h d", d=32)
                tBv = tB[s2][:].re("p (h d) -> p h d", d=32)
                P.tt("dve", tAv, x1, cos, ALU.mult)
                P.tt("dve", tBv, x2, sin, ALU.mult)
                yield
                P.tt("pool", ov[:, :, 0, :], tAv, tBv, ALU.subtract)
                tAv2 = tA[1 - s2][:].re("p (h d) -> p h d", d=32)
                tBv2 = tB[1 - s2][:].re("p (h d) -> p h d", d=32)
                P.tt("dve", tAv2, x2, cos, ALU.mult)
                P.tt("dve", tBv2, x1, sin, ALU.mult)
                yield
                P.tt("pool", ov[:, :, 1, :], tAv2, tBv2, ALU.add)
                yield
                P.tt("pool", kz[s2][:].re("p (h d) -> p h d", d=64), qk[s2][:, 256:512].re("p (h d) -> p h d", d=64),
                     zeta_tab[:].m(lambda a: a.unsqueeze(2)).bc([128, 4, 64]), ALU.mult)
                P.copy("act", vt[s2][:], psB[:, 0:256])
                P.act(gs[s2][:], psB[:, 256:512], AF.Silu)
                yield
                yield
                pTb = psT[:].bitcast(BF16)
                for j in range(4):
                    P.tr(pTb[:, j * 128:(j + 1) * 128], qk[s2][:, j * 128:(j + 1) * 128], identb[:])
                P.copy("act", qkT[s2][:, 0:256], pTb[:, 0:256])
                kview = kTz[s2][:].re("p (pr two) i -> p pr two i", two=2)
                P.copy("act", kview[0:64, :, 0, :], pTb[0:64, 256:512].re("p (pr i) -> p pr i", i=128))
                P.copy("act", kview[64:128, :, 1, :], pTb[64:128, 256:512].re("p (pr i) -> p pr i", i=128))
                P.tt("dve", qxT[s2][:].re("p (a i) -> p a i", i=128), pTb[:, 0:256].re("p (a i) -> p a i", i=128),
                     xi_tab[:], ALU.mult)
                yield
                for h in range(4):
                    pr, b0 = h // 2, (h % 2) * 64
                    P.mm(psS[:, h * 128:(h + 1) * 128], kTz[s2][:, h, :], qkT[s2][:, pr * 128:(pr + 1) * 128])
                P.tt("dve", sT[s2][:].re("p (h i) -> p h i", i=128), psS[:].re("p (h i) -> p h i", i=128), decT[:], ALU.mult)
                yield
                for h in range(4):
                    pr, b0 = h // 2, (h % 2) * 64
                    P.mm(psO[:, h * 64:(h + 1) * 64], sT[s2][:, h * 128:(h + 1) * 128], vt[s2][:, h * 64:(h + 1) * 64],
                         start=True, stop=(t == 0))
                    if t > 0:
                        P.mm(psO[:, h * 64:(h + 1) * 64], qxT[s2][:, pr * 128:(pr + 1) * 128],
                             Rz[s2][:, h, :], start=False, stop=True)
                yield
                if t < NT - 1:
                    for h in range(4):
                        pr, b0 = h // 2, (h % 2) * 64
                        P.mm(psU[b0:b0 + 64, pr * 64:(pr + 1) * 64], kz[s2][:, h * 64:(h + 1) * 64],
                             vt[s2][:, h * 64:(h + 1) * 64])
                    for pr in range(2):
                        P.stt(R[:, pr, :], R[:, pr, :], gc_tab[:, pr:pr + 1], psU[:, pr * 64:(pr + 1) * 64], ALU.mult, ALU.add)
                    rzv = Rz[1 - s2][:].re("p (pr two) e -> p pr two e", two=2)
                    P.copy("pool", rzv[0:64, :, 0, :], R[0:64, :, :])
                    P.copy("pool", rzv[64:128, :, 1, :], R[64:128, :, :])
                yield "mark"
                yield
                P.act(sqo[:], psO[:, 0:256], AF.Square)
                P.red(ss[s2][:], sqo[:].re("p (h d) -> p h d", d=64), ALU.add)
                P.act(ss[s2][:], ss[s2][:], AF.Sqrt, bias=EPS, scale=1.0 / 64)
                yield
                P.recip(ss[s2][:], ss[s2][:])
                P.tt("dve", o1[s2][:].re("p (h d) -> p h d", d=64), psO[:, 0:256].re("p (h d) -> p h d", d=64),
                     ss[s2][:].m(lambda a: a.unsqueeze(2)).bc([128, 4, 64]), ALU.mult)
                yield
                P.tt("pool", o1[s2][:], o1[s2][:], gs[s2][:], ALU.mult)
                yield
                P.tt("pool", A_[s2][:], o1[s2][:], ng[:], ALU.mult)
                yield
                yield from tail.run_g(t, A_[s2])
            run_pipelined(body, NT)
            P.barrier([W.toks[0], tail.Wo.toks[0], ng.toks[0], decT.toks[0], xi_tab.toks[0]])

        MAGIC = 12582912.0
        TWO_PI = 2.0 * math.pi
        C1 = 6.28125
        C2 = TWO_PI - C1

        def sincos(th, out_sin, out_cos, scr):
            k, r = scr
            for out, shift in ((out_sin, 0.0), (out_cos, 0.5 * math.pi)):
                if out is None:
                    continue
                if shift:
                    P.ts("dve", r, th, shift, ALU.add)
                    src = r
                else:
                    src = th
                P.ts("dve", k, src, 1.0 / TWO_PI, ALU.mult, MAGIC, ALU.add)
                P.ts("dve", k, k, -MAGIC, ALU.add)
                P.stt(r, k, -C1, src, ALU.mult, ALU.add)
                P.stt(r, k, -C2, r, ALU.mult, ALU.add)
                P.ts("dve", r, r, math.pi, ALU.min, -math.pi, ALU.max)
                P.act(out, r, AF.Sin)

        def s5_stage(l, sq, c3):
            def sbt(name, shape, dt, n=1):
                r = [T(c3.enter_context(SBT("s5_%s%d" % (name, i), shape, dt)), "s5_%s%d" % (name, i)) for i in range(n)]
                return r if n > 1 else r[0]
            W = sbt("W", [128, KC, 256], BF16)
            load_w(W, "w_in", l, 0, KC, 1024, 256)
            Wo = sbt("Wo", [128, 2, D], BF16)
            load_w(Wo, "w_out", l, 256, 2, 0, D)
            Wg = sbt("Wg", [128, 2, 256], BF16)
            load_w(Wg, "s5_glu_w", l, 0, 2, 0, 256)
            gb = sbt("gb", [128, 2], F32)
            P.load(gb[:], dr["s5_glu_b"][l].rearrange("(c p) -> p c", p=128))
            dT = sbt("dT", [128, 2], F32)
            P.load(dT[:], dr["s5_d"][l].rearrange("(c p) -> p c", p=128))
            BD = sbt("BD", [128, 2, 4, 512], BF16)
            ARi = sbt("ARi", [128, 2, 512], F32)
            AIi = sbt("AIi", [128, 2, 512], F32)
            Ar = sbt("Ar", [128, 2, 4, 129], F32)
            Ai = sbt("Ai", [128, 2, 4, 129], F32)
            Cre = sbt("Cre", [128, 2, 4, 64], BF16)
            nCre = sbt("nCre", [128, 2, 4, 64], BF16)
            nCim = sbt("nCim", [128, 2, 4, 64], BF16)
            tabs = {"BD": BD, "ARi": ARi, "AIi": AIi, "Ar": Ar, "Ai": Ai, "Cre": Cre, "nCre": nCre, "nCim": nCim}
            if sq > 0:
                for n_, t_ in tabs.items():
                    dap, dtok = s5c[(l, n_)]
                    P.dma("sp", t_[:].ap, dap, reads=[dtok], writes=t_.toks)
            else:
                with ExitStack() as c5:
                    def tb(name, shape):
                        return T(c5.enter_context(SBT("s5t_" + name, shape, F32)), "s5t_" + name)
                    lrE, liE, bRe, bIm = tb("lrE", [128, 2, 64]), tb("liE", [128, 2, 64]), tb("bRe", [128, 2, 64]), tb("bIm", [128, 2, 64])
                    dtE = tb("dtE", [128, 2])
                    for hf in range(2):
                        for g8 in range(8):
                            g = hf * 8 + g8
                            sl = slice(g8 * 16, (g8 + 1) * 16)
                            P.load(lrE[sl, hf, :], dr["s5_lambda_re"][l, g:g + 1, :].to_broadcast([16, 64]))
                            P.load(liE[sl, hf, :], dr["s5_lambda_im"][l, g:g + 1, :].to_broadcast([16, 64]))
                            P.load(dtE[sl, hf:hf + 1], dr["s5_log_dt"][l:l + 1, g:g + 1].to_broadcast([16, 1]))
                            P.load(bRe[sl, hf, :], dr["s5_b_re"][l, g].rearrange("p h -> h p"))
                            P.load(bIm[sl, hf, :], dr["s5_b_im"][l, g].rearrange("p h -> h p"))
                    P.act(dtE[:], dtE[:], AF.Exp)
                    dtb = dtE[:].m(lambda a: a.unsqueeze(2)).bc([128, 2, 64])
                    lrdt, lidt = tb("lrdt", [128, 2, 64]), tb("lidt", [128, 2, 64])
                    P.tt("dve", lrdt[:], lrE[:], dtb, ALU.mult)
                    P.tt("dve", lidt[:], liE[:], dtb, ALU.mult)
                    mag, sn, cs_, k1, k2 = tb("mag", [128, 2, 64]), tb("sn", [128, 2, 64]), tb("cs", [128, 2, 64]), tb("k1", [128, 2, 64]), tb("k2", [128, 2, 64])
                    P.act(mag[:], lrdt[:], AF.Exp)
                    sincos(lidt[:], sn[:], cs_[:], [k1[:], k2[:]])
                    ar, ai = tb("ar", [128, 2, 64]), tb("ai", [128, 2, 64])
                    P.tt("dve", ar[:], mag[:], cs_[:], ALU.mult)
                    P.tt("dve", ai[:], mag[:], sn[:], ALU.mult)
                    den = tb("den", [128, 2, 64])
                    P.tt("dve", den[:], lrE[:], lrE[:], ALU.mult)
                    P.tt("dve", k1[:], liE[:], liE[:], ALU.mult)
                    P.tt("dve", den[:], den[:], k1[:], ALU.add)
                    P.recip(den[:], den[:])
                    P.ts("dve", ar[:], ar[:], -1.0, ALU.add)
                    fre, fim = tb("fre", [128, 2, 64]), tb("fim", [128, 2, 64])
                    P.tt("dve", k1[:], ar[:], lrE[:], ALU.mult)
                    P.tt("dve", k2[:], ai[:], liE[:], ALU.mult)
                    P.tt("dve", k1[:], k1[:], k2[:], ALU.add)
                    P.tt("dve", fre[:], k1[:], den[:], ALU.mult)
                    P.tt("dve", k1[:], ai[:], lrE[:], ALU.mult)
                    P.tt("dve", k2[:], ar[:], liE[:], ALU.mult)
                    P.tt("dve", k1[:], k1[:], k2[:], ALU.subtract)
                    P.tt("dve", fim[:], k1[:], den[:], ALU.mult)
                    Bfr, Bfi, nBfi = tb("Bfr", [128, 2, 64]), tb("Bfi", [128, 2, 64]), tb("nBfi", [128, 2, 64])
                    P.tt("dve", k1[:], fre[:], bRe[:], ALU.mult)
                    P.tt("dve", k2[:], fim[:], bIm[:], ALU.mult)
                    P.tt("dve", Bfr[:], k1[:], k2[:], ALU.subtract)
                    P.tt("dve", k1[:], fre[:], bIm[:], ALU.mult)
                    P.tt("dve", k2[:], fim[:], bRe[:], ALU.mult)
                    P.tt("dve", Bfi[:], k1[:], k2[:], ALU.add)
                    P.ts("dve", nBfi[:], Bfi[:], -1.0, ALU.mult)
                    m8b = mask8[:].m(lambda a: a.unsqueeze(2)).bc([128, 8, 64])
                    for hf in range(2):
                        for q4, src in enumerate((Bfr, Bfi, nBfi, Bfr)):
                            P.tt("dve", BD[:, hf, q4, :].re("p (g q) -> p g q", q=64),
                                 src[:, hf, :].m(lambda a: a.unsqueeze(1)).bc([128, 8, 64]), m8b, ALU.mult)
                    P.barrier([t_.toks[0] for t_ in (lrE, liE, dtE, bRe, bIm)])
                with ExitStack() as c5:
                    lrB, liB, snB, csB, kB1, kB2 = (tb(n, [128, 512]) for n in ("lrB", "liB", "snB", "csB", "kB1", "kB2"))
                    dtB = tb("dtB", [128, 8])
                    for hf in range(2):
                        P.load(lrB[:], dr["s5_lambda_re"][l:l + 1, hf * 8:(hf + 1) * 8, :].rearrange("o g p -> o (g p)").to_broadcast([128, 512]))
                        P.load(liB[:], dr["s5_lambda_im"][l:l + 1, hf * 8:(hf + 1) * 8, :].rearrange("o g p -> o (g p)").to_broadcast([128, 512]))
                        P.load(dtB[:], dr["s5_log_dt"][l:l + 1, hf * 8:(hf + 1) * 8].to_broadcast([128, 8]))
                        P.act(dtB[:], dtB[:], AF.Exp)
                        dtbb = dtB[:].m(lambda a: a.unsqueeze(2)).bc([128, 8, 64])
                        P.tt("dve", lrB[:].re("p (g q) -> p g q", q=64), lrB[:].re("p (g q) -> p g q", q=64), dtbb, ALU.mult)
                        P.tt("dve", liB[:].re("p (g q) -> p g q", q=64), liB[:].re("p (g q) -> p g q", q=64), dtbb, ALU.mult)
                        P.act(lrB[:], lrB[:], AF.Exp, scale=sidx[:, 1:2])
                        P.ts("dve", liB[:], liB[:], sidx[:, 0:1], ALU.mult)
                        sincos(liB[:], snB[:], csB[:], [kB1[:], kB2[:]])
                        P.tt("dve", ARi[:, hf, :], lrB[:], csB[:], ALU.mult)
                        P.stt(AIi[:, hf, :], snB[:], -1.0, lrB[:], ALU.mult, ALU.mult)
                    P.barrier([t_.toks[0] for t_ in (lrB, liB, dtB)])
                with ExitStack() as c5:
                    lrS, liS, dtS = tb("lrS", [128, 2, 4]), tb("liS", [128, 2, 4]), tb("dtS", [128, 2, 4])
                    for hf in range(2):
                        P.load(lrS[:, hf, :], dr["s5_lambda_re"][l, hf * 8:(hf + 1) * 8, :].rearrange("(b gl) p -> (gl p) b", gl=2))
                        P.load(liS[:, hf, :], dr["s5_lambda_im"][l, hf * 8:(hf + 1) * 8, :].rearrange("(b gl) p -> (gl p) b", gl=2))
                        for gl in range(2):
                            P.load(dtS[gl * 64:(gl + 1) * 64, hf, :],
                                   dr["s5_log_dt"][l:l + 1, hf * 8:(hf + 1) * 8].rearrange("o (b gl) -> o gl b", gl=2)[:, gl, :].to_broadcast([64, 4]))
                    P.act(dtS[:], dtS[:], AF.Exp)
                    P.tt("dve", lrS[:], lrS[:], dtS[:], ALU.mult)
                    P.tt("dve", liS[:], liS[:], dtS[:], ALU.mult)
                    io = iota129[:].m(lambda a: a.unsqueeze(1).unsqueeze(1)).bc([128, 2, 4, 129])
                    exS, thS, snS, csS, kS1, kS2 = (tb(n, [128, 2, 4, 129]) for n in ("exS", "thS", "snS", "csS", "kS1", "kS2"))
                    P.tt("dve", exS[:], io, lrS[:].m(lambda a: a.unsqueeze(3)).bc([128, 2, 4, 129]), ALU.mult)
                    P.tt("dve", thS[:], io, liS[:].m(lambda a: a.unsqueeze(3)).bc([128, 2, 4, 129]), ALU.mult)
                    P.act(exS[:], exS[:], AF.Exp)
                    sincos(thS[:], snS[:], csS[:], [kS1[:], kS2[:]])
                    P.tt("dve", Ar[:], exS[:], csS[:], ALU.mult)
                    P.tt("dve", Ai[:], exS[:], snS[:], ALU.mult)
                    P.barrier([t_.toks[0] for t_ in (lrS, liS, dtS)])
                with ExitStack() as c5:
                    Cr, Ci = tb("Cr", [128, 2, 4, 64]), tb("Ci", [128, 2, 4, 64])
                    P.memset("dve", Cr[:], 0.0)
                    P.memset("dve", Ci[:], 0.0)
                    for hf in range(2):
                        for b4 in range(4):
                            for gl in range(2):
                                g = hf * 8 + 2 * b4 + gl
                                co = (b4 % 2) * 32 + gl * 16
                                P.load(Cr[gl * 64:(gl + 1) * 64, hf, b4, co:co + 16], dr["s5_c_re"][l, g].rearrange("h p -> p h"))
                                P.load(Ci[gl * 64:(gl + 1) * 64, hf, b4, co:co + 16], dr["s5_c_im"][l, g].rearrange("h p -> p h"))
                    P.copy("dve", Cre[:], Cr[:])
                    P.ts("dve", nCre[:], Cr[:], -1.0, ALU.mult)
                    P.ts("dve", nCim[:], Ci[:], -1.0, ALU.mult)
                    P.barrier([t_.toks[0] for t_ in (Cr, Ci)])
                for n_, t_ in tabs.items():
                    dap, dtok = s5c[(l, n_)]
                    P.dma("sp", dap, t_[:].ap, reads=t_.toks, writes=[dtok])
            uT = sbt("uT", [128, 2, 512], BF16)
            X1 = sbt("X1", [128, 2, 512], BF16, 2)
            X2 = sbt("X2", [128, 2, 512], BF16, 2)
            Gc = sbt("Gc", [128, 2, 512], F32, 2)
            Pp = sbt("Pp", [128, 4, 512], BF16, 2)
            car = sbt("car", [128, 2, 2, 4], F32)
            ct = sbt("ct", [128, 4, 4], F32)
            yS = sbt("yS", [128, 2, 512], F32)
            gyb = sbt("gyb", [128, 2, 512], BF16)
            sg = [sbt("sg", [128, 512], F32)] * 2
            oT = sbt("oT", [128, 2, 512], F32)
            sqm = [sbt("sqm", [128, 512], F32)] * 2
            rs = sqm[0]
            yTb = sbt("yTb", [128, 2, 512], BF16, 2)
            P.memset("dve", car[:], 0.0)
            psZ, psG, psYh = ps[0:4], ps[4:6], ps[6:8]
            it = 0
            for w in range(4):
                a, b = w * 512, (w + 1) * 512
                for hf in range(2):
                    for kc in range(KC):
                        P.mm(psZ[hf][:], W[:, kc, hf * 128:(hf + 1) * 128], tv(hTbox['h'], kc, a, b), start=(kc == 0), stop=(kc == KC - 1))
                    P.copy("act", uT[:, hf, :], psZ[hf][:])
                def s5_body(idx, w=w):
                    n, hf = idx // 2, idx % 2
                    i0 = n * 128
                    s2 = idx % 2
                    for q4 in range(4):
                        P.mm(psZ[q4][:], uT[:, hf, i0:i0 + 128], BD[:, hf, q4, :])
                    yield
                    P.tt("dve", X1[s2][:, 0, :], psZ[0][:], ARi[:, hf, :], ALU.mult)
                    P.tt("dve", X1[s2][:, 1, :], psZ[1][:], ARi[:, hf, :], ALU.mult)
                    P.tt("dve", X2[s2][:, 0, :], psZ[2][:], AIi[:, hf, :], ALU.mult)
                    P.tt("dve", X2[s2][:, 1, :], psZ[3][:], AIi[:, hf, :], ALU.mult)
                    yield "mark"
                    for ri in range(2):
                        for b4 in range(4):
                            cb = slice(b4 * 128, (b4 + 1) * 128)
                            P.mm(psG[ri][:, cb], X1[s2][:, ri, cb], triub[:], start=True, stop=False)
                            P.mm(psG[ri][:, cb], X2[s2][:, ri, cb], triub[:], start=False, stop=True)
                    yield
                    for ri in range(2):
                        P.tt("dve", Gc[s2][:, ri, :].re("p (b i) -> p b i", i=128), psG[ri][:].re("p (b i) -> p b i", i=128),
                             car[:, hf, ri, :].m(lambda x: x.unsqueeze(2)).bc([128, 4, 128]), ALU.add)
                    Arv = Ar[:, hf, :, 0:128]
                    Aiv = Ai[:, hf, :, 0:128]
                    gre = Gc[s2][:, 0, :].re("p (b i) -> p b i", i=128)
                    gim = Gc[s2][:, 1, :].re("p (b i) -> p b i", i=128)
                    pv = [Pp[s2][:, i_, :].re("p (b i) -> p b i", i=128) for i_ in range(4)]
                    P.tt("dve", pv[0], gre, Arv, ALU.mult)
                    P.tt("pool", pv[1], gim, Aiv, ALU.mult)
                    P.tt("pool", pv[2], gre, Aiv, ALU.mult)
                    P.tt("dve", pv[3], gim, Arv, ALU.mult)
                    yield
                    for b4 in range(4):
                        r0 = 64 * (b4 // 2)
                        o_ = psYh[hf][r0:r0 + 64, i0:i0 + 128]
                        cb = slice(b4 * 128, (b4 + 1) * 128)
                        P.mm(o_, Cre[:, hf, b4, :], Pp[s2][:, 0, cb], start=(b4 % 2 == 0), stop=False)
                        P.mm(o_, nCre[:, hf, b4, :], Pp[s2][:, 1, cb], start=False, stop=False)
                        P.mm(o_, nCim[:, hf, b4, :], Pp[s2][:, 2, cb], start=False, stop=False)
                        P.mm(o_, nCim[:, hf, b4, :], Pp[s2][:, 3, cb], start=False, stop=(b4 % 2 == 1))
                    yield
                    g7r = Gc[s2][:, 0, :].re("p (b i) -> p b i", i=128)[:, :, 127]
                    g7i = Gc[s2][:, 1, :].re("p (b i) -> p b i", i=128)[:, :, 127]
                    a8r, a8i = Ar[:, hf, :, 128], Ai[:, hf, :, 128]
                    P.tt("dve", ct[:, 0, :], a8r, g7r, ALU.mult)
                    P.tt("dve", ct[:, 1, :], a8i, g7i, ALU.mult)
                    P.tt("dve", ct[:, 2, :], a8r, g7i, ALU.mult)
                    P.tt("dve", ct[:, 3, :], a8i, g7r, ALU.mult)
                    P.tt("dve", car[:, hf, 0, :], ct[:, 0, :], ct[:, 1, :], ALU.subtract)
                    P.tt("dve", car[:, hf, 1, :], ct[:, 2, :], ct[:, 3, :], ALU.add)

                run_pipelined(s5_body, 8)
                for hf in range(2):
                    P.stt(yS[:, hf, :], uT[:, hf, :], dT[:, hf:hf + 1], psYh[hf][:], ALU.mult, ALU.add)
                P.act(gyb[:], yS[:], AF.Gelu_apprx_tanh)
                for mc in range(2):
                    for fc in range(2):
                        P.mm(psZ[mc][:], Wg[:, fc, mc * 128:(mc + 1) * 128], gyb[:, fc, :], start=(fc == 0), stop=(fc == 1))
                    P.act(sg[mc][:], psZ[mc][:], AF.Sigmoid, bias=gb[:, mc:mc + 1])
                    P.tt("dve", oT[:, mc, :], gyb[:, mc, :], sg[mc][:], ALU.mult)
                    P.act(sqm[mc][:], oT[:, mc, :], AF.Square)
                    P.mm(psZ[2][:], ones[:], sqm[mc][:], start=(mc == 0), stop=(mc == 1))
                P.act(rs[:], psZ[2][:], AF.Sqrt, bias=EPS, scale=1.0 / 256)
                P.recip(rs[:], rs[:])
                yb = yTb[w % 2]
                for mc in range(2):
                    P.stt(yb[:, mc, :], oT[:, mc, :], mgT[:, l, 2 + mc:3 + mc], rs[:], ALU.mult, ALU.mult)
                if "ys5" in dbg:
                    P.copy("dve", ys5db[:, :, a:b], yb[:])
                out_proj(l, sq, Wo, 2, yb, w, [psZ[3]])
            if "ys5" in dbg:
                P.store(dbg["ys5"], ys5db[:])
                dbgtoks.append(ys5db.toks[0])
            P.barrier([W.toks[0], Wo.toks[0], Wg.toks[0], gb.toks[0], dT.toks[0]])

        def sgu_stage(l, sq, c3):
            def sbt(name, shape, dt, n=1):
                r = [T(c3.enter_context(SBT("sg_%s%d" % (name, i), shape, dt)), "sg_%s%d" % (name, i)) for i in range(n)]
                return r if n > 1 else r[0]
            psA, psB, psT, psS, psO, psU, psY, psX = ps
            W = sbt("W", [128, KC, 512], BF16)
            load_w(W, "w_in", l, 0, KC, 1280, 512)
            sgt = sbt("sgt", [128, 256], F32)
            P.load(sgt[:], dr["sgu_norm_g"][l:l + 1, :].to_broadcast([128, 256]))
            wraw = sbt("wraw", [128, 4, 128], F32)
            P.load(wraw[:], dr["sgu_w"][l].rearrange("g t s -> t g s"))
            bT = sbt("bT", [128, 4], F32)
            P.load(bT[:], dr["sgu_b"][l].rearrange("g t -> t g"))
            wT = sbt("wT", [128, 4, 128], BF16)
            P.tt("dve", wraw[:], wraw[:], tril[:].m(lambda a: a.unsqueeze(1)).bc([128, 4, 128]), ALU.mult)
            for g in range(4):
                P.tr(psT[:, g * 128:(g + 1) * 128], wraw[:, g, :], ident[:])
            P.copy("dve", wT[:], psT[:].re("p (g t) -> p g t", t=128))
            tail = MixerTail(l, sq, 2, c3, psY, psX)
            u = sbt("u", [128, 256], F32, 2)
            gv = sbt("gv", [128, 256], F32, 2)
            sqv = sbt("sqv", [128, 256], F32, 2)
            ss = sbt("ss", [128, 4], F32, 2)
            vt = sbt("vt", [128, 256], BF16, 2)
            A_ = sbt("A", [128, 256], F32, 2)

            def body(t):
                s2 = t % 2
                proj_tok(psA, W, 0, 512, t)
                yield
                P.act(u[s2][:], psA[:, 0:256], AF.Gelu_apprx_tanh)
                P.act(gv[s2][:], psA[:, 256:512], AF.Gelu_apprx_tanh)
                P.act(sqv[s2][:], gv[s2][:], AF.Square)
                yield
                P.red(ss[s2][:], sqv[s2][:].re("p (h d) -> p h d", d=64), ALU.add)
                P.act(ss[s2][:], ss[s2][:], AF.Sqrt, bias=EPS, scale=1.0 / 64)
                yield
                P.recip(ss[s2][:], ss[s2][:])
                P.tt("dve", gv[s2][:].re("p (h d) -> p h d", d=64), gv[s2][:].re("p (h d) -> p h d", d=64),
                     ss[s2][:].m(lambda a: a.unsqueeze(2)).bc([128, 4, 64]), ALU.mult)
                yield
                P.tt("pool", vt[s2][:], gv[s2][:], sgt[:], ALU.mult)
                yield "mark"
                for g in range(4):
                    P.mm(psS[:, g * 64:(g + 1) * 64], wT[:, g, :], vt[s2][:, g * 64:(g + 1) * 64])
                yield
                P.tt("dve", A_[s2][:].re("p (h d) -> p h d", d=64), psS[:, 0:256].re("p (h d) -> p h d", d=64),
                     bT[:].m(lambda a: a.unsqueeze(2)).bc([128, 4, 64]), ALU.add)
                yield
                P.tt("pool", A_[s2][:], A_[s2][:], u[s2][:], ALU.mult)
                yield
                yield from tail.run_g(t, A_[s2])
            run_pipelined(body, NT)
            P.barrier([W.toks[0], tail.Wo.toks[0], sgt.toks[0], wraw.toks[0], bT.toks[0]])

        def ffn_stage(l, sq, c3):
            def sbt(name, shape, dt, n=1):
                r = [T(c3.enter_context(SBT("ff_%s%d" % (name, i), shape, dt)), "ff_%s%d" % (name, i)) for i in range(n)]
                return r if n > 1 else r[0]
            FT = 1024
            NSC = FT // 512
            h2 = sbt("h2", [128, KC, FT], BF16, 1)
            zT = sbt("zT", [128, NFF, FT], BF16, 1)
            Wu = sbt("Wu", [128, KC, 512], BF16, 2)
            Wd = sbt("Wd", [128, NFF, 128], BF16, 2)
            cw = sbt("cw", [128, 3, 2 * NFF], F32)
            cb = sbt("cb", [128, 2 * NFF], F32)
            P.load(cw[:], dr["ffn_conv_w"][l].rearrange("k (c p) -> p k c", p=128))
            P.load(cb[:], dr["ffn_conv_b"][l].rearrange("(c p) -> p c", p=128))
            atail = sbt("atail", [128, 2 * NFF, 2], F32)
            P.memset("dve", atail[:], 0.0)
            cg = sbt("cg", [128, 512], F32, 2)
            cu = sbt("cu", [128, 512], F32, 2)
            sqb = sbt("nsq", [128, 512], F32, 2)
            rs = sbt("nrs", [128, 512], F32)
            tmp = sbt("ntm", [128, 512], F32, 2)
            mb = l * 48 + 24
            P.stt(scl[:, 1, :], modT[:, mb + 8:mb + 16, sq], 1.0, g2n[:, l, :], ALU.add, ALU.mult)
            nq = 0
            na = 0
            ng_ = 0
            nd = 0
            npb = 0
            pend = [None]

            def load_up(g):
                Wg = Wu[g % 2]
                for side in range(2):
                    for k0 in range(0, KC, 4):
                        st = stg[stgn["n"] % 2]
                        stgn["n"] += 1
                        c0 = side * DFF + g * 256
                        P.load(st[:, 0:1024].re("p (k c) -> p k c", c=256),
                               dr["ffn_w_up"][l, k0 * 128:(k0 + 4) * 128, c0:c0 + 256].rearrange("(k p) c -> p k c", p=128))
                        P.copy("act", Wg[:, k0:k0 + 4, side * 256:(side + 1) * 256], st[:, 0:1024].re("p (k c) -> p k c", c=256))

            def load_dn(dc):
                Wdb = Wd[dc % 2]
                for j0 in range(0, NFF, 11):
                    st = stg[stgn["n"] % 2]
                    stgn["n"] += 1
                    P.load(st[:, 0:11 * 128].re("p (j c) -> p j c", c=128),
                           dr["ffn_w_down"][l, j0 * 128:(j0 + 11) * 128, dc * 128:(dc + 1) * 128].rearrange("(j p) c -> p j c", p=128))
                    P.copy("act", Wdb[:, j0:j0 + 11, :], st[:, 0:11 * 128].re("p (j c) -> p j c", c=128))

            for tp in range(S // FT):
                for sc in range(NSC):
                    a, b = tp * FT + sc * 512, tp * FT + (sc + 1) * 512
                    pss = ps[6 + sc % 2]
                    for kc in range(KC):
                        sb_ = sqb[nq % 2]
                        nq += 1
                        P.act(sb_[:], tv(xT, kc, a, b), AF.Square)
                        P.mm(pss[:], ones[:], sb_[:], start=(kc == 0), stop=(kc == KC - 1))
                    P.act(rs[:], pss[:], AF.Sqrt, bias=EPS, scale=1.0 / D)
                    P.recip(rs[:], rs[:])
                    for kc in range(KC):
                        tm = tmp[kc % 2]
                        P.tt("dve", tm[:], tv(xT, kc, a, b), rs[:], ALU.mult)
                        P.act(h2[:, kc, sc * 512:(sc + 1) * 512], tm[:], AF.Identity, scale=scl[:, 1, kc:kc + 1],
                              bias=modT[:, mb + kc, sq:sq + 1])
                for g in range(NFF // 2):
                    Wg = Wu[g % 2]
                    if g == 0 and tp == 0:
                        load_up(0)
                    if g + 1 < NFF // 2:
                        load_up(g + 1)
                    else:
                        load_dn(0)
                    for jj in range(2):
                        j = g * 2 + jj
                        for sc in range(NSC):
                            res = []
                            for side in range(2):
                                pst = ps[npb % 6]
                                npb += 1
                                ch = j + side * NFF
                                for kc in range(KC):
                                    P.mm(pst[:], Wg[:, kc, side * 256 + jj * 128:side * 256 + (jj + 1) * 128],
                                         h2[:, kc, sc * 512:(sc + 1) * 512], start=(kc == 0), stop=(kc == KC - 1))
                                cc = (cg if side == 0 else cu)[(j * NSC + sc) % 2]
                                P.act(cc[:], pst[:], AF.Identity, scale=cw[:, 2, ch:ch + 1], bias=cb[:, ch:ch + 1])
                                P.stt(cc[:, 1:512], pst[:, 0:511], cw[:, 1, ch:ch + 1], cc[:, 1:512], ALU.mult, ALU.add)
                                P.stt(cc[:, 2:512], pst[:, 0:510], cw[:, 0, ch:ch + 1], cc[:, 2:512], ALU.mult, ALU.add)
                                P.stt(cc[:, 0:1], atail[:, ch, 1:2], cw[:, 1, ch:ch + 1], cc[:, 0:1], ALU.mult, ALU.add)
                                P.stt(cc[:, 0:2], atail[:, ch, 0:2], cw[:, 0, ch:ch + 1], cc[:, 0:2], ALU.mult, ALU.add)
                                P.copy("dve", atail[:, ch, :], pst[:, 510:512])
                                res.append(cc)
                            if pend[0] is not None:
                                pend[0]()

                            def fin(res=res, j=j, sc=sc):
                                P.act(res[0][:], res[0][:], AF.Silu)
                                P.tt("pool", zT[:, j, sc * 512:(sc + 1) * 512], res[0][:], res[1][:], ALU.mult)
                            pend[0] = fin
                if pend[0] is not None:
                    pend[0]()
                    pend[0] = None
                for dc in range(KC):
                    Wdb = Wd[dc % 2]
                    if dc + 1 < KC:
                        load_dn(dc + 1)
                    elif tp + 1 < S // FT:
                        load_up(0)
                    for sc in range(NSC):
                        a, b = tp * FT + sc * 512, tp * FT + (sc + 1) * 512
                        pst = ps[6 + (dc * NSC + sc) % 2]
                        for j in range(NFF):
                            P.mm(pst[:], Wdb[:, j, :], zT[:, j, sc * 512:(sc + 1) * 512], start=(j == 0), stop=(j == NFF - 1))
                        P.stt(tv(xT, dc, a, b), pst[:], modT[:, l * 48 + 40 + dc, sq:sq + 1], tv(xT, dc, a, b), ALU.mult, ALU.add)
            P.barrier()

        def nsa_stage(l, sq, c3):
            def mk(cx, pre):
                def sbt(name, shape, dt, n=1):
                    r = [T(cx.enter_context(SBT("%s_%s%d" % (pre, name, i), shape, dt)), "%s_%s%d" % (pre, name, i)) for i in range(n)]
                    return r if n > 1 else r[0]
                return sbt
            sbt = mk(c3, "ns")
            tail = MixerTail(l, sq, 3, c3, ps[6], ps[7])
            qT = sbt("qT", [128, 2, S], BF16)
            ksTz = sbt("ksTz", [128, 2, S], BF16)
            kwTz = sbt("kwTz", [128, 2, S], BF16)
            vs1 = sbt("vs1", [128, NT, 2, 65], BF16)
            vw1 = sbt("vw1", [128, NT, 2, 65], BF16)
            gates = sbt("gates", [128, NT, 12], F32)
            kcmpTz = sbt("kcmpTz", [128, 2, 128], BF16)
            vc1 = sbt("vc1", [128, 2, 97], BF16)
            qg = sbt("qg", [128, 64], F32)
            kg = sbt("kg", [128, 3, 64], F32)
            csc = sbt("csc", [128, 64], F32)
            P.memset("pool", ksTz[:], 0.0)
            P.memset("pool", kwTz[:], 0.0)
            P.memset("pool", kcmpTz[:], 0.0)
            P.memset("pool", vs1[:], 1.0)
            P.memset("pool", vw1[:], 1.0)
            P.load(qg[:], dr["nsa_q_norm_g"][l:l + 1, :].to_broadcast([128, 64]))
            P.ts("dve", qg[:], qg[:], 0.125, ALU.mult)
            P.load(kg[:], dr["nsa_k_norm_g"][l:l + 1, :, :].rearrange("o a d -> o (a d)").to_broadcast([128, 192]))
            P.load(csc[:], dr["cs_c"])

            def norm_rope(sb2, src, npart, nh, gain, cos, sin, outs, iv=lambda v: v):
                for _ in norm_rope_g(sb2, src, npart, nh, gain, cos, sin, outs, iv):
                    pass

            def norm_rope_g(sb2, src, npart, nh, gain, cos, sin, outs, iv=lambda v: v):
                sqb, ss, xn, tA, tB, tC, tD = sb2
                pp = slice(0, npart)
                P.act(sqb[pp, 0:nh, :], src, AF.Square)
                yield
                P.red(ss[pp, 0:nh], sqb[pp, 0:nh, :], ALU.add)
                P.act(ss[pp, 0:nh], ss[pp, 0:nh], AF.Sqrt, bias=EPS, scale=1.0 / 64)
                yield
                P.recip(ss[pp, 0:nh], ss[pp, 0:nh])
                P.tt("dve", xn[pp, 0:nh, :], src, ss[pp, 0:nh].m(lambda a: a.unsqueeze(2)).bc([npart, nh, 64]), ALU.mult)
                yield
                P.tt("pool", xn[pp, 0:nh, :], xn[pp, 0:nh, :], gain, ALU.mult)
                yield
                x1, x2 = xn[pp, 0:nh, 0:32], xn[pp, 0:nh, 32:64]
                P.tt("dve", tA[pp, 0:nh, :], x1, cos, ALU.mult)
                P.tt("pool", tB[pp, 0:nh, :], x2, sin, ALU.mult)
                P.tt("dve", tC[pp, 0:nh, :], x2, cos, ALU.mult)
                P.tt("pool", tD[pp, 0:nh, :], x1, sin, ALU.mult)
                yield
                P.tt("dve", outs(0), iv(tA[pp, 0:nh, :]), iv(tB[pp, 0:nh, :]), ALU.subtract)
                P.tt("dve", outs(1), iv(tC[pp, 0:nh, :]), iv(tD[pp, 0:nh, :]), ALU.add)
                yield

            with ExitStack() as c4:
                sb4 = mk(c4, "n4")
                W = sb4("W", [128, KC, 1036], BF16)
                load_w(W, "w_in", l, 0, KC, 1792, 1036)
                nrs = [(sb4("sqb%d" % i_, [128, 4, 64], F32), sb4("ss%d" % i_, [128, 4], F32), sb4("xn%d" % i_, [128, 4, 64], F32),
                        sb4("tA%d" % i_, [128, 4, 32], F32), sb4("tB%d" % i_, [128, 4, 32], F32),
                        sb4("tC%d" % i_, [128, 4, 32], F32), sb4("tD%d" % i_, [128, 4, 32], F32)) for i_ in range(2)]
                nr = nrs[0]
                with ExitStack() as c5:
                    sb5 = mk(c5, "n5")
                    kvT = sb5("kvT", [128, 2, S], BF16)
                    for c in range(4):
                        for cb in range(2):
                            pst = ps[(2 * c + cb) % 4]
                            for k8 in range(KC):
                                P.mm(pst[:], W[:, k8, 256 + cb * 128:256 + (cb + 1) * 128], tv(hTbox['h'], k8, c * 512, (c + 1) * 512),
                                     start=(k8 == 0), stop=(k8 == KC - 1))
                            P.copy("act" if cb else "dve", kvT[:, cb, c * 512:(c + 1) * 512], pst[:])
                    W1 = sb5("W1", [128, 32, 128], BF16)
                    W2 = sb5("W2", [128, 64], BF16)
                    w2s = sb5("w2s", [128, 64], F32)
                    peT = sb5("peT", [128, 32], F32)
                    peb = sb5("peb", [128, 32], BF16)
                    bias = sb5("bias", [128, 1], F32)
                    hidb = sb5("hidb", [128, 128], BF16)
                    kct = sb5("kct", [128, 2, 64], BF16)
                    for side in range(2):
                        w1v = dr["nsa_cmp_w1"][l, side].rearrange("(p d) h -> d p h", d=64)
                        for half in range(2):
                            for pc in range(0, 32, 8):
                                st = stg[stgn["n"] % 2]
                                stgn["n"] += 1
                                hs = slice(half * 64, (half + 1) * 64)
                                P.load(st[hs, 0:1024].re("d (p h) -> d p h", h=128), w1v[:, pc:pc + 8, :])
                                P.copy("pool", W1[hs, pc:pc + 8, :], st[hs, 0:1024].re("d (p h) -> d p h", h=128))
                        P.load(w2s[:], dr["nsa_cmp_w2"][l, side])
                        P.copy("pool", W2[:], w2s[:])
                        for half in range(2):
                            P.load(peT[half * 64:(half + 1) * 64, :], dr["nsa_cmp_pe"][l, side].rearrange("p d -> d p"))
                        P.copy("dve", peb[:], peT[:])
                        for p_ in range(32):
                            P.mm(ps[3][:, 0:1], W1[0:64, p_, :], peb[0:64, p_:p_ + 1], start=(p_ == 0), stop=(p_ == 31))
                        P.copy("dve", bias[:], ps[3][:, 0:1])
                        for kvh in range(2):
                            pst = ps[kvh]
                            hs = slice(kvh * 64, (kvh + 1) * 64)
                            for p_ in range(32):
                                P.mm(pst[:, 0:127], W1[hs, p_, :], kvT[hs, side, p_:p_ + 16 * 126 + 1:16],
                                     start=(p_ == 0), stop=(p_ == 31))
                            P.act(hidb[:, 0:127], pst[:, 0:127], AF.Gelu_apprx_tanh, bias=bias[:, 0:1])
                            P.mm(ps[2][0:127, kvh * 64:(kvh + 1) * 64], hidb[:, 0:127], W2[:])
                        if side == 0:
                            norm_rope(nr, ps[2][0:127, 0:128].re("p (h d) -> p h d", d=64), 127, 2,
                                      kg[0:127, 0, :].m(lambda a: a.unsqueeze(1)).bc([127, 2, 64]),
                                      csc[0:127, 0:32].m(lambda a: a.unsqueeze(1)).bc([127, 2, 32]),
                                      csc[0:127, 32:64].m(lambda a: a.unsqueeze(1)).bc([127, 2, 32]),
                                      lambda hf_: kct[0:127, :, hf_ * 32:(hf_ + 1) * 32])
                            pTb = ps[3][:].bitcast(BF16)
                            P.tr(pTb[:, 0:127], kct[0:127, :, :].re("p h d -> p (h d)"), identb[0:127, 0:127])
                            P.copy("act", kcmpTz[0:64, 0, 0:127], pTb[0:64, 0:127])
                            P.copy("act", kcmpTz[64:128, 1, 0:127], pTb[64:128, 0:127])
                        else:
                            P.copy("act", vc1[0:127, :, 0:64], ps[2][0:127, 0:128].re("p (h d) -> p h d", d=64))
                            o1f = sb5("o1f", [128, 33], F32)
                            P.load(o1f[:], dr["ovl1"])
                            P.copy("dve", vc1[0:127, :, 64:97], o1f[0:127, :].m(lambda a: a.unsqueeze(1)).bc([127, 2, 33]))
                    P.barrier([w2s.toks[0], peT.toks[0]])
                with ExitStack() as c5:
                    sb5 = mk(c5, "n6")
                    kk = sb5("kk", [128, 4, 64], F32, 2)
                    qtok = sb5("qtok", [128, 256], BF16, 2)
                    ktok = sb5("ktok", [128, 256], BF16, 2)
                    kg4 = sb5("kg4", [128, 4, 64], F32)
                    P.copy("dve", kg4[:, 0:2, :], kg[:, 1, :].m(lambda a: a.unsqueeze(1)).bc([128, 2, 64]))
                    P.copy("dve", kg4[:, 2:4, :], kg[:, 2, :].m(lambda a: a.unsqueeze(1)).bc([128, 2, 64]))
                    def pa_body(t):
                        s2 = t % 2
                        psA, psB, psC, psT = ps[0], ps[1], ps[2], ps[3]
                        proj_tok(psA, W, 0, 256, t)
                        yield
                        proj_tok(psB, W, 512, 512, t)
                        yield
                        proj_tok(psC, W, 1024, 12, t)
                        yield
                        cos = cs_tab[:, t, 0:32].m(lambda a: a.unsqueeze(1)).bc([128, 4, 32])
                        sin = cs_tab[:, t, 32:64].m(lambda a: a.unsqueeze(1)).bc([128, 4, 32])
                        qo = qtok[s2][:].re("p (g k two d) -> p k g two d", g=2, k=2, two=2, d=32)
                        yield from norm_rope_g(nrs[s2], psA[:, 0:256].re("p (h d) -> p h d", d=64), 128, 4,
                                  qg[:].m(lambda a: a.unsqueeze(1)).bc([128, 4, 64]), cos, sin,
                                  lambda hf_: qo[:, :, :, hf_, :], iv=lambda v: v.re("p (k g) d -> p k g d", g=2))
                        P.copy("act", kk[s2][:, 0:2, :], psB[:, 0:128].re("p (h d) -> p h d", d=64))
                        P.copy("act", kk[s2][:, 2:4, :], psB[:, 256:384].re("p (h d) -> p h d", d=64))
                        ko = ktok[s2][:].re("p (h two d) -> p h two d", two=2, d=32)
                        P.copy("act", vs1[:, t, :, 0:64], psB[:, 128:256].re("p (h d) -> p h d", d=64))
                        P.copy("act", vw1[:, t, :, 0:64], psB[:, 384:512].re("p (h d) -> p h d", d=64))
                        P.act(gates[:, t, :], psC[:, 0:12], AF.Sigmoid)
                        yield
                        yield "mark"
                        yield from norm_rope_g(nrs[s2], kk[s2][:], 128, 4, kg4[:], cos, sin, lambda hf_: ko[:, :, hf_, :])
                        pTb = psT[:].bitcast(BF16)
                        for j in range(2):
                            P.tr(pTb[:, j * 128:(j + 1) * 128], qtok[s2][:, j * 128:(j + 1) * 128], identb[:])
                            P.tr(pTb[:, (2 + j) * 128:(3 + j) * 128], ktok[s2][:, j * 128:(j + 1) * 128], identb[:])
                        yield
                        tsl = slice(t * 128, (t + 1) * 128)
                        P.copy("act", qT[:, :, tsl], pTb[:, 0:256].re("p (g i) -> p g i", i=128))
                        P.copy("dve", ksTz[0:64, 0, tsl], pTb[0:64, 256:384])
                        P.copy("dve", ksTz[64:128, 1, tsl], pTb[64:128, 256:384])
                        P.copy("dve", kwTz[0:64, 0, tsl], pTb[0:64, 384:512])
                        P.copy("dve", kwTz[64:128, 1, tsl], pTb[64:128, 384:512])
                    run_pipelined(pa_body, NT)
                    P.barrier()
                P.barrier([W.toks[0]])
            with ExitStack() as c4:
                sb4 = mk(c4, "n7")
                negv = sb4("negv", [128, S], BF16)
                for cc in range(0, S, 1024):
                    st = stg[stgn["n"] % 2]
                    stgn["n"] += 1
                    P.load(st[:, 0:1024], dr["negvalid"][:, cc:cc + 1024])
                    P.copy("pool", negv[:, cc:cc + 1024], st[:, 0:1024])
                negTf = sb4("negTf", [128, 2, 128], F32)
                negTb = sb4("negTb", [128, 2, 128], BF16)
                P.load(negTf[:], dr["negT"])
                P.copy("dve", negTb[:], negTf[:])
                emf = sb4("emf", [32, 16, 128], F32)
                emb = sb4("emb", [32, 16, 128], BF16)
                P.load(emf[:], dr["emat"])
                P.copy("dve", emb[:], emf[:])
                bonus = sb4("bonus", [128, NT, 32], F32)
                P.load(bonus[:], dr["bonus"])
                Ych = sb4("Ych", [128, 4, 256], F32, 2)
                eb = sb4("eb", [128, 512], BF16, 2)
                rsc = sb4("rsc", [128, 2, 4], F32)
                imp = sb4("imp", [128, 4, 32], F32)
                imp2 = sb4("imp2", [128, 4, 32], F32)
                m8 = sb4("m8", [128, 8], F32)
                selm = sb4("selm", [128, 32], F32)
                negsel = sb4("negsel", [128, 4, 32], BF16)
                nsT = sb4("nsT", [32, 512], BF16)
                coef = sb4("coef", [128, 4], F32, 2)
                tmpo = sb4("tmpo", [128, 4, 64], F32, 2)
                ne = 0
                for c in range(4):
                    q0, q1 = c * 512, (c + 1) * 512
                    Y = Ych[c % 2]
                    for kvh in range(2):
                        psOc = [ps[2], ps[3]]
                        for g in range(2):
                            psc = ps[ne % 2]
                            e_ = eb[ne % 2]
                            ne += 1
                            P.mm(psc[0:127, :], kcmpTz[:, kvh, 0:127], qT[:, g, q0:q1], start=True, stop=False)
                            P.mm(psc[0:127, :], identb[0:127, 0:127], negv[0:127, q0:q1], start=False, stop=True)
                            P.act(e_[0:127, :], psc[0:127, :], AF.Exp)
                            for j in range(4):
                                P.mm(psOc[g][:, j * 97:(j + 1) * 97], e_[0:127, j * 128:(j + 1) * 128], vc1[0:127, kvh, :])
                            ocv = psOc[g][:, 0:388].re("p (j e) -> p j e", e=97)
                            P.ts("dve", rsc[:, g, :], ocv[:, :, 96], 1e-30, ALU.add)
                            P.recip(rsc[:, g, :], rsc[:, g, :])
                        oc0 = psOc[0][:, 0:388].re("p (j e) -> p j e", e=97)
                        oc1 = psOc[1][:, 0:388].re("p (j e) -> p j e", e=97)
                        P.tt("dve", imp[:], oc0[:, :, 64:96], rsc[:, 0, :].m(lambda a: a.unsqueeze(2)).bc([128, 4, 32]), ALU.mult)
                        P.tt("dve", imp2[:], oc1[:, :, 64:96], rsc[:, 1, :].m(lambda a: a.unsqueeze(2)).bc([128, 4, 32]), ALU.mult)
                        P.tt("dve", imp[:], imp[:], imp2[:], ALU.add)
                        P.tt("dve", imp[:], imp[:], bonus[:, 4 * c:4 * c + 4, :], ALU.add)
                        pTb = ps[6][:].bitcast(BF16)
                        for j in range(4):
                            P.op("dve", lambda e, j=j: e.max(out=m8.h[:], in_=imp.h[:, j, :]), reads=imp.toks, writes=m8.toks)
                            P.ts("dve", selm[:], imp[:, j, :], m8[:, 7:8], ALU.is_ge)
                            P.ts("dve", negsel[:, j, :], selm[:], -NEGB, ALU.mult, NEGB, ALU.add)
                            P.tr(pTb[0:32, j * 128:(j + 1) * 128], negsel[:, j, :], identb[:])
                        P.copy("act", nsT[:], pTb[0:32, 0:512])
                        for g in range(2):
                            qh = 2 * kvh + g
                            ocv = psOc[g][:, 0:388].re("p (j e) -> p j e", e=97)
                            cf = coef[g]
                            P.tt("dve", cf[:], rsc[:, g, :], gates[:, 4 * c:4 * c + 4, qh * 3 + 0], ALU.mult)
                            P.tt("dve", Y[:, :, qh * 64:(qh + 1) * 64], ocv[:, :, 0:64],
                                 cf[:].m(lambda a: a.unsqueeze(2)).bc([128, 4, 64]), ALU.mult)
                        items = []
                        for g in range(2):
                            qh = 2 * kvh + g
                            for br, (KTz, V1, pso) in enumerate(((ksTz, vs1, ps[4]), (kwTz, vw1, ps[5]))):
                                kb0 = 0 if br == 0 else max(0, 4 * c - 4)
                                kbs = list(range(kb0, 4 * c + 4))
                                for ii, kb in enumerate(kbs):
                                    items.append((g, qh, br, KTz, V1, pso, kb, ii == 0, ii == len(kbs) - 1))

                        def mkA(it_, slot):
                            g, qh, br, KTz, V1, pso, kb, isfirst, islast = it_
                            r = kb - 4 * c
                            ja = max(0, r)
                            jb = 3 if br == 0 else min(3, r + 4)
                            ca, cbn = ja * 128, (jb + 1) * 128
                            pss, e_ = ps[slot], eb[slot]

                            def A():
                                P.mm(pss[:, ca:cbn], KTz[:, kvh, kb * 128:(kb + 1) * 128], qT[:, g, q0 + ca:q0 + cbn],
                                     start=True, stop=False)
                                if br == 0:
                                    P.mm(pss[:, ca:cbn], emb[:, kb, :], nsT[:, ca:cbn], start=False, stop=(r < 0))
                                if r >= 0:
                                    P.mm(pss[:, r * 128:(r + 1) * 128], identb[:], negTb[:, 0, :], start=False, stop=True)
                                if br == 1 and r <= -1:
                                    P.mm(pss[:, (r + 4) * 128:(r + 5) * 128], identb[:], negTb[:, 1, :], start=False, stop=True)
                                P.act(e_[:, ca:cbn], pss[:, ca:cbn], AF.Exp)

                            def B():
                                first = isfirst
                                for j in range(ja, jb + 1):
                                    P.mm(pso[:, j * 65:(j + 1) * 65], e_[:, j * 128:(j + 1) * 128], V1[:, kb, kvh, :],
                                         start=first, stop=(kb == 4 * c + j), skip_group_check=True)
                                    first = False
                                if islast:
                                    ov = pso[:, 0:260].re("p (j e) -> p j e", e=65)
                                    cf = coef[br]
                                    P.recip(cf[:], ov[:, :, 64])
                                    P.tt("dve", cf[:], cf[:], gates[:, 4 * c:4 * c + 4, qh * 3 + 1 + br], ALU.mult)
                                    P.tt("dve", tmpo[br][:], ov[:, :, 0:64], cf[:].m(lambda a: a.unsqueeze(2)).bc([128, 4, 64]), ALU.mult)
                                    P.tt("pool", Y[:, :, qh * 64:(qh + 1) * 64], Y[:, :, qh * 64:(qh + 1) * 64], tmpo[br][:], ALU.add)
                            return A, B
                        prevB = None
                        for it_ in items:
                            A, B = mkA(it_, ne % 2)
                            ne += 1
                            A()
                            if prevB is not None:
                                prevB()
                            prevB = B
                        if prevB is not None:
                            prevB()
                    for j in range(4):
                        tail.run(4 * c + j, Y[:, j, :])
                P.barrier([negTf.toks[0], emf.toks[0], bonus.toks[0]])
            P.barrier([tail.Wo.toks[0], qg.toks[0], kg.toks[0], csc.toks[0]])

        outtoks = []
        for sq in range(nseq):
            with ExitStack() as c3:
                load_x(sq, c3)
            for l in range(nlayers):
              with ExitStack() as c4:
                hTbox['h'] = T(c4.enter_context(SBT("hT_%d_%d" % (sq, l), [128, KC, S], BF16)), "hT", NT)
                with ExitStack() as c3:
                    norm_stage(l, sq, 0, c3)
                if "hT" in dbg and l == 0 and sq == 0:
                    with ExitStack() as c3:
                        hd = T(c3.enter_context(SBT("hd", [128, KC, S], F32)), "hd")
                        P.copy("dve", hd[:], hTbox['h'][:])
                        P.store(dbg["hT"], hd[:])
                        P.barrier([hd.toks[0]])
                if "ret" in stages:
                    with ExitStack() as c3:
                        retention_stage(l, sq, c3)
                if "s5" in stages:
                    with ExitStack() as c3:
                        s5_stage(l, sq, c3)
                if "sgu" in stages:
                    with ExitStack() as c3:
                        sgu_stage(l, sq, c3)
                if "nsa" in stages:
                    with ExitStack() as c3:
                        nsa_stage(l, sq, c3)
                P.barrier()
              if True:
                if "ffn" in stages:
                    with ExitStack() as c3:
                        ffn_stage(l, sq, c3)
            if "xT" in dbg and sq == 0:
                P.store(dbg["xT"], xT[:])
                dbgtoks.append(xT.toks[0])
            with ExitStack() as c3:
                outtoks += store_x(sq, c3)
        P.finish(outtoks + dbgtoks)
        print("instructions:", P.nins, "sems:", P.nsem)
    return nc


_CACHE = {}


def kernel(**inputs):
    n = 8
    if "nc" not in _CACHE:
        _CACHE["nc"] = build(nseq=2, nlayers=DEPTH, wl=DEPTH)
        _CACHE["consts"] = host_consts()
    nc = _CACHE["nc"]
    consts = _CACHE["consts"]
    x = np.asarray(inputs["x"], dtype=np.float32)
    c = np.asarray(inputs["c"], dtype=np.float32)
    w = {k: np.ascontiguousarray(np.asarray(inputs[k], dtype=np.float32)) for k in WEIGHT_SHAPES}
    in_maps = []
    for i in range(n):
        m = {"x": np.ascontiguousarray(x[2 * i:2 * i + 2]), "c": np.ascontiguousarray(c[2 * i:2 * i + 2])}
        m.update(w)
        m.update(consts)
        in_maps.append(m)
    res = run_bass_kernel_spmd(nc, in_maps, core_ids=list(range(n)))
    return np.concatenate([np.asarray(r["out"], dtype=np.float32) for r in res.results], axis=0)
```

```python
import math
import os
from contextlib import ExitStack

import numpy as np
import concourse.bass as bass
import concourse.mybir as mybir
from concourse.bass_utils import run_bass_kernel_spmd

F32 = mybir.dt.float32
BF16 = mybir.dt.bfloat16
AF = mybir.ActivationFunctionType
ALU = mybir.AluOpType
AX = mybir.AxisListType

S = 2048
D = 1024
KC = 8
NT = 16
DEPTH = 4
DFF = 2816
NFF = 22
INC = 2828
EPS = 1e-6
NEGB = -30000.0


class Tok:
    __slots__ = ("name", "w", "r", "dsem", "dtot", "dw", "uid", "psum")
    _n = [0]

    def __init__(self, name):
        Tok._n[0] += 1
        self.uid = Tok._n[0]
        self.psum = False
        self.name = name
        self.w = None
        self.r = {}
        self.dsem = None
        self.dtot = 0
        self.dw = 0


class V:
    __slots__ = ("ap", "toks")

    def __init__(self, ap, toks):
        self.ap = ap
        self.toks = toks

    def __getitem__(self, idx):
        return V(self.ap[idx], self.toks)

    def m(self, fn):
        return V(fn(self.ap), self.toks)

    def bc(self, shape):
        return V(self.ap.to_broadcast(shape), self.toks)

    def re(self, pat, **kw):
        return V(self.ap.rearrange(pat, **kw), self.toks)

    def bitcast(self, dt):
        return V(self.ap.bitcast(dt), self.toks)


class TK:
    def __init__(self, t, i):
        self.t = t
        self.i = i

    def __getitem__(self, idx):
        return V(self.t.h[idx], [self.t.toks[self.i]])


class T:
    def __init__(self, handle, name, ntok=1):
        self.h = handle
        self.toks = [Tok("%s.%d" % (name, i)) for i in range(ntok)]

    def __getitem__(self, idx):
        return V(self.h[idx], self.toks)

    def k(self, i):
        return TK(self, i)


class Eng:
    def __init__(self, name, obj):
        self.name = name
        self.obj = obj
        self.sems = []
        self.gen = -1
        self.cnt = 0
        self.known = {}
        self.dknown = {}


class Prog:
    ROT = 30000

    def __init__(self, nc, ctx):
        self.nc = nc
        self.ctx = ctx
        self.eng = {
            "pe": Eng("pe", nc.tensor),
            "act": Eng("act", nc.scalar),
            "dve": Eng("dve", nc.vector),
            "pool": Eng("pool", nc.gpsimd),
            "sp": Eng("sp", nc.sync),
        }
        self.nsem = 0
        for e in self.eng.values():
            self._newsem(e)
        self.nins = 0
        self.dfree = []
        self.dlive = {}

    def _sem(self, name):
        self.nsem += 1
        return self.ctx.enter_context(self.nc.semaphore("%s_%d" % (name, self.nsem)))

    def _newsem(self, e):
        e.sems.append(self._sem(e.name))
        e.gen += 1
        e.cnt = 0

    def sb(self, name, shape, dt, ntok=1):
        h = self.ctx.enter_context(self.nc.sbuf_tensor("sb_" + name, list(shape), dt))
        return T(h, name, ntok)

    def psum(self, name, shape, dt):
        h = self.ctx.enter_context(self.nc.psum_tensor(name, list(shape), dt))
        t = T(h, name, 1)
        t.toks[0].psum = True
        return t

    def _wait_ticket(self, E, tk):
        if tk is None:
            return
        e2, gen, val = tk
        if e2 is E and E.name == "pe":
            return
        k = E.known.get(e2.name)
        if k is not None and k >= (gen, val):
            return
        E.obj.wait_ge(e2.sems[gen], val)
        E.known[e2.name] = (gen, val)

    def _wait_dma(self, E, t, val):
        if val <= 0 or t.dsem is None:
            return
        if E.dknown.get(t.uid, 0) >= val:
            return
        E.obj.wait_ge(t.dsem, val)
        E.dknown[t.uid] = val

    def _deps(self, E, reads, writes):
        for t in reads:
            self._wait_ticket(E, t.w)
            self._wait_dma(E, t, t.dw)
            if t.psum:
                for en2, tk in t.r.items():
                    if en2 != E.name:
                        self._wait_ticket(E, tk)
        for t in writes:
            self._wait_ticket(E, t.w)
            for tk in t.r.values():
                self._wait_ticket(E, tk)
            self._wait_dma(E, t, t.dtot)

    def op(self, en, fn, reads=(), writes=()):
        E = self.eng[en]
        self._deps(E, reads, writes)
        ins = fn(E.obj)
        if E.cnt >= self.ROT:
            self._newsem(E)
        E.cnt += 1
        ins.then_inc(E.sems[E.gen], 1)
        tk = (E, E.gen, E.cnt)
        for t in reads:
            t.r[en] = tk
        for t in writes:
            t.w = tk
            t.r = {}
        self.nins += 1
        return ins

    def dma(self, qn, out, in_, reads=(), writes=(), **kw):
        E = self.eng[qn]
        self._deps(E, reads, writes)
        ins = E.obj.dma_start(out=out, in_=in_, **kw)
        t = (list(writes) + list(reads))[0]
        if t.dsem is None:
            if self.dfree:
                t.dsem, base = self.dfree.pop()
            else:
                t.dsem, base = self._sem("d"), 0
            t.dtot = base
            t.dw = 0
            self.dlive[t.uid] = t
        ins.then_inc(t.dsem, 16)
        t.dtot += 16
        if writes:
            t.dw = t.dtot
        self.nins += 1
        return ins

    def barrier(self, toks=()):
        SP = self.eng["sp"]
        for t in self.dlive.values():
            self._wait_dma(SP, t, t.dtot)
        self.op("sp", lambda e: e.nop(), reads=(), writes=())
        last = {n: (e, e.gen, e.cnt) for n, e in self.eng.items()}
        for n, E in self.eng.items():
            for n2, tk in last.items():
                if tk[2] > 0:
                    self._wait_ticket(E, tk)
        for t in self.dlive.values():
            self.dfree.append((t.dsem, t.dtot))
            t.dsem = None
            t.dtot = 0
            t.dw = 0
        self.dlive = {}
        for E in self.eng.values():
            E.dknown = {}

    def finish(self, toks):
        E = self.eng["sp"]
        for t in toks:
            self._wait_dma(E, t, t.dtot)

    @staticmethod
    def _tk(*vs):
        out = []
        for v in vs:
            if isinstance(v, V):
                out.extend(v.toks)
        return out

    @staticmethod
    def _a(v):
        return v.ap if isinstance(v, V) else v

    def mm(self, out, lhsT, rhs, start=True, stop=True, **kw):
        return self.op("pe", lambda e: e.matmul(out.ap, lhsT=lhsT.ap, rhs=rhs.ap, start=start, stop=stop, **kw),
                       reads=self._tk(lhsT, rhs), writes=out.toks)

    def tr(self, out, in_, ident):
        return self.op("pe", lambda e: e.transpose(out.ap, in_.ap, ident.ap),
                       reads=self._tk(in_, ident), writes=out.toks)

    def act(self, out, in_, func, bias=None, scale=None, accum_out=None):
        kw = {}
        if bias is not None:
            kw["bias"] = self._a(bias)
        if scale is not None:
            kw["scale"] = self._a(scale)
        if accum_out is not None:
            kw["accum_out"] = accum_out.ap
        return self.op("act", lambda e: e.activation(out=out.ap, in_=in_.ap, func=func, **kw),
                       reads=self._tk(in_, bias, scale), writes=self._tk(out, accum_out))

    def tt(self, en, out, a, b, op):
        return self.op(en, lambda e: e.tensor_tensor(out=out.ap, in0=a.ap, in1=b.ap, op=op),
                       reads=self._tk(a, b), writes=out.toks)

    def ts(self, en, out, a, s1, op0, s2=None, op1=None, accum_out=None):
        kw = {}
        if op1 is not None:
            kw["op1"] = op1
        if accum_out is not None:
            kw["accum_out"] = accum_out.ap
        return self.op(en, lambda e: e.tensor_scalar(out=out.ap, in0=a.ap, scalar1=self._a(s1), scalar2=self._a(s2),
                                                     op0=op0, **kw),
                       reads=self._tk(a, s1, s2), writes=self._tk(out, accum_out))

    def stt(self, out, a, scalar, b, op0, op1, accum_out=None):
        kw = {}
        if accum_out is not None:
            kw["accum_out"] = accum_out.ap
        return self.op("dve", lambda e: e.scalar_tensor_tensor(out=out.ap, in0=a.ap, scalar=self._a(scalar), in1=b.ap,
                                                               op0=op0, op1=op1, **kw),
                       reads=self._tk(a, scalar, b), writes=self._tk(out, accum_out))

    def copy(self, en, out, in_):
        if en == "act":
            return self.op("act", lambda e: e.copy(out=out.ap, in_=in_.ap), reads=in_.toks, writes=out.toks)
        return self.op(en, lambda e: e.tensor_copy(out=out.ap, in_=in_.ap), reads=in_.toks, writes=out.toks)

    def red(self, out, in_, op, axis=AX.X):
        return self.op("dve", lambda e: e.tensor_reduce(out=out.ap, in_=in_.ap, axis=axis, op=op),
                       reads=in_.toks, writes=out.toks)

    def memset(self, en, out, val):
        return self.op(en, lambda e: e.memset(out.ap, val), reads=(), writes=out.toks)

    def recip(self, out, in_):
        return self.op("dve", lambda e: e.reciprocal(out=out.ap, in_=in_.ap), reads=in_.toks, writes=out.toks)

    def load(self, dst, src_ap, q="sp", **kw):
        return self.dma(q, dst.ap, src_ap, writes=dst.toks, **kw)

    def store(self, dst_ap, src, q="sp", **kw):
        return self.dma(q, dst_ap, src.ap, reads=src.toks, **kw)


def host_consts():
    c = {}
    pos = np.arange(S, dtype=np.float64)
    inv = 10000.0 ** (-np.arange(0, 64, 2, dtype=np.float64) / 64)
    ang = (pos[:, None].astype(np.float32) * inv[None, :].astype(np.float32)).astype(np.float32)
    cos = np.cos(ang).astype(np.float32)
    sin = np.sin(ang).astype(np.float32)
    cs = np.concatenate([cos, sin], axis=1)
    c["cs_tab"] = np.ascontiguousarray(cs.reshape(NT, 128, 64).transpose(1, 0, 2))
    H = 4
    lg = np.log1p(-(2.0 ** (-5.0 - np.arange(H, dtype=np.float64))))
    idx = np.arange(128, dtype=np.float64)
    diff = idx[None, :] - idx[:, None]
    dec = np.where(diff >= 0, np.exp(lg[:, None, None] * np.maximum(diff, 0.0)), 0.0) * 0.125
    c["decT"] = np.ascontiguousarray(dec.transpose(1, 0, 2)).astype(np.float32)
    xi = np.exp(lg[:, None] * (idx + 1)[None, :]) * 0.125
    xit = np.zeros((128, 2, 128), np.float32)
    for h in range(H):
        xit[(h % 2) * 64:(h % 2) * 64 + 64, h // 2, :] = xi[h][None, :]
    c["xi_tab"] = xit
    zeta = np.exp(lg[:, None] * (127 - idx)[None, :])
    c["zeta_tab"] = np.ascontiguousarray(zeta.T).astype(np.float32)
    gch = np.exp(lg * 128)
    gct = np.zeros((128, 2), np.float32)
    for h in range(H):
        gct[(h % 2) * 64:(h % 2) * 64 + 64, h // 2] = gch[h]
    c["gc_tab"] = gct
    c["tril"] = np.tril(np.ones((128, 128), np.float32))
    c["triu"] = np.triu(np.ones((128, 128), np.float32))
    m8 = np.zeros((128, 8), np.float32)
    for g8 in range(8):
        m8[g8 * 16:(g8 + 1) * 16, g8] = 1.0
    c["mask8"] = m8
    c["iota129"] = np.broadcast_to(np.arange(129, dtype=np.float32)[None, :], (128, 129)).copy()
    kk_ = np.arange(128)[:, None]
    qq_ = np.arange(128)[None, :]
    nt = np.zeros((128, 2, 128), np.float32)
    nt[:, 0, :] = np.where(kk_ <= qq_, 0.0, NEGB)
    nt[:, 1, :] = np.where(kk_ > qq_, 0.0, NEGB)
    c["negT"] = nt
    nn = np.arange(128)[:, None]
    qpos = np.arange(S)[None, :]
    c["negvalid"] = np.where(16 * nn + 31 <= qpos, 0.0, NEGB).astype(np.float32)
    em = np.zeros((32, 16, 128), np.float32)
    for kb in range(16):
        em[2 * kb, kb, 0:64] = 1.0
        em[2 * kb + 1, kb, 64:128] = 1.0
    c["emat"] = em
    posf = np.arange(S)
    cur = posf // 64
    blk = np.arange(32)[None, :]
    bon = np.zeros((S, 32), np.float32)
    forced = (blk == 0) | (blk == cur[:, None]) | (blk == cur[:, None] - 1)
    bon[forced] = 1e3
    bon[np.broadcast_to(blk, (S, 32)) > cur[:, None]] = -1e9
    c["bonus"] = np.ascontiguousarray(bon.reshape(NT, 128, 32).transpose(1, 0, 2))
    ii = np.arange(127)[:, None]
    jj = np.arange(32)[None, :]
    ovl = ((ii * 16 < (jj + 1) * 64) & (ii * 16 + 32 > jj * 64)).astype(np.float32)
    o1 = np.zeros((128, 33), np.float32)
    o1[:127, :32] = ovl
    o1[:127, 32] = 1.0
    c["ovl1"] = o1
    cend = (np.arange(127) * 16 + 31).astype(np.float32)
    angc = (cend[:, None] * inv[None, :].astype(np.float32)).astype(np.float32)
    csc = np.zeros((128, 64), np.float32)
    csc[:127, :32] = np.cos(angc)
    csc[:127, 32:] = np.sin(angc)
    c["cs_c"] = csc
    sx = np.zeros((128, 2), np.float32)
    sx[:, 0] = np.arange(128)
    sx[:, 1] = -np.arange(128)
    c["sidx"] = sx
    c["ident"] = np.eye(128, dtype=np.float32)
    c["ones"] = np.ones((128, 128), np.float32)
    return c


CONST_SHAPES = {"cs_tab": (128, NT, 64), "decT": (128, 4, 128), "xi_tab": (128, 2, 128), "zeta_tab": (128, 4),
                "gc_tab": (128, 2), "ident": (128, 128), "ones": (128, 128), "tril": (128, 128),
                "triu": (128, 128), "mask8": (128, 8), "iota129": (128, 129), "sidx": (128, 2),
                "negT": (128, 2, 128), "negvalid": (128, S), "emat": (32, 16, 128), "bonus": (128, NT, 32),
                "ovl1": (128, 33), "cs_c": (128, 64)}

WEIGHT_SHAPES = {
    "norm1_g": (DEPTH, D), "norm2_g": (DEPTH, D), "ada_w": (DEPTH, D, 6 * D), "ada_b": (DEPTH, 6 * D),
    "w_in": (DEPTH, D, INC), "ret_norm_g": (DEPTH, 256),
    "s5_lambda_re": (DEPTH, 16, 64), "s5_lambda_im": (DEPTH, 16, 64), "s5_log_dt": (DEPTH, 16),
    "s5_b_re": (DEPTH, 16, 64, 16), "s5_b_im": (DEPTH, 16, 64, 16), "s5_c_re": (DEPTH, 16, 16, 64),
    "s5_c_im": (DEPTH, 16, 16, 64), "s5_d": (DEPTH, 256), "s5_glu_w": (DEPTH, 256, 256), "s5_glu_b": (DEPTH, 256),
    "sgu_norm_g": (DEPTH, 256), "sgu_w": (DEPTH, 4, 128, 128), "sgu_b": (DEPTH, 4, 128),
    "nsa_q_norm_g": (DEPTH, 64), "nsa_k_norm_g": (DEPTH, 3, 64), "nsa_cmp_pe": (DEPTH, 2, 32, 64),
    "nsa_cmp_w1": (DEPTH, 2, 2048, 128), "nsa_cmp_w2": (DEPTH, 2, 128, 64),
    "mix_norm_g": (DEPTH, D), "w_out": (DEPTH, D, D), "ffn_w_up": (DEPTH, D, 2 * DFF),
    "ffn_conv_w": (DEPTH, 3, 2 * DFF), "ffn_conv_b": (DEPTH, 2 * DFF), "ffn_w_down": (DEPTH, DFF, D),
}


OVERLAP = True


def build(nseq=2, nlayers=DEPTH, stages=("ret", "s5", "sgu", "nsa", "ffn"), debug=(), wl=DEPTH):
    nc = bass.Bass("TRN2", target_bir_lowering=False)
    dr = {}
    dr["x"] = nc.dram_tensor("x", [nseq, S, D], F32, kind="ExternalInput").ap()
    dr["c"] = nc.dram_tensor("c", [nseq, D], F32, kind="ExternalInput").ap()
    for n, shp in WEIGHT_SHAPES.items():
        dr[n] = nc.dram_tensor(n, [wl] + list(shp[1:]), F32, kind="ExternalInput").ap()
    for n, shp in CONST_SHAPES.items():
        dr[n] = nc.dram_tensor(n, list(shp), F32, kind="ExternalInput").ap()
    dr["out"] = nc.dram_tensor("out", [nseq, S, D], F32, kind="ExternalOutput").ap()
    S5TAB = {"BD": ([128, 2, 4, 512], BF16), "ARi": ([128, 2, 512], F32), "AIi": ([128, 2, 512], F32),
             "Ar": ([128, 2, 4, 129], F32), "Ai": ([128, 2, 4, 129], F32),
             "Cre": ([128, 2, 4, 64], BF16), "nCre": ([128, 2, 4, 64], BF16), "nCim": ([128, 2, 4, 64], BF16)}
    s5c = {}
    for l_ in range(nlayers):
        for n, (shp, dt_) in S5TAB.items():
            s5c[(l_, n)] = (nc.dram_tensor("s5c_%s_%d" % (n, l_), shp, dt_, kind="Internal").ap(), Tok("s5c_%s_%d" % (n, l_)))
    dbg = {}
    DBG_SHAPES = {"hT": [128, KC, S], "yret": [128, NT, 256], "xT": [128, KC, S], "mod": [128, wl * 48, 2], "ys5": [128, 2, S]}
    for n in debug:
        dbg[n] = nc.dram_tensor("dbg_" + n, DBG_SHAPES[n], BF16 if n in ("ys5", "yret") else F32, kind="ExternalOutput").ap()
    dbgtoks = []

    ucnt = [0]

    def SBT(name, shape, dt):
        ucnt[0] += 1
        return nc.sbuf_tensor("%s_u%d" % (name, ucnt[0]), list(shape), dt)

    with ExitStack() as ctx:
        P = Prog(nc, ctx)
        ctx.enter_context(nc.allow_non_contiguous_dma(reason="small param loads"))
        ctx.enter_context(nc.allow_low_precision(reason="bf16 matmul operands"))
        xT = P.sb("xT", [128, KC, S], F32, ntok=NT)
        hTbox = {}

        def tv(Tt, kcs, a, b):
            return V(Tt.h[:, kcs, a:b], [Tt.toks[i] for i in range(a // 128, (b + 127) // 128)])

        ps = [P.psum("ps%d" % i, [128, 512], F32) for i in range(8)]
        cs_tab = P.sb("cs_tab", [128, NT, 64], F32)
        zeta_tab = P.sb("zeta_tab", [128, 4], F32)
        gc_tab = P.sb("gc_tab", [128, 2], F32)
        ident = P.sb("ident", [128, 128], F32)
        identb = P.sb("identb", [128, 128], BF16)
        ones = P.sb("ones", [128, 128], F32)
        onesb = P.sb("onesb", [128, 128], BF16)
        tril = P.sb("tril", [128, 128], F32)
        triu = P.sb("triu", [128, 128], F32)
        triub = P.sb("triub", [128, 128], BF16)
        mask8 = P.sb("mask8", [128, 8], F32)
        iota129 = P.sb("iota129", [128, 129], F32)
        sidx = P.sb("sidx", [128, 2], F32)
        for t, n in ((cs_tab, "cs_tab"), (zeta_tab, "zeta_tab"),
                     (gc_tab, "gc_tab"), (ident, "ident"), (ones, "ones"), (tril, "tril"),
                     (triu, "triu"), (mask8, "mask8"), (iota129, "iota129"), (sidx, "sidx")):
            P.load(t[:], dr[n])
        P.copy("dve", identb[:], ident[:])
        P.copy("dve", triub[:], triu[:])
        P.copy("dve", onesb[:], ones[:])

        modT = P.sb("modT", [128, wl * 48, 2], F32)
        cT = P.sb("cT", [128, KC, 2], F32)
        g1n = P.sb("g1n", [128, wl, KC], F32)
        g2n = P.sb("g2n", [128, wl, KC], F32)
        scl = P.sb("scl", [128, 2, KC], F32)
        mgT = P.sb("mgT", [128, wl, KC], F32)

        for b_ in range(nseq):
            P.load(cT[:, :, b_], dr["c"][b_].rearrange("(k p) -> p k", p=128))
        if nseq < 2:
            P.memset("dve", cT[:, :, nseq:2], 0.0)
        P.act(cT[:], cT[:], AF.Silu)
        P.load(g1n[:], dr["norm1_g"].rearrange("l (k p) -> p l k", p=128))
        P.load(g2n[:], dr["norm2_g"].rearrange("l (k p) -> p l k", p=128))
        P.load(mgT[:], dr["mix_norm_g"].rearrange("l (k p) -> p l k", p=128))
        adab = P.sb("adab", [128, wl * 48], F32)
        P.load(adab[:], dr["ada_b"].rearrange("l (o p) -> p (l o)", p=128))
        with ExitStack() as c2:
            NAB = 2
            PW = 1536
            awb = [T(c2.enter_context(SBT("awb%d" % i, [128, KC, PW], F32)), "awb%d" % i) for i in range(NAB)]
            modrow = T(c2.enter_context(SBT("modrow", [2, 512], F32)), "modrow")
            brow = [T(c2.enter_context(SBT("brow%d" % i, [2, 512], F32)), "brow%d" % i) for i in range(2)]
            it = 0
            nb_ = 0
            for l in range(nlayers):
                for pc in range(6 * D // PW):
                    buf = awb[it % NAB]
                    it += 1
                    for kc in range(KC):
                        P.load(buf[:, kc, :], dr["ada_w"][l, kc * 128:(kc + 1) * 128, pc * PW:(pc + 1) * PW],
                               q=("sp" if kc % 2 == 0 else "act"))
                    for sbk in range(PW // 512):
                        cc = pc * (PW // 512) + sbk
                        br_ = brow[nb_ % 2]
                        nb_ += 1
                        P.load(br_[:], dr["ada_b"][l:l + 1, cc * 512:(cc + 1) * 512].to_broadcast([2, 512]))
                        pst = ps[nb_ % 2]
                        for kc in range(KC):
                            P.mm(pst[0:2, :], cT[:, kc, :], buf[:, kc, sbk * 512:(sbk + 1) * 512], start=(kc == 0), stop=(kc == KC - 1))
                        P.tt("dve", modrow[:], pst[0:2, :], br_[:], ALU.add)
                        ptr = ps[2 + nb_ % 2]
                        for j in range(4):
                            P.tr(ptr[:, 2 * j:2 * j + 2], modrow[0:2, j * 128:(j + 1) * 128], ident[0:2, 0:2])
                        o0 = l * 48 + cc * 4
                        P.copy("act", modT[:, o0:o0 + 4, :], ptr[:, 0:8].re("p (j b) -> p j b", b=2))
            P.barrier()
        if "mod" in dbg:
            P.store(dbg["mod"], modT[:])
            dbgtoks.append(modT.toks[0])

        if "ys5" in dbg:
            ys5db = P.sb("ys5db", [128, 2, S], BF16)
        if "yret" in dbg:
            ydb = P.sb("ydb", [128, NT, 256], BF16)
        rr = {"ev": 0}

        def pipelined_gen(make, n):
            DONE = object()
            act_ = []
            nxt = [0]

            def start():
                act_.append([make(nxt[0]), False])
                nxt[0] += 1
            start()
            while act_:
                if len(act_) == 1 and act_[0][1] and nxt[0] < n:
                    start()
                for ent in list(act_):
                    r = next(ent[0], DONE)
                    if r is DONE:
                        act_.remove(ent)
                    elif r == "mark":
                        ent[1] = True
                if not act_ and nxt[0] < n:
                    start()
                yield

        def run_pipelined(make, n):
            for _ in pipelined_gen(make, n):
                pass

        def load_x(sq, c3):
            xin = [T(c3.enter_context(SBT("xin%d" % i, [128, D], F32)), "xin%d" % i) for i in range(2)]
            for t in range(NT):
                xb = xin[t % 2]
                P.load(xb[:], dr["x"][sq, t * 128:(t + 1) * 128, :])
                en = "act" if t % 2 else "dve"
                for half in range(2):
                    pst = ps[(2 * t + half) % 4]
                    for j in range(4):
                        kc = half * 4 + j
                        P.tr(pst[:, j * 128:(j + 1) * 128], xb[:, kc * 128:(kc + 1) * 128], ident[:])
                    P.copy(en, tv(xT, slice(half * 4, half * 4 + 4), t * 128, (t + 1) * 128),
                           pst[:].re("p (j c) -> p j c", c=128))
            P.barrier([b.toks[0] for b in xin])

        def store_x(sq, c3):
            xo = [T(c3.enter_context(SBT("xo%d" % i, [128, D], F32)), "xo%d" % i) for i in range(2)]
            for t in range(NT):
                xb = xo[t % 2]
                en = "act" if t % 2 else "dve"
                for half in range(2):
                    pst = ps[(2 * t + half) % 4]
                    for j in range(4):
                        kc = half * 4 + j
                        P.tr(pst[:, j * 128:(j + 1) * 128], tv(xT, kc, t * 128, (t + 1) * 128), ident[:])
                    P.copy(en, xb[:, half * 512:(half + 1) * 512], pst[:])
                P.store(dr["out"][sq, t * 128:(t + 1) * 128, :], xb[:])
            P.barrier([b.toks[0] for b in xo])
            return [b.toks[0] for b in xo]

        def norm_stage(l, sq, which, c3):
            gN = g1n if which == 0 else g2n
            mb = l * 48 + 24 * which
            P.stt(scl[:, which, :], modT[:, mb + 8:mb + 16, sq], 1.0, gN[:, l, :], ALU.add, ALU.mult)
            sqb = [T(c3.enter_context(SBT("nsq%d_%d" % (i, which), [128, 512], BF16)), "nsq%d" % i) for i in range(3)]
            rs = [T(c3.enter_context(SBT("nrs%d_%d" % (i, which), [128, 512], F32)), "nrs%d" % i) for i in range(2)]
            tmp = [T(c3.enter_context(SBT("ntm%d_%d" % (i, which), [128, 512], F32)), "ntm%d" % i) for i in range(2)]
            n = 0
            for tc in range(4):
                a, b = tc * 512, (tc + 1) * 512
                pss = ps[4 + tc % 2]
                for kc in range(KC):
                    sb_ = sqb[n % 3]
                    n += 1
                    if kc % 2 == 0:
                        P.act(sb_[:], tv(xT, kc, a, b), AF.Square)
                    else:
                        P.tt("pool", sb_[:], tv(xT, kc, a, b), tv(xT, kc, a, b), ALU.mult)
                    P.mm(pss[:], onesb[:], sb_[:], start=(kc == 0), stop=(kc == KC - 1))
                r = rs[tc % 2]
                P.act(r[:], pss[:], AF.Sqrt, bias=EPS, scale=1.0 / D)
                P.recip(r[:], r[:])
                for kc in range(KC):
                    tm = tmp[kc % 2]
                    P.tt("dve", tm[:], tv(xT, kc, a, b), r[:], ALU.mult)
                    P.act(tv(hTbox['h'], kc, a, b), tm[:], AF.Identity, scale=scl[:, which, kc:kc + 1],
                          bias=modT[:, mb + kc, sq:sq + 1])
            P.barrier()

        stg = [P.sb("stg%d" % i, [128, 1408], F32) for i in range(2)]
        stgn = {"n": 0, "c": 0}

        def cast_eng():
            stgn["c"] += 1
            return "act" if stgn["c"] % 2 else "dve"

        def load_w(dst, name, l, r0, nk, c0, ncol):
            for k in range(nk):
                for cc in range(0, ncol, 1408):
                    n = min(1408, ncol - cc)
                    st = stg[stgn["n"] % 2]
                    stgn["n"] += 1
                    P.load(st[:, 0:n], dr[name][l, r0 + k * 128:r0 + (k + 1) * 128, c0 + cc:c0 + cc + n])
                    P.copy(cast_eng(), dst[:, k, cc:cc + n], st[:, 0:n])

        def proj_tok(pst, W, wc0, ncol, t):
            for kc in range(KC):
                P.mm(pst[:, 0:ncol], tv(hTbox['h'], kc, t * 128, (t + 1) * 128), W[:, kc, wc0:wc0 + ncol],
                     start=(kc == 0), stop=(kc == KC - 1))

        def out_proj(l, sq, Wo, nyc, yT, tc, psx):
            a, b = tc * 512, (tc + 1) * 512
            for dc in range(KC):
                pst = psx[dc % len(psx)]
                for yc in range(nyc):
                    P.mm(pst[:], Wo[:, yc, dc * 128:(dc + 1) * 128], yT[:, yc, :], start=(yc == 0), stop=(yc == nyc - 1))
                P.stt(tv(xT, dc, a, b), pst[:], modT[:, l * 48 + 16 + dc, sq:sq + 1], tv(xT, dc, a, b), ALU.mult, ALU.add)

        class MixerTail:
            def __init__(self, l, sq, m, c3, psY, psX):
                self.l, self.sq, self.m, self.psY, self.psX = l, sq, m, psY, psX

                def sbt(name, shape, dt, n=1):
                    r = [T(c3.enter_context(SBT("mt_%s%d" % (name, i), shape, dt)), "mt_%s%d" % (name, i)) for i in range(n)]
                    return r if n > 1 else r[0]
                self.Wo = sbt("Wo", [128, 2, D], BF16)
                load_w(self.Wo, "w_out", l, m * 256, 2, 0, D)
                self.mg = sbt("mg", [128, 256], F32)
                P.load(self.mg[:], dr["mix_norm_g"][l:l + 1, m * 256:(m + 1) * 256].to_broadcast([128, 256]))
                self.sqo = sbt("sqo", [128, 256], F32, 2)
                self.ss2 = sbt("ss2", [128, 1], F32, 2)
                self.yt = sbt("yt", [128, 256], BF16, 2)
                self.yT = sbt("yT", [128, 2, 512], BF16, 2)

            def run(self, t, A):
                for _ in self.run_g(t, A):
                    pass

            def run_g(self, t, A):
                s2, m, l, sq = t % 2, self.m, self.l, self.sq
                ss2, yt, yT = self.ss2, self.yt, self.yT
                P.act(self.sqo[s2][:], A[:], AF.Square, accum_out=ss2[s2][:])
                P.act(ss2[s2][:], ss2[s2][:], AF.Sqrt, bias=EPS, scale=1.0 / 256)
                yield
                P.recip(ss2[s2][:], ss2[s2][:])
                P.stt(yt[s2][:], A[:], ss2[s2][:], self.mg[:], ALU.mult, ALU.mult)
                yield
                if "yret" in dbg:
                    P.copy("dve", ydb[:, t, :], yt[s2][:])
                    if t == NT - 1:
                        P.store(dbg["yret"], ydb[:])
                        dbgtoks.append(ydb.toks[0])
                tc, j = t // 4, t % 4
                pYb = self.psY if isinstance(self.psY, V) else self.psY[:].bitcast(BF16)
                for yc in range(2):
                    P.tr(pYb[:, yc * 128:(yc + 1) * 128], yt[s2][:, yc * 128:(yc + 1) * 128], identb[:])
                P.copy("act", yT[tc % 2][:, :, j * 128:(j + 1) * 128], pYb[:, 0:256].re("p (c i) -> p c i", i=128))
                yield
                if j == 3:
                    out_proj(l, sq, self.Wo, 2, yT[tc % 2], tc, [self.psX])
                    yield

        def retention_stage(l, sq, c3, banks=None, run=True):
            def sbt(name, shape, dt, n=1):
                r = [T(c3.enter_context(SBT("rt_%s%d" % (name, i), shape, dt)), "rt_%s%d" % (name, i)) for i in range(n)]
                return r if n > 1 else r[0]
            W = sbt("W", [128, KC, 1024], BF16)
            load_w(W, "w_in", l, 0, KC, 0, 1024)
            ng = sbt("ng", [128, 256], F32)
            P.load(ng[:], dr["ret_norm_g"][l:l + 1, :].to_broadcast([128, 256]))
            decT = sbt("decT", [128, 4, 128], F32)
            xi_tab = sbt("xi_tab", [128, 2, 128], F32)
            P.load(decT[:], dr["decT"])
            P.load(xi_tab[:], dr["xi_tab"])
            if banks is None:
                psA, psB, psT, psS, psO, psU, psY, psX = ps
                pTb_ = psT[:].bitcast(BF16)
                pYb_ = psY
                psUv = psU
                uoff = 0
            else:
                psA, psB, psTY, psS, psOU, psX = banks
                psO = psOU
                psUv = psOU
                uoff = 256
                pTb_ = psTY[:].bitcast(BF16)
                pYb_ = pTb_[:, 512:768]
            qk = sbt("qk", [128, 512], BF16, 2)
            kz = sbt("kz", [128, 256], BF16, 2)
            vt = sbt("vt", [128, 256], BF16, 2)
            gs = sbt("gs", [128, 256], BF16, 2)
            tA = sbt("tA", [128, 256], F32, 2)
            tB = sbt("tB", [128, 256], F32, 2)
            qkT = sbt("qkT", [128, 512], BF16, 2)
            qxT = sbt("qxT", [128, 256], BF16, 2)
            sT = sbt("sT", [128, 512], BF16, 2)
            R = sbt("R", [128, 2, 64], F32)
            Rz = sbt("Rz", [128, 4, 64], BF16, 2)
            kTz = sbt("kTz", [128, 4, 128], BF16, 2)
            for i_ in range(2):
                P.memset("pool", Rz[i_][:], 0.0)
                P.memset("pool", kTz[i_][:], 0.0)
            sqo = sbt("sqo", [128, 256], F32)
            ss = sbt("ss", [128, 4], F32, 2)
            o1 = sbt("o1", [128, 256], F32, 2)
            A_ = sbt("A", [128, 256], F32, 2)
            tail = MixerTail(l, sq, 0, c3, pYb_, psX)
            P.memset("dve", R[:], 0.0)
            def body(t):
                s2 = t % 2
                cos = cs_tab[:, t, 0:32].m(lambda a: a.unsqueeze(1)).bc([128, 8, 32])
                sin = cs_tab[:, t, 32:64].m(lambda a: a.unsqueeze(1)).bc([128, 8, 32])
                proj_tok(psA, W, 0, 512, t)
                yield
                proj_tok(psB, W, 512, 512, t)
                yield
                xv = psA[:].re("p (h two d) -> p h two d", two=2, d=32)
                x1, x2 = xv[:, :, 0, :], xv[:, :, 1, :]
                ov = qk[s2][:].re("p (h two d) -> p h two d", two=2, d=32)
                tAv = tA[s2][:].re("p (h d) -> p h d", d=32)
                tBv = tB[s2][:].re("p (h d) -> p h d", d=32)
                P.tt("dve", tAv, x1, cos, ALU.mult)
                P.tt("dve", tBv, x2, sin, ALU.mult)
                yield
                P.tt("pool", ov[:, :, 0, :], tAv, tBv, ALU.subtract)
                tAv2 = tA[1 - s2][:].re("p (h d) -> p h d", d=32)
                tBv2 = tB[1 - s2][:].re("p (h d) -> p h d", d=32)
                P.tt("dve", tAv2, x2, cos, ALU.mult)
                P.tt("dve", tBv2, x1, sin, ALU.mult)
                yield
                P.tt("pool", ov[:, :, 1, :], tAv2, tBv2, ALU.add)
                yield
                P.tt("pool", kz[s2][:].re("p (h d) -> p h d", d=64), qk[s2][:, 256:512].re("p (h d) -> p h d", d=64),
                     zeta_tab[:].m(lambda a: a.unsqueeze(2)).bc([128, 4, 64]), ALU.mult)
                P.copy("act", vt[s2][:], psB[:, 0:256])
                P.act(gs[s2][:], psB[:, 256:512], AF.Silu)
                yield
                yield
                pTb = pTb_
                for j in range(4):
                    P.tr(pTb[:, j * 128:(j + 1) * 128], qk[s2][:, j * 128:(j + 1) * 128], identb[:])
                P.copy("act", qkT[s2][:, 0:256], pTb[:, 0:256])
                kview = kTz[s2][:].re("p (pr two) i -> p pr two i", two=2)
                P.copy("act", kview[0:64, :, 0, :], pTb[0:64, 256:512].re("p (pr i) -> p pr i", i=128))
                P.copy("act", kview[64:128, :, 1, :], pTb[64:128, 256:512].re("p (pr i) -> p pr i", i=128))
                P.tt("dve", qxT[s2][:].re("p (a i) -> p a i", i=128), pTb[:, 0:256].re("p (a i) -> p a i", i=128),
                     xi_tab[:], ALU.mult)
                yield
                for h in range(4):
                    pr, b0 = h // 2, (h % 2) * 64
                    P.mm(psS[:, h * 128:(h + 1) * 128], kTz[s2][:, h, :], qkT[s2][:, pr * 128:(pr + 1) * 128])
                P.tt("dve", sT[s2][:].re("p (h i) -> p h i", i=128), psS[:].re("p (h i) -> p h i", i=128), decT[:], ALU.mult)
                yield
                for h in range(4):
                    pr, b0 = h // 2, (h % 2) * 64
                    P.mm(psO[:, h * 64:(h + 1) * 64], sT[s2][:, h * 128:(h + 1) * 128], vt[s2][:, h * 64:(h + 1) * 64],
                         start=True, stop=(t == 0))
                    if t > 0:
                        P.mm(psO[:, h * 64:(h + 1) * 64], qxT[s2][:, pr * 128:(pr + 1) * 128],
                             Rz[s2][:, h, :], start=False, stop=True)
                yield
                if t < NT - 1:
                    for h in range(4):
                        pr, b0 = h // 2, (h % 2) * 64
                        P.mm(psUv[b0:b0 + 64, uoff + pr * 64:uoff + (pr + 1) * 64], kz[s2][:, h * 64:(h + 1) * 64],
                             vt[s2][:, h * 64:(h + 1) * 64])
                    for pr in range(2):
                        P.stt(R[:, pr, :], R[:, pr, :], gc_tab[:, pr:pr + 1], psUv[:, uoff + pr * 64:uoff + (pr + 1) * 64], ALU.mult, ALU.add)
                    rzv = Rz[1 - s2][:].re("p (pr two) e -> p pr two e", two=2)
                    P.copy("pool", rzv[0:64, :, 0, :], R[0:64, :, :])
                    P.copy("pool", rzv[64:128, :, 1, :], R[64:128, :, :])
                yield "mark"
                yield
                P.act(sqo[:], psO[:, 0:256], AF.Square)
                P.red(ss[s2][:], sqo[:].re("p (h d) -> p h d", d=64), ALU.add)
                P.act(ss[s2][:], ss[s2][:], AF.Sqrt, bias=EPS, scale=1.0 / 64)
                yield
                P.recip(ss[s2][:], ss[s2][:])
                P.tt("dve", o1[s2][:].re("p (h d) -> p h d", d=64), psO[:, 0:256].re("p (h d) -> p h d", d=64),
                     ss[s2][:].m(lambda a: a.unsqueeze(2)).bc([128, 4, 64]), ALU.mult)
                yield
                P.tt("pool", o1[s2][:], o1[s2][:], gs[s2][:], ALU.mult)
                yield
                P.tt("pool", A_[s2][:], o1[s2][:], ng[:], ALU.mult)
                yield
                yield from tail.run_g(t, A_[s2])
            if not run:
                return body
            run_pipelined(body, NT)
            P.barrier([W.toks[0], tail.Wo.toks[0], ng.toks[0], decT.toks[0], xi_tab.toks[0]])

        MAGIC = 12582912.0
        TWO_PI = 2.0 * math.pi
        C1 = 6.28125
        C2 = TWO_PI - C1

        def sincos(th, out_sin, out_cos, scr):
            k, r = scr
            for out, shift in ((out_sin, 0.0), (out_cos, 0.5 * math.pi)):
                if out is None:
                    continue
                if shift:
                    P.ts("dve", r, th, shift, ALU.add)
                    src = r
                else:
                    src = th
                P.ts("dve", k, src, 1.0 / TWO_PI, ALU.mult, MAGIC, ALU.add)
                P.ts("dve", k, k, -MAGIC, ALU.add)
                P.stt(r, k, -C1, src, ALU.mult, ALU.add)
                P.stt(r, k, -C2, r, ALU.mult, ALU.add)
                P.ts("dve", r, r, math.pi, ALU.min, -math.pi, ALU.max)
                P.act(out, r, AF.Sin)

        def s5_stage(l, sq, c3):
            def sbt(name, shape, dt, n=1):
                r = [T(c3.enter_context(SBT("s5_%s%d" % (name, i), shape, dt)), "s5_%s%d" % (name, i)) for i in range(n)]
                return r if n > 1 else r[0]
            W = sbt("W", [128, KC, 256], BF16)
            load_w(W, "w_in", l, 0, KC, 1024, 256)
            Wo = sbt("Wo", [128, 2, D], BF16)
            load_w(Wo, "w_out", l, 256, 2, 0, D)
            Wg = sbt("Wg", [128, 2, 256], BF16)
            load_w(Wg, "s5_glu_w", l, 0, 2, 0, 256)
            gb = sbt("gb", [128, 2], F32)
            P.load(gb[:], dr["s5_glu_b"][l].rearrange("(c p) -> p c", p=128))
            dT = sbt("dT", [128, 2], F32)
            P.load(dT[:], dr["s5_d"][l].rearrange("(c p) -> p c", p=128))
            BD = sbt("BD", [128, 2, 4, 512], BF16)
            ARi = sbt("ARi", [128, 2, 512], F32)
            AIi = sbt("AIi", [128, 2, 512], F32)
            Ar = sbt("Ar", [128, 2, 4, 129], F32)
            Ai = sbt("Ai", [128, 2, 4, 129], F32)
            Cre = sbt("Cre", [128, 2, 4, 64], BF16)
            nCre = sbt("nCre", [128, 2, 4, 64], BF16)
            nCim = sbt("nCim", [128, 2, 4, 64], BF16)
            tabs = {"BD": BD, "ARi": ARi, "AIi": AIi, "Ar": Ar, "Ai": Ai, "Cre": Cre, "nCre": nCre, "nCim": nCim}
            if sq > 0:
                for n_, t_ in tabs.items():
                    dap, dtok = s5c[(l, n_)]
                    P.dma("sp", t_[:].ap, dap, reads=[dtok], writes=t_.toks)
            else:
                with ExitStack() as c5:
                    def tb(name, shape):
                        return T(c5.enter_context(SBT("s5t_" + name, shape, F32)), "s5t_" + name)
                    lrE, liE, bRe, bIm = tb("lrE", [128, 2, 64]), tb("liE", [128, 2, 64]), tb("bRe", [128, 2, 64]), tb("bIm", [128, 2, 64])
                    dtE = tb("dtE", [128, 2])
                    for hf in range(2):
                        for g8 in range(8):
                            g = hf * 8 + g8
                            sl = slice(g8 * 16, (g8 + 1) * 16)
                            P.load(lrE[sl, hf, :], dr["s5_lambda_re"][l, g:g + 1, :].to_broadcast([16, 64]))
                            P.load(liE[sl, hf, :], dr["s5_lambda_im"][l, g:g + 1, :].to_broadcast([16, 64]))
                            P.load(dtE[sl, hf:hf + 1], dr["s5_log_dt"][l:l + 1, g:g + 1].to_broadcast([16, 1]))
                            P.load(bRe[sl, hf, :], dr["s5_b_re"][l, g].rearrange("p h -> h p"))
                            P.load(bIm[sl, hf, :], dr["s5_b_im"][l, g].rearrange("p h -> h p"))
                    P.act(dtE[:], dtE[:], AF.Exp)
                    dtb = dtE[:].m(lambda a: a.unsqueeze(2)).bc([128, 2, 64])
                    lrdt, lidt = tb("lrdt", [128, 2, 64]), tb("lidt", [128, 2, 64])
                    P.tt("dve", lrdt[:], lrE[:], dtb, ALU.mult)
                    P.tt("dve", lidt[:], liE[:], dtb, ALU.mult)
                    mag, sn, cs_, k1, k2 = tb("mag", [128, 2, 64]), tb("sn", [128, 2, 64]), tb("cs", [128, 2, 64]), tb("k1", [128, 2, 64]), tb("k2", [128, 2, 64])
                    P.act(mag[:], lrdt[:], AF.Exp)
                    sincos(lidt[:], sn[:], cs_[:], [k1[:], k2[:]])
                    ar, ai = tb("ar", [128, 2, 64]), tb("ai", [128, 2, 64])
                    P.tt("dve", ar[:], mag[:], cs_[:], ALU.mult)
                    P.tt("dve", ai[:], mag[:], sn[:], ALU.mult)
                    den = tb("den", [128, 2, 64])
                    P.tt("dve", den[:], lrE[:], lrE[:], ALU.mult)
                    P.tt("dve", k1[:], liE[:], liE[:], ALU.mult)
                    P.tt("dve", den[:], den[:], k1[:], ALU.add)
                    P.recip(den[:], den[:])
                    P.ts("dve", ar[:], ar[:], -1.0, ALU.add)
                    fre, fim = tb("fre", [128, 2, 64]), tb("fim", [128, 2, 64])
                    P.tt("dve", k1[:], ar[:], lrE[:], ALU.mult)
                    P.tt("dve", k2[:], ai[:], liE[:], ALU.mult)
                    P.tt("dve", k1[:], k1[:], k2[:], ALU.add)
                    P.tt("dve", fre[:], k1[:], den[:], ALU.mult)
                    P.tt("dve", k1[:], ai[:], lrE[:], ALU.mult)
                    P.tt("dve", k2[:], ar[:], liE[:], ALU.mult)
                    P.tt("dve", k1[:], k1[:], k2[:], ALU.subtract)
                    P.tt("dve", fim[:], k1[:], den[:], ALU.mult)
                    Bfr, Bfi, nBfi = tb("Bfr", [128, 2, 64]), tb("Bfi", [128, 2, 64]), tb("nBfi", [128, 2, 64])
                    P.tt("dve", k1[:], fre[:], bRe[:], ALU.mult)
                    P.tt("dve", k2[:], fim[:], bIm[:], ALU.mult)
                    P.tt("dve", Bfr[:], k1[:], k2[:], ALU.subtract)
                    P.tt("dve", k1[:], fre[:], bIm[:], ALU.mult)
                    P.tt("dve", k2[:], fim[:], bRe[:], ALU.mult)
                    P.tt("dve", Bfi[:], k1[:], k2[:], ALU.add)
                    P.ts("dve", nBfi[:], Bfi[:], -1.0, ALU.mult)
                    m8b = mask8[:].m(lambda a: a.unsqueeze(2)).bc([128, 8, 64])
                    for hf in range(2):
                        for q4, src in enumerate((Bfr, Bfi, nBfi, Bfr)):
                            P.tt("dve", BD[:, hf, q4, :].re("p (g q) -> p g q", q=64),
                                 src[:, hf, :].m(lambda a: a.unsqueeze(1)).bc([128, 8, 64]), m8b, ALU.mult)
                    P.barrier([t_.toks[0] for t_ in (lrE, liE, dtE, bRe, bIm)])
                with ExitStack() as c5:
                    lrB, liB, snB, csB, kB1, kB2 = (tb(n, [128, 512]) for n in ("lrB", "liB", "snB", "csB", "kB1", "kB2"))
                    dtB = tb("dtB", [128, 8])
                    for hf in range(2):
                        P.load(lrB[:], dr["s5_lambda_re"][l:l + 1, hf * 8:(hf + 1) * 8, :].rearrange("o g p -> o (g p)").to_broadcast([128, 512]))
                        P.load(liB[:], dr["s5_lambda_im"][l:l + 1, hf * 8:(hf + 1) * 8, :].rearrange("o g p -> o (g p)").to_broadcast([128, 512]))
                        P.load(dtB[:], dr["s5_log_dt"][l:l + 1, hf * 8:(hf + 1) * 8].to_broadcast([128, 8]))
                        P.act(dtB[:], dtB[:], AF.Exp)
                        dtbb = dtB[:].m(lambda a: a.unsqueeze(2)).bc([128, 8, 64])
                        P.tt("dve", lrB[:].re("p (g q) -> p g q", q=64), lrB[:].re("p (g q) -> p g q", q=64), dtbb, ALU.mult)
                        P.tt("dve", liB[:].re("p (g q) -> p g q", q=64), liB[:].re("p (g q) -> p g q", q=64), dtbb, ALU.mult)
                        P.act(lrB[:], lrB[:], AF.Exp, scale=sidx[:, 1:2])
                        P.ts("dve", liB[:], liB[:], sidx[:, 0:1], ALU.mult)
                        sincos(liB[:], snB[:], csB[:], [kB1[:], kB2[:]])
                        P.tt("dve", ARi[:, hf, :], lrB[:], csB[:], ALU.mult)
                        P.stt(AIi[:, hf, :], snB[:], -1.0, lrB[:], ALU.mult, ALU.mult)
                    P.barrier([t_.toks[0] for t_ in (lrB, liB, dtB)])
                with ExitStack() as c5:
                    lrS, liS, dtS = tb("lrS", [128, 2, 4]), tb("liS", [128, 2, 4]), tb("dtS", [128, 2, 4])
                    for hf in range(2):
                        P.load(lrS[:, hf, :], dr["s5_lambda_re"][l, hf * 8:(hf + 1) * 8, :].rearrange("(b gl) p -> (gl p) b", gl=2))
                        P.load(liS[:, hf, :], dr["s5_lambda_im"][l, hf * 8:(hf + 1) * 8, :].rearrange("(b gl) p -> (gl p) b", gl=2))
                        for gl in range(2):
                            P.load(dtS[gl * 64:(gl + 1) * 64, hf, :],
                                   dr["s5_log_dt"][l:l + 1, hf * 8:(hf + 1) * 8].rearrange("o (b gl) -> o gl b", gl=2)[:, gl, :].to_broadcast([64, 4]))
                    P.act(dtS[:], dtS[:], AF.Exp)
                    P.tt("dve", lrS[:], lrS[:], dtS[:], ALU.mult)
                    P.tt("dve", liS[:], liS[:], dtS[:], ALU.mult)
                    io = iota129[:].m(lambda a: a.unsqueeze(1).unsqueeze(1)).bc([128, 2, 4, 129])
                    exS, thS, snS, csS, kS1, kS2 = (tb(n, [128, 2, 4, 129]) for n in ("exS", "thS", "snS", "csS", "kS1", "kS2"))
                    P.tt("dve", exS[:], io, lrS[:].m(lambda a: a.unsqueeze(3)).bc([128, 2, 4, 129]), ALU.mult)
                    P.tt("dve", thS[:], io, liS[:].m(lambda a: a.unsqueeze(3)).bc([128, 2, 4, 129]), ALU.mult)
                    P.act(exS[:], exS[:], AF.Exp)
                    sincos(thS[:], snS[:], csS[:], [kS1[:], kS2[:]])
                    P.tt("dve", Ar[:], exS[:], csS[:], ALU.mult)
                    P.tt("dve", Ai[:], exS[:], snS[:], ALU.mult)
                    P.barrier([t_.toks[0] for t_ in (lrS, liS, dtS)])
                with ExitStack() as c5:
                    Cr, Ci = tb("Cr", [128, 2, 4, 64]), tb("Ci", [128, 2, 4, 64])
                    P.memset("dve", Cr[:], 0.0)
                    P.memset("dve", Ci[:], 0.0)
                    for hf in range(2):
                        for b4 in range(4):
                            for gl in range(2):
                                g = hf * 8 + 2 * b4 + gl
                                co = (b4 % 2) * 32 + gl * 16
                                P.load(Cr[gl * 64:(gl + 1) * 64, hf, b4, co:co + 16], dr["s5_c_re"][l, g].rearrange("h p -> p h"))
                                P.load(Ci[gl * 64:(gl + 1) * 64, hf, b4, co:co + 16], dr["s5_c_im"][l, g].rearrange("h p -> p h"))
                    P.copy("dve", Cre[:], Cr[:])
                    P.ts("dve", nCre[:], Cr[:], -1.0, ALU.mult)
                    P.ts("dve", nCim[:], Ci[:], -1.0, ALU.mult)
                    P.barrier([t_.toks[0] for t_ in (Cr, Ci)])
                for n_, t_ in tabs.items():
                    dap, dtok = s5c[(l, n_)]
                    P.dma("sp", dap, t_[:].ap, reads=t_.toks, writes=[dtok])
            uT = sbt("uT", [128, 2, 512], BF16)
            X1 = sbt("X1", [128, 2, 512], BF16, 2)
            X2 = sbt("X2", [128, 2, 512], BF16, 2)
            Gc = sbt("Gc", [128, 2, 512], F32, 2)
            Pp = sbt("Pp", [128, 4, 512], BF16, 2)
            car = sbt("car", [128, 2, 2, 4], F32)
            ct = sbt("ct", [128, 4, 4], F32)
            yS = sbt("yS", [128, 2, 512], F32)
            gyb = sbt("gyb", [128, 2, 512], BF16)
            sg = [sbt("sg", [128, 512], F32)] * 2
            oT = sbt("oT", [128, 2, 512], F32)
            sqm = [sbt("sqm", [128, 512], F32)] * 2
            rs = sqm[0]
            yTb = sbt("yTb", [128, 2, 512], BF16, 2)
            P.memset("dve", car[:], 0.0)
            psZ, psG, psYh = ps[0:4], ps[4:6], ps[6:8]
            it = 0
            for w in range(4):
                a, b = w * 512, (w + 1) * 512
                for hf in range(2):
                    for kc in range(KC):
                        P.mm(psZ[hf][:], W[:, kc, hf * 128:(hf + 1) * 128], tv(hTbox['h'], kc, a, b), start=(kc == 0), stop=(kc == KC - 1))
                    P.copy("act", uT[:, hf, :], psZ[hf][:])
                def s5_body(idx, w=w):
                    n, hf = idx // 2, idx % 2
                    i0 = n * 128
                    s2 = idx % 2
                    for q4 in range(4):
                        P.mm(psZ[q4][:], uT[:, hf, i0:i0 + 128], BD[:, hf, q4, :])
                    yield
                    P.tt("dve", X1[s2][:, 0, :], psZ[0][:], ARi[:, hf, :], ALU.mult)
                    P.tt("dve", X1[s2][:, 1, :], psZ[1][:], ARi[:, hf, :], ALU.mult)
                    P.tt("dve", X2[s2][:, 0, :], psZ[2][:], AIi[:, hf, :], ALU.mult)
                    P.tt("dve", X2[s2][:, 1, :], psZ[3][:], AIi[:, hf, :], ALU.mult)
                    yield "mark"
                    for ri in range(2):
                        for b4 in range(4):
                            cb = slice(b4 * 128, (b4 + 1) * 128)
                            P.mm(psG[ri][:, cb], X1[s2][:, ri, cb], triub[:], start=True, stop=False)
                            P.mm(psG[ri][:, cb], X2[s2][:, ri, cb], triub[:], start=False, stop=True)
                    yield
                    for ri in range(2):
                        P.tt("dve", Gc[s2][:, ri, :].re("p (b i) -> p b i", i=128), psG[ri][:].re("p (b i) -> p b i", i=128),
                             car[:, hf, ri, :].m(lambda x: x.unsqueeze(2)).bc([128, 4, 128]), ALU.add)
                    Arv = Ar[:, hf, :, 0:128]
                    Aiv = Ai[:, hf, :, 0:128]
                    gre = Gc[s2][:, 0, :].re("p (b i) -> p b i", i=128)
                    gim = Gc[s2][:, 1, :].re("p (b i) -> p b i", i=128)
                    pv = [Pp[s2][:, i_, :].re("p (b i) -> p b i", i=128) for i_ in range(4)]
                    P.tt("dve", pv[0], gre, Arv, ALU.mult)
                    P.tt("pool", pv[1], gim, Aiv, ALU.mult)
                    P.tt("pool", pv[2], gre, Aiv, ALU.mult)
                    P.tt("dve", pv[3], gim, Arv, ALU.mult)
                    yield
                    for b4 in range(4):
                        r0 = 64 * (b4 // 2)
                        o_ = psYh[hf][r0:r0 + 64, i0:i0 + 128]
                        cb = slice(b4 * 128, (b4 + 1) * 128)
                        P.mm(o_, Cre[:, hf, b4, :], Pp[s2][:, 0, cb], start=(b4 % 2 == 0), stop=False)
                        P.mm(o_, nCre[:, hf, b4, :], Pp[s2][:, 1, cb], start=False, stop=False)
                        P.mm(o_, nCim[:, hf, b4, :], Pp[s2][:, 2, cb], start=False, stop=False)
                        P.mm(o_, nCim[:, hf, b4, :], Pp[s2][:, 3, cb], start=False, stop=(b4 % 2 == 1))
                    yield
                    g7r = Gc[s2][:, 0, :].re("p (b i) -> p b i", i=128)[:, :, 127]
                    g7i = Gc[s2][:, 1, :].re("p (b i) -> p b i", i=128)[:, :, 127]
                    a8r, a8i = Ar[:, hf, :, 128], Ai[:, hf, :, 128]
                    P.tt("dve", ct[:, 0, :], a8r, g7r, ALU.mult)
                    P.tt("dve", ct[:, 1, :], a8i, g7i, ALU.mult)
                    P.tt("dve", ct[:, 2, :], a8r, g7i, ALU.mult)
                    P.tt("dve", ct[:, 3, :], a8i, g7r, ALU.mult)
                    P.tt("dve", car[:, hf, 0, :], ct[:, 0, :], ct[:, 1, :], ALU.subtract)
                    P.tt("dve", car[:, hf, 1, :], ct[:, 2, :], ct[:, 3, :], ALU.add)

                run_pipelined(s5_body, 8)
                for hf in range(2):
                    P.stt(yS[:, hf, :], uT[:, hf, :], dT[:, hf:hf + 1], psYh[hf][:], ALU.mult, ALU.add)
                P.act(gyb[:], yS[:], AF.Gelu_apprx_tanh)
                for mc in range(2):
                    for fc in range(2):
                        P.mm(psZ[mc][:], Wg[:, fc, mc * 128:(mc + 1) * 128], gyb[:, fc, :], start=(fc == 0), stop=(fc == 1))
                    P.act(sg[mc][:], psZ[mc][:], AF.Sigmoid, bias=gb[:, mc:mc + 1])
                    P.tt("dve", oT[:, mc, :], gyb[:, mc, :], sg[mc][:], ALU.mult)
                    P.act(sqm[mc][:], oT[:, mc, :], AF.Square)
                    P.mm(psZ[2][:], ones[:], sqm[mc][:], start=(mc == 0), stop=(mc == 1))
                P.act(rs[:], psZ[2][:], AF.Sqrt, bias=EPS, scale=1.0 / 256)
                P.recip(rs[:], rs[:])
                yb = yTb[w % 2]
                for mc in range(2):
                    P.stt(yb[:, mc, :], oT[:, mc, :], mgT[:, l, 2 + mc:3 + mc], rs[:], ALU.mult, ALU.mult)
                if "ys5" in dbg:
                    P.copy("dve", ys5db[:, :, a:b], yb[:])
                out_proj(l, sq, Wo, 2, yb, w, [psZ[3]])
            if "ys5" in dbg:
                P.store(dbg["ys5"], ys5db[:])
                dbgtoks.append(ys5db.toks[0])
            P.barrier([W.toks[0], Wo.toks[0], Wg.toks[0], gb.toks[0], dT.toks[0]])

        def sgu_stage(l, sq, c3, banks=None, run=True):
            def sbt(name, shape, dt, n=1):
                r = [T(c3.enter_context(SBT("sg_%s%d" % (name, i), shape, dt)), "sg_%s%d" % (name, i)) for i in range(n)]
                return r if n > 1 else r[0]
            if banks is None:
                psA, psB, psT, psS, psO, psU, psY, psX = ps
                psSv = psS[:, 0:256]
                pYb_ = psY
            else:
                psA, psSY, psX = banks
                psT = psA
                psSv = psSY[:, 0:256]
                pYb_ = psSY[:].bitcast(BF16)[:, 512:768]
            W = sbt("W", [128, KC, 512], BF16)
            load_w(W, "w_in", l, 0, KC, 1280, 512)
            sgt = sbt("sgt", [128, 256], F32)
            P.load(sgt[:], dr["sgu_norm_g"][l:l + 1, :].to_broadcast([128, 256]))
            wraw = sbt("wraw", [128, 4, 128], F32)
            P.load(wraw[:], dr["sgu_w"][l].rearrange("g t s -> t g s"))
            bT = sbt("bT", [128, 4], F32)
            P.load(bT[:], dr["sgu_b"][l].rearrange("g t -> t g"))
            wT = sbt("wT", [128, 4, 128], BF16)
            P.tt("dve", wraw[:], wraw[:], tril[:].m(lambda a: a.unsqueeze(1)).bc([128, 4, 128]), ALU.mult)
            for g in range(4):
                P.tr(psT[:, g * 128:(g + 1) * 128], wraw[:, g, :], ident[:])
            P.copy("dve", wT[:], psT[:].re("p (g t) -> p g t", t=128))
            tail = MixerTail(l, sq, 2, c3, pYb_, psX)
            u = sbt("u", [128, 256], F32, 2)
            gv = sbt("gv", [128, 256], F32, 2)
            sqv = sbt("sqv", [128, 256], F32, 2)
            ss = sbt("ss", [128, 4], F32, 2)
            vt = sbt("vt", [128, 256], BF16, 2)
            A_ = sbt("A", [128, 256], F32, 2)

            def body(t):
                s2 = t % 2
                proj_tok(psA, W, 0, 512, t)
                yield
                P.act(u[s2][:], psA[:, 0:256], AF.Gelu_apprx_tanh)
                P.act(gv[s2][:], psA[:, 256:512], AF.Gelu_apprx_tanh)
                P.act(sqv[s2][:], gv[s2][:], AF.Square)
                yield
                P.red(ss[s2][:], sqv[s2][:].re("p (h d) -> p h d", d=64), ALU.add)
                P.act(ss[s2][:], ss[s2][:], AF.Sqrt, bias=EPS, scale=1.0 / 64)
                yield
                P.recip(ss[s2][:], ss[s2][:])
                P.tt("dve", gv[s2][:].re("p (h d) -> p h d", d=64), gv[s2][:].re("p (h d) -> p h d", d=64),
                     ss[s2][:].m(lambda a: a.unsqueeze(2)).bc([128, 4, 64]), ALU.mult)
                yield
                P.tt("pool", vt[s2][:], gv[s2][:], sgt[:], ALU.mult)
                yield "mark"
                for g in range(4):
                    P.mm(psSv[:, g * 64:(g + 1) * 64], wT[:, g, :], vt[s2][:, g * 64:(g + 1) * 64])
                yield
                P.tt("dve", A_[s2][:].re("p (h d) -> p h d", d=64), psSv.re("p (h d) -> p h d", d=64),
                     bT[:].m(lambda a: a.unsqueeze(2)).bc([128, 4, 64]), ALU.add)
                yield
                P.tt("pool", A_[s2][:], A_[s2][:], u[s2][:], ALU.mult)
                yield
                yield from tail.run_g(t, A_[s2])
            if not run:
                return body
            run_pipelined(body, NT)
            P.barrier([W.toks[0], tail.Wo.toks[0], sgt.toks[0], wraw.toks[0], bT.toks[0]])

        def ffn_stage(l, sq, c3):
            def sbt(name, shape, dt, n=1):
                r = [T(c3.enter_context(SBT("ff_%s%d" % (name, i), shape, dt)), "ff_%s%d" % (name, i)) for i in range(n)]
                return r if n > 1 else r[0]
            FT = 1024
            NSC = FT // 512
            h2 = sbt("h2", [128, KC, FT], BF16, 1)
            zT = sbt("zT", [128, NFF, FT], BF16, 1)
            Wu = sbt("Wu", [128, KC, 512], BF16, 2)
            Wd = sbt("Wd", [128, NFF, 128], BF16, 2)
            cw = sbt("cw", [128, 3, 2 * NFF], F32)
            cb = sbt("cb", [128, 2 * NFF], F32)
            P.load(cw[:], dr["ffn_conv_w"][l].rearrange("k (c p) -> p k c", p=128))
            P.load(cb[:], dr["ffn_conv_b"][l].rearrange("(c p) -> p c", p=128))
            atail = sbt("atail", [128, 2 * NFF, 2], F32)
            P.memset("dve", atail[:], 0.0)
            cg = sbt("cg", [128, 512], F32, 2)
            cu = sbt("cu", [128, 512], F32, 2)
            sqb = sbt("nsq", [128, 512], BF16, 2)
            rs = sbt("nrs", [128, 512], F32)
            tmp = sbt("ntm", [128, 512], F32, 2)
            mb = l * 48 + 24
            P.stt(scl[:, 1, :], modT[:, mb + 8:mb + 16, sq], 1.0, g2n[:, l, :], ALU.add, ALU.mult)
            nq = 0
            na = 0
            ng_ = 0
            nd = 0
            npb = 0
            pend = [None]

            def load_up(g):
                Wg = Wu[g % 2]
                for side in range(2):
                    for k0 in range(0, KC, 4):
                        st = stg[stgn["n"] % 2]
                        stgn["n"] += 1
                        c0 = side * DFF + g * 256
                        P.load(st[:, 0:1024].re("p (k c) -> p k c", c=256),
                               dr["ffn_w_up"][l, k0 * 128:(k0 + 4) * 128, c0:c0 + 256].rearrange("(k p) c -> p k c", p=128))
                        P.copy("act", Wg[:, k0:k0 + 4, side * 256:(side + 1) * 256], st[:, 0:1024].re("p (k c) -> p k c", c=256))

            def load_dn(dc):
                Wdb = Wd[dc % 2]
                for j0 in range(0, NFF, 11):
                    st = stg[stgn["n"] % 2]
                    stgn["n"] += 1
                    P.load(st[:, 0:11 * 128].re("p (j c) -> p j c", c=128),
                           dr["ffn_w_down"][l, j0 * 128:(j0 + 11) * 128, dc * 128:(dc + 1) * 128].rearrange("(j p) c -> p j c", p=128))
                    P.copy("act", Wdb[:, j0:j0 + 11, :], st[:, 0:11 * 128].re("p (j c) -> p j c", c=128))

            for tp in range(S // FT):
                for sc in range(NSC):
                    a, b = tp * FT + sc * 512, tp * FT + (sc + 1) * 512
                    pss = ps[6 + sc % 2]
                    for kc in range(KC):
                        sb_ = sqb[nq % 2]
                        nq += 1
                        P.act(sb_[:], tv(xT, kc, a, b), AF.Square)
                        P.mm(pss[:], onesb[:], sb_[:], start=(kc == 0), stop=(kc == KC - 1))
                    P.act(rs[:], pss[:], AF.Sqrt, bias=EPS, scale=1.0 / D)
                    P.recip(rs[:], rs[:])
                    for kc in range(KC):
                        tm = tmp[kc % 2]
                        P.tt("dve", tm[:], tv(xT, kc, a, b), rs[:], ALU.mult)
                        P.act(h2[:, kc, sc * 512:(sc + 1) * 512], tm[:], AF.Identity, scale=scl[:, 1, kc:kc + 1],
                              bias=modT[:, mb + kc, sq:sq + 1])
                for g in range(NFF // 2):
                    Wg = Wu[g % 2]
                    if g == 0 and tp == 0:
                        load_up(0)
                    if g + 1 < NFF // 2:
                        load_up(g + 1)
                    else:
                        load_dn(0)
                    for jj in range(2):
                        j = g * 2 + jj
                        for sc in range(NSC):
                            res = []
                            for side in range(2):
                                pst = ps[npb % 6]
                                npb += 1
                                ch = j + side * NFF
                                for kc in range(KC):
                                    P.mm(pst[:], Wg[:, kc, side * 256 + jj * 128:side * 256 + (jj + 1) * 128],
                                         h2[:, kc, sc * 512:(sc + 1) * 512], start=(kc == 0), stop=(kc == KC - 1))
                                cc = (cg if side == 0 else cu)[(j * NSC + sc) % 2]
                                P.act(cc[:], pst[:], AF.Identity, scale=cw[:, 2, ch:ch + 1], bias=cb[:, ch:ch + 1])
                                P.stt(cc[:, 1:512], pst[:, 0:511], cw[:, 1, ch:ch + 1], cc[:, 1:512], ALU.mult, ALU.add)
                                P.stt(cc[:, 2:512], pst[:, 0:510], cw[:, 0, ch:ch + 1], cc[:, 2:512], ALU.mult, ALU.add)
                                P.stt(cc[:, 0:1], atail[:, ch, 1:2], cw[:, 1, ch:ch + 1], cc[:, 0:1], ALU.mult, ALU.add)
                                P.stt(cc[:, 0:2], atail[:, ch, 0:2], cw[:, 0, ch:ch + 1], cc[:, 0:2], ALU.mult, ALU.add)
                                P.copy("dve", atail[:, ch, :], pst[:, 510:512])
                                res.append(cc)
                            if pend[0] is not None:
                                pend[0]()

                            def fin(res=res, j=j, sc=sc):
                                P.act(res[0][:], res[0][:], AF.Silu)
                                P.tt("pool", zT[:, j, sc * 512:(sc + 1) * 512], res[0][:], res[1][:], ALU.mult)
                            pend[0] = fin
                if pend[0] is not None:
                    pend[0]()
                    pend[0] = None
                for dc in range(KC):
                    Wdb = Wd[dc % 2]
                    if dc + 1 < KC:
                        load_dn(dc + 1)
                    elif tp + 1 < S // FT:
                        load_up(0)
                    for sc in range(NSC):
                        a, b = tp * FT + sc * 512, tp * FT + (sc + 1) * 512
                        pst = ps[6 + (dc * NSC + sc) % 2]
                        for j in range(NFF):
                            P.mm(pst[:], Wdb[:, j, :], zT[:, j, sc * 512:(sc + 1) * 512], start=(j == 0), stop=(j == NFF - 1))
                        P.stt(tv(xT, dc, a, b), pst[:], modT[:, l * 48 + 40 + dc, sq:sq + 1], tv(xT, dc, a, b), ALU.mult, ALU.add)
            P.barrier()

        def nsa_stage(l, sq, c3):
            def mk(cx, pre):
                def sbt(name, shape, dt, n=1):
                    r = [T(cx.enter_context(SBT("%s_%s%d" % (pre, name, i), shape, dt)), "%s_%s%d" % (pre, name, i)) for i in range(n)]
                    return r if n > 1 else r[0]
                return sbt
            sbt = mk(c3, "ns")
            tail = MixerTail(l, sq, 3, c3, ps[6], ps[7])
            qT = sbt("qT", [128, 2, S], BF16)
            ksTz = sbt("ksTz", [128, 2, S], BF16)
            kwTz = sbt("kwTz", [128, 2, S], BF16)
            vs1 = sbt("vs1", [128, NT, 2, 65], BF16)
            vw1 = sbt("vw1", [128, NT, 2, 65], BF16)
            gates = sbt("gates", [128, NT, 12], F32)
            kcmpTz = sbt("kcmpTz", [128, 2, 128], BF16)
            vc1 = sbt("vc1", [128, 2, 97], BF16)
            qg = sbt("qg", [128, 64], F32)
            kg = sbt("kg", [128, 3, 64], F32)
            csc = sbt("csc", [128, 64], F32)
            P.memset("pool", ksTz[:], 0.0)
            P.memset("pool", kwTz[:], 0.0)
            P.memset("pool", kcmpTz[:], 0.0)
            P.memset("pool", vs1[:], 1.0)
            P.memset("pool", vw1[:], 1.0)
            P.load(qg[:], dr["nsa_q_norm_g"][l:l + 1, :].to_broadcast([128, 64]))
            P.ts("dve", qg[:], qg[:], 0.125, ALU.mult)
            P.load(kg[:], dr["nsa_k_norm_g"][l:l + 1, :, :].rearrange("o a d -> o (a d)").to_broadcast([128, 192]))
            P.load(csc[:], dr["cs_c"])

            def norm_rope(sb2, src, npart, nh, gain, cos, sin, outs, iv=lambda v: v):
                for _ in norm_rope_g(sb2, src, npart, nh, gain, cos, sin, outs, iv):
                    pass

            def norm_rope_g(sb2, src, npart, nh, gain, cos, sin, outs, iv=lambda v: v):
                sqb, ss, xn, tA, tB, tC, tD = sb2
                pp = slice(0, npart)
                P.act(sqb[pp, 0:nh, :], src, AF.Square)
                yield
                P.red(ss[pp, 0:nh], sqb[pp, 0:nh, :], ALU.add)
                P.act(ss[pp, 0:nh], ss[pp, 0:nh], AF.Sqrt, bias=EPS, scale=1.0 / 64)
                yield
                P.recip(ss[pp, 0:nh], ss[pp, 0:nh])
                P.tt("dve", xn[pp, 0:nh, :], src, ss[pp, 0:nh].m(lambda a: a.unsqueeze(2)).bc([npart, nh, 64]), ALU.mult)
                yield
                P.tt("pool", xn[pp, 0:nh, :], xn[pp, 0:nh, :], gain, ALU.mult)
                yield
                x1, x2 = xn[pp, 0:nh, 0:32], xn[pp, 0:nh, 32:64]
                P.tt("dve", tA[pp, 0:nh, :], x1, cos, ALU.mult)
                P.tt("pool", tB[pp, 0:nh, :], x2, sin, ALU.mult)
                P.tt("dve", tC[pp, 0:nh, :], x2, cos, ALU.mult)
                P.tt("pool", tD[pp, 0:nh, :], x1, sin, ALU.mult)
                yield
                P.tt("dve", outs(0), iv(tA[pp, 0:nh, :]), iv(tB[pp, 0:nh, :]), ALU.subtract)
                P.tt("dve", outs(1), iv(tC[pp, 0:nh, :]), iv(tD[pp, 0:nh, :]), ALU.add)
                yield

            with ExitStack() as c4:
                sb4 = mk(c4, "n4")
                W = sb4("W", [128, KC, 1036], BF16)
                load_w(W, "w_in", l, 0, KC, 1792, 1036)
                nrs = [(sb4("sqb%d" % i_, [128, 4, 64], F32), sb4("ss%d" % i_, [128, 4], F32), sb4("xn%d" % i_, [128, 4, 64], F32),
                        sb4("tA%d" % i_, [128, 4, 32], F32), sb4("tB%d" % i_, [128, 4, 32], F32),
                        sb4("tC%d" % i_, [128, 4, 32], F32), sb4("tD%d" % i_, [128, 4, 32], F32)) for i_ in range(2)]
                nr = nrs[0]
                with ExitStack() as c5:
                    sb5 = mk(c5, "n5")
                    kvT = sb5("kvT", [128, 2, S], BF16)
                    for c in range(4):
                        for cb in range(2):
                            pst = ps[(2 * c + cb) % 4]
                            for k8 in range(KC):
                                P.mm(pst[:], W[:, k8, 256 + cb * 128:256 + (cb + 1) * 128], tv(hTbox['h'], k8, c * 512, (c + 1) * 512),
                                     start=(k8 == 0), stop=(k8 == KC - 1))
                            P.copy("act" if cb else "dve", kvT[:, cb, c * 512:(c + 1) * 512], pst[:])
                    W1 = sb5("W1", [128, 32, 128], BF16)
                    W2 = sb5("W2", [128, 64], BF16)
                    w2s = sb5("w2s", [128, 64], F32)
                    peT = sb5("peT", [128, 32], F32)
                    peb = sb5("peb", [128, 32], BF16)
                    bias = sb5("bias", [128, 1], F32)
                    hidb = sb5("hidb", [128, 128], BF16)
                    kct = sb5("kct", [128, 2, 64], BF16)
                    for side in range(2):
                        w1v = dr["nsa_cmp_w1"][l, side].rearrange("(p d) h -> d p h", d=64)
                        for half in range(2):
                            for pc in range(0, 32, 8):
                                st = stg[stgn["n"] % 2]
                                stgn["n"] += 1
                                hs = slice(half * 64, (half + 1) * 64)
                                P.load(st[hs, 0:1024].re("d (p h) -> d p h", h=128), w1v[:, pc:pc + 8, :])
                                P.copy("pool", W1[hs, pc:pc + 8, :], st[hs, 0:1024].re("d (p h) -> d p h", h=128))
                        P.load(w2s[:], dr["nsa_cmp_w2"][l, side])
                        P.copy("pool", W2[:], w2s[:])
                        for half in range(2):
                            P.load(peT[half * 64:(half + 1) * 64, :], dr["nsa_cmp_pe"][l, side].rearrange("p d -> d p"))
                        P.copy("dve", peb[:], peT[:])
                        for p_ in range(32):
                            P.mm(ps[3][:, 0:1], W1[0:64, p_, :], peb[0:64, p_:p_ + 1], start=(p_ == 0), stop=(p_ == 31))
                        P.copy("dve", bias[:], ps[3][:, 0:1])
                        for kvh in range(2):
                            pst = ps[kvh]
                            hs = slice(kvh * 64, (kvh + 1) * 64)
                            for p_ in range(32):
                                P.mm(pst[:, 0:127], W1[hs, p_, :], kvT[hs, side, p_:p_ + 16 * 126 + 1:16],
                                     start=(p_ == 0), stop=(p_ == 31))
                            P.act(hidb[:, 0:127], pst[:, 0:127], AF.Gelu_apprx_tanh, bias=bias[:, 0:1])
                            P.mm(ps[2][0:127, kvh * 64:(kvh + 1) * 64], hidb[:, 0:127], W2[:])
                        if side == 0:
                            norm_rope(nr, ps[2][0:127, 0:128].re("p (h d) -> p h d", d=64), 127, 2,
                                      kg[0:127, 0, :].m(lambda a: a.unsqueeze(1)).bc([127, 2, 64]),
                                      csc[0:127, 0:32].m(lambda a: a.unsqueeze(1)).bc([127, 2, 32]),
                                      csc[0:127, 32:64].m(lambda a: a.unsqueeze(1)).bc([127, 2, 32]),
                                      lambda hf_: kct[0:127, :, hf_ * 32:(hf_ + 1) * 32])
                            pTb = ps[3][:].bitcast(BF16)
                            P.tr(pTb[:, 0:127], kct[0:127, :, :].re("p h d -> p (h d)"), identb[0:127, 0:127])
                            P.copy("act", kcmpTz[0:64, 0, 0:127], pTb[0:64, 0:127])
                            P.copy("act", kcmpTz[64:128, 1, 0:127], pTb[64:128, 0:127])
                        else:
                            P.copy("act", vc1[0:127, :, 0:64], ps[2][0:127, 0:128].re("p (h d) -> p h d", d=64))
                            o1f = sb5("o1f", [128, 33], F32)
                            P.load(o1f[:], dr["ovl1"])
                            P.copy("dve", vc1[0:127, :, 64:97], o1f[0:127, :].m(lambda a: a.unsqueeze(1)).bc([127, 2, 33]))
                    P.barrier([w2s.toks[0], peT.toks[0]])
                with ExitStack() as c5:
                    sb5 = mk(c5, "n6")
                    kk = sb5("kk", [128, 4, 64], F32, 2)
                    qtok = sb5("qtok", [128, 256], BF16, 2)
                    ktok = sb5("ktok", [128, 256], BF16, 2)
                    kg4 = sb5("kg4", [128, 4, 64], F32)
                    P.copy("dve", kg4[:, 0:2, :], kg[:, 1, :].m(lambda a: a.unsqueeze(1)).bc([128, 2, 64]))
                    P.copy("dve", kg4[:, 2:4, :], kg[:, 2, :].m(lambda a: a.unsqueeze(1)).bc([128, 2, 64]))
                    def pa_body(t):
                        s2 = t % 2
                        psA, psB, psC, psT = ps[0], ps[1], ps[2], ps[3]
                        proj_tok(psA, W, 0, 256, t)
                        yield
                        proj_tok(psB, W, 512, 512, t)
                        yield
                        proj_tok(psC, W, 1024, 12, t)
                        yield
                        cos = cs_tab[:, t, 0:32].m(lambda a: a.unsqueeze(1)).bc([128, 4, 32])
                        sin = cs_tab[:, t, 32:64].m(lambda a: a.unsqueeze(1)).bc([128, 4, 32])
                        qo = qtok[s2][:].re("p (g k two d) -> p k g two d", g=2, k=2, two=2, d=32)
                        yield from norm_rope_g(nrs[s2], psA[:, 0:256].re("p (h d) -> p h d", d=64), 128, 4,
                                  qg[:].m(lambda a: a.unsqueeze(1)).bc([128, 4, 64]), cos, sin,
                                  lambda hf_: qo[:, :, :, hf_, :], iv=lambda v: v.re("p (k g) d -> p k g d", g=2))
                        P.copy("act", kk[s2][:, 0:2, :], psB[:, 0:128].re("p (h d) -> p h d", d=64))
                        P.copy("act", kk[s2][:, 2:4, :], psB[:, 256:384].re("p (h d) -> p h d", d=64))
                        ko = ktok[s2][:].re("p (h two d) -> p h two d", two=2, d=32)
                        P.copy("act", vs1[:, t, :, 0:64], psB[:, 128:256].re("p (h d) -> p h d", d=64))
                        P.copy("act", vw1[:, t, :, 0:64], psB[:, 384:512].re("p (h d) -> p h d", d=64))
                        P.act(gates[:, t, :], psC[:, 0:12], AF.Sigmoid)
                        yield
                        yield "mark"
                        yield from norm_rope_g(nrs[s2], kk[s2][:], 128, 4, kg4[:], cos, sin, lambda hf_: ko[:, :, hf_, :])
                        pTb = psT[:].bitcast(BF16)
                        for j in range(2):
                            P.tr(pTb[:, j * 128:(j + 1) * 128], qtok[s2][:, j * 128:(j + 1) * 128], identb[:])
                            P.tr(pTb[:, (2 + j) * 128:(3 + j) * 128], ktok[s2][:, j * 128:(j + 1) * 128], identb[:])
                        yield
                        tsl = slice(t * 128, (t + 1) * 128)
                        P.copy("act", qT[:, :, tsl], pTb[:, 0:256].re("p (g i) -> p g i", i=128))
                        P.copy("dve", ksTz[0:64, 0, tsl], pTb[0:64, 256:384])
                        P.copy("dve", ksTz[64:128, 1, tsl], pTb[64:128, 256:384])
                        P.copy("dve", kwTz[0:64, 0, tsl], pTb[0:64, 384:512])
                        P.copy("dve", kwTz[64:128, 1, tsl], pTb[64:128, 384:512])
                    run_pipelined(pa_body, NT)
                    P.barrier()
                P.barrier([W.toks[0]])
            with ExitStack() as c4:
                sb4 = mk(c4, "n7")
                negv = sb4("negv", [128, S], BF16)
                for cc in range(0, S, 1024):
                    st = stg[stgn["n"] % 2]
                    stgn["n"] += 1
                    P.load(st[:, 0:1024], dr["negvalid"][:, cc:cc + 1024])
                    P.copy("pool", negv[:, cc:cc + 1024], st[:, 0:1024])
                negTf = sb4("negTf", [128, 2, 128], F32)
                negTb = sb4("negTb", [128, 2, 128], BF16)
                P.load(negTf[:], dr["negT"])
                P.copy("dve", negTb[:], negTf[:])
                emf = sb4("emf", [32, 16, 128], F32)
                emb = sb4("emb", [32, 16, 128], BF16)
                P.load(emf[:], dr["emat"])
                P.copy("dve", emb[:], emf[:])
                bonus = sb4("bonus", [128, NT, 32], F32)
                P.load(bonus[:], dr["bonus"])
                Ych = sb4("Ych", [128, 4, 256], F32, 2)
                eb = sb4("eb", [128, 512], BF16, 2)
                rsc = sb4("rsc", [128, 2, 4], F32)
                imp = sb4("imp", [128, 4, 32], F32)
                imp2 = sb4("imp2", [128, 4, 32], F32)
                m8 = sb4("m8", [128, 8], F32)
                selm = sb4("selm", [128, 32], F32)
                negsel = sb4("negsel", [128, 4, 32], BF16)
                nsT = sb4("nsT", [32, 512], BF16)
                coef = sb4("coef", [128, 4], F32, 2)
                tmpo = sb4("tmpo", [128, 4, 64], F32, 2)
                ne = 0
                for c in range(4):
                    q0, q1 = c * 512, (c + 1) * 512
                    Y = Ych[c % 2]
                    for kvh in range(2):
                        psOc = [ps[2], ps[3]]
                        for g in range(2):
                            psc = ps[ne % 2]
                            e_ = eb[ne % 2]
                            ne += 1
                            P.mm(psc[0:127, :], kcmpTz[:, kvh, 0:127], qT[:, g, q0:q1], start=True, stop=False)
                            P.mm(psc[0:127, :], identb[0:127, 0:127], negv[0:127, q0:q1], start=False, stop=True)
                            P.act(e_[0:127, :], psc[0:127, :], AF.Exp)
                            for j in range(4):
                                P.mm(psOc[g][:, j * 97:(j + 1) * 97], e_[0:127, j * 128:(j + 1) * 128], vc1[0:127, kvh, :])
                            ocv = psOc[g][:, 0:388].re("p (j e) -> p j e", e=97)
                            P.ts("dve", rsc[:, g, :], ocv[:, :, 96], 1e-30, ALU.add)
                            P.recip(rsc[:, g, :], rsc[:, g, :])
                        oc0 = psOc[0][:, 0:388].re("p (j e) -> p j e", e=97)
                        oc1 = psOc[1][:, 0:388].re("p (j e) -> p j e", e=97)
                        P.tt("dve", imp[:], oc0[:, :, 64:96], rsc[:, 0, :].m(lambda a: a.unsqueeze(2)).bc([128, 4, 32]), ALU.mult)
                        P.tt("dve", imp2[:], oc1[:, :, 64:96], rsc[:, 1, :].m(lambda a: a.unsqueeze(2)).bc([128, 4, 32]), ALU.mult)
                        P.tt("dve", imp[:], imp[:], imp2[:], ALU.add)
                        P.tt("dve", imp[:], imp[:], bonus[:, 4 * c:4 * c + 4, :], ALU.add)
                        pTb = ps[6][:].bitcast(BF16)
                        for j in range(4):
                            P.op("dve", lambda e, j=j: e.max(out=m8.h[:], in_=imp.h[:, j, :]), reads=imp.toks, writes=m8.toks)
                            P.ts("dve", selm[:], imp[:, j, :], m8[:, 7:8], ALU.is_ge)
                            P.ts("dve", negsel[:, j, :], selm[:], -NEGB, ALU.mult, NEGB, ALU.add)
                            P.tr(pTb[0:32, j * 128:(j + 1) * 128], negsel[:, j, :], identb[:])
                        P.copy("act", nsT[:], pTb[0:32, 0:512])
                        for g in range(2):
                            qh = 2 * kvh + g
                            ocv = psOc[g][:, 0:388].re("p (j e) -> p j e", e=97)
                            cf = coef[g]
                            P.tt("dve", cf[:], rsc[:, g, :], gates[:, 4 * c:4 * c + 4, qh * 3 + 0], ALU.mult)
                            P.tt("dve", Y[:, :, qh * 64:(qh + 1) * 64], ocv[:, :, 0:64],
                                 cf[:].m(lambda a: a.unsqueeze(2)).bc([128, 4, 64]), ALU.mult)
                        items = []
                        for g in range(2):
                            qh = 2 * kvh + g
                            for br, (KTz, V1, pso) in enumerate(((ksTz, vs1, ps[4]), (kwTz, vw1, ps[5]))):
                                kb0 = 0 if br == 0 else max(0, 4 * c - 4)
                                kbs = list(range(kb0, 4 * c + 4))
                                for ii, kb in enumerate(kbs):
                                    items.append((g, qh, br, KTz, V1, pso, kb, ii == 0, ii == len(kbs) - 1))

                        def mkA(it_, slot):
                            g, qh, br, KTz, V1, pso, kb, isfirst, islast = it_
                            r = kb - 4 * c
                            ja = max(0, r)
                            jb = 3 if br == 0 else min(3, r + 4)
                            ca, cbn = ja * 128, (jb + 1) * 128
                            pss, e_ = ps[slot], eb[slot]

                            def A():
                                P.mm(pss[:, ca:cbn], KTz[:, kvh, kb * 128:(kb + 1) * 128], qT[:, g, q0 + ca:q0 + cbn],
                                     start=True, stop=False)
                                if br == 0:
                                    P.mm(pss[:, ca:cbn], emb[:, kb, :], nsT[:, ca:cbn], start=False, stop=(r < 0))
                                if r >= 0:
                                    P.mm(pss[:, r * 128:(r + 1) * 128], identb[:], negTb[:, 0, :], start=False, stop=True)
                                if br == 1 and r <= -1:
                                    P.mm(pss[:, (r + 4) * 128:(r + 5) * 128], identb[:], negTb[:, 1, :], start=False, stop=True)
                                P.act(e_[:, ca:cbn], pss[:, ca:cbn], AF.Exp)

                            def B():
                                first = isfirst
                                for j in range(ja, jb + 1):
                                    P.mm(pso[:, j * 65:(j + 1) * 65], e_[:, j * 128:(j + 1) * 128], V1[:, kb, kvh, :],
                                         start=first, stop=(kb == 4 * c + j), skip_group_check=True)
                                    first = False
                                if islast:
                                    ov = pso[:, 0:260].re("p (j e) -> p j e", e=65)
                                    cf = coef[br]
                                    P.recip(cf[:], ov[:, :, 64])
                                    P.tt("dve", cf[:], cf[:], gates[:, 4 * c:4 * c + 4, qh * 3 + 1 + br], ALU.mult)
                                    P.tt("dve", tmpo[br][:], ov[:, :, 0:64], cf[:].m(lambda a: a.unsqueeze(2)).bc([128, 4, 64]), ALU.mult)
                                    P.tt("pool", Y[:, :, qh * 64:(qh + 1) * 64], Y[:, :, qh * 64:(qh + 1) * 64], tmpo[br][:], ALU.add)
                            return A, B
                        prevB = None
                        for it_ in items:
                            A, B = mkA(it_, ne % 2)
                            ne += 1
                            A()
                            if prevB is not None:
                                prevB()
                            prevB = B
                        if prevB is not None:
                            prevB()
                    for j in range(4):
                        tail.run(4 * c + j, Y[:, j, :])
                P.barrier([negTf.toks[0], emf.toks[0], bonus.toks[0]])
            P.barrier([tail.Wo.toks[0], qg.toks[0], kg.toks[0], csc.toks[0]])

        outtoks = []
        for sq in range(nseq):
            with ExitStack() as c3:
                load_x(sq, c3)
            for l in range(nlayers):
              with ExitStack() as c4:
                hTbox['h'] = T(c4.enter_context(SBT("hT_%d_%d" % (sq, l), [128, KC, S], BF16)), "hT", NT)
                with ExitStack() as c3:
                    norm_stage(l, sq, 0, c3)
                if "hT" in dbg and l == 0 and sq == 0:
                    with ExitStack() as c3:
                        hd = T(c3.enter_context(SBT("hd", [128, KC, S], F32)), "hd")
                        P.copy("dve", hd[:], hTbox['h'][:])
                        P.store(dbg["hT"], hd[:])
                        P.barrier([hd.toks[0]])
                if "ret" in stages and "sgu" in stages and OVERLAP:
                    with ExitStack() as c3:
                        rb = retention_stage(l, sq, c3, banks=ps[0:6], run=False)
                        sb_ = sgu_stage(l, sq, c3, banks=(ps[6], ps[7], ps[5]), run=False)
                        g1_ = pipelined_gen(rb, NT)
                        g2_ = pipelined_gen(sb_, NT)
                        d1 = d2 = False
                        while not (d1 and d2):
                            if not d1:
                                d1 = next(g1_, "END") == "END"
                            if not d2:
                                d2 = next(g2_, "END") == "END"
                        P.barrier()
                elif "ret" in stages:
                    with ExitStack() as c3:
                        retention_stage(l, sq, c3)
                if "s5" in stages:
                    with ExitStack() as c3:
                        s5_stage(l, sq, c3)
                if "sgu" in stages and not ("ret" in stages and OVERLAP):
                    with ExitStack() as c3:
                        sgu_stage(l, sq, c3)
                if "nsa" in stages:
                    with ExitStack() as c3:
                        nsa_stage(l, sq, c3)
                P.barrier()
              if True:
                if "ffn" in stages:
                    with ExitStack() as c3:
                        ffn_stage(l, sq, c3)
            if "xT" in dbg and sq == 0:
                P.store(dbg["xT"], xT[:])
                dbgtoks.append(xT.toks[0])
            with ExitStack() as c3:
                outtoks += store_x(sq, c3)
        P.finish(outtoks + dbgtoks)
        print("instructions:", P.nins, "sems:", P.nsem)
    return nc


_CACHE = {}


def kernel(**inputs):
    n = 8
    if "nc" not in _CACHE:
        _CACHE["nc"] = build(nseq=2, nlayers=DEPTH, wl=DEPTH)
        _CACHE["consts"] = host_consts()
    nc = _CACHE["nc"]
    consts = _CACHE["consts"]
    x = np.asarray(inputs["x"], dtype=np.float32)
    c = np.asarray(inputs["c"], dtype=np.float32)
    w = {k: np.ascontiguousarray(np.asarray(inputs[k], dtype=np.float32)) for k in WEIGHT_SHAPES}
    in_maps = []
    for i in range(n):
        m = {"x": np.ascontiguousarray(x[2 * i:2 * i + 2]), "c": np.ascontiguousarray(c[2 * i:2 * i + 2])}
        m.update(w)
        m.update(consts)
        in_maps.append(m)
    res = run_bass_kernel_spmd(nc, in_maps, core_ids=list(range(n)))
    return np.concatenate([np.asarray(r["out"], dtype=np.float32) for r in res.results], axis=0)
```

```python
import math
import os
from contextlib import ExitStack

import numpy as np
import concourse.bass as bass
import concourse.mybir as mybir
from concourse.bass_utils import run_bass_kernel_spmd

F32 = mybir.dt.float32
BF16 = mybir.dt.bfloat16
AF = mybir.ActivationFunctionType
ALU = mybir.AluOpType
AX = mybir.AxisListType

S = 2048
D = 1024
KC = 8
NT = 16
DEPTH = 4
DFF = 2816
NFF = 22
INC = 2828
EPS = 1e-6
NEGB = -30000.0


class Tok:
    __slots__ = ("name", "w", "r", "dsem", "dtot", "dw", "uid", "psum")
    _n = [0]

    def __init__(self, name):
        Tok._n[0] += 1
        self.uid = Tok._n[0]
        self.psum = False
        self.name = name
        self.w = None
        self.r = {}
        self.dsem = None
        self.dtot = 0
        self.dw = 0


class V:
    __slots__ = ("ap", "toks")

    def __init__(self, ap, toks):
        self.ap = ap
        self.toks = toks

    def __getitem__(self, idx):
        return V(self.ap[idx], self.toks)

    def m(self, fn):
        return V(fn(self.ap), self.toks)

    def bc(self, shape):
        return V(self.ap.to_broadcast(shape), self.toks)

    def re(self, pat, **kw):
        return V(self.ap.rearrange(pat, **kw), self.toks)

    def bitcast(self, dt):
        return V(self.ap.bitcast(dt), self.toks)


class TK:
    def __init__(self, t, i):
        self.t = t
        self.i = i

    def __getitem__(self, idx):
        return V(self.t.h[idx], [self.t.toks[self.i]])


class T:
    def __init__(self, handle, name, ntok=1):
        self.h = handle
        self.toks = [Tok("%s.%d" % (name, i)) for i in range(ntok)]

    def __getitem__(self, idx):
        return V(self.h[idx], self.toks)

    def k(self, i):
        return TK(self, i)


class Eng:
    def __init__(self, name, obj):
        self.name = name
        self.obj = obj
        self.sems = []
        self.gen = -1
        self.cnt = 0
        self.known = {}
        self.dknown = {}


class Prog:
    ROT = 30000

    def __init__(self, nc, ctx):
        self.nc = nc
        self.ctx = ctx
        self.eng = {
            "pe": Eng("pe", nc.tensor),
            "act": Eng("act", nc.scalar),
            "dve": Eng("dve", nc.vector),
            "pool": Eng("pool", nc.gpsimd),
            "sp": Eng("sp", nc.sync),
        }
        self.nsem = 0
        for e in self.eng.values():
            self._newsem(e)
        self.nins = 0
        self.dfree = []
        self.dlive = {}

    def _sem(self, name):
        self.nsem += 1
        return self.ctx.enter_context(self.nc.semaphore("%s_%d" % (name, self.nsem)))

    def _newsem(self, e):
        e.sems.append(self._sem(e.name))
        e.gen += 1
        e.cnt = 0

    def sb(self, name, shape, dt, ntok=1):
        h = self.ctx.enter_context(self.nc.sbuf_tensor("sb_" + name, list(shape), dt))
        return T(h, name, ntok)

    def psum(self, name, shape, dt):
        h = self.ctx.enter_context(self.nc.psum_tensor(name, list(shape), dt))
        t = T(h, name, 1)
        t.toks[0].psum = True
        return t

    def _wait_ticket(self, E, tk):
        if tk is None:
            return
        e2, gen, val = tk
        if e2 is E and E.name == "pe":
            return
        k = E.known.get(e2.name)
        if k is not None and k >= (gen, val):
            return
        E.obj.wait_ge(e2.sems[gen], val)
        E.known[e2.name] = (gen, val)

    def _wait_dma(self, E, t, val):
        if val <= 0 or t.dsem is None:
            return
        if E.dknown.get(t.uid, 0) >= val:
            return
        E.obj.wait_ge(t.dsem, val)
        E.dknown[t.uid] = val

    def _deps(self, E, reads, writes):
        for t in reads:
            self._wait_ticket(E, t.w)
            self._wait_dma(E, t, t.dw)
            if t.psum:
                for en2, tk in t.r.items():
                    if en2 != E.name:
                        self._wait_ticket(E, tk)
        for t in writes:
            self._wait_ticket(E, t.w)
            for tk in t.r.values():
                self._wait_ticket(E, tk)
            self._wait_dma(E, t, t.dtot)

    def op(self, en, fn, reads=(), writes=()):
        E = self.eng[en]
        self._deps(E, reads, writes)
        ins = fn(E.obj)
        if E.cnt >= self.ROT:
            self._newsem(E)
        E.cnt += 1
        ins.then_inc(E.sems[E.gen], 1)
        tk = (E, E.gen, E.cnt)
        for t in reads:
            t.r[en] = tk
        for t in writes:
            t.w = tk
            t.r = {}
        self.nins += 1
        return ins

    def dma(self, qn, out, in_, reads=(), writes=(), **kw):
        E = self.eng[qn]
        self._deps(E, reads, writes)
        ins = E.obj.dma_start(out=out, in_=in_, **kw)
        t = (list(writes) + list(reads))[0]
        if t.dsem is None:
            if self.dfree:
                t.dsem, base = self.dfree.pop()
            else:
                t.dsem, base = self._sem("d"), 0
            t.dtot = base
            t.dw = 0
            self.dlive[t.uid] = t
        ins.then_inc(t.dsem, 16)
        t.dtot += 16
        if writes:
            t.dw = t.dtot
        self.nins += 1
        return ins

    def barrier(self, toks=()):
        SP = self.eng["sp"]
        for t in self.dlive.values():
            self._wait_dma(SP, t, t.dtot)
        self.op("sp", lambda e: e.nop(), reads=(), writes=())
        last = {n: (e, e.gen, e.cnt) for n, e in self.eng.items()}
        for n, E in self.eng.items():
            for n2, tk in last.items():
                if tk[2] > 0:
                    self._wait_ticket(E, tk)
        for t in self.dlive.values():
            self.dfree.append((t.dsem, t.dtot))
            t.dsem = None
            t.dtot = 0
            t.dw = 0
        self.dlive = {}
        for E in self.eng.values():
            E.dknown = {}

    def finish(self, toks):
        E = self.eng["sp"]
        for t in toks:
            self._wait_dma(E, t, t.dtot)

    @staticmethod
    def _tk(*vs):
        out = []
        for v in vs:
            if isinstance(v, V):
                out.extend(v.toks)
        return out

    @staticmethod
    def _a(v):
        return v.ap if isinstance(v, V) else v

    def mm(self, out, lhsT, rhs, start=True, stop=True, **kw):
        return self.op("pe", lambda e: e.matmul(out.ap, lhsT=lhsT.ap, rhs=rhs.ap, start=start, stop=stop, **kw),
                       reads=self._tk(lhsT, rhs), writes=out.toks)

    def tr(self, out, in_, ident):
        return self.op("pe", lambda e: e.transpose(out.ap, in_.ap, ident.ap),
                       reads=self._tk(in_, ident), writes=out.toks)

    def act(self, out, in_, func, bias=None, scale=None, accum_out=None):
        kw = {}
        if bias is not None:
            kw["bias"] = self._a(bias)
        if scale is not None:
            kw["scale"] = self._a(scale)
        if accum_out is not None:
            kw["accum_out"] = accum_out.ap
        return self.op("act", lambda e: e.activation(out=out.ap, in_=in_.ap, func=func, **kw),
                       reads=self._tk(in_, bias, scale), writes=self._tk(out, accum_out))

    def tt(self, en, out, a, b, op):
        return self.op(en, lambda e: e.tensor_tensor(out=out.ap, in0=a.ap, in1=b.ap, op=op),
                       reads=self._tk(a, b), writes=out.toks)

    def ts(self, en, out, a, s1, op0, s2=None, op1=None, accum_out=None):
        kw = {}
        if op1 is not None:
            kw["op1"] = op1
        if accum_out is not None:
            kw["accum_out"] = accum_out.ap
        return self.op(en, lambda e: e.tensor_scalar(out=out.ap, in0=a.ap, scalar1=self._a(s1), scalar2=self._a(s2),
                                                     op0=op0, **kw),
                       reads=self._tk(a, s1, s2), writes=self._tk(out, accum_out))

    def stt(self, out, a, scalar, b, op0, op1, accum_out=None):
        kw = {}
        if accum_out is not None:
            kw["accum_out"] = accum_out.ap
        return self.op("dve", lambda e: e.scalar_tensor_tensor(out=out.ap, in0=a.ap, scalar=self._a(scalar), in1=b.ap,
                                                               op0=op0, op1=op1, **kw),
                       reads=self._tk(a, scalar, b), writes=self._tk(out, accum_out))

    def copy(self, en, out, in_):
        if en == "act":
            return self.op("act", lambda e: e.copy(out=out.ap, in_=in_.ap), reads=in_.toks, writes=out.toks)
        return self.op(en, lambda e: e.tensor_copy(out=out.ap, in_=in_.ap), reads=in_.toks, writes=out.toks)

    def red(self, out, in_, op, axis=AX.X):
        return self.op("dve", lambda e: e.tensor_reduce(out=out.ap, in_=in_.ap, axis=axis, op=op),
                       reads=in_.toks, writes=out.toks)

    def memset(self, en, out, val):
        return self.op(en, lambda e: e.memset(out.ap, val), reads=(), writes=out.toks)

    def recip(self, out, in_):
        return self.op("dve", lambda e: e.reciprocal(out=out.ap, in_=in_.ap), reads=in_.toks, writes=out.toks)

    def load(self, dst, src_ap, q="sp", **kw):
        return self.dma(q, dst.ap, src_ap, writes=dst.toks, **kw)

    def store(self, dst_ap, src, q="sp", **kw):
        return self.dma(q, dst_ap, src.ap, reads=src.toks, **kw)


def host_consts():
    c = {}
    pos = np.arange(S, dtype=np.float64)
    inv = 10000.0 ** (-np.arange(0, 64, 2, dtype=np.float64) / 64)
    ang = (pos[:, None].astype(np.float32) * inv[None, :].astype(np.float32)).astype(np.float32)
    cos = np.cos(ang).astype(np.float32)
    sin = np.sin(ang).astype(np.float32)
    cs = np.concatenate([cos, sin], axis=1)
    c["cs_tab"] = np.ascontiguousarray(cs.reshape(NT, 128, 64).transpose(1, 0, 2))
    H = 4
    lg = np.log1p(-(2.0 ** (-5.0 - np.arange(H, dtype=np.float64))))
    idx = np.arange(128, dtype=np.float64)
    diff = idx[None, :] - idx[:, None]
    dec = np.where(diff >= 0, np.exp(lg[:, None, None] * np.maximum(diff, 0.0)), 0.0) * 0.125
    c["decT"] = np.ascontiguousarray(dec.transpose(1, 0, 2)).astype(np.float32)
    xi = np.exp(lg[:, None] * (idx + 1)[None, :]) * 0.125
    xit = np.zeros((128, 2, 128), np.float32)
    for h in range(H):
        xit[(h % 2) * 64:(h % 2) * 64 + 64, h // 2, :] = xi[h][None, :]
    c["xi_tab"] = xit
    zeta = np.exp(lg[:, None] * (127 - idx)[None, :])
    c["zeta_tab"] = np.ascontiguousarray(zeta.T).astype(np.float32)
    gch = np.exp(lg * 128)
    gct = np.zeros((128, 2), np.float32)
    for h in range(H):
        gct[(h % 2) * 64:(h % 2) * 64 + 64, h // 2] = gch[h]
    c["gc_tab"] = gct
    c["tril"] = np.tril(np.ones((128, 128), np.float32))
    c["triu"] = np.triu(np.ones((128, 128), np.float32))
    m8 = np.zeros((128, 8), np.float32)
    for g8 in range(8):
        m8[g8 * 16:(g8 + 1) * 16, g8] = 1.0
    c["mask8"] = m8
    c["iota129"] = np.broadcast_to(np.arange(129, dtype=np.float32)[None, :], (128, 129)).copy()
    kk_ = np.arange(128)[:, None]
    qq_ = np.arange(128)[None, :]
    nt = np.zeros((128, 2, 128), np.float32)
    nt[:, 0, :] = np.where(kk_ <= qq_, 0.0, NEGB)
    nt[:, 1, :] = np.where(kk_ > qq_, 0.0, NEGB)
    c["negT"] = nt
    nn = np.arange(128)[:, None]
    qpos = np.arange(S)[None, :]
    c["negvalid"] = np.where(16 * nn + 31 <= qpos, 0.0, NEGB).astype(np.float32)
    em = np.zeros((32, 16, 128), np.float32)
    for kb in range(16):
        em[2 * kb, kb, 0:64] = 1.0
        em[2 * kb + 1, kb, 64:128] = 1.0
    c["emat"] = em
    posf = np.arange(S)
    cur = posf // 64
    blk = np.arange(32)[None, :]
    bon = np.zeros((S, 32), np.float32)
    forced = (blk == 0) | (blk == cur[:, None]) | (blk == cur[:, None] - 1)
    bon[forced] = 1e3
    bon[np.broadcast_to(blk, (S, 32)) > cur[:, None]] = -1e9
    c["bonus"] = np.ascontiguousarray(bon.reshape(NT, 128, 32).transpose(1, 0, 2))
    ii = np.arange(127)[:, None]
    jj = np.arange(32)[None, :]
    ovl = ((ii * 16 < (jj + 1) * 64) & (ii * 16 + 32 > jj * 64)).astype(np.float32)
    o1 = np.zeros((128, 33), np.float32)
    o1[:127, :32] = ovl
    o1[:127, 32] = 1.0
    c["ovl1"] = o1
    cend = (np.arange(127) * 16 + 31).astype(np.float32)
    angc = (cend[:, None] * inv[None, :].astype(np.float32)).astype(np.float32)
    csc = np.zeros((128, 64), np.float32)
    csc[:127, :32] = np.cos(angc)
    csc[:127, 32:] = np.sin(angc)
    c["cs_c"] = csc
    sx = np.zeros((128, 2), np.float32)
    sx[:, 0] = np.arange(128)
    sx[:, 1] = -np.arange(128)
    c["sidx"] = sx
    c["ident"] = np.eye(128, dtype=np.float32)
    c["ones"] = np.ones((128, 128), np.float32)
    return c


CONST_SHAPES = {"cs_tab": (128, NT, 64), "decT": (128, 4, 128), "xi_tab": (128, 2, 128), "zeta_tab": (128, 4),
                "gc_tab": (128, 2), "ident": (128, 128), "ones": (128, 128), "tril": (128, 128),
                "triu": (128, 128), "mask8": (128, 8), "iota129": (128, 129), "sidx": (128, 2),
                "negT": (128, 2, 128), "negvalid": (128, S), "emat": (32, 16, 128), "bonus": (128, NT, 32),
                "ovl1": (128, 33), "cs_c": (128, 64)}

WEIGHT_SHAPES = {
    "norm1_g": (DEPTH, D), "norm2_g": (DEPTH, D), "ada_w": (DEPTH, D, 6 * D), "ada_b": (DEPTH, 6 * D),
    "w_in": (DEPTH, D, INC), "ret_norm_g": (DEPTH, 256),
    "s5_lambda_re": (DEPTH, 16, 64), "s5_lambda_im": (DEPTH, 16, 64), "s5_log_dt": (DEPTH, 16),
    "s5_b_re": (DEPTH, 16, 64, 16), "s5_b_im": (DEPTH, 16, 64, 16), "s5_c_re": (DEPTH, 16, 16, 64),
    "s5_c_im": (DEPTH, 16, 16, 64), "s5_d": (DEPTH, 256), "s5_glu_w": (DEPTH, 256, 256), "s5_glu_b": (DEPTH, 256),
    "sgu_norm_g": (DEPTH, 256), "sgu_w": (DEPTH, 4, 128, 128), "sgu_b": (DEPTH, 4, 128),
    "nsa_q_norm_g": (DEPTH, 64), "nsa_k_norm_g": (DEPTH, 3, 64), "nsa_cmp_pe": (DEPTH, 2, 32, 64),
    "nsa_cmp_w1": (DEPTH, 2, 2048, 128), "nsa_cmp_w2": (DEPTH, 2, 128, 64),
    "mix_norm_g": (DEPTH, D), "w_out": (DEPTH, D, D), "ffn_w_up": (DEPTH, D, 2 * DFF),
    "ffn_conv_w": (DEPTH, 3, 2 * DFF), "ffn_conv_b": (DEPTH, 2 * DFF), "ffn_w_down": (DEPTH, DFF, D),
}


OVERLAP = True


def build(nseq=2, nlayers=DEPTH, stages=("ret", "s5", "sgu", "nsa", "ffn"), debug=(), wl=DEPTH):
    nc = bass.Bass("TRN2", target_bir_lowering=False)
    dr = {}
    dr["x"] = nc.dram_tensor("x", [nseq, S, D], F32, kind="ExternalInput").ap()
    dr["c"] = nc.dram_tensor("c", [nseq, D], F32, kind="ExternalInput").ap()
    for n, shp in WEIGHT_SHAPES.items():
        dr[n] = nc.dram_tensor(n, [wl] + list(shp[1:]), F32, kind="ExternalInput").ap()
    for n, shp in CONST_SHAPES.items():
        dr[n] = nc.dram_tensor(n, list(shp), F32, kind="ExternalInput").ap()
    dr["out"] = nc.dram_tensor("out", [nseq, S, D], F32, kind="ExternalOutput").ap()
    S5TAB = {"BD": ([128, 2, 4, 512], BF16), "ARi": ([128, 2, 512], F32), "AIi": ([128, 2, 512], F32),
             "Ar": ([128, 2, 4, 129], F32), "Ai": ([128, 2, 4, 129], F32),
             "Cre": ([128, 2, 4, 64], BF16), "nCre": ([128, 2, 4, 64], BF16), "nCim": ([128, 2, 4, 64], BF16)}
    s5c = {}
    for l_ in range(nlayers):
        for n, (shp, dt_) in S5TAB.items():
            s5c[(l_, n)] = (nc.dram_tensor("s5c_%s_%d" % (n, l_), shp, dt_, kind="Internal").ap(), Tok("s5c_%s_%d" % (n, l_)))
    dbg = {}
    DBG_SHAPES = {"hT": [128, KC, S], "yret": [128, NT, 256], "xT": [128, KC, S], "mod": [128, wl * 48, 2], "ys5": [128, 2, S]}
    for n in debug:
        dbg[n] = nc.dram_tensor("dbg_" + n, DBG_SHAPES[n], BF16 if n in ("ys5", "yret") else F32, kind="ExternalOutput").ap()
    dbgtoks = []

    ucnt = [0]

    def SBT(name, shape, dt):
        ucnt[0] += 1
        return nc.sbuf_tensor("%s_u%d" % (name, ucnt[0]), list(shape), dt)

    with ExitStack() as ctx:
        P = Prog(nc, ctx)
        ctx.enter_context(nc.allow_non_contiguous_dma(reason="small param loads"))
        ctx.enter_context(nc.allow_low_precision(reason="bf16 matmul operands"))
        xT = P.sb("xT", [128, KC, S], F32, ntok=NT)
        hTbox = {}

        def tv(Tt, kcs, a, b):
            return V(Tt.h[:, kcs, a:b], [Tt.toks[i] for i in range(a // 128, (b + 127) // 128)])

        ps = [P.psum("ps%d" % i, [128, 512], F32) for i in range(8)]
        cs_tab = P.sb("cs_tab", [128, NT, 64], F32)
        zeta_tab = P.sb("zeta_tab", [128, 4], F32)
        gc_tab = P.sb("gc_tab", [128, 2], F32)
        ident = P.sb("ident", [128, 128], F32)
        identb = P.sb("identb", [128, 128], BF16)
        ones = P.sb("ones", [128, 128], F32)
        onesb = P.sb("onesb", [128, 128], BF16)
        tril = P.sb("tril", [128, 128], F32)
        triu = P.sb("triu", [128, 128], F32)
        triub = P.sb("triub", [128, 128], BF16)
        mask8 = P.sb("mask8", [128, 8], F32)
        iota129 = P.sb("iota129", [128, 129], F32)
        sidx = P.sb("sidx", [128, 2], F32)
        for t, n in ((cs_tab, "cs_tab"), (zeta_tab, "zeta_tab"),
                     (gc_tab, "gc_tab"), (ident, "ident"), (ones, "ones"), (tril, "tril"),
                     (triu, "triu"), (mask8, "mask8"), (iota129, "iota129"), (sidx, "sidx")):
            P.load(t[:], dr[n])
        P.copy("dve", identb[:], ident[:])
        P.copy("dve", triub[:], triu[:])
        P.copy("dve", onesb[:], ones[:])

        modT = P.sb("modT", [128, wl * 48, 2], F32)
        cT = P.sb("cT", [128, KC, 2], F32)
        g1n = P.sb("g1n", [128, wl, KC], F32)
        g2n = P.sb("g2n", [128, wl, KC], F32)
        scl = P.sb("scl", [128, 2, KC], F32)
        mgT = P.sb("mgT", [128, wl, KC], F32)

        for b_ in range(nseq):
            P.load(cT[:, :, b_], dr["c"][b_].rearrange("(k p) -> p k", p=128))
        if nseq < 2:
            P.memset("dve", cT[:, :, nseq:2], 0.0)
        P.act(cT[:], cT[:], AF.Silu)
        P.load(g1n[:], dr["norm1_g"].rearrange("l (k p) -> p l k", p=128))
        P.load(g2n[:], dr["norm2_g"].rearrange("l (k p) -> p l k", p=128))
        P.load(mgT[:], dr["mix_norm_g"].rearrange("l (k p) -> p l k", p=128))
        adab = P.sb("adab", [128, wl * 48], F32)
        P.load(adab[:], dr["ada_b"].rearrange("l (o p) -> p (l o)", p=128))
        with ExitStack() as c2:
            NAB = 2
            PW = 1536
            awb = [T(c2.enter_context(SBT("awb%d" % i, [128, KC, PW], F32)), "awb%d" % i) for i in range(NAB)]
            modrow = T(c2.enter_context(SBT("modrow", [2, 512], F32)), "modrow")
            brow = [T(c2.enter_context(SBT("brow%d" % i, [2, 512], F32)), "brow%d" % i) for i in range(2)]
            it = 0
            nb_ = 0
            for l in range(nlayers):
                for pc in range(6 * D // PW):
                    buf = awb[it % NAB]
                    it += 1
                    for kc in range(KC):
                        P.load(buf[:, kc, :], dr["ada_w"][l, kc * 128:(kc + 1) * 128, pc * PW:(pc + 1) * PW],
                               q=("sp" if kc % 2 == 0 else "act"))
                    for sbk in range(PW // 512):
                        cc = pc * (PW // 512) + sbk
                        br_ = brow[nb_ % 2]
                        nb_ += 1
                        P.load(br_[:], dr["ada_b"][l:l + 1, cc * 512:(cc + 1) * 512].to_broadcast([2, 512]))
                        pst = ps[nb_ % 2]
                        for kc in range(KC):
                            P.mm(pst[0:2, :], cT[:, kc, :], buf[:, kc, sbk * 512:(sbk + 1) * 512], start=(kc == 0), stop=(kc == KC - 1))
                        P.tt("dve", modrow[:], pst[0:2, :], br_[:], ALU.add)
                        ptr = ps[2 + nb_ % 2]
                        for j in range(4):
                            P.tr(ptr[:, 2 * j:2 * j + 2], modrow[0:2, j * 128:(j + 1) * 128], ident[0:2, 0:2])
                        o0 = l * 48 + cc * 4
                        P.copy("act", modT[:, o0:o0 + 4, :], ptr[:, 0:8].re("p (j b) -> p j b", b=2))
            P.barrier()
        if "mod" in dbg:
            P.store(dbg["mod"], modT[:])
            dbgtoks.append(modT.toks[0])

        if "ys5" in dbg:
            ys5db = P.sb("ys5db", [128, 2, S], BF16)
        if "yret" in dbg:
            ydb = P.sb("ydb", [128, NT, 256], BF16)
        rr = {"ev": 0}

        def pipelined_gen(make, n):
            DONE = object()
            act_ = []
            nxt = [0]

            def start():
                act_.append([make(nxt[0]), False])
                nxt[0] += 1
            start()
            while act_:
                if len(act_) == 1 and act_[0][1] and nxt[0] < n:
                    start()
                for ent in list(act_):
                    r = next(ent[0], DONE)
                    if r is DONE:
                        act_.remove(ent)
                    elif r == "mark":
                        ent[1] = True
                if not act_ and nxt[0] < n:
                    start()
                yield

        def run_pipelined(make, n):
            for _ in pipelined_gen(make, n):
                pass

        def load_x(sq, c3):
            xin = [T(c3.enter_context(SBT("xin%d" % i, [128, D], F32)), "xin%d" % i) for i in range(2)]
            for t in range(NT):
                xb = xin[t % 2]
                P.load(xb[:], dr["x"][sq, t * 128:(t + 1) * 128, :])
                en = "act" if t % 2 else "dve"
                for half in range(2):
                    pst = ps[(2 * t + half) % 4]
                    for j in range(4):
                        kc = half * 4 + j
                        P.tr(pst[:, j * 128:(j + 1) * 128], xb[:, kc * 128:(kc + 1) * 128], ident[:])
                    P.copy(en, tv(xT, slice(half * 4, half * 4 + 4), t * 128, (t + 1) * 128),
                           pst[:].re("p (j c) -> p j c", c=128))
            P.barrier([b.toks[0] for b in xin])

        def store_x(sq, c3):
            xo = [T(c3.enter_context(SBT("xo%d" % i, [128, D], F32)), "xo%d" % i) for i in range(2)]
            for t in range(NT):
                xb = xo[t % 2]
                en = "act" if t % 2 else "dve"
                for half in range(2):
                    pst = ps[(2 * t + half) % 4]
                    for j in range(4):
                        kc = half * 4 + j
                        P.tr(pst[:, j * 128:(j + 1) * 128], tv(xT, kc, t * 128, (t + 1) * 128), ident[:])
                    P.copy(en, xb[:, half * 512:(half + 1) * 512], pst[:])
                P.store(dr["out"][sq, t * 128:(t + 1) * 128, :], xb[:])
            P.barrier([b.toks[0] for b in xo])
            return [b.toks[0] for b in xo]

        def norm_stage(l, sq, which, c3):
            gN = g1n if which == 0 else g2n
            mb = l * 48 + 24 * which
            P.stt(scl[:, which, :], modT[:, mb + 8:mb + 16, sq], 1.0, gN[:, l, :], ALU.add, ALU.mult)
            sqb = [T(c3.enter_context(SBT("nsq%d_%d" % (i, which), [128, 512], BF16)), "nsq%d" % i) for i in range(3)]
            rs = [T(c3.enter_context(SBT("nrs%d_%d" % (i, which), [128, 512], F32)), "nrs%d" % i) for i in range(2)]
            tmp = [T(c3.enter_context(SBT("ntm%d_%d" % (i, which), [128, 512], F32)), "ntm%d" % i) for i in range(2)]
            n = 0
            for tc in range(4):
                a, b = tc * 512, (tc + 1) * 512
                pss = ps[4 + tc % 2]
                for kc in range(KC):
                    sb_ = sqb[n % 3]
                    n += 1
                    if kc % 2 == 0:
                        P.act(sb_[:], tv(xT, kc, a, b), AF.Square)
                    else:
                        P.tt("pool", sb_[:], tv(xT, kc, a, b), tv(xT, kc, a, b), ALU.mult)
                    P.mm(pss[:], onesb[:], sb_[:], start=(kc == 0), stop=(kc == KC - 1))
                r = rs[tc % 2]
                P.act(r[:], pss[:], AF.Sqrt, bias=EPS, scale=1.0 / D)
                P.recip(r[:], r[:])
                for kc in range(KC):
                    tm = tmp[kc % 2]
                    P.tt("dve", tm[:], tv(xT, kc, a, b), r[:], ALU.mult)
                    P.act(tv(hTbox['h'], kc, a, b), tm[:], AF.Identity, scale=scl[:, which, kc:kc + 1],
                          bias=modT[:, mb + kc, sq:sq + 1])
            P.barrier()

        stg = [P.sb("stg%d" % i, [128, 1408], F32) for i in range(2)]
        stgn = {"n": 0, "c": 0}

        def cast_eng():
            stgn["c"] += 1
            return "act" if stgn["c"] % 2 else "dve"

        def load_w(dst, name, l, r0, nk, c0, ncol):
            for k in range(nk):
                for cc in range(0, ncol, 1408):
                    n = min(1408, ncol - cc)
                    st = stg[stgn["n"] % 2]
                    stgn["n"] += 1
                    P.load(st[:, 0:n], dr[name][l, r0 + k * 128:r0 + (k + 1) * 128, c0 + cc:c0 + cc + n])
                    P.copy(cast_eng(), dst[:, k, cc:cc + n], st[:, 0:n])

        def proj_tok(pst, W, wc0, ncol, t):
            for kc in range(KC):
                P.mm(pst[:, 0:ncol], tv(hTbox['h'], kc, t * 128, (t + 1) * 128), W[:, kc, wc0:wc0 + ncol],
                     start=(kc == 0), stop=(kc == KC - 1))

        def out_proj(l, sq, Wo, nyc, yT, tc, psx):
            a, b = tc * 512, (tc + 1) * 512
            for dc in range(KC):
                pst = psx[dc % len(psx)]
                for yc in range(nyc):
                    P.mm(pst[:], Wo[:, yc, dc * 128:(dc + 1) * 128], yT[:, yc, :], start=(yc == 0), stop=(yc == nyc - 1))
                P.stt(tv(xT, dc, a, b), pst[:], modT[:, l * 48 + 16 + dc, sq:sq + 1], tv(xT, dc, a, b), ALU.mult, ALU.add)

        class MixerTail:
            def __init__(self, l, sq, m, c3, psY, psX):
                self.l, self.sq, self.m, self.psY, self.psX = l, sq, m, psY, psX

                def sbt(name, shape, dt, n=1):
                    r = [T(c3.enter_context(SBT("mt_%s%d" % (name, i), shape, dt)), "mt_%s%d" % (name, i)) for i in range(n)]
                    return r if n > 1 else r[0]
                self.Wo = sbt("Wo", [128, 2, D], BF16)
                load_w(self.Wo, "w_out", l, m * 256, 2, 0, D)
                self.mg = sbt("mg", [128, 256], F32)
                P.load(self.mg[:], dr["mix_norm_g"][l:l + 1, m * 256:(m + 1) * 256].to_broadcast([128, 256]))
                self.sqo = sbt("sqo", [128, 256], F32, 2)
                self.ss2 = sbt("ss2", [128, 1], F32, 2)
                self.yt = sbt("yt", [128, 256], BF16, 2)
                self.yT = sbt("yT", [128, 2, 512], BF16, 2)

            def run(self, t, A):
                for _ in self.run_g(t, A):
                    pass

            def run_g(self, t, A):
                s2, m, l, sq = t % 2, self.m, self.l, self.sq
                ss2, yt, yT = self.ss2, self.yt, self.yT
                P.act(self.sqo[s2][:], A[:], AF.Square, accum_out=ss2[s2][:])
                P.act(ss2[s2][:], ss2[s2][:], AF.Sqrt, bias=EPS, scale=1.0 / 256)
                yield
                P.recip(ss2[s2][:], ss2[s2][:])
                P.stt(yt[s2][:], A[:], ss2[s2][:], self.mg[:], ALU.mult, ALU.mult)
                yield
                if "yret" in dbg:
                    P.copy("dve", ydb[:, t, :], yt[s2][:])
                    if t == NT - 1:
                        P.store(dbg["yret"], ydb[:])
                        dbgtoks.append(ydb.toks[0])
                tc, j = t // 4, t % 4
                pYb = self.psY if isinstance(self.psY, V) else self.psY[:].bitcast(BF16)
                for yc in range(2):
                    P.tr(pYb[:, yc * 128:(yc + 1) * 128], yt[s2][:, yc * 128:(yc + 1) * 128], identb[:])
                P.copy("act", yT[tc % 2][:, :, j * 128:(j + 1) * 128], pYb[:, 0:256].re("p (c i) -> p c i", i=128))
                yield
                if j == 3:
                    out_proj(l, sq, self.Wo, 2, yT[tc % 2], tc, [self.psX])
                    yield

        def retention_stage(l, sq, c3, banks=None, run=True):
            def sbt(name, shape, dt, n=1):
                r = [T(c3.enter_context(SBT("rt_%s%d" % (name, i), shape, dt)), "rt_%s%d" % (name, i)) for i in range(n)]
                return r if n > 1 else r[0]
            W = sbt("W", [128, KC, 1024], BF16)
            load_w(W, "w_in", l, 0, KC, 0, 1024)
            ng = sbt("ng", [128, 256], F32)
            P.load(ng[:], dr["ret_norm_g"][l:l + 1, :].to_broadcast([128, 256]))
            decT = sbt("decT", [128, 4, 128], F32)
            xi_tab = sbt("xi_tab", [128, 2, 128], F32)
            P.load(decT[:], dr["decT"])
            P.load(xi_tab[:], dr["xi_tab"])
            if banks is None:
                psA, psB, psT, psS, psO, psU, psY, psX = ps
                pTb_ = psT[:].bitcast(BF16)
                pYb_ = psY
                psUv = psU
                uoff = 0
            else:
                psA, psB, psTY, psS, psOU, psX = banks
                psO = psOU
                psUv = psOU
                uoff = 256
                pTb_ = psTY[:].bitcast(BF16)
                pYb_ = pTb_[:, 512:768]
            qk = sbt("qk", [128, 512], BF16, 2)
            kz = sbt("kz", [128, 256], BF16, 2)
            vt = sbt("vt", [128, 256], BF16, 2)
            gs = sbt("gs", [128, 256], BF16, 2)
            tA = sbt("tA", [128, 256], F32, 2)
            tB = sbt("tB", [128, 256], F32, 2)
            qkT = sbt("qkT", [128, 512], BF16, 2)
            qxT = sbt("qxT", [128, 256], BF16, 2)
            sT = sbt("sT", [128, 512], BF16, 2)
            R = sbt("R", [128, 2, 64], F32)
            Rz = sbt("Rz", [128, 4, 64], BF16, 2)
            kTz = sbt("kTz", [128, 4, 128], BF16, 2)
            for i_ in range(2):
                P.memset("pool", Rz[i_][:], 0.0)
                P.memset("pool", kTz[i_][:], 0.0)
            sqo = sbt("sqo", [128, 256], F32)
            ss = sbt("ss", [128, 4], F32, 2)
            o1 = sbt("o1", [128, 256], F32, 2)
            A_ = sbt("A", [128, 256], F32, 2)
            tail = MixerTail(l, sq, 0, c3, pYb_, psX)
            P.memset("dve", R[:], 0.0)
            def body(t):
                s2 = t % 2
                cos = cs_tab[:, t, 0:32].m(lambda a: a.unsqueeze(1)).bc([128, 8, 32])
                sin = cs_tab[:, t, 32:64].m(lambda a: a.unsqueeze(1)).bc([128, 8, 32])
                proj_tok(psA, W, 0, 512, t)
                yield
                proj_tok(psB, W, 512, 512, t)
                yield
                xv = psA[:].re("p (h two d) -> p h two d", two=2, d=32)
                x1, x2 = xv[:, :, 0, :], xv[:, :, 1, :]
                ov = qk[s2][:].re("p (h two d) -> p h two d", two=2, d=32)
                tAv = tA[s2][:].re("p (h d) -> p h d", d=32)
                tBv = tB[s2][:].re("p (h d) -> p h d", d=32)
                P.tt("dve", tAv, x1, cos, ALU.mult)
                P.tt("dve", tBv, x2, sin, ALU.mult)
                yield
                P.tt("pool", ov[:, :, 0, :], tAv, tBv, ALU.subtract)
                tAv2 = tA[1 - s2][:].re("p (h d) -> p h d", d=32)
                tBv2 = tB[1 - s2][:].re("p (h d) -> p h d", d=32)
                P.tt("dve", tAv2, x2, cos, ALU.mult)
                P.tt("dve", tBv2, x1, sin, ALU.mult)
                yield
                P.tt("pool", ov[:, :, 1, :], tAv2, tBv2, ALU.add)
                yield
                P.tt("pool", kz[s2][:].re("p (h d) -> p h d", d=64), qk[s2][:, 256:512].re("p (h d) -> p h d", d=64),
                     zeta_tab[:].m(lambda a: a.unsqueeze(2)).bc([128, 4, 64]), ALU.mult)
                P.copy("act", vt[s2][:], psB[:, 0:256])
                P.act(gs[s2][:], psB[:, 256:512], AF.Silu)
                yield
                yield
                pTb = pTb_
                for j in range(4):
                    P.tr(pTb[:, j * 128:(j + 1) * 128], qk[s2][:, j * 128:(j + 1) * 128], identb[:])
                P.copy("act", qkT[s2][:, 0:256], pTb[:, 0:256])
                kview = kTz[s2][:].re("p (pr two) i -> p pr two i", two=2)
                P.copy("act", kview[0:64, :, 0, :], pTb[0:64, 256:512].re("p (pr i) -> p pr i", i=128))
                P.copy("act", kview[64:128, :, 1, :], pTb[64:128, 256:512].re("p (pr i) -> p pr i", i=128))
                P.tt("dve", qxT[s2][:].re("p (a i) -> p a i", i=128), pTb[:, 0:256].re("p (a i) -> p a i", i=128),
                     xi_tab[:], ALU.mult)
                yield
                for h in range(4):
                    pr, b0 = h // 2, (h % 2) * 64
                    P.mm(psS[:, h * 128:(h + 1) * 128], kTz[s2][:, h, :], qkT[s2][:, pr * 128:(pr + 1) * 128])
                P.tt("dve", sT[s2][:].re("p (h i) -> p h i", i=128), psS[:].re("p (h i) -> p h i", i=128), decT[:], ALU.mult)
                yield
                for h in range(4):
                    pr, b0 = h // 2, (h % 2) * 64
                    P.mm(psO[:, h * 64:(h + 1) * 64], sT[s2][:, h * 128:(h + 1) * 128], vt[s2][:, h * 64:(h + 1) * 64],
                         start=True, stop=(t == 0))
                    if t > 0:
                        P.mm(psO[:, h * 64:(h + 1) * 64], qxT[s2][:, pr * 128:(pr + 1) * 128],
                             Rz[s2][:, h, :], start=False, stop=True)
                yield
                if t < NT - 1:
                    for h in range(4):
                        pr, b0 = h // 2, (h % 2) * 64
                        P.mm(psUv[b0:b0 + 64, uoff + pr * 64:uoff + (pr + 1) * 64], kz[s2][:, h * 64:(h + 1) * 64],
                             vt[s2][:, h * 64:(h + 1) * 64])
                    for pr in range(2):
                        P.stt(R[:, pr, :], R[:, pr, :], gc_tab[:, pr:pr + 1], psUv[:, uoff + pr * 64:uoff + (pr + 1) * 64], ALU.mult, ALU.add)
                    rzv = Rz[1 - s2][:].re("p (pr two) e -> p pr two e", two=2)
                    P.copy("pool", rzv[0:64, :, 0, :], R[0:64, :, :])
                    P.copy("pool", rzv[64:128, :, 1, :], R[64:128, :, :])
                yield "mark"
                yield
                P.act(sqo[:], psO[:, 0:256], AF.Square)
                P.red(ss[s2][:], sqo[:].re("p (h d) -> p h d", d=64), ALU.add)
                P.act(ss[s2][:], ss[s2][:], AF.Sqrt, bias=EPS, scale=1.0 / 64)
                yield
                P.recip(ss[s2][:], ss[s2][:])
                P.tt("dve", o1[s2][:].re("p (h d) -> p h d", d=64), psO[:, 0:256].re("p (h d) -> p h d", d=64),
                     ss[s2][:].m(lambda a: a.unsqueeze(2)).bc([128, 4, 64]), ALU.mult)
                yield
                P.tt("pool", o1[s2][:], o1[s2][:], gs[s2][:], ALU.mult)
                yield
                P.tt("pool", A_[s2][:], o1[s2][:], ng[:], ALU.mult)
                yield
                yield from tail.run_g(t, A_[s2])
            if not run:
                return body
            run_pipelined(body, NT)
            P.barrier([W.toks[0], tail.Wo.toks[0], ng.toks[0], decT.toks[0], xi_tab.toks[0]])

        MAGIC = 12582912.0
        TWO_PI = 2.0 * math.pi
        C1 = 6.28125
        C2 = TWO_PI - C1

        def sincos(th, out_sin, out_cos, scr):
            k, r = scr
            for out, shift in ((out_sin, 0.0), (out_cos, 0.5 * math.pi)):
                if out is None:
                    continue
                if shift:
                    P.ts("dve", r, th, shift, ALU.add)
                    src = r
                else:
                    src = th
                P.ts("dve", k, src, 1.0 / TWO_PI, ALU.mult, MAGIC, ALU.add)
                P.ts("dve", k, k, -MAGIC, ALU.add)
                P.stt(r, k, -C1, src, ALU.mult, ALU.add)
                P.stt(r, k, -C2, r, ALU.mult, ALU.add)
                P.ts("dve", r, r, math.pi, ALU.min, -math.pi, ALU.max)
                P.act(out, r, AF.Sin)

        def s5_stage(l, sq, c3):
            def sbt(name, shape, dt, n=1):
                r = [T(c3.enter_context(SBT("s5_%s%d" % (name, i), shape, dt)), "s5_%s%d" % (name, i)) for i in range(n)]
                return r if n > 1 else r[0]
            W = sbt("W", [128, KC, 256], BF16)
            load_w(W, "w_in", l, 0, KC, 1024, 256)
            Wo = sbt("Wo", [128, 2, D], BF16)
            load_w(Wo, "w_out", l, 256, 2, 0, D)
            Wg = sbt("Wg", [128, 2, 256], BF16)
            load_w(Wg, "s5_glu_w", l, 0, 2, 0, 256)
            gb = sbt("gb", [128, 2], F32)
            P.load(gb[:], dr["s5_glu_b"][l].rearrange("(c p) -> p c", p=128))
            dT = sbt("dT", [128, 2], F32)
            P.load(dT[:], dr["s5_d"][l].rearrange("(c p) -> p c", p=128))
            BD = sbt("BD", [128, 2, 4, 512], BF16)
            ARi = sbt("ARi", [128, 2, 512], F32)
            AIi = sbt("AIi", [128, 2, 512], F32)
            Ar = sbt("Ar", [128, 2, 4, 129], F32)
            Ai = sbt("Ai", [128, 2, 4, 129], F32)
            Cre = sbt("Cre", [128, 2, 4, 64], BF16)
            nCre = sbt("nCre", [128, 2, 4, 64], BF16)
            nCim = sbt("nCim", [128, 2, 4, 64], BF16)
            tabs = {"BD": BD, "ARi": ARi, "AIi": AIi, "Ar": Ar, "Ai": Ai, "Cre": Cre, "nCre": nCre, "nCim": nCim}
            if sq > 0:
                for n_, t_ in tabs.items():
                    dap, dtok = s5c[(l, n_)]
                    P.dma("sp", t_[:].ap, dap, reads=[dtok], writes=t_.toks)
            else:
                with ExitStack() as c5:
                    def tb(name, shape):
                        return T(c5.enter_context(SBT("s5t_" + name, shape, F32)), "s5t_" + name)
                    lrE, liE, bRe, bIm = tb("lrE", [128, 2, 64]), tb("liE", [128, 2, 64]), tb("bRe", [128, 2, 64]), tb("bIm", [128, 2, 64])
                    dtE = tb("dtE", [128, 2])
                    for hf in range(2):
                        for g8 in range(8):
                            g = hf * 8 + g8
                            sl = slice(g8 * 16, (g8 + 1) * 16)
                            P.load(lrE[sl, hf, :], dr["s5_lambda_re"][l, g:g + 1, :].to_broadcast([16, 64]))
                            P.load(liE[sl, hf, :], dr["s5_lambda_im"][l, g:g + 1, :].to_broadcast([16, 64]))
                            P.load(dtE[sl, hf:hf + 1], dr["s5_log_dt"][l:l + 1, g:g + 1].to_broadcast([16, 1]))
                            P.load(bRe[sl, hf, :], dr["s5_b_re"][l, g].rearrange("p h -> h p"))
                            P.load(bIm[sl, hf, :], dr["s5_b_im"][l, g].rearrange("p h -> h p"))
                    P.act(dtE[:], dtE[:], AF.Exp)
                    dtb = dtE[:].m(lambda a: a.unsqueeze(2)).bc([128, 2, 64])
                    lrdt, lidt = tb("lrdt", [128, 2, 64]), tb("lidt", [128, 2, 64])
                    P.tt("dve", lrdt[:], lrE[:], dtb, ALU.mult)
                    P.tt("dve", lidt[:], liE[:], dtb, ALU.mult)
                    mag, sn, cs_, k1, k2 = tb("mag", [128, 2, 64]), tb("sn", [128, 2, 64]), tb("cs", [128, 2, 64]), tb("k1", [128, 2, 64]), tb("k2", [128, 2, 64])
                    P.act(mag[:], lrdt[:], AF.Exp)
                    sincos(lidt[:], sn[:], cs_[:], [k1[:], k2[:]])
                    ar, ai = tb("ar", [128, 2, 64]), tb("ai", [128, 2, 64])
                    P.tt("dve", ar[:], mag[:], cs_[:], ALU.mult)
                    P.tt("dve", ai[:], mag[:], sn[:], ALU.mult)
                    den = tb("den", [128, 2, 64])
                    P.tt("dve", den[:], lrE[:], lrE[:], ALU.mult)
                    P.tt("dve", k1[:], liE[:], liE[:], ALU.mult)
                    P.tt("dve", den[:], den[:], k1[:], ALU.add)
                    P.recip(den[:], den[:])
                    P.ts("dve", ar[:], ar[:], -1.0, ALU.add)
                    fre, fim = tb("fre", [128, 2, 64]), tb("fim", [128, 2, 64])
                    P.tt("dve", k1[:], ar[:], lrE[:], ALU.mult)
                    P.tt("dve", k2[:], ai[:], liE[:], ALU.mult)
                    P.tt("dve", k1[:], k1[:], k2[:], ALU.add)
                    P.tt("dve", fre[:], k1[:], den[:], ALU.mult)
                    P.tt("dve", k1[:], ai[:], lrE[:], ALU.mult)
                    P.tt("dve", k2[:], ar[:], liE[:], ALU.mult)
                    P.tt("dve", k1[:], k1[:], k2[:], ALU.subtract)
                    P.tt("dve", fim[:], k1[:], den[:], ALU.mult)
                    Bfr, Bfi, nBfi = tb("Bfr", [128, 2, 64]), tb("Bfi", [128, 2, 64]), tb("nBfi", [128, 2, 64])
                    P.tt("dve", k1[:], fre[:], bRe[:], ALU.mult)
                    P.tt("dve", k2[:], fim[:], bIm[:], ALU.mult)
                    P.tt("dve", Bfr[:], k1[:], k2[:], ALU.subtract)
                    P.tt("dve", k1[:], fre[:], bIm[:], ALU.mult)
                    P.tt("dve", k2[:], fim[:], bRe[:], ALU.mult)
                    P.tt("dve", Bfi[:], k1[:], k2[:], ALU.add)
                    P.ts("dve", nBfi[:], Bfi[:], -1.0, ALU.mult)
                    m8b = mask8[:].m(lambda a: a.unsqueeze(2)).bc([128, 8, 64])
                    for hf in range(2):
                        for q4, src in enumerate((Bfr, Bfi, nBfi, Bfr)):
                            P.tt("dve", BD[:, hf, q4, :].re("p (g q) -> p g q", q=64),
                                 src[:, hf, :].m(lambda a: a.unsqueeze(1)).bc([128, 8, 64]), m8b, ALU.mult)
                    P.barrier([t_.toks[0] for t_ in (lrE, liE, dtE, bRe, bIm)])
                with ExitStack() as c5:
                    lrB, liB, snB, csB, kB1, kB2 = (tb(n, [128, 512]) for n in ("lrB", "liB", "snB", "csB", "kB1", "kB2"))
                    dtB = tb("dtB", [128, 8])
                    for hf in range(2):
                        P.load(lrB[:], dr["s5_lambda_re"][l:l + 1, hf * 8:(hf + 1) * 8, :].rearrange("o g p -> o (g p)").to_broadcast([128, 512]))
                        P.load(liB[:], dr["s5_lambda_im"][l:l + 1, hf * 8:(hf + 1) * 8, :].rearrange("o g p -> o (g p)").to_broadcast([128, 512]))
                        P.load(dtB[:], dr["s5_log_dt"][l:l + 1, hf * 8:(hf + 1) * 8].to_broadcast([128, 8]))
                        P.act(dtB[:], dtB[:], AF.Exp)
                        dtbb = dtB[:].m(lambda a: a.unsqueeze(2)).bc([128, 8, 64])
                        P.tt("dve", lrB[:].re("p (g q) -> p g q", q=64), lrB[:].re("p (g q) -> p g q", q=64), dtbb, ALU.mult)
                        P.tt("dve", liB[:].re("p (g q) -> p g q", q=64), liB[:].re("p (g q) -> p g q", q=64), dtbb, ALU.mult)
                        P.act(lrB[:], lrB[:], AF.Exp, scale=sidx[:, 1:2])
                        P.ts("dve", liB[:], liB[:], sidx[:, 0:1], ALU.mult)
                        sincos(liB[:], snB[:], csB[:], [kB1[:], kB2[:]])
                        P.tt("dve", ARi[:, hf, :], lrB[:], csB[:], ALU.mult)
                        P.stt(AIi[:, hf, :], snB[:], -1.0, lrB[:], ALU.mult, ALU.mult)
                    P.barrier([t_.toks[0] for t_ in (lrB, liB, dtB)])
                with ExitStack() as c5:
                    lrS, liS, dtS = tb("lrS", [128, 2, 4]), tb("liS", [128, 2, 4]), tb("dtS", [128, 2, 4])
                    for hf in range(2):
                        P.load(lrS[:, hf, :], dr["s5_lambda_re"][l, hf * 8:(hf + 1) * 8, :].rearrange("(b gl) p -> (gl p) b", gl=2))
                        P.load(liS[:, hf, :], dr["s5_lambda_im"][l, hf * 8:(hf + 1) * 8, :].rearrange("(b gl) p -> (gl p) b", gl=2))
                        for gl in range(2):
                            P.load(dtS[gl * 64:(gl + 1) * 64, hf, :],
                                   dr["s5_log_dt"][l:l + 1, hf * 8:(hf + 1) * 8].rearrange("o (b gl) -> o gl b", gl=2)[:, gl, :].to_broadcast([64, 4]))
                    P.act(dtS[:], dtS[:], AF.Exp)
                    P.tt("dve", lrS[:], lrS[:], dtS[:], ALU.mult)
                    P.tt("dve", liS[:], liS[:], dtS[:], ALU.mult)
                    io = iota129[:].m(lambda a: a.unsqueeze(1).unsqueeze(1)).bc([128, 2, 4, 129])
                    exS, thS, snS, csS, kS1, kS2 = (tb(n, [128, 2, 4, 129]) for n in ("exS", "thS", "snS", "csS", "kS1", "kS2"))
                    P.tt("dve", exS[:], io, lrS[:].m(lambda a: a.unsqueeze(3)).bc([128, 2, 4, 129]), ALU.mult)
                    P.tt("dve", thS[:], io, liS[:].m(lambda a: a.unsqueeze(3)).bc([128, 2, 4, 129]), ALU.mult)
                    P.act(exS[:], exS[:], AF.Exp)
                    sincos(thS[:], snS[:], csS[:], [kS1[:], kS2[:]])
                    P.tt("dve", Ar[:], exS[:], csS[:], ALU.mult)
                    P.tt("dve", Ai[:], exS[:], snS[:], ALU.mult)
                    P.barrier([t_.toks[0] for t_ in (lrS, liS, dtS)])
                with ExitStack() as c5:
                    Cr, Ci = tb("Cr", [128, 2, 4, 64]), tb("Ci", [128, 2, 4, 64])
                    P.memset("dve", Cr[:], 0.0)
                    P.memset("dve", Ci[:], 0.0)
                    for hf in range(2):
                        for b4 in range(4):
                            for gl in range(2):
                                g = hf * 8 + 2 * b4 + gl
                                co = (b4 % 2) * 32 + gl * 16
                                P.load(Cr[gl * 64:(gl + 1) * 64, hf, b4, co:co + 16], dr["s5_c_re"][l, g].rearrange("h p -> p h"))
                                P.load(Ci[gl * 64:(gl + 1) * 64, hf, b4, co:co + 16], dr["s5_c_im"][l, g].rearrange("h p -> p h"))
                    P.copy("dve", Cre[:], Cr[:])
                    P.ts("dve", nCre[:], Cr[:], -1.0, ALU.mult)
                    P.ts("dve", nCim[:], Ci[:], -1.0, ALU.mult)
                    P.barrier([t_.toks[0] for t_ in (Cr, Ci)])
                for n_, t_ in tabs.items():
                    dap, dtok = s5c[(l, n_)]
                    P.dma("sp", dap, t_[:].ap, reads=t_.toks, writes=[dtok])
            uT = sbt("uT", [128, 2, 512], BF16)
            X1 = sbt("X1", [128, 2, 512], BF16, 2)
            X2 = sbt("X2", [128, 2, 512], BF16, 2)
            Gc = sbt("Gc", [128, 2, 512], F32, 2)
            Pp = sbt("Pp", [128, 4, 512], BF16, 2)
            car = sbt("car", [128, 2, 2, 4], F32)
            ct = sbt("ct", [128, 4, 4], F32)
            yS = sbt("yS", [128, 2, 512], F32)
            gyb = sbt("gyb", [128, 2, 512], BF16)
            sg = [sbt("sg", [128, 512], F32)] * 2
            oT = sbt("oT", [128, 2, 512], F32)
            sqm = [sbt("sqm", [128, 512], F32)] * 2
            rs = sqm[0]
            yTb = sbt("yTb", [128, 2, 512], BF16, 2)
            P.memset("dve", car[:], 0.0)
            psZ, psG, psYh = ps[0:4], ps[4:6], ps[6:8]
            it = 0
            for w in range(4):
                a, b = w * 512, (w + 1) * 512
                for hf in range(2):
                    for kc in range(KC):
                        P.mm(psZ[hf][:], W[:, kc, hf * 128:(hf + 1) * 128], tv(hTbox['h'], kc, a, b), start=(kc == 0), stop=(kc == KC - 1))
                    P.copy("act", uT[:, hf, :], psZ[hf][:])
                def s5_body(idx, w=w):
                    n, hf = idx // 2, idx % 2
                    i0 = n * 128
                    s2 = idx % 2
                    for q4 in range(4):
                        P.mm(psZ[q4][:], uT[:, hf, i0:i0 + 128], BD[:, hf, q4, :])
                    yield
                    P.tt("dve", X1[s2][:, 0, :], psZ[0][:], ARi[:, hf, :], ALU.mult)
                    P.tt("dve", X1[s2][:, 1, :], psZ[1][:], ARi[:, hf, :], ALU.mult)
                    P.tt("dve", X2[s2][:, 0, :], psZ[2][:], AIi[:, hf, :], ALU.mult)
                    P.tt("dve", X2[s2][:, 1, :], psZ[3][:], AIi[:, hf, :], ALU.mult)
                    yield "mark"
                    for ri in range(2):
                        for b4 in range(4):
                            cb = slice(b4 * 128, (b4 + 1) * 128)
                            P.mm(psG[ri][:, cb], X1[s2][:, ri, cb], triub[:], start=True, stop=False)
                            P.mm(psG[ri][:, cb], X2[s2][:, ri, cb], triub[:], start=False, stop=True)
                    yield
                    for ri in range(2):
                        P.tt("dve", Gc[s2][:, ri, :].re("p (b i) -> p b i", i=128), psG[ri][:].re("p (b i) -> p b i", i=128),
                             car[:, hf, ri, :].m(lambda x: x.unsqueeze(2)).bc([128, 4, 128]), ALU.add)
                    Arv = Ar[:, hf, :, 0:128]
                    Aiv = Ai[:, hf, :, 0:128]
                    gre = Gc[s2][:, 0, :].re("p (b i) -> p b i", i=128)
                    gim = Gc[s2][:, 1, :].re("p (b i) -> p b i", i=128)
                    pv = [Pp[s2][:, i_, :].re("p (b i) -> p b i", i=128) for i_ in range(4)]
                    P.tt("dve", pv[0], gre, Arv, ALU.mult)
                    P.tt("pool", pv[1], gim, Aiv, ALU.mult)
                    P.tt("pool", pv[2], gre, Aiv, ALU.mult)
                    P.tt("dve", pv[3], gim, Arv, ALU.mult)
                    yield
                    for b4 in range(4):
                        r0 = 64 * (b4 // 2)
                        o_ = psYh[hf][r0:r0 + 64, i0:i0 + 128]
                        cb = slice(b4 * 128, (b4 + 1) * 128)
                        P.mm(o_, Cre[:, hf, b4, :], Pp[s2][:, 0, cb], start=(b4 % 2 == 0), stop=False)
                        P.mm(o_, nCre[:, hf, b4, :], Pp[s2][:, 1, cb], start=False, stop=False)
                        P.mm(o_, nCim[:, hf, b4, :], Pp[s2][:, 2, cb], start=False, stop=False)
                        P.mm(o_, nCim[:, hf, b4, :], Pp[s2][:, 3, cb], start=False, stop=(b4 % 2 == 1))
                    yield
                    g7r = Gc[s2][:, 0, :].re("p (b i) -> p b i", i=128)[:, :, 127]
                    g7i = Gc[s2][:, 1, :].re("p (b i) -> p b i", i=128)[:, :, 127]
                    a8r, a8i = Ar[:, hf, :, 128], Ai[:, hf, :, 128]
                    P.tt("dve", ct[:, 0, :], a8r, g7r, ALU.mult)
                    P.tt("dve", ct[:, 1, :], a8i, g7i, ALU.mult)
                    P.tt("dve", ct[:, 2, :], a8r, g7i, ALU.mult)
                    P.tt("dve", ct[:, 3, :], a8i, g7r, ALU.mult)
                    P.tt("dve", car[:, hf, 0, :], ct[:, 0, :], ct[:, 1, :], ALU.subtract)
                    P.tt("dve", car[:, hf, 1, :], ct[:, 2, :], ct[:, 3, :], ALU.add)

                run_pipelined(s5_body, 8)
                for hf in range(2):
                    P.stt(yS[:, hf, :], uT[:, hf, :], dT[:, hf:hf + 1], psYh[hf][:], ALU.mult, ALU.add)
                P.act(gyb[:], yS[:], AF.Gelu_apprx_tanh)
                for mc in range(2):
                    for fc in range(2):
                        P.mm(psZ[mc][:], Wg[:, fc, mc * 128:(mc + 1) * 128], gyb[:, fc, :], start=(fc == 0), stop=(fc == 1))
                    P.act(sg[mc][:], psZ[mc][:], AF.Sigmoid, bias=gb[:, mc:mc + 1])
                    P.tt("dve", oT[:, mc, :], gyb[:, mc, :], sg[mc][:], ALU.mult)
                    P.act(sqm[mc][:], oT[:, mc, :], AF.Square)
                    P.mm(psZ[2][:], ones[:], sqm[mc][:], start=(mc == 0), stop=(mc == 1))
                P.act(rs[:], psZ[2][:], AF.Sqrt, bias=EPS, scale=1.0 / 256)
                P.recip(rs[:], rs[:])
                yb = yTb[w % 2]
                for mc in range(2):
                    P.stt(yb[:, mc, :], oT[:, mc, :], mgT[:, l, 2 + mc:3 + mc], rs[:], ALU.mult, ALU.mult)
                if "ys5" in dbg:
                    P.copy("dve", ys5db[:, :, a:b], yb[:])
                out_proj(l, sq, Wo, 2, yb, w, [psZ[3]])
            if "ys5" in dbg:
                P.store(dbg["ys5"], ys5db[:])
                dbgtoks.append(ys5db.toks[0])
            P.barrier([W.toks[0], Wo.toks[0], Wg.toks[0], gb.toks[0], dT.toks[0]])

        def sgu_stage(l, sq, c3, banks=None, run=True):
            def sbt(name, shape, dt, n=1):
                r = [T(c3.enter_context(SBT("sg_%s%d" % (name, i), shape, dt)), "sg_%s%d" % (name, i)) for i in range(n)]
                return r if n > 1 else r[0]
            if banks is None:
                psA, psB, psT, psS, psO, psU, psY, psX = ps
                psSv = psS[:, 0:256]
                pYb_ = psY
            else:
                psA, psSY, psX = banks
                psT = psA
                psSv = psSY[:, 0:256]
                pYb_ = psSY[:].bitcast(BF16)[:, 512:768]
            W = sbt("W", [128, KC, 512], BF16)
            load_w(W, "w_in", l, 0, KC, 1280, 512)
            sgt = sbt("sgt", [128, 256], F32)
            P.load(sgt[:], dr["sgu_norm_g"][l:l + 1, :].to_broadcast([128, 256]))
            wraw = sbt("wraw", [128, 4, 128], F32)
            P.load(wraw[:], dr["sgu_w"][l].rearrange("g t s -> t g s"))
            bT = sbt("bT", [128, 4], F32)
            P.load(bT[:], dr["sgu_b"][l].rearrange("g t -> t g"))
            wT = sbt("wT", [128, 4, 128], BF16)
            P.tt("dve", wraw[:], wraw[:], tril[:].m(lambda a: a.unsqueeze(1)).bc([128, 4, 128]), ALU.mult)
            for g in range(4):
                P.tr(psT[:, g * 128:(g + 1) * 128], wraw[:, g, :], ident[:])
            P.copy("dve", wT[:], psT[:].re("p (g t) -> p g t", t=128))
            tail = MixerTail(l, sq, 2, c3, pYb_, psX)
            u = sbt("u", [128, 256], F32, 2)
            gv = sbt("gv", [128, 256], F32, 2)
            sqv = sbt("sqv", [128, 256], F32, 2)
            ss = sbt("ss", [128, 4], F32, 2)
            vt = sbt("vt", [128, 256], BF16, 2)
            A_ = sbt("A", [128, 256], F32, 2)

            def body(t):
                s2 = t % 2
                proj_tok(psA, W, 0, 512, t)
                yield
                P.act(u[s2][:], psA[:, 0:256], AF.Gelu_apprx_tanh)
                P.act(gv[s2][:], psA[:, 256:512], AF.Gelu_apprx_tanh)
                P.act(sqv[s2][:], gv[s2][:], AF.Square)
                yield
                P.red(ss[s2][:], sqv[s2][:].re("p (h d) -> p h d", d=64), ALU.add)
                P.act(ss[s2][:], ss[s2][:], AF.Sqrt, bias=EPS, scale=1.0 / 64)
                yield
                P.recip(ss[s2][:], ss[s2][:])
                P.tt("dve", gv[s2][:].re("p (h d) -> p h d", d=64), gv[s2][:].re("p (h d) -> p h d", d=64),
                     ss[s2][:].m(lambda a: a.unsqueeze(2)).bc([128, 4, 64]), ALU.mult)
                yield
                P.tt("pool", vt[s2][:], gv[s2][:], sgt[:], ALU.mult)
                yield "mark"
                for g in range(4):
                    P.mm(psSv[:, g * 64:(g + 1) * 64], wT[:, g, :], vt[s2][:, g * 64:(g + 1) * 64])
                yield
                P.tt("dve", A_[s2][:].re("p (h d) -> p h d", d=64), psSv.re("p (h d) -> p h d", d=64),
                     bT[:].m(lambda a: a.unsqueeze(2)).bc([128, 4, 64]), ALU.add)
                yield
                P.tt("pool", A_[s2][:], A_[s2][:], u[s2][:], ALU.mult)
                yield
                yield from tail.run_g(t, A_[s2])
            if not run:
                return body
            run_pipelined(body, NT)
            P.barrier([W.toks[0], tail.Wo.toks[0], sgt.toks[0], wraw.toks[0], bT.toks[0]])

        def ffn_stage(l, sq, c3):
            def sbt(name, shape, dt, n=1):
                r = [T(c3.enter_context(SBT("ff_%s%d" % (name, i), shape, dt)), "ff_%s%d" % (name, i)) for i in range(n)]
                return r if n > 1 else r[0]
            FT = 1024
            NSC = FT // 512
            h2 = sbt("h2", [128, KC, FT], BF16, 1)
            zT = sbt("zT", [128, NFF, FT], BF16, 1)
            Wu = sbt("Wu", [128, KC, 512], BF16, 2)
            Wd = sbt("Wd", [128, NFF, 128], BF16, 2)
            cw = sbt("cw", [128, 3, 2 * NFF], F32)
            cb = sbt("cb", [128, 2 * NFF], F32)
            P.load(cw[:], dr["ffn_conv_w"][l].rearrange("k (c p) -> p k c", p=128))
            P.load(cb[:], dr["ffn_conv_b"][l].rearrange("(c p) -> p c", p=128))
            atail = sbt("atail", [128, 2 * NFF, 2], F32)
            P.memset("dve", atail[:], 0.0)
            cg = sbt("cg", [128, 512], F32, 2)
            cu = sbt("cu", [128, 512], F32, 2)
            sqb = sbt("nsq", [128, 512], BF16, 2)
            rs = sbt("nrs", [128, 512], F32)
            tmp = sbt("ntm", [128, 512], F32, 2)
            mb = l * 48 + 24
            P.stt(scl[:, 1, :], modT[:, mb + 8:mb + 16, sq], 1.0, g2n[:, l, :], ALU.add, ALU.mult)
            nq = 0
            na = 0
            ng_ = 0
            nd = 0
            npb = 0
            pend = [None]
            stgF = list(stg) + sbt("stgx", [128, 1408], F32, 2)

            def load_up(g):
                Wg = Wu[g % 2]
                for side in range(2):
                    for k0 in range(0, KC, 4):
                        st = stgF[stgn["n"] % 4]
                        stgn["n"] += 1
                        c0 = side * DFF + g * 256
                        P.load(st[:, 0:1024].re("p (k c) -> p k c", c=256),
                               dr["ffn_w_up"][l, k0 * 128:(k0 + 4) * 128, c0:c0 + 256].rearrange("(k p) c -> p k c", p=128))
                        P.copy("act", Wg[:, k0:k0 + 4, side * 256:(side + 1) * 256], st[:, 0:1024].re("p (k c) -> p k c", c=256))

            def load_dn(dc):
                Wdb = Wd[dc % 2]
                for j0 in range(0, NFF, 11):
                    st = stgF[stgn["n"] % 4]
                    stgn["n"] += 1
                    P.load(st[:, 0:11 * 128].re("p (j c) -> p j c", c=128),
                           dr["ffn_w_down"][l, j0 * 128:(j0 + 11) * 128, dc * 128:(dc + 1) * 128].rearrange("(j p) c -> p j c", p=128))
                    P.copy("act", Wdb[:, j0:j0 + 11, :], st[:, 0:11 * 128].re("p (j c) -> p j c", c=128))

            for tp in range(S // FT):
                for sc in range(NSC):
                    a, b = tp * FT + sc * 512, tp * FT + (sc + 1) * 512
                    pss = ps[6 + sc % 2]
                    for kc in range(KC):
                        sb_ = sqb[nq % 2]
                        nq += 1
                        P.act(sb_[:], tv(xT, kc, a, b), AF.Square)
                        P.mm(pss[:], onesb[:], sb_[:], start=(kc == 0), stop=(kc == KC - 1))
                    P.act(rs[:], pss[:], AF.Sqrt, bias=EPS, scale=1.0 / D)
                    P.recip(rs[:], rs[:])
                    for kc in range(KC):
                        tm = tmp[kc % 2]
                        P.tt("dve", tm[:], tv(xT, kc, a, b), rs[:], ALU.mult)
                        P.act(h2[:, kc, sc * 512:(sc + 1) * 512], tm[:], AF.Identity, scale=scl[:, 1, kc:kc + 1],
                              bias=modT[:, mb + kc, sq:sq + 1])
                for g in range(NFF // 2):
                    Wg = Wu[g % 2]
                    if g == 0 and tp == 0:
                        load_up(0)
                    if g + 1 < NFF // 2:
                        load_up(g + 1)
                    else:
                        load_dn(0)
                    for jj in range(2):
                        j = g * 2 + jj
                        for sc in range(NSC):
                            res = []
                            for side in range(2):
                                pst = ps[npb % 6]
                                npb += 1
                                ch = j + side * NFF
                                for kc in range(KC):
                                    P.mm(pst[:], Wg[:, kc, side * 256 + jj * 128:side * 256 + (jj + 1) * 128],
                                         h2[:, kc, sc * 512:(sc + 1) * 512], start=(kc == 0), stop=(kc == KC - 1))
                                cc = (cg if side == 0 else cu)[(j * NSC + sc) % 2]
                                P.act(cc[:], pst[:], AF.Identity, scale=cw[:, 2, ch:ch + 1], bias=cb[:, ch:ch + 1])
                                P.stt(cc[:, 1:512], pst[:, 0:511], cw[:, 1, ch:ch + 1], cc[:, 1:512], ALU.mult, ALU.add)
                                P.stt(cc[:, 2:512], pst[:, 0:510], cw[:, 0, ch:ch + 1], cc[:, 2:512], ALU.mult, ALU.add)
                                P.stt(cc[:, 0:1], atail[:, ch, 1:2], cw[:, 1, ch:ch + 1], cc[:, 0:1], ALU.mult, ALU.add)
                                P.stt(cc[:, 0:2], atail[:, ch, 0:2], cw[:, 0, ch:ch + 1], cc[:, 0:2], ALU.mult, ALU.add)
                                P.copy("dve", atail[:, ch, :], pst[:, 510:512])
                                res.append(cc)
                            if pend[0] is not None:
                                pend[0]()

                            def fin(res=res, j=j, sc=sc):
                                P.act(res[0][:], res[0][:], AF.Silu)
                                P.tt("pool", zT[:, j, sc * 512:(sc + 1) * 512], res[0][:], res[1][:], ALU.mult)
                            pend[0] = fin
                if pend[0] is not None:
                    pend[0]()
                    pend[0] = None
                for dc in range(KC):
                    Wdb = Wd[dc % 2]
                    if dc + 1 < KC:
                        load_dn(dc + 1)
                    elif tp + 1 < S // FT:
                        load_up(0)
                    for sc in range(NSC):
                        a, b = tp * FT + sc * 512, tp * FT + (sc + 1) * 512
                        pst = ps[6 + (dc * NSC + sc) % 2]
                        for j in range(NFF):
                            P.mm(pst[:], Wdb[:, j, :], zT[:, j, sc * 512:(sc + 1) * 512], start=(j == 0), stop=(j == NFF - 1))
                        P.stt(tv(xT, dc, a, b), pst[:], modT[:, l * 48 + 40 + dc, sq:sq + 1], tv(xT, dc, a, b), ALU.mult, ALU.add)
            P.barrier()

        def nsa_stage(l, sq, c3):
            def mk(cx, pre):
                def sbt(name, shape, dt, n=1):
                    r = [T(cx.enter_context(SBT("%s_%s%d" % (pre, name, i), shape, dt)), "%s_%s%d" % (pre, name, i)) for i in range(n)]
                    return r if n > 1 else r[0]
                return sbt
            sbt = mk(c3, "ns")
            tail = MixerTail(l, sq, 3, c3, ps[6], ps[7])
            qT = sbt("qT", [128, 2, S], BF16)
            ksTz = sbt("ksTz", [128, 2, S], BF16)
            kwTz = sbt("kwTz", [128, 2, S], BF16)
            vs1 = sbt("vs1", [128, NT, 2, 65], BF16)
            vw1 = sbt("vw1", [128, NT, 2, 65], BF16)
            gates = sbt("gates", [128, NT, 12], F32)
            kcmpTz = sbt("kcmpTz", [128, 2, 128], BF16)
            vc1 = sbt("vc1", [128, 2, 97], BF16)
            qg = sbt("qg", [128, 64], F32)
            kg = sbt("kg", [128, 3, 64], F32)
            csc = sbt("csc", [128, 64], F32)
            P.memset("pool", ksTz[:], 0.0)
            P.memset("pool", kwTz[:], 0.0)
            P.memset("pool", kcmpTz[:], 0.0)
            P.memset("pool", vs1[:], 1.0)
            P.memset("pool", vw1[:], 1.0)
            P.load(qg[:], dr["nsa_q_norm_g"][l:l + 1, :].to_broadcast([128, 64]))
            P.ts("dve", qg[:], qg[:], 0.125, ALU.mult)
            P.load(kg[:], dr["nsa_k_norm_g"][l:l + 1, :, :].rearrange("o a d -> o (a d)").to_broadcast([128, 192]))
            P.load(csc[:], dr["cs_c"])

            def norm_rope(sb2, src, npart, nh, gain, cos, sin, outs, iv=lambda v: v):
                for _ in norm_rope_g(sb2, src, npart, nh, gain, cos, sin, outs, iv):
                    pass

            def norm_rope_g(sb2, src, npart, nh, gain, cos, sin, outs, iv=lambda v: v):
                sqb, ss, xn, tA, tB, tC, tD = sb2
                pp = slice(0, npart)
                P.act(sqb[pp, 0:nh, :], src, AF.Square)
                yield
                P.red(ss[pp, 0:nh], sqb[pp, 0:nh, :], ALU.add)
                P.act(ss[pp, 0:nh], ss[pp, 0:nh], AF.Sqrt, bias=EPS, scale=1.0 / 64)
                yield
                P.recip(ss[pp, 0:nh], ss[pp, 0:nh])
                P.tt("dve", xn[pp, 0:nh, :], src, ss[pp, 0:nh].m(lambda a: a.unsqueeze(2)).bc([npart, nh, 64]), ALU.mult)
                yield
                P.tt("pool", xn[pp, 0:nh, :], xn[pp, 0:nh, :], gain, ALU.mult)
                yield
                x1, x2 = xn[pp, 0:nh, 0:32], xn[pp, 0:nh, 32:64]
                P.tt("dve", tA[pp, 0:nh, :], x1, cos, ALU.mult)
                P.tt("pool", tB[pp, 0:nh, :], x2, sin, ALU.mult)
                P.tt("dve", tC[pp, 0:nh, :], x2, cos, ALU.mult)
                P.tt("pool", tD[pp, 0:nh, :], x1, sin, ALU.mult)
                yield
                P.tt("dve", outs(0), iv(tA[pp, 0:nh, :]), iv(tB[pp, 0:nh, :]), ALU.subtract)
                P.tt("dve", outs(1), iv(tC[pp, 0:nh, :]), iv(tD[pp, 0:nh, :]), ALU.add)
                yield

            with ExitStack() as c4:
                sb4 = mk(c4, "n4")
                W = sb4("W", [128, KC, 1036], BF16)
                load_w(W, "w_in", l, 0, KC, 1792, 1036)
                nrs = [(sb4("sqb%d" % i_, [128, 4, 64], F32), sb4("ss%d" % i_, [128, 4], F32), sb4("xn%d" % i_, [128, 4, 64], F32),
                        sb4("tA%d" % i_, [128, 4, 32], F32), sb4("tB%d" % i_, [128, 4, 32], F32),
                        sb4("tC%d" % i_, [128, 4, 32], F32), sb4("tD%d" % i_, [128, 4, 32], F32)) for i_ in range(2)]
                nr = nrs[0]
                with ExitStack() as c5:
                    sb5 = mk(c5, "n5")
                    kvT = sb5("kvT", [128, 2, S], BF16)
                    for c in range(4):
                        for cb in range(2):
                            pst = ps[(2 * c + cb) % 4]
                            for k8 in range(KC):
                                P.mm(pst[:], W[:, k8, 256 + cb * 128:256 + (cb + 1) * 128], tv(hTbox['h'], k8, c * 512, (c + 1) * 512),
                                     start=(k8 == 0), stop=(k8 == KC - 1))
                            P.copy("act" if cb else "dve", kvT[:, cb, c * 512:(c + 1) * 512], pst[:])
                    W1 = sb5("W1", [128, 32, 128], BF16)
                    W2 = sb5("W2", [128, 64], BF16)
                    w2s = sb5("w2s", [128, 64], F32)
                    peT = sb5("peT", [128, 32], F32)
                    peb = sb5("peb", [128, 32], BF16)
                    bias = sb5("bias", [128, 1], F32)
                    hidb = sb5("hidb", [128, 128], BF16)
                    kct = sb5("kct", [128, 2, 64], BF16)
                    for side in range(2):
                        w1v = dr["nsa_cmp_w1"][l, side].rearrange("(p d) h -> d p h", d=64)
                        for half in range(2):
                            for pc in range(0, 32, 8):
                                st = stg[stgn["n"] % 2]
                                stgn["n"] += 1
                                hs = slice(half * 64, (half + 1) * 64)
                                P.load(st[hs, 0:1024].re("d (p h) -> d p h", h=128), w1v[:, pc:pc + 8, :])
                                P.copy("pool", W1[hs, pc:pc + 8, :], st[hs, 0:1024].re("d (p h) -> d p h", h=128))
                        P.load(w2s[:], dr["nsa_cmp_w2"][l, side])
                        P.copy("pool", W2[:], w2s[:])
                        for half in range(2):
                            P.load(peT[half * 64:(half + 1) * 64, :], dr["nsa_cmp_pe"][l, side].rearrange("p d -> d p"))
                        P.copy("dve", peb[:], peT[:])
                        for p_ in range(32):
                            P.mm(ps[3][:, 0:1], W1[0:64, p_, :], peb[0:64, p_:p_ + 1], start=(p_ == 0), stop=(p_ == 31))
                        P.copy("dve", bias[:], ps[3][:, 0:1])
                        for kvh in range(2):
                            pst = ps[kvh]
                            hs = slice(kvh * 64, (kvh + 1) * 64)
                            for p_ in range(32):
                                P.mm(pst[:, 0:127], W1[hs, p_, :], kvT[hs, side, p_:p_ + 16 * 126 + 1:16],
                                     start=(p_ == 0), stop=(p_ == 31))
                            P.act(hidb[:, 0:127], pst[:, 0:127], AF.Gelu_apprx_tanh, bias=bias[:, 0:1])
                            P.mm(ps[2][0:127, kvh * 64:(kvh + 1) * 64], hidb[:, 0:127], W2[:])
                        if side == 0:
                            norm_rope(nr, ps[2][0:127, 0:128].re("p (h d) -> p h d", d=64), 127, 2,
                                      kg[0:127, 0, :].m(lambda a: a.unsqueeze(1)).bc([127, 2, 64]),
                                      csc[0:127, 0:32].m(lambda a: a.unsqueeze(1)).bc([127, 2, 32]),
                                      csc[0:127, 32:64].m(lambda a: a.unsqueeze(1)).bc([127, 2, 32]),
                                      lambda hf_: kct[0:127, :, hf_ * 32:(hf_ + 1) * 32])
                            pTb = ps[3][:].bitcast(BF16)
                            P.tr(pTb[:, 0:127], kct[0:127, :, :].re("p h d -> p (h d)"), identb[0:127, 0:127])
                            P.copy("act", kcmpTz[0:64, 0, 0:127], pTb[0:64, 0:127])
                            P.copy("act", kcmpTz[64:128, 1, 0:127], pTb[64:128, 0:127])
                        else:
                            P.copy("act", vc1[0:127, :, 0:64], ps[2][0:127, 0:128].re("p (h d) -> p h d", d=64))
                            o1f = sb5("o1f", [128, 33], F32)
                            P.load(o1f[:], dr["ovl1"])
                            P.copy("dve", vc1[0:127, :, 64:97], o1f[0:127, :].m(lambda a: a.unsqueeze(1)).bc([127, 2, 33]))
                    P.barrier([w2s.toks[0], peT.toks[0]])
                with ExitStack() as c5:
                    sb5 = mk(c5, "n6")
                    kk = sb5("kk", [128, 4, 64], F32, 2)
                    qtok = sb5("qtok", [128, 256], BF16, 2)
                    ktok = sb5("ktok", [128, 256], BF16, 2)
                    kg4 = sb5("kg4", [128, 4, 64], F32)
                    P.copy("dve", kg4[:, 0:2, :], kg[:, 1, :].m(lambda a: a.unsqueeze(1)).bc([128, 2, 64]))
                    P.copy("dve", kg4[:, 2:4, :], kg[:, 2, :].m(lambda a: a.unsqueeze(1)).bc([128, 2, 64]))
                    def pa_body(t):
                        s2 = t % 2
                        psA, psB, psC, psT = ps[0], ps[1], ps[2], ps[3]
                        proj_tok(psA, W, 0, 256, t)
                        yield
                        proj_tok(psB, W, 512, 512, t)
                        yield
                        proj_tok(psC, W, 1024, 12, t)
                        yield
                        cos = cs_tab[:, t, 0:32].m(lambda a: a.unsqueeze(1)).bc([128, 4, 32])
                        sin = cs_tab[:, t, 32:64].m(lambda a: a.unsqueeze(1)).bc([128, 4, 32])
                        qo = qtok[s2][:].re("p (g k two d) -> p k g two d", g=2, k=2, two=2, d=32)
                        yield from norm_rope_g(nrs[s2], psA[:, 0:256].re("p (h d) -> p h d", d=64), 128, 4,
                                  qg[:].m(lambda a: a.unsqueeze(1)).bc([128, 4, 64]), cos, sin,
                                  lambda hf_: qo[:, :, :, hf_, :], iv=lambda v: v.re("p (k g) d -> p k g d", g=2))
                        P.copy("act", kk[s2][:, 0:2, :], psB[:, 0:128].re("p (h d) -> p h d", d=64))
                        P.copy("act", kk[s2][:, 2:4, :], psB[:, 256:384].re("p (h d) -> p h d", d=64))
                        ko = ktok[s2][:].re("p (h two d) -> p h two d", two=2, d=32)
                        P.copy("act", vs1[:, t, :, 0:64], psB[:, 128:256].re("p (h d) -> p h d", d=64))
                        P.copy("act", vw1[:, t, :, 0:64], psB[:, 384:512].re("p (h d) -> p h d", d=64))
                        P.act(gates[:, t, :], psC[:, 0:12], AF.Sigmoid)
                        yield
                        yield "mark"
                        yield from norm_rope_g(nrs[s2], kk[s2][:], 128, 4, kg4[:], cos, sin, lambda hf_: ko[:, :, hf_, :])
                        pTb = psT[:].bitcast(BF16)
                        for j in range(2):
                            P.tr(pTb[:, j * 128:(j + 1) * 128], qtok[s2][:, j * 128:(j + 1) * 128], identb[:])
                            P.tr(pTb[:, (2 + j) * 128:(3 + j) * 128], ktok[s2][:, j * 128:(j + 1) * 128], identb[:])
                        yield
                        tsl = slice(t * 128, (t + 1) * 128)
                        P.copy("act", qT[:, :, tsl], pTb[:, 0:256].re("p (g i) -> p g i", i=128))
                        P.copy("dve", ksTz[0:64, 0, tsl], pTb[0:64, 256:384])
                        P.copy("dve", ksTz[64:128, 1, tsl], pTb[64:128, 256:384])
                        P.copy("dve", kwTz[0:64, 0, tsl], pTb[0:64, 384:512])
                        P.copy("dve", kwTz[64:128, 1, tsl], pTb[64:128, 384:512])
                    run_pipelined(pa_body, NT)
                    P.barrier()
                P.barrier([W.toks[0]])
            with ExitStack() as c4:
                sb4 = mk(c4, "n7")
                negv = sb4("negv", [128, S], BF16)
                for cc in range(0, S, 1024):
                    st = stg[stgn["n"] % 2]
                    stgn["n"] += 1
                    P.load(st[:, 0:1024], dr["negvalid"][:, cc:cc + 1024])
                    P.copy("pool", negv[:, cc:cc + 1024], st[:, 0:1024])
                negTf = sb4("negTf", [128, 2, 128], F32)
                negTb = sb4("negTb", [128, 2, 128], BF16)
                P.load(negTf[:], dr["negT"])
                P.copy("dve", negTb[:], negTf[:])
                emf = sb4("emf", [32, 16, 128], F32)
                emb = sb4("emb", [32, 16, 128], BF16)
                P.load(emf[:], dr["emat"])
                P.copy("dve", emb[:], emf[:])
                bonus = sb4("bonus", [128, NT, 32], F32)
                P.load(bonus[:], dr["bonus"])
                Ych = sb4("Ych", [128, 4, 256], F32, 2)
                eb = sb4("eb", [128, 512], BF16, 2)
                rsc = sb4("rsc", [128, 2, 4], F32)
                imp = sb4("imp", [128, 4, 32], F32)
                imp2 = sb4("imp2", [128, 4, 32], F32)
                m8 = sb4("m8", [128, 8], F32)
                selm = sb4("selm", [128, 32], F32)
                negsel = sb4("negsel", [128, 4, 32], BF16)
                nsT = sb4("nsT", [32, 512], BF16)
                coef = sb4("coef", [128, 4], F32, 2)
                tmpo = sb4("tmpo", [128, 4, 64], F32, 2)
                ne = 0
                for c in range(4):
                    q0, q1 = c * 512, (c + 1) * 512
                    Y = Ych[c % 2]
                    for kvh in range(2):
                        psOc = [ps[2], ps[3]]
                        for g in range(2):
                            psc = ps[ne % 2]
                            e_ = eb[ne % 2]
                            ne += 1
                            P.mm(psc[0:127, :], kcmpTz[:, kvh, 0:127], qT[:, g, q0:q1], start=True, stop=False)
                            P.mm(psc[0:127, :], identb[0:127, 0:127], negv[0:127, q0:q1], start=False, stop=True)
                            P.act(e_[0:127, :], psc[0:127, :], AF.Exp)
                            for j in range(4):
                                P.mm(psOc[g][:, j * 97:(j + 1) * 97], e_[0:127, j * 128:(j + 1) * 128], vc1[0:127, kvh, :])
                            ocv = psOc[g][:, 0:388].re("p (j e) -> p j e", e=97)
                            P.ts("dve", rsc[:, g, :], ocv[:, :, 96], 1e-30, ALU.add)
                            P.recip(rsc[:, g, :], rsc[:, g, :])
                        oc0 = psOc[0][:, 0:388].re("p (j e) -> p j e", e=97)
                        oc1 = psOc[1][:, 0:388].re("p (j e) -> p j e", e=97)
                        P.tt("dve", imp[:], oc0[:, :, 64:96], rsc[:, 0, :].m(lambda a: a.unsqueeze(2)).bc([128, 4, 32]), ALU.mult)
                        P.tt("dve", imp2[:], oc1[:, :, 64:96], rsc[:, 1, :].m(lambda a: a.unsqueeze(2)).bc([128, 4, 32]), ALU.mult)
                        P.tt("dve", imp[:], imp[:], imp2[:], ALU.add)
                        P.tt("dve", imp[:], imp[:], bonus[:, 4 * c:4 * c + 4, :], ALU.add)
                        pTb = ps[6][:].bitcast(BF16)
                        for j in range(4):
                            P.op("dve", lambda e, j=j: e.max(out=m8.h[:], in_=imp.h[:, j, :]), reads=imp.toks, writes=m8.toks)
                            P.ts("dve", selm[:], imp[:, j, :], m8[:, 7:8], ALU.is_ge)
                            P.ts("dve", negsel[:, j, :], selm[:], -NEGB, ALU.mult, NEGB, ALU.add)
                            P.tr(pTb[0:32, j * 128:(j + 1) * 128], negsel[:, j, :], identb[:])
                        P.copy("act", nsT[:], pTb[0:32, 0:512])
                        for g in range(2):
                            qh = 2 * kvh + g
                            ocv = psOc[g][:, 0:388].re("p (j e) -> p j e", e=97)
                            cf = coef[g]
                            P.tt("dve", cf[:], rsc[:, g, :], gates[:, 4 * c:4 * c + 4, qh * 3 + 0], ALU.mult)
                            P.tt("dve", Y[:, :, qh * 64:(qh + 1) * 64], ocv[:, :, 0:64],
                                 cf[:].m(lambda a: a.unsqueeze(2)).bc([128, 4, 64]), ALU.mult)
                        items = []
                        for g in range(2):
                            qh = 2 * kvh + g
                            for br, (KTz, V1, pso) in enumerate(((ksTz, vs1, ps[4]), (kwTz, vw1, ps[5]))):
                                kb0 = 0 if br == 0 else max(0, 4 * c - 4)
                                kbs = list(range(kb0, 4 * c + 4))
                                for ii, kb in enumerate(kbs):
                                    items.append((g, qh, br, KTz, V1, pso, kb, ii == 0, ii == len(kbs) - 1))

                        def mkA(it_, slot):
                            g, qh, br, KTz, V1, pso, kb, isfirst, islast = it_
                            r = kb - 4 * c
                            ja = max(0, r)
                            jb = 3 if br == 0 else min(3, r + 4)
                            ca, cbn = ja * 128, (jb + 1) * 128
                            pss, e_ = ps[slot], eb[slot]

                            def A():
                                P.mm(pss[:, ca:cbn], KTz[:, kvh, kb * 128:(kb + 1) * 128], qT[:, g, q0 + ca:q0 + cbn],
                                     start=True, stop=False)
                                if br == 0:
                                    P.mm(pss[:, ca:cbn], emb[:, kb, :], nsT[:, ca:cbn], start=False, stop=(r < 0))
                                if r >= 0:
                                    P.mm(pss[:, r * 128:(r + 1) * 128], identb[:], negTb[:, 0, :], start=False, stop=True)
                                if br == 1 and r <= -1:
                                    P.mm(pss[:, (r + 4) * 128:(r + 5) * 128], identb[:], negTb[:, 1, :], start=False, stop=True)
                                P.act(e_[:, ca:cbn], pss[:, ca:cbn], AF.Exp)

                            def B():
                                first = isfirst
                                for j in range(ja, jb + 1):
                                    P.mm(pso[:, j * 65:(j + 1) * 65], e_[:, j * 128:(j + 1) * 128], V1[:, kb, kvh, :],
                                         start=first, stop=(kb == 4 * c + j), skip_group_check=True)
                                    first = False
                                if islast:
                                    ov = pso[:, 0:260].re("p (j e) -> p j e", e=65)
                                    cf = coef[br]
                                    P.recip(cf[:], ov[:, :, 64])
                                    P.tt("dve", cf[:], cf[:], gates[:, 4 * c:4 * c + 4, qh * 3 + 1 + br], ALU.mult)
                                    P.tt("dve", tmpo[br][:], ov[:, :, 0:64], cf[:].m(lambda a: a.unsqueeze(2)).bc([128, 4, 64]), ALU.mult)
                                    P.tt("pool", Y[:, :, qh * 64:(qh + 1) * 64], Y[:, :, qh * 64:(qh + 1) * 64], tmpo[br][:], ALU.add)
                            return A, B
                        prevB = None
                        for it_ in items:
                            A, B = mkA(it_, ne % 2)
                            ne += 1
                            A()
                            if prevB is not None:
                                prevB()
                            prevB = B
                        if prevB is not None:
                            prevB()
                    for j in range(4):
                        tail.run(4 * c + j, Y[:, j, :])
                P.barrier([negTf.toks[0], emf.toks[0], bonus.toks[0]])
            P.barrier([tail.Wo.toks[0], qg.toks[0], kg.toks[0], csc.toks[0]])

        outtoks = []
        for sq in range(nseq):
            with ExitStack() as c3:
                load_x(sq, c3)
            for l in range(nlayers):
              with ExitStack() as c4:
                hTbox['h'] = T(c4.enter_context(SBT("hT_%d_%d" % (sq, l), [128, KC, S], BF16)), "hT", NT)
                with ExitStack() as c3:
                    norm_stage(l, sq, 0, c3)
                if "hT" in dbg and l == 0 and sq == 0:
                    with ExitStack() as c3:
                        hd = T(c3.enter_context(SBT("hd", [128, KC, S], F32)), "hd")
                        P.copy("dve", hd[:], hTbox['h'][:])
                        P.store(dbg["hT"], hd[:])
                        P.barrier([hd.toks[0]])
                if "ret" in stages and "sgu" in stages and OVERLAP:
                    with ExitStack() as c3:
                        rb = retention_stage(l, sq, c3, banks=ps[0:6], run=False)
                        sb_ = sgu_stage(l, sq, c3, banks=(ps[6], ps[7], ps[5]), run=False)
                        g1_ = pipelined_gen(rb, NT)
                        g2_ = pipelined_gen(sb_, NT)
                        d1 = d2 = False
                        while not (d1 and d2):
                            if not d1:
                                d1 = next(g1_, "END") == "END"
                            if not d2:
                                d2 = next(g2_, "END") == "END"
                        P.barrier()
                elif "ret" in stages:
                    with ExitStack() as c3:
                        retention_stage(l, sq, c3)
                if "s5" in stages:
                    with ExitStack() as c3:
                        s5_stage(l, sq, c3)
                if "sgu" in stages and not ("ret" in stages and OVERLAP):
                    with ExitStack() as c3:
                        sgu_stage(l, sq, c3)
                if "nsa" in stages:
                    with ExitStack() as c3:
                        nsa_stage(l, sq, c3)
                P.barrier()
              if True:
                if "ffn" in stages:
                    with ExitStack() as c3:
                        ffn_stage(l, sq, c3)
            if "xT" in dbg and sq == 0:
                P.store(dbg["xT"], xT[:])
                dbgtoks.append(xT.toks[0])
            with ExitStack() as c3:
                outtoks += store_x(sq, c3)
        P.finish(outtoks + dbgtoks)
        print("instructions:", P.nins, "sems:", P.nsem)
    return nc


_CACHE = {}


def kernel(**inputs):
    n = 8
    if "nc" not in _CACHE:
        _CACHE["nc"] = build(nseq=2, nlayers=DEPTH, wl=DEPTH)
        _CACHE["consts"] = host_consts()
    nc = _CACHE["nc"]
    consts = _CACHE["consts"]
    x = np.asarray(inputs["x"], dtype=np.float32)
    c = np.asarray(inputs["c"], dtype=np.float32)
    w = {k: np.ascontiguousarray(np.asarray(inputs[k], dtype=np.float32)) for k in WEIGHT_SHAPES}
    in_maps = []
    for i in range(n):
        m = {"x": np.ascontiguousarray(x[2 * i:2 * i + 2]), "c": np.ascontiguousarray(c[2 * i:2 * i + 2])}
        m.update(w)
        m.update(consts)
        in_maps.append(m)
    res = run_bass_kernel_spmd(nc, in_maps, core_ids=list(range(n)))
    return np.concatenate([np.asarray(r["out"], dtype=np.float32) for r in res.results], axis=0)
```
